# Optimizing a Trainium2 kernel written in Bass

```python
import jax, jax.numpy as jnp
from jax import lax
import numpy as np

D_MODEL = 2048
BATCH = 4
SEQ = 2048
DEPTH = 4

MIX_WIDTH = D_MODEL
HEAD_DIM = 128
SB_HEADS = 8
SB_WIDTH = SB_HEADS * HEAD_DIM
SB_BLOCK = 128
GLA_HEADS = 4
GLA_HEAD_V = HEAD_DIM
GLA_HEAD_K = HEAD_DIM // 2
GLA_WIDTH = GLA_HEADS * GLA_HEAD_V
GLA_KEY_WIDTH = GLA_HEADS * GLA_HEAD_K
GLA_GATE_RANK = 16
GLA_GATE_TAU = 16.0
GLA_CHUNK = 64
SGU_GROUPS = 4
SGU_WIDTH = MIX_WIDTH - SB_WIDTH - GLA_WIDTH
SGU_GROUP_DIM = SGU_WIDTH // SGU_GROUPS
SGU_CHUNK = 128
N_MIX_HEADS = MIX_WIDTH // HEAD_DIM
N_MEM = 256
XA_HEADS = 4
XA_HEAD_DIM = 128
XA_WIDTH = XA_HEADS * XA_HEAD_DIM
EPS = 1e-6

IN_SPLITS = [SB_WIDTH, SB_WIDTH, SB_WIDTH, SB_WIDTH,
             GLA_KEY_WIDTH, GLA_KEY_WIDTH, GLA_WIDTH, GLA_GATE_RANK, GLA_WIDTH,
             SGU_WIDTH, SGU_WIDTH, SGU_WIDTH]
IN_WIDTH = sum(IN_SPLITS)
IN_OFFSETS = [int(o) for o in np.cumsum(IN_SPLITS)[:-1]]

kernel_name = "hybrid_sb_gla_sgu_parallel_heads"


def rmsnorm(x, g):
    xf = x.astype(jnp.float32)
    y = xf * lax.rsqrt(jnp.mean(xf * xf, axis=-1, keepdims=True) + EPS)
    return (y * g.astype(jnp.float32)).astype(x.dtype)


def to_heads(t, n_heads):
    b, s, _ = t.shape
    return t.reshape(b, s, n_heads, -1).transpose(0, 2, 1, 3)


def from_heads(t):
    b, h, s, d = t.shape
    return t.transpose(0, 2, 1, 3).reshape(b, s, h * d)


def stick_breaking_attention(q, k, v):
    B, H, S, d = q.shape
    nblk = S // SB_BLOCK
    scale = d ** -0.5
    qb = q.reshape(B, H, nblk, SB_BLOCK, d).transpose(2, 0, 1, 3, 4)
    key_pos = jnp.arange(S)

    def block(args):
        q_blk, i = args
        z = jnp.einsum('bhqd,bhkd->bhqk', q_blk, k).astype(jnp.float32) * scale
        q_pos = i * SB_BLOCK + jnp.arange(SB_BLOCK)
        causal = key_pos[None, :] < q_pos[:, None]
        log_keep = jnp.where(causal, jax.nn.log_sigmoid(-z), 0.0)
        suffix = lax.cumsum(log_keep, axis=3, reverse=True) - log_keep
        a = jnp.where(causal, jnp.exp(jax.nn.log_sigmoid(z) + suffix), 0.0)
        return jnp.einsum('bhqk,bhkd->bhqd', a.astype(v.dtype), v)

    out = lax.map(block, (qb, jnp.arange(nblk)))
    return out.transpose(1, 2, 0, 3, 4).reshape(B, H, S, d)


def gla_chunked(q, k, v, log_alpha):
    B, H, S, dk = q.shape
    dv = v.shape[-1]
    C = GLA_CHUNK
    N = S // C
    qc = q.reshape(B, H, N, C, dk).astype(jnp.float32) * (dk ** -0.5)
    kc = k.reshape(B, H, N, C, dk).astype(jnp.float32)
    vc = v.reshape(B, H, N, C, dv).astype(jnp.float32)
    bcum = jnp.cumsum(log_alpha.reshape(B, H, N, C, dk).astype(jnp.float32), axis=3)
    tril = jnp.tril(jnp.ones((C, C), dtype=bool))
    rel = bcum[:, :, :, :, None, :] - bcum[:, :, :, None, :, :]
    decay = jnp.exp(jnp.where(tril[:, :, None], rel, -jnp.inf))
    attn = jnp.einsum('bhntd,bhnsd,bhntsd->bhnts', qc, kc, decay)
    o_intra = jnp.einsum('bhnts,bhnsv->bhntv', attn, vc)
    q_dec = qc * jnp.exp(bcum)
    b_last = bcum[:, :, :, -1:, :]
    k_dec = kc * jnp.exp(b_last - bcum)
    chunk_decay = jnp.exp(b_last[:, :, :, 0, :])
    xs = (jnp.moveaxis(q_dec, 2, 0), jnp.moveaxis(k_dec, 2, 0),
          jnp.moveaxis(vc, 2, 0), jnp.moveaxis(chunk_decay, 2, 0))

    def step(state, inp):
        qn, kn, vn, dn = inp
        o = jnp.einsum('bhtd,bhdv->bhtv', qn, state)
        state = dn[..., None] * state + jnp.einsum('bhsd,bhsv->bhdv', kn, vn)
        return state, o

    state0 = jnp.zeros((B, H, dk, dv), jnp.float32)
    _, o_inter = lax.scan(step, state0, xs)
    o = o_intra + jnp.moveaxis(o_inter, 0, 2)
    return o.reshape(B, H, S, dv).astype(v.dtype)


def chunked_sgu(u, v, g_norm, w_s, b_s):
    B, S, _ = v.shape
    N = S // SGU_CHUNK
    v = rmsnorm(v, g_norm)
    vb = v.reshape(B, N, SGU_CHUNK, SGU_GROUPS, SGU_GROUP_DIM)
    w = w_s * jnp.tril(jnp.ones((SGU_CHUNK, SGU_CHUNK), w_s.dtype))[None]
    mixed = jnp.einsum('gts,bnsgc->bntgc', w, vb) + b_s.T[None, None, :, :, None]
    return u * mixed.reshape(B, S, SGU_WIDTH)


def memory_cross_attention(h, m, w_q, w_kv, w_o):
    B, S, _ = h.shape
    q = (h @ w_q).reshape(B, S, XA_HEADS, XA_HEAD_DIM)
    k, v = jnp.split(m @ w_kv, 2, axis=-1)
    k = k.reshape(B, -1, XA_HEADS, XA_HEAD_DIM)
    v = v.reshape(B, -1, XA_HEADS, XA_HEAD_DIM)
    s = jnp.einsum('bqhd,bkhd->bhqk', q, k).astype(jnp.float32) * (XA_HEAD_DIM ** -0.5)
    p = jax.nn.softmax(s, axis=-1).astype(v.dtype)
    o = jnp.einsum('bhqk,bkhd->bqhd', p, v).reshape(B, S, XA_WIDTH)
    return o @ w_o


def setup_inputs(seed: int = 0) -> dict:
    key = jax.random.key(seed)
    ks = jax.random.split(key, 17)
    f32 = jnp.float32
    nrm = lambda k, shape, s: jax.random.normal(k, shape, f32) * s
    return {
        "x": nrm(ks[0], (BATCH, SEQ, D_MODEL), 1.0),
        "mem": nrm(ks[1], (BATCH, N_MEM, D_MODEL), 1.0),
        "norm_mix": 1.0 + nrm(ks[2], (DEPTH, D_MODEL), 0.02),
        "w_in": nrm(ks[3], (DEPTH, D_MODEL, IN_WIDTH), D_MODEL ** -0.5),
        "w_gla_gate_up": nrm(ks[4], (DEPTH, GLA_GATE_RANK, GLA_KEY_WIDTH), GLA_GATE_RANK ** -0.5),
        "b_gla_gate": nrm(ks[5], (DEPTH, GLA_KEY_WIDTH), 0.1),
        "sgu_norm": 1.0 + nrm(ks[6], (DEPTH, SGU_WIDTH), 0.02),
        "w_sgu": nrm(ks[7], (DEPTH, SGU_GROUPS, SGU_CHUNK, SGU_CHUNK), SGU_CHUNK ** -0.5),
        "b_sgu": 1.0 + nrm(ks[8], (DEPTH, SGU_GROUPS, SGU_CHUNK), 0.1),
        "out_norm": 1.0 + nrm(ks[9], (DEPTH, MIX_WIDTH), 0.02),
        "w_out": nrm(ks[10], (DEPTH, MIX_WIDTH, D_MODEL), MIX_WIDTH ** -0.5),
        "norm_xattn": 1.0 + nrm(ks[11], (DEPTH, D_MODEL), 0.02),
        "norm_mem": 1.0 + nrm(ks[12], (DEPTH, D_MODEL), 0.02),
        "w_xq": nrm(ks[13], (DEPTH, D_MODEL, XA_WIDTH), D_MODEL ** -0.5),
        "w_xkv": nrm(ks[14], (DEPTH, D_MODEL, 2 * XA_WIDTH), D_MODEL ** -0.5),
        "w_xo": nrm(ks[15], (DEPTH, XA_WIDTH, D_MODEL), XA_WIDTH ** -0.5),
        "final_norm": 1.0 + nrm(ks[16], (D_MODEL,), 0.02),
    }


def reference(x, mem, norm_mix, w_in, w_gla_gate_up, b_gla_gate, sgu_norm, w_sgu, b_sgu,
              out_norm, w_out, norm_xattn, norm_mem, w_xq, w_xkv, w_xo, final_norm):
    B, S, _ = x.shape
    for l in range(DEPTH):
        h = rmsnorm(x, norm_mix[l])
        (sb_q, sb_k, sb_v, sb_g,
         gla_q, gla_k, gla_v, gla_r, gla_g,
         sgu_u, sgu_v, sgu_g) = jnp.split(h @ w_in[l], IN_OFFSETS, axis=-1)

        o_sb = from_heads(stick_breaking_attention(
            to_heads(sb_q, SB_HEADS), to_heads(sb_k, SB_HEADS), to_heads(sb_v, SB_HEADS)))

        gate_logits = (gla_r @ w_gla_gate_up[l] + b_gla_gate[l]).astype(jnp.float32)
        log_alpha = jax.nn.log_sigmoid(gate_logits) / GLA_GATE_TAU
        o_gla = from_heads(gla_chunked(
            to_heads(gla_q, GLA_HEADS), to_heads(gla_k, GLA_HEADS),
            to_heads(gla_v, GLA_HEADS), to_heads(log_alpha, GLA_HEADS)))

        o_sgu = chunked_sgu(jax.nn.gelu(sgu_u), jax.nn.gelu(sgu_v), sgu_norm[l], w_sgu[l], b_sgu[l])

        mix = jnp.concatenate([o_sb, o_gla, o_sgu], axis=-1)
        mix = rmsnorm(mix.reshape(B, S, N_MIX_HEADS, HEAD_DIM),
                      out_norm[l].reshape(N_MIX_HEADS, HEAD_DIM)).reshape(B, S, MIX_WIDTH)
        gate = jax.nn.silu(jnp.concatenate([sb_g, gla_g, sgu_g], axis=-1))
        x = x + (mix * gate) @ w_out[l]
        x = x + memory_cross_attention(rmsnorm(x, norm_xattn[l]), rmsnorm(mem, norm_mem[l]),
                                       w_xq[l], w_xkv[l], w_xo[l])
    return rmsnorm(x, final_norm)
```

```python
import numpy as np
from contextlib import ExitStack
import concourse.bass as bass
import concourse.mybir as mybir
from concourse.bass_utils import run_bass_kernel_spmd

F32 = mybir.dt.float32
BF16 = mybir.dt.bfloat16
AF = mybir.ActivationFunctionType
ALU = mybir.AluOpType

D = 2048
S = 2048
L = 4
NM = 256
NCH = 16
TB = 512
NTB = 4
EPS = 1e-6
NEG = -30000.0
GELU_C = 1.5957691216057308

SPLIT = True
N_CORES = 8


def core_units(hf, split):
    if split:
        return list(range(4 * hf, 4 * hf + 4)), [hf], [2 * hf, 2 * hf + 1]
    return list(range(8)), [0, 1], [0, 1, 2, 3]


def col_slots(hf, split):
    sbh, pairs, groups = core_units(hf, split)
    slots = []
    for h in sbh:
        slots.append([(128 * h, 128), (1024 + 128 * h, 128), (2048 + 128 * h, 128), (3072 + 128 * h, 128)])
    for p in pairs:
        slots.append([(4096 + 128 * p, 128), (4352 + 128 * p, 128), (4608 + 256 * p, 256)])
        slots.append([(5136 + 256 * p, 256), (5120, 16), (None, 240)])
    gord = list(groups) + [g for g in range(4) if g not in groups]
    slots.append([(6160 + 128 * g, 128) for g in gord])
    for i in range(0, len(groups), 2):
        g0, g1 = groups[i], groups[i + 1]
        slots.append([(5648 + 128 * g0, 128), (6672 + 128 * g0, 128), (5648 + 128 * g1, 128), (6672 + 128 * g1, 128)])
    return slots


def local_heads(hf, split):
    sbh, pairs, groups = core_units(hf, split)
    return list(sbh) + [8 + 2 * p + h for p in pairs for h in range(2)] + [12 + g for g in groups]


def cc_groups(split):
    return [[0, 1, 2, 3], [4, 5], [6, 7]] if split else []


def chunk_order(split):
    if not split:
        return list(range(16))
    order = []
    for grp in cc_groups(split):
        for r in range(2):
            lh = local_heads(r, split)
            order += [lh[li] for li in grp]
    return order


def vec_layout(split):
    off = {}
    n = 0
    for nm in ("nmix", "nxa", "nmem"):
        off[nm] = n
        n += L * 16
    off["fin"] = n
    n += 16
    off["onorm"] = n
    n += L * 16
    off["bgate"] = n
    n += L * 2
    return off, n


def pack_inputs(inputs, split, nlw=L, ncores=N_CORES, lite=False):
    f = np.float32
    x = np.asarray(inputs["x"], f)
    mem = np.asarray(inputs["mem"], f)
    w_in = np.asarray(inputs["w_in"], f)[:nlw]
    voff, nv = vec_layout(split)
    per_half = {}
    for hf in (0, 1):
        sbh, pairs, groups = core_units(hf, split)
        slots = col_slots(hf, split)
        ncol = 512 * len(slots)
        wl = np.zeros((nlw, D, ncol), f)
        c = 0
        for sl in slots:
            for (st, w) in sl:
                if st is not None:
                    wl[:, :, c:c + w] = w_in[:, :, st:st + w]
                c += w
        vec = np.zeros((128, nv), f)
        for l in range(L):
            vec[:, voff["nmix"] + 16 * l: voff["nmix"] + 16 * l + 16] = np.asarray(inputs["norm_mix"], f)[l].reshape(16, 128).T
            vec[:, voff["nxa"] + 16 * l: voff["nxa"] + 16 * l + 16] = np.asarray(inputs["norm_xattn"], f)[l].reshape(16, 128).T
            vec[:, voff["nmem"] + 16 * l: voff["nmem"] + 16 * l + 16] = np.asarray(inputs["norm_mem"], f)[l].reshape(16, 128).T
            on = np.asarray(inputs["out_norm"], f)[l].reshape(16, 128)
            for li, m in enumerate(local_heads(hf, split)):
                vec[:, voff["onorm"] + 16 * l + li] = on[m]
            bg = np.asarray(inputs["b_gla_gate"], f)[l].reshape(2, 128)
            for j, p in enumerate(pairs):
                vec[:, voff["bgate"] + 2 * l + j] = bg[p]
        vec[:, voff["fin"]: voff["fin"] + 16] = np.asarray(inputs["final_norm"], f).reshape(16, 128).T
        wup = np.asarray(inputs["w_gla_gate_up"], f).reshape(L, 16, 2, 128)[:nlw, :, pairs, :].reshape(nlw, 16, 128 * len(pairs))
        wsg = np.ascontiguousarray(np.asarray(inputs["w_sgu"], f)[:nlw, groups].transpose(0, 1, 3, 2))
        bsg = np.ascontiguousarray(np.asarray(inputs["b_sgu"], f)[:nlw, groups])
        gord = list(groups) + [g for g in range(4) if g not in groups]
        sgn = np.ascontiguousarray(np.asarray(inputs["sgu_norm"], f)[:nlw].reshape(nlw, 4, 128)[:, gord].reshape(nlw, 512))
        per_half[hf] = dict(w_in=np.ascontiguousarray(wl), vecs=vec, wup=np.ascontiguousarray(wup), wsg=wsg, bsg=bsg, sgn=sgn)
    shared = dict(
        w_out=np.ascontiguousarray(np.asarray(inputs["w_out"], f)[:nlw].reshape(nlw, 16, 128, D)[:, chunk_order(split)].reshape(nlw, D, D)),
        w_xq=np.ascontiguousarray(np.asarray(inputs["w_xq"], f)[:nlw]),
        w_xkv=np.ascontiguousarray(np.asarray(inputs["w_xkv"], f)[:nlw]),
        w_xo=np.ascontiguousarray(np.asarray(inputs["w_xo"], f)[:nlw]),
    )
    if lite:
        for k in ("w_out", "w_xq", "w_xkv", "w_xo"):
            shared[k] = np.zeros((1, 128, 128), f)
    maps = []
    for c in range(ncores):
        b, hf = c // 2, (c % 2 if split else 0)
        m = dict(xT=np.ascontiguousarray(x[b].T), memT=np.ascontiguousarray(mem[b].T))
        m.update(per_half[hf])
        m.update(shared)
        maps.append(m)
    return maps


class Sched:
    COMPUTE = ("pe", "act", "dve", "pool")
    QUEUES = ("sp", "pool")
    NSLOT = 6

    def __init__(self, nc, es):
        self.nc = nc
        self.streams = {e: [] for e in ("pe", "act", "dve", "pool", "sp")}
        self.sems = {}
        for e in self.COMPUTE:
            self.sems[("c", e)] = es.enter_context(nc.semaphore("c_" + e))
        for q in self.QUEUES:
            for i in range(self.NSLOT):
                self.sems[("d", q, i)] = es.enter_context(nc.semaphore("d_%s%d" % (q, i)))
        self.NCC = 4
        for i in range(self.NCC):
            self.sems[("k", i)] = es.enter_context(nc.semaphore("k_%d" % i))
        self.kgen = [0] * self.NCC
        self.knext = 0
        self.ccount = {e: 0 for e in self.COMPUTE}
        self.dgen = {q: [0] * self.NSLOT for q in self.QUEUES}
        self.dnext = {q: 0 for q in self.QUEUES}
        self.waited = {e: {} for e in self.streams}
        self.lastw = {}
        self.readers = {}
        self.nops = 0

    def _need(self, eng, tok, waits):
        sid, val = tok
        if self.waited[eng].get(sid, 0) < val:
            self.waited[eng][sid] = val
            waits.append((sid, val))

    def op(self, eng, fn, reads=(), writes=(), dma=False):
        waits = []
        deps = []
        for k in reads:
            t = self.lastw.get(k)
            if t is not None:
                deps.append(t)
        for k in writes:
            t = self.lastw.get(k)
            if t is not None:
                deps.append(t)
            deps.extend(self.readers.get(k, {}).values())
        for t in deps:
            if (not dma) and eng == "pe" and t[0] == ("c", "pe"):
                continue
            self._need(eng, t, waits)
        if dma == "cc":
            slot = self.knext
            self.knext = (slot + 1) % self.NCC
            sid = ("k", slot)
            if self.kgen[slot] > 0:
                self._need(eng, (sid, self.kgen[slot]), waits)
            self.kgen[slot] += 1
            tok = (sid, self.kgen[slot])
            inc = 1
        elif dma:
            slot = self.dnext[eng]
            self.dnext[eng] = (slot + 1) % self.NSLOT
            prev = self.dgen[eng][slot]
            sid = ("d", eng, slot)
            if prev > 0:
                self._need(eng, (sid, 16 * prev), waits)
            self.dgen[eng][slot] += 1
            tok = (sid, 16 * self.dgen[eng][slot])
            inc = 16
        else:
            self.ccount[eng] += 1
            sid = ("c", eng)
            tok = (sid, self.ccount[eng])
            inc = 1
        self.streams[eng].append((waits, fn, sid, inc))
        for k in reads:
            self.readers.setdefault(k, {})[sid] = tok
        for k in writes:
            self.lastw[k] = tok
            self.readers[k] = {}
        self.nops += 1
        return tok

    def final_wait(self, eng, toks):
        waits = []
        for t in toks:
            self._need(eng, t, waits)
        self.streams[eng].append((waits, None, None, 0))

    def emit(self, block):
        def mk(name):
            def f(eng):
                for waits, fn, sid, inc in self.streams[name]:
                    for (ws, val) in waits:
                        eng.wait_ge(self.sems[ws], val)
                    if fn is not None:
                        ins = fn(eng)
                        ins.then_inc(self.sems[sid], inc)
            return f
        block.tensor(mk("pe"))
        block.scalar(mk("act"))
        block.vector(mk("dve"))
        block.gpsimd(mk("pool"))
        block.sync(mk("sp"))


def build_program(split=SPLIT, nlayers=L, stop=None, debug=False, lite=False, ncores=N_CORES):
    LW = nlayers
    nc = bass.Bass("TRN2", target_bir_lowering=False)
    sbh, pairs, groups = core_units(0, split)
    NSB, NP, NG = len(sbh), len(pairs), len(groups)
    nslots = NSB + 2 * NP + 1 + NG // 2
    voff, nv = vec_layout(split)
    dk = "ExternalOutput" if debug else "Internal"

    xT = nc.dram_tensor("xT", [D, S], F32, kind="ExternalInput").ap()
    memT = nc.dram_tensor("memT", [D, NM], F32, kind="ExternalInput").ap()
    w_in = nc.dram_tensor("w_in", [LW, D, nslots * 512], F32, kind="ExternalInput").ap()
    vecs_d = nc.dram_tensor("vecs", [128, nv], F32, kind="ExternalInput").ap()
    wup_d = nc.dram_tensor("wup", [LW, 16, 128 * NP], F32, kind="ExternalInput").ap()
    wsg_d = nc.dram_tensor("wsg", [LW, NG, 128, 128], F32, kind="ExternalInput").ap()
    bsg_d = nc.dram_tensor("bsg", [LW, NG, 128], F32, kind="ExternalInput").ap()
    if lite:
        w_out = nc.dram_tensor("w_out", [1, 128, 128], F32, kind="ExternalInput").ap()
        w_xq = nc.dram_tensor("w_xq", [1, 128, 128], F32, kind="ExternalInput").ap()
        w_xkv = nc.dram_tensor("w_xkv", [1, 128, 128], F32, kind="ExternalInput").ap()
        w_xo = nc.dram_tensor("w_xo", [1, 128, 128], F32, kind="ExternalInput").ap()
    else:
        w_out = nc.dram_tensor("w_out", [LW, D, D], F32, kind="ExternalInput").ap()
        w_xq = nc.dram_tensor("w_xq", [LW, D, 512], F32, kind="ExternalInput").ap()
        w_xkv = nc.dram_tensor("w_xkv", [LW, D, 1024], F32, kind="ExternalInput").ap()
        w_xo = nc.dram_tensor("w_xo", [LW, 512, D], F32, kind="ExternalInput").ap()
    sgn_d = nc.dram_tensor("sgn", [LW, 512], F32, kind="ExternalInput").ap()
    yT = nc.dram_tensor("yT", [D, S], F32, kind="ExternalOutput").ap()
    xA = nc.dram_tensor("xA", [D, S], F32, kind=dk).ap()
    xB = nc.dram_tensor("xB", [D, S], F32, kind=dk).ap()
    NLH = NSB + 2 * NP + NG
    ccg = cc_groups(split)
    mgd = nc.dram_tensor("mgd", [NLH, 128, S], BF16, kind=("Internal" if split else dk)).ap()
    mga = [nc.dram_tensor("mga%d" % k, [2 * len(g) * 128, S], BF16).ap() for k, g in enumerate(ccg)]
    npairs_cc = ncores // 2
    rgroups = [[2 * i, 2 * i + 1] for i in range(npairs_cc)]
    hdbg = nc.dram_tensor("hdbg", [16, 128, S], BF16, kind=dk).ap() if debug else None

    def xv(ap):
        return ap.rearrange("(c p) s -> c p s", p=128)

    with ExitStack() as es:
        def sb(name, shape, dt):
            return es.enter_context(nc.sbuf_tensor(name, shape, dt))

        def ps(name, shape, dt):
            return es.enter_context(nc.psum_tensor(name, shape, dt))

        H = sb("H", [128, NCH, S], BF16)
        WS = [sb("WS%d" % i, [128, NCH, 512], BF16) for i in range(2)]
        vecs = sb("vecs_sb", [128, nv], F32)
        ones_f = sb("ones_f", [128, 128], F32)
        ones_b = sb("ones_b", [128, 128], BF16)
        ident_b = sb("ident_b", [128, 128], BF16)
        tri_incl = sb("tri_incl", [128, 128], BF16)
        tri_low = sb("tri_low", [128, 128], BF16)
        tri_ui = sb("tri_ui", [128, 128], F32)
        blkmask = sb("blkmask", [128, 128], F32)
        negmask = sb("negmask", [128, 896], BF16)
        rmask = sb("rmask", [128, 512], F32)
        tmpf = sb("tmpf", [128, 128], F32)
        qT = [sb("qT%d" % i, [128, S], BF16) for i in range(2)]
        kT = [sb("kT%d" % i, [128, S], BF16) for i in range(2)]
        vtok = [sb("vtok%d" % i, [128, 16, 128], BF16) for i in range(2)]
        NZ = 5
        zs = [sb("zs%d" % i, [128, 512], F32) for i in range(NZ)]
        spb = [sb("spb%d" % i, [128, 512], BF16) for i in range(NZ)]
        et = [sb("et%d" % i, [128, 512], F32) for i in range(2)]
        Ab = [sb("Ab%d" % i, [128, 512], BF16) for i in range(NZ)]
        NE = 2
        e_o = [sb("e_o%d" % i, [128, 512], F32) for i in range(NE)]
        e_sq = [sb("e_sq%d" % i, [128, 512], F32) for i in range(NE)]
        e_rs = [sb("e_rs%d" % i, [128, 512], F32) for i in range(NE)]
        e_g = [sb("e_g%d" % i, [128, 512], F32) for i in range(NE)]
        e_mg = [sb("e_mg%d" % i, [128, 512], BF16) for i in range(NE)]
        NW = 5
        wk = [sb("wk%d" % i, [128, 512], F32) for i in range(NW)]
        NX = 4
        xt = [sb("xt%d" % i, [128, 512], F32) for i in range(NX)]
        rT = sb("rT_sb", [16, S], BF16)
        wup = sb("wup_sb", [16, 128], BF16)
        g_q = sb("g_q", [128, 512], BF16)
        g_k = sb("g_k", [128, 512], BF16)
        g_kd = sb("g_kd", [128, 512], BF16)
        g_kdt = sb("g_kdt", [128, 4, 128], BF16)
        g_v = sb("g_v", [128, 4, 256], BF16)
        g_at = [sb("g_at%d" % i, [128, 128], BF16) for i in range(2)]
        S32 = sb("S32", [128, 128], F32)
        Sbf = sb("Sbf", [128, 128], BF16)
        gn_bc = sb("gn_bc", [128, 512], F32)
        wsT = [sb("wsT%d" % i, [128, 128], BF16) for i in range(NG)]
        wsTf = sb("wsTf", [128, 128], F32)
        bs_bc = [sb("bs_bc%d" % i, [128, 128], F32) for i in range(NG)]
        s_vn = [sb("s_vn%d" % i, [128, 512], BF16) for i in range(2)]
        s_ss = sb("s_ss", [128, 8], F32)
        Z = [ps("Z%d" % i, [128, 512], F32) for i in range(2)]
        C = [ps("C%d" % i, [128, 512], F32) for i in range(2)]
        O = [ps("O%d" % i, [128, 512], F32) for i in range(2)]
        P = [ps("P%d" % i, [128, 512], F32) for i in range(2)]
        Tb = O[1][:].bitcast(BF16)[:, 0:512]

        g_sp, g_cum, g_eb, g_ebi, g_k32 = wk[0], wk[1], wk[2], wk[3], wk[4]
        K_sp, K_cum, K_eb, K_ebi, K_k32 = ("wk", 0), ("wk", 1), ("wk", 2), ("wk", 3), ("wk", 4)
        kxT = qT[0][:, 0:1024].rearrange("p (h m) -> p h m", h=4)
        vx = qT[0][:, 1024:2048].rearrange("p (b n) -> p b n", b=2)
        K_kxT = [("qT", 0, 0), ("qT", 0, 1)]
        K_vx = [("qT", 0, 2), ("qT", 0, 3)]
        qx = [spb[1], spb[2]]
        K_qx = [("spb", 1), ("spb", 2)]
        pT = [Ab[0], Ab[1], Ab[2], spb[0]]
        K_pT = [("Ab", 0), ("Ab", 1), ("Ab", 2), ("spb", 0)]
        oxT = kT[0][:, :].rearrange("p (h n) -> p h n", h=4)
        print("SBUF bytes remaining:", nc.sbuf_bytes_remaining)

        sc = Sched(nc, es)
        op = sc.op
        ctr = {"p": 0, "e": 0, "w": 0, "x": 0, "z": 0}

        def nxt(k, n):
            v = ctr[k]
            ctr[k] = (v + 1) % n
            return v

        def consts():
            op("pool", lambda e: e.memset(ones_f[:], 1.0), writes=["ones_f"])
            op("pool", lambda e: e.memset(ones_b[:], 1.0), writes=["ones_b"])
            op("pool", lambda e: e.memset(tmpf[:], 1.0), writes=["tmpf"])
            op("pool", lambda e: e.affine_select(out=ident_b[:], in_=ones_b[:], pattern=[[-1, 128]], compare_op=ALU.is_equal,
                                                 fill=0.0, base=0, channel_multiplier=1), reads=["ones_b"], writes=["ident_b"])
            op("pool", lambda e: e.affine_select(out=tri_incl[:], in_=ones_b[:], pattern=[[-1, 128]], compare_op=ALU.is_ge,
                                                 fill=0.0, base=0, channel_multiplier=1), reads=["ones_b"], writes=["tri_incl"])
            op("pool", lambda e: e.affine_select(out=tri_low[:], in_=ones_b[:], pattern=[[1, 128]], compare_op=ALU.is_gt,
                                                 fill=0.0, base=0, channel_multiplier=-1), reads=["ones_b"], writes=["tri_low"])
            op("pool", lambda e: e.affine_select(out=tri_ui[:], in_=ones_f[:], pattern=[[1, 128]], compare_op=ALU.is_ge,
                                                 fill=0.0, base=0, channel_multiplier=-1), reads=["ones_f"], writes=["tri_ui"])
            op("pool", lambda e: e.affine_select(out=blkmask[:], in_=ones_f[:], pattern=[[1, 128]], compare_op=ALU.is_ge,
                                                 fill=0.0, base=0, channel_multiplier=-1), reads=["ones_f"], writes=["blkmask"])
            op("pool", lambda e: e.memset(blkmask[0:64, 64:128], 0.0), reads=["blkmask"], writes=["blkmask"])
            op("pool", lambda e: e.memset(negmask[:], 0.0), writes=["negmask"])
            op("pool", lambda e: e.affine_select(out=negmask[:], in_=negmask[:], pattern=[[1, 896]],
                                                 compare_op=ALU.is_gt, fill=NEG, base=-384, channel_multiplier=-1),
               reads=["negmask"], writes=["negmask"])
            op("pool", lambda e: e.memset(rmask[:], 1.0), writes=["rmask"])
            op("pool", lambda e: e.memset(rmask[:].rearrange("p (c t) -> p c t", t=64)[:, :, 0:1], 0.0), reads=["rmask"], writes=["rmask"])
            op("sp", lambda e: e.dma_start(out=vecs[:], in_=vecs_d[:, :]), writes=["vecs"], dma=True)
            b0 = voff["bgate"]
            op("dve", lambda e: e.tensor_scalar(out=vecs[:, b0:b0 + 2 * L], in0=vecs[:, b0:b0 + 2 * L], scalar1=-1.0, scalar2=None,
                                                op0=ALU.mult), reads=["vecs"], writes=["vecs"])

        def rstd_from_ss(ss_ap, ss_key, out_tile, out_key, inv_n, tmp_tile, tmp_key):
            op("act", lambda e: e.activation(out=tmp_tile, in_=ss_ap, func=AF.Ln, bias=EPS, scale=inv_n),
               reads=[ss_key], writes=[tmp_key])
            op("act", lambda e: e.activation(out=out_tile, in_=tmp_tile, func=AF.Exp, scale=-0.5),
               reads=[tmp_key], writes=[out_key])

        def norm_phase(src, srckey, gcol, ntok, dst_fn, final=False):
            nblk = max(1, ntok // TB)
            w = min(TB, ntok)
            for tb in range(nblk):
                t0 = tb * w
                si = nxt("p", 2)
                for c in range(NCH):
                    xi = nxt("x", NX)
                    op("sp", lambda e, c=c, xi=xi, t0=t0: e.dma_start(out=xt[xi][:, 0:w], in_=src[c][:, t0:t0 + w]),
                       reads=[(srckey, c, tb)], writes=[("xt", xi)], dma=True)
                    wi = nxt("w", NW)
                    op("act", lambda e, xi=xi, wi=wi: e.activation(out=wk[wi][:, 0:w], in_=xt[xi][:, 0:w], func=AF.Square),
                       reads=[("xt", xi)], writes=[("wk", wi)])
                    op("pe", lambda e, c=c, wi=wi, si=si: e.matmul(P[si][:, 0:w], ones_f[:], wk[wi][:, 0:w], start=(c == 0), stop=(c == NCH - 1)),
                       reads=[("wk", wi), "ones_f"], writes=[("P", si)])
                ri = nxt("e", NE)
                rstd_from_ss(P[si][:, 0:w], ("P", si), e_rs[ri][:, 0:w], ("e_rs", ri), 1.0 / D, e_sq[ri][:, 0:w], ("e_sq", ri))
                for c in range(NCH):
                    xi = nxt("x", NX)
                    op("sp", lambda e, c=c, xi=xi, t0=t0: e.dma_start(out=xt[xi][:, 0:w], in_=src[c][:, t0:t0 + w]),
                       reads=[(srckey, c, tb)], writes=[("xt", xi)], dma=True)
                    if not final:
                        dap, dkeys = dst_fn(c, t0, w, tb)
                        op("dve", lambda e, c=c, xi=xi, dap=dap, ri=ri: e.scalar_tensor_tensor(
                            out=dap, in0=xt[xi][:, 0:w], scalar=vecs[:, gcol + c:gcol + c + 1], in1=e_rs[ri][:, 0:w],
                            op0=ALU.mult, op1=ALU.mult), reads=[("xt", xi), ("e_rs", ri), "vecs"], writes=dkeys)
                    else:
                        op("dve", lambda e, c=c, xi=xi, ri=ri: e.scalar_tensor_tensor(
                            out=xt[xi][:, 0:w], in0=xt[xi][:, 0:w], scalar=vecs[:, gcol + c:gcol + c + 1], in1=e_rs[ri][:, 0:w],
                            op0=ALU.mult, op1=ALU.mult), reads=[("xt", xi), ("e_rs", ri), "vecs"], writes=[("xt", xi)])
                        tok = op("sp", lambda e, c=c, xi=xi, t0=t0: e.dma_start(out=xv(yT)[c][:, t0:t0 + w], in_=xt[xi][:, 0:w]),
                                 reads=[("xt", xi)], writes=[("yT", c, tb)], dma=True)
                        out_toks.append(tok)

        def h_dst(c, t0, w, tb):
            return H[:, c, t0:t0 + w], [("H", c, tb)]

        def hkeys(tb):
            return [("H", c, tb) for c in range(NCH)]

        def proj_fm(slot, col0, tb, ncols=128):
            pi = nxt("p", 2)

            def f(e):
                ins = None
                for c in range(NCH):
                    ins = e.matmul(P[pi][0:ncols, :], WS[slot][:, c, col0:col0 + ncols], H[:, c, tb * TB:(tb + 1) * TB],
                                   start=(c == 0), stop=(c == NCH - 1))
                return ins
            op("pe", f, reads=[("WS", slot)] + hkeys(tb), writes=[("P", pi)])
            return pi

        def proj_tm(slot, col0, ncols, tb, blocks):
            pi = nxt("p", 2)

            def f(e):
                ins = None
                for j, blk in enumerate(blocks):
                    for c in range(NCH):
                        ins = e.matmul(P[pi][:, j * ncols:(j + 1) * ncols], H[:, c, blk * 128:(blk + 1) * 128],
                                       WS[slot][:, c, col0:col0 + ncols], start=(c == 0), stop=(c == NCH - 1))
                return ins
            op("pe", f, reads=[("WS", slot)] + hkeys(tb), writes=[("P", pi)])
            return pi

        def load_ws(slot, dram_view):
            op("pool", lambda e: e.dma_start(out=WS[slot][:], in_=dram_view), writes=[("WS", slot)], dma=True)

        def win_view(l, s):
            return w_in[l].rearrange("(c p) n -> p c n", p=128)[:, :, s * 512:(s + 1) * 512]

        def epilogue(l, m, tb, src_ap, src_key, gslot, gcol):
            ei = nxt("e", NE)
            pi = proj_fm(gslot, gcol, tb)
            op("act", lambda e: e.activation(out=e_g[ei][:], in_=P[pi][:], func=AF.Exp, scale=-1.0), reads=[("P", pi)], writes=[("e_g", ei)])
            op("dve", lambda e: e.tensor_scalar(out=e_g[ei][:], in0=e_g[ei][:], scalar1=1.0, scalar2=None, op0=ALU.add),
               reads=[("e_g", ei)], writes=[("e_g", ei)])
            op("dve", lambda e: e.reciprocal(out=e_g[ei][:], in_=e_g[ei][:]), reads=[("e_g", ei)], writes=[("e_g", ei)])
            op("dve", lambda e: e.tensor_tensor(out=e_g[ei][:], in0=P[pi][:], in1=e_g[ei][:], op=ALU.mult),
               reads=[("P", pi), ("e_g", ei)], writes=[("e_g", ei)])
            op("act", lambda e: e.activation(out=e_o[ei][:], in_=src_ap, func=AF.Copy), reads=[src_key], writes=[("e_o", ei)])
            op("act", lambda e: e.activation(out=e_sq[ei][:], in_=e_o[ei][:], func=AF.Square), reads=[("e_o", ei)], writes=[("e_sq", ei)])
            si = nxt("p", 2)
            op("pe", lambda e: e.matmul(P[si][:], ones_f[:], e_sq[ei][:], start=True, stop=True), reads=[("e_sq", ei), "ones_f"], writes=[("P", si)])
            rstd_from_ss(P[si][:], ("P", si), e_rs[ei][:], ("e_rs", ei), 1.0 / 128, e_sq[ei][:], ("e_sq", ei))
            gc = voff["onorm"] + 16 * l + m
            op("dve", lambda e: e.scalar_tensor_tensor(out=e_o[ei][:], in0=e_o[ei][:], scalar=vecs[:, gc:gc + 1], in1=e_rs[ei][:],
                                                       op0=ALU.mult, op1=ALU.mult), reads=[("e_o", ei), ("e_rs", ei), "vecs"], writes=[("e_o", ei)])
            op("dve", lambda e: e.tensor_tensor(out=e_mg[ei][:], in0=e_o[ei][:], in1=e_g[ei][:], op=ALU.mult),
               reads=[("e_o", ei), ("e_g", ei)], writes=[("e_mg", ei)])
            op("sp", lambda e: e.dma_start(out=mgd[m][:, tb * TB:(tb + 1) * TB], in_=e_mg[ei][:]), reads=[("e_mg", ei)],
               writes=[("MG", m, tb)], dma=True)

        def sb_proj_items(i, slot):
            bi = i % 2
            scale = 128.0 ** -0.5
            items = []

            def mkq(tb):
                def f():
                    pi = proj_fm(slot, 0, tb)
                    op("act", lambda e: e.activation(out=qT[bi][:, tb * TB:(tb + 1) * TB], in_=P[pi][:], func=AF.Copy, scale=scale),
                       reads=[("P", pi)], writes=[("qT", bi, tb)])
                return f

            def mkk(tb):
                def f():
                    pi = proj_fm(slot, 128, tb)
                    op("dve", lambda e: e.tensor_copy(out=kT[bi][:, tb * TB:(tb + 1) * TB], in_=P[pi][:]),
                       reads=[("P", pi)], writes=[("kT", bi, tb)])
                return f

            def mkv(tb):
                def f():
                    pi = proj_tm(slot, 256, 128, tb, [tb * 4 + j for j in range(4)])
                    op("dve", lambda e: e.tensor_copy(out=vtok[bi][:, tb * 4:(tb + 1) * 4, :], in_=P[pi][:].rearrange("p (j d) -> p j d", d=128)),
                       reads=[("P", pi)], writes=[("vtok", bi, tb)])
                return f
            for tb in range(NTB):
                items += [mkk(tb), mkv(tb), mkq(tb)]
            return items

        def sb_attn(l, i, slot, bg):
            bi = i % 2
            LA = 2

            def stageA(ch, kb, qb):
                zi = nxt("z", NZ)
                zb = zi % 2
                ei = zi % 2
                op("pe", lambda e: e.matmul(Z[zb][:], kT[bi][:, kb * 128:(kb + 1) * 128], qT[bi][:, qb * TB:(qb + 1) * TB], start=True, stop=True),
                   reads=[("kT", bi, kb // 4), ("qT", bi, qb)], writes=[("Z", zb)])
                r = kb - 4 * qb
                if r >= 0:
                    op("dve", lambda e: e.tensor_tensor(out=zs[zi][:], in0=Z[zb][:], in1=negmask[:, 384 - 128 * r:896 - 128 * r], op=ALU.add),
                       reads=[("Z", zb), "negmask"], writes=[("zs", zi)])
                else:
                    op("dve", lambda e: e.tensor_copy(out=zs[zi][:], in_=Z[zb][:]), reads=[("Z", zb)], writes=[("zs", zi)])
                op("act", lambda e: e.activation(out=et[ei][:], in_=zs[zi][:], func=AF.Exp), reads=[("zs", zi)], writes=[("et", ei)])
                op("act", lambda e: e.activation(out=spb[zi][:], in_=et[ei][:], func=AF.Ln, bias=1.0), reads=[("et", ei)], writes=[("spb", zi)])
                return zi

            def stageB(ch, kb, qb, zi):
                first = (kb == 4 * qb + 3)
                last = (kb == 0)
                op("pe", lambda e: e.matmul(C[ch][:], tri_incl[:], spb[zi][:], start=first, stop=last),
                   reads=[("spb", zi), "tri_incl"], writes=[("C", ch)])
                op("dve", lambda e: e.scalar_tensor_tensor(out=zs[zi][:], in0=C[ch][:], scalar=-1.0, in1=zs[zi][:], op0=ALU.mult, op1=ALU.add),
                   reads=[("C", ch), ("zs", zi)], writes=[("zs", zi)])
                if not last:
                    op("pe", lambda e: e.matmul(C[ch][:], tri_low[:], spb[zi][:], start=False, stop=False),
                       reads=[("spb", zi), "tri_low"], writes=[("C", ch)])
                op("act", lambda e: e.activation(out=Ab[zi][:], in_=zs[zi][:], func=AF.Exp), reads=[("zs", zi)], writes=[("Ab", zi)])
                op("pe", lambda e: e.matmul(O[ch][:], vtok[bi][:, kb, :], Ab[zi][:], start=first, stop=last),
                   reads=[("Ab", zi), ("vtok", bi, kb // 4)], writes=[("O", ch)])
                if last:
                    epilogue(l, i, qb, O[ch][:], ("O", ch), slot, 384)

            ntile = 0
            for (qa, qbb) in ((0, 3), (1, 2)):
                ta = [(0, kb, qa) for kb in range(4 * qa + 3, -1, -1)]
                tbl = [(1, kb, qbb) for kb in range(4 * qbb + 3, -1, -1)]
                T = []
                ia = ib = 0
                while ia < len(ta) or ib < len(tbl):
                    if ib < len(tbl) and (ia >= len(ta) or ib * len(ta) <= ia * len(tbl)):
                        T.append(tbl[ib])
                        ib += 1
                    else:
                        T.append(ta[ia])
                        ia += 1
                zis = {}
                for idx in range(len(T) + LA):
                    if idx < len(T):
                        zis[idx] = stageA(*T[idx])
                    if idx - LA >= 0:
                        stageB(*T[idx - LA], zis[idx - LA])
                    ntile += 1
                    if bg and ntile % 3 == 0:
                        bg.pop(0)()
            while bg:
                bg.pop(0)()

        def gla_pair(l, j, slotA, slotB, first_pair):
            gp = pairs[j]
            if first_pair:
                for tb in range(NTB):
                    pi = proj_fm(slotB, 256, tb, ncols=16)
                    op("act", lambda e, pi=pi, tb=tb: e.activation(out=rT[:, tb * TB:(tb + 1) * TB], in_=P[pi][0:16, :], func=AF.Copy),
                       reads=[("P", pi)], writes=[("rT", tb)])
            op("pool", lambda e: e.dma_start(out=wup[:], in_=wup_d[l][:, j * 128:(j + 1) * 128]), writes=["wup"], dma=True)
            op("dve", lambda e: e.memset(S32[:], 0.0), writes=["S32"])
            op("dve", lambda e: e.memset(Sbf[:], 0.0), writes=["Sbf"])
            bcol = voff["bgate"] + 2 * l + j
            for tb in range(NTB):
                tsl = slice(tb * TB, (tb + 1) * TB)
                pi = nxt("p", 2)
                op("pe", lambda e, pi=pi, tsl=tsl: e.matmul(P[pi][:], wup[:], rT[:, tsl], start=True, stop=True),
                   reads=["wup", ("rT", tb)], writes=[("P", pi)])
                op("act", lambda e, pi=pi: e.activation(out=g_sp[:], in_=P[pi][:], func=AF.Exp, scale=-1.0, bias=vecs[:, bcol:bcol + 1]),
                   reads=[("P", pi), "vecs"], writes=[K_sp])
                op("act", lambda e: e.activation(out=g_sp[:], in_=g_sp[:], func=AF.Ln, bias=1.0), reads=[K_sp], writes=[K_sp])
                op("dve", lambda e: e.tensor_tensor_scan(out=g_cum[:], data0=rmask[:], data1=g_sp[:], initial=0.0, op0=ALU.mult, op1=ALU.add),
                   reads=[K_sp, "rmask"], writes=[K_cum])
                op("act", lambda e: e.activation(out=g_eb[:], in_=g_cum[:], func=AF.Exp, scale=-1.0 / 16), reads=[K_cum], writes=[K_eb])
                op("act", lambda e: e.activation(out=g_ebi[:], in_=g_cum[:], func=AF.Exp, scale=1.0 / 16), reads=[K_cum], writes=[K_ebi])
                pi = proj_fm(slotA, 0, tb)
                op("dve", lambda e, pi=pi: e.scalar_tensor_tensor(out=g_q[:], in0=P[pi][:], scalar=0.125, in1=g_eb[:], op0=ALU.mult, op1=ALU.mult),
                   reads=[("P", pi), K_eb], writes=["g_q"])
                pi = proj_fm(slotA, 128, tb)
                op("dve", lambda e, pi=pi: e.tensor_tensor(out=g_k32[:], in0=P[pi][:], in1=g_ebi[:], op=ALU.mult),
                   reads=[("P", pi), K_ebi], writes=[K_k32])
                op("act", lambda e: e.activation(out=g_k[:], in_=g_k32[:], func=AF.Copy), reads=[K_k32], writes=["g_k"])
                dec_bc = g_eb[:].rearrange("p (c t) -> p c t", t=64)[:, :, 63:64].broadcast_to([128, 8, 64])
                op("dve", lambda e: e.tensor_tensor(out=g_kd[:].rearrange("p (c t) -> p c t", t=64), in0=g_k32[:].rearrange("p (c t) -> p c t", t=64),
                                                    in1=dec_bc, op=ALU.mult), reads=[K_k32, K_eb], writes=["g_kd"])

                def ftr(e):
                    ins = None
                    for j4 in range(4):
                        ins = e.transpose(Tb[:, j4 * 128:(j4 + 1) * 128], g_kd[:, j4 * 128:(j4 + 1) * 128], ident_b[:])
                    return ins
                op("pe", ftr, reads=["g_kd", "ident_b"], writes=[("O", 1)])
                op("dve", lambda e: e.tensor_copy(out=g_kdt[:], in_=Tb.rearrange("p (j d) -> p j d", d=128)), reads=[("O", 1)], writes=["g_kdt"])
                for half in range(2):
                    pi = proj_tm(slotA, 256, 256, tb, [tb * 4 + 2 * half, tb * 4 + 2 * half + 1])
                    op("act", lambda e, pi=pi, half=half: e.activation(out=g_v[:, 2 * half:2 * half + 2, :],
                                                                       in_=P[pi][:].rearrange("p (j d) -> p j d", d=256), func=AF.Copy),
                       reads=[("P", pi)], writes=["g_v"])
                OG = [O[0], C[0]]
                OGk = [("O", 0), ("C", 0)]
                for cp in range(4):
                    csl = slice(cp * 128, (cp + 1) * 128)
                    for h in range(2):
                        hs = slice(h * 64, (h + 1) * 64)
                        ai = (cp * 2 + h) % 2
                        op("pe", lambda e, hs=hs, csl=csl: e.matmul(Z[0][:, 0:128], g_k[hs, csl], g_q[hs, csl], start=True, stop=True),
                           reads=["g_k", "g_q"], writes=[("Z", 0)])
                        op("dve", lambda e, ai=ai: e.tensor_tensor(out=g_at[ai][:], in0=Z[0][:, 0:128], in1=blkmask[:], op=ALU.mult),
                           reads=[("Z", 0), "blkmask"], writes=[("g_at", ai)])
                        op("pe", lambda e, h=h, ai=ai, csl=csl, cp=cp: e.matmul(OG[h][:, csl], g_v[:, cp, h * 128:(h + 1) * 128], g_at[ai][:],
                                                                                 start=True, stop=False),
                           reads=["g_v", ("g_at", ai)], writes=[OGk[h]])
                    for c2 in range(2):
                        ch = cp * 2 + c2
                        tsl64 = slice(cp * 128 + c2 * 64, cp * 128 + c2 * 64 + 64)
                        psl = slice(c2 * 64, c2 * 64 + 64)

                        def finter(e, tsl64=tsl64, c2=c2):
                            ins = None
                            for h in range(2):
                                hs = slice(h * 64, (h + 1) * 64)
                                ins = e.matmul(OG[h][:, tsl64], Sbf[hs, :], g_q[hs, tsl64], start=False, stop=(c2 == 1))
                            return ins
                        op("pe", finter, reads=["Sbf", "g_q"], writes=[("O", 0), ("C", 0)])

                        def fkv(e, psl=psl, cp=cp):
                            ins = None
                            for h in range(2):
                                ins = e.matmul(Z[1][h * 64:(h + 1) * 64, 0:128], g_kdt[psl, cp, h * 64:(h + 1) * 64],
                                               g_v[psl, cp, h * 128:(h + 1) * 128], start=True, stop=True)
                            return ins
                        op("pe", fkv, reads=["g_kdt", "g_v"], writes=[("Z", 1)])
                        dcol = ch * 64 + 63
                        op("dve", lambda e, dcol=dcol: e.scalar_tensor_tensor(out=S32[:], in0=S32[:], scalar=g_eb[:, dcol:dcol + 1], in1=Z[1][:, 0:128],
                                                                              op0=ALU.mult, op1=ALU.add),
                           reads=["S32", K_eb, ("Z", 1)], writes=["S32"])
                        op("dve", lambda e: e.tensor_copy(out=Sbf[:], in_=S32[:]), reads=["S32"], writes=["Sbf"])
                for h in range(2):
                    epilogue(l, NSB + 2 * j + h, tb, OG[h][:], OGk[h], slotB, h * 128)

        def gelu_from_psum(pi, out_tile, out_key):
            a = nxt("w", NW)
            b = nxt("w", NW)
            op("dve", lambda e: e.tensor_copy(out=wk[a][:], in_=P[pi][:]), reads=[("P", pi)], writes=[("wk", a)])
            op("dve", lambda e: e.tensor_tensor(out=wk[b][:], in0=wk[a][:], in1=wk[a][:], op=ALU.mult), reads=[("wk", a)], writes=[("wk", b)])
            op("dve", lambda e: e.tensor_scalar(out=wk[b][:], in0=wk[b][:], scalar1=0.044715, scalar2=1.0, op0=ALU.mult, op1=ALU.add),
               reads=[("wk", b)], writes=[("wk", b)])
            op("dve", lambda e: e.tensor_tensor(out=wk[b][:], in0=wk[b][:], in1=wk[a][:], op=ALU.mult), reads=[("wk", a), ("wk", b)], writes=[("wk", b)])
            op("act", lambda e: e.activation(out=wk[b][:], in_=wk[b][:], func=AF.Exp, scale=-GELU_C), reads=[("wk", b)], writes=[("wk", b)])
            op("dve", lambda e: e.tensor_scalar(out=wk[b][:], in0=wk[b][:], scalar1=1.0, scalar2=None, op0=ALU.add), reads=[("wk", b)], writes=[("wk", b)])
            op("dve", lambda e: e.reciprocal(out=wk[b][:], in_=wk[b][:]), reads=[("wk", b)], writes=[("wk", b)])
            op("dve", lambda e: e.tensor_tensor(out=out_tile, in0=wk[a][:], in1=wk[b][:], op=ALU.mult), reads=[("wk", a), ("wk", b)], writes=[out_key])

        def sgu_unit(l, slotV, uslot, lgs):
            op("sp", lambda e: e.dma_start(out=gn_bc[:], in_=sgn_d[l].partition_broadcast(128)), writes=["gn_bc"], dma=True)
            for lg in lgs:
                op("sp", lambda e, lg=lg: e.dma_start(out=wsTf[:], in_=wsg_d[l][lg]), writes=["wsTf"], dma=True)
                op("dve", lambda e, lg=lg: e.tensor_tensor(out=wsT[lg][:], in0=wsTf[:], in1=tri_ui[:], op=ALU.mult),
                   reads=["wsTf", "tri_ui"], writes=[("wsT", lg)])
                op("sp", lambda e, lg=lg: e.dma_start(out=bs_bc[lg][:], in_=bsg_d[l][lg].partition_broadcast(128)), writes=[("bs_bc", lg)], dma=True)
            MX = [O[0], C[0]]
            MXk = [("O", 0), ("C", 0)]
            for tb in range(NTB):
                for j4 in range(4):
                    blk = tb * 4 + j4
                    pi = proj_tm(slotV, 0, 512, tb, [blk])
                    a = nxt("w", NW)
                    gelu_from_psum(pi, wk[a][:], ("wk", a))
                    b = nxt("w", NW)
                    sscol = blk % 8
                    op("dve", lambda e, sscol=sscol: e.memset(s_ss[:, sscol:sscol + 1], 0.0), writes=[("s_ss", sscol)])
                    op("act", lambda e, a=a, b=b, sscol=sscol: e.activation(out=wk[b][:], in_=wk[a][:], func=AF.Square, accum_out=s_ss[:, sscol:sscol + 1]),
                       reads=[("wk", a), ("s_ss", sscol)], writes=[("wk", b), ("s_ss", sscol)])
                    op("act", lambda e, sscol=sscol: e.activation(out=s_ss[:, sscol:sscol + 1], in_=s_ss[:, sscol:sscol + 1], func=AF.Ln, bias=EPS, scale=1.0 / 512),
                       reads=[("s_ss", sscol)], writes=[("s_ss", sscol)])
                    op("act", lambda e, sscol=sscol: e.activation(out=s_ss[:, sscol:sscol + 1], in_=s_ss[:, sscol:sscol + 1], func=AF.Exp, scale=-0.5),
                       reads=[("s_ss", sscol)], writes=[("s_ss", sscol)])
                    vi = blk % 2
                    op("dve", lambda e, a=a, vi=vi, sscol=sscol: e.scalar_tensor_tensor(out=s_vn[vi][:], in0=wk[a][:], scalar=s_ss[:, sscol:sscol + 1],
                                                                                       in1=gn_bc[:], op0=ALU.mult, op1=ALU.mult),
                       reads=[("wk", a), ("s_ss", sscol), "gn_bc"], writes=[("s_vn", vi)])

                    def fmx(e, vi=vi, j4=j4):
                        ins = None
                        for k, lg in enumerate(lgs):
                            g = lg
                            ins = e.matmul(MX[k][:, j4 * 128:(j4 + 1) * 128], s_vn[vi][:, g * 128:(g + 1) * 128], wsT[lg][:], start=True, stop=True)
                        return ins
                    op("pe", fmx, reads=[("s_vn", vi)] + [("wsT", lg) for lg in lgs], writes=MXk[:len(lgs)])
                for k, lg in enumerate(lgs):
                    ucol = k * 256
                    pi = proj_fm(uslot, ucol, tb)
                    a = nxt("w", NW)
                    gelu_from_psum(pi, wk[a][:], ("wk", a))
                    b = nxt("w", NW)
                    op("dve", lambda e, k=k, lg=lg, b=b: e.tensor_tensor(out=wk[b][:].rearrange("p (j t) -> p j t", t=128),
                                                                         in0=MX[k][:].rearrange("p (j t) -> p j t", t=128),
                                                                         in1=bs_bc[lg][:].unsqueeze(1).broadcast_to([128, 4, 128]), op=ALU.add),
                       reads=[MXk[k], ("bs_bc", lg)], writes=[("wk", b)])
                    op("dve", lambda e, a=a, b=b: e.tensor_tensor(out=wk[b][:], in0=wk[b][:], in1=wk[a][:], op=ALU.mult),
                       reads=[("wk", a), ("wk", b)], writes=[("wk", b)])
                    epilogue(l, NSB + 2 * NP + lg, tb, wk[b][:], ("wk", b), uslot, ucol + 128)

        def mixer(l):
            s = 0
            units = []
            for i in range(NSB):
                units.append(("sb", i, [s]))
                s += 1
            for j in range(NP):
                units.append(("gla", j, [s, s + 1]))
                s += 2
            units.append(("sgu", 0, list(range(s, s + 1 + NG // 2))))
            wsn = {"n": 0}

            def alloc(k):
                r = []
                for _ in range(k):
                    r.append(wsn["n"] % 2)
                    wsn["n"] += 1
                return r
            def exchange(k):
                grp = ccg[k]
                n = len(grp)
                src = mgd[grp[0]:grp[0] + n].rearrange("h p s -> (h p) s")
                op("pool", lambda e: e.collective_compute("AllGather", ALU.bypass, replica_groups=rgroups, ins=[src.opt()], outs=[mga[k].opt()]),
                   reads=[("MG", li, tb) for li in grp for tb in range(NTB)], writes=[("MGA", k)], dma="cc")

            for ui, (kind, idx, dsl) in enumerate(units):
                if split and kind == "gla" and idx == 0:
                    exchange(0)
                if split and kind == "sgu":
                    exchange(1)
                if kind == "sb":
                    if idx == 0:
                        sbws = [alloc(1)[0] for _ in range(NSB)]
                        load_ws(sbws[0], win_view(l, dsl[0]))
                        for f in sb_proj_items(0, sbws[0]):
                            f()
                    bg = []
                    if idx + 1 < NSB:
                        load_ws(sbws[idx + 1], win_view(l, dsl[0] + 1))
                        bg = sb_proj_items(idx + 1, sbws[idx + 1])
                    sb_attn(l, idx, sbws[idx], bg)
                elif kind == "gla":
                    ws = alloc(2)
                    load_ws(ws[0], win_view(l, dsl[0]))
                    load_ws(ws[1], win_view(l, dsl[1]))
                    gla_pair(l, idx, ws[0], ws[1], idx == 0)
                else:
                    for k in range(NG // 2):
                        ws = alloc(2)
                        load_ws(ws[0], win_view(l, dsl[0]))
                        load_ws(ws[1], win_view(l, dsl[1 + k]))
                        sgu_unit(l, ws[0], ws[1], [2 * k, 2 * k + 1])
                    if split:
                        exchange(2)

        def out_proj(l, xsrc, skey, xdst, dkey):
            if not split:
                for m in range(16):
                    op("sp", lambda e, m=m: e.dma_start(out=H[:, m, :], in_=mgd[m]), reads=[("MG", m, tb) for tb in range(NTB)],
                       writes=[("H", m, tb) for tb in range(NTB)], dma=True)
            else:
                jj = 0
                for k, grp in enumerate(ccg):
                    for r in range(2):
                        for idx in range(len(grp)):
                            row = (r * len(grp) + idx) * 128
                            op("sp", lambda e, jj=jj, k=k, row=row: e.dma_start(out=H[:, jj, :], in_=mga[k][row:row + 128, :]),
                               reads=[("MGA", k)], writes=[("H", jj, tb) for tb in range(NTB)], dma=True)
                            jj += 1
            if stop == "mixload":
                for c in range(NCH):
                    out_toks.append(op("sp", lambda e, c=c: e.dma_start(out=hdbg[c], in_=H[:, c, :]), reads=[("H", c, tb) for tb in range(NTB)],
                                       writes=[("hdbg", c)], dma=True))
                return
            wv = w_out[l].rearrange("(c p) n -> p c n", p=128)
            for cs in range(4):
                slot = cs % 2
                load_ws(slot, wv[:, :, cs * 512:(cs + 1) * 512])
                for tb in range(NTB):
                    for dc in range(4):
                        dch = cs * 4 + dc
                        pi = proj_fm(slot, dc * 128, tb)
                        xi = nxt("x", NX)
                        op("sp", lambda e, xi=xi, dch=dch, tb=tb: e.dma_start(out=xt[xi][:], in_=xv(xsrc)[dch][:, tb * TB:(tb + 1) * TB]),
                           reads=[(skey, dch, tb)], writes=[("xt", xi)], dma=True)
                        op("dve", lambda e, xi=xi, pi=pi: e.tensor_tensor(out=xt[xi][:], in0=P[pi][:], in1=xt[xi][:], op=ALU.add),
                           reads=[("P", pi), ("xt", xi)], writes=[("xt", xi)])
                        op("sp", lambda e, xi=xi, dch=dch, tb=tb: e.dma_start(out=xv(xdst)[dch][:, tb * TB:(tb + 1) * TB], in_=xt[xi][:]),
                           reads=[("xt", xi)], writes=[(dkey, dch, tb)], dma=True)

        def xattn(l, xsrc, skey, xdst, dkey):
            mv = memT.rearrange("(c p) m -> c p m", p=128)

            def mem_dst(c, t0, w, tb):
                return H[:, c, 0:NM], [("H", c, 0)]
            norm_phase(mv, "memT", voff["nmem"] + 16 * l, NM, mem_dst)
            kvv = w_xkv[l].rearrange("(c p) n -> p c n", p=128)
            load_ws(0, kvv[:, :, 0:512])
            load_ws(1, kvv[:, :, 512:1024])
            mkeys = [("H", c, 0) for c in range(NCH)]
            for h in range(4):
                pi = nxt("p", 2)

                def fk(e, pi=pi, h=h):
                    ins = None
                    for c in range(NCH):
                        ins = e.matmul(P[pi][:, 0:NM], WS[0][:, c, h * 128:(h + 1) * 128], H[:, c, 0:NM], start=(c == 0), stop=(c == NCH - 1))
                    return ins
                op("pe", fk, reads=[("WS", 0)] + mkeys, writes=[("P", pi)])
                op("act", lambda e, pi=pi, h=h: e.activation(out=kxT[:, h, :], in_=P[pi][:, 0:NM], func=AF.Copy), reads=[("P", pi)], writes=K_kxT)
            for mb in range(2):
                pi = nxt("p", 2)

                def fv(e, pi=pi, mb=mb):
                    ins = None
                    for c in range(NCH):
                        ins = e.matmul(P[pi][:], H[:, c, mb * 128:(mb + 1) * 128], WS[1][:, c, :], start=(c == 0), stop=(c == NCH - 1))
                    return ins
                op("pe", fv, reads=[("WS", 1)] + mkeys, writes=[("P", pi)])
                op("act", lambda e, pi=pi, mb=mb: e.activation(out=vx[:, mb, :], in_=P[pi][:], func=AF.Copy), reads=[("P", pi)], writes=K_vx)
            norm_phase(xv(xsrc), skey, voff["nxa"] + 16 * l, S, h_dst)
            load_ws(0, w_xq[l].rearrange("(c p) n -> p c n", p=128))
            WO = WS[1][:].rearrange("p c n -> p (c n)").rearrange("p (h n) -> p h n", h=4)
            op("pool", lambda e: e.dma_start(out=WO, in_=w_xo[l].rearrange("(h p) n -> p h n", p=128)), writes=[("WS", 1)], dma=True)
            scale = 128.0 ** -0.5
            for tb in range(NTB):
                for h in range(4):
                    pi = proj_fm(0, h * 128, tb)
                    qi = h % 2
                    op("act", lambda e, pi=pi, qi=qi: e.activation(out=qx[qi][:], in_=P[pi][:], func=AF.Copy, scale=scale),
                       reads=[("P", pi)], writes=[K_qx[qi]])
                    for mb in range(2):
                        zb = mb
                        pti = (h % 2) * 2 + mb
                        op("pe", lambda e, zb=zb, h=h, mb=mb, qi=qi: e.matmul(Z[zb][:], kxT[:, h, mb * 128:(mb + 1) * 128], qx[qi][:], start=True, stop=True),
                           reads=K_kxT + [K_qx[qi]], writes=[("Z", zb)])
                        op("act", lambda e, zb=zb, pti=pti: e.activation(out=pT[pti][:], in_=Z[zb][:], func=AF.Exp), reads=[("Z", zb)], writes=[K_pT[pti]])

                    def fden(e, h=h):
                        ins = None
                        for mb in range(2):
                            ins = e.matmul(C[0][:], ones_b[:], pT[(h % 2) * 2 + mb][:], start=(mb == 0), stop=(mb == 1))
                        return ins
                    op("pe", fden, reads=["ones_b", K_pT[(h % 2) * 2], K_pT[(h % 2) * 2 + 1]], writes=[("C", 0)])

                    def fnum(e, h=h):
                        ins = None
                        for mb in range(2):
                            ins = e.matmul(O[0][:], vx[:, mb, h * 128:(h + 1) * 128], pT[(h % 2) * 2 + mb][:], start=(mb == 0), stop=(mb == 1))
                        return ins
                    op("pe", fnum, reads=K_vx + [K_pT[(h % 2) * 2], K_pT[(h % 2) * 2 + 1]], writes=[("O", 0)])
                    a = nxt("w", NW)
                    op("dve", lambda e, a=a: e.reciprocal(out=wk[a][:], in_=C[0][:]), reads=[("C", 0)], writes=[("wk", a)])
                    op("dve", lambda e, a=a, h=h: e.tensor_tensor(out=oxT[:, h, :], in0=O[0][:], in1=wk[a][:], op=ALU.mult),
                       reads=[("O", 0), ("wk", a)], writes=[("kT", 0, h)])
                for dch in range(NCH):
                    pi = nxt("p", 2)

                    def fo(e, pi=pi, dch=dch):
                        ins = None
                        for h in range(4):
                            ins = e.matmul(P[pi][:], WO[:, h, dch * 128:(dch + 1) * 128], oxT[:, h, :], start=(h == 0), stop=(h == 3))
                        return ins
                    op("pe", fo, reads=[("WS", 1)] + [("kT", 0, h) for h in range(4)], writes=[("P", pi)])
                    xi = nxt("x", NX)
                    op("sp", lambda e, xi=xi, dch=dch, tb=tb: e.dma_start(out=xt[xi][:], in_=xv(xsrc)[dch][:, tb * TB:(tb + 1) * TB]),
                       reads=[(skey, dch, tb)], writes=[("xt", xi)], dma=True)
                    op("dve", lambda e, xi=xi, pi=pi: e.tensor_tensor(out=xt[xi][:], in0=P[pi][:], in1=xt[xi][:], op=ALU.add),
                       reads=[("P", pi), ("xt", xi)], writes=[("xt", xi)])
                    op("sp", lambda e, xi=xi, dch=dch, tb=tb: e.dma_start(out=xv(xdst)[dch][:, tb * TB:(tb + 1) * TB], in_=xt[xi][:]),
                       reads=[("xt", xi)], writes=[(dkey, dch, tb)], dma=True)

        out_toks = []
        consts()
        cur, ckey = xT, "xT"
        done = False
        for l in range(nlayers):
            norm_phase(xv(cur), ckey, voff["nmix"] + 16 * l, S, h_dst)
            if stop == "norm1":
                for c in range(NCH):
                    out_toks.append(op("sp", lambda e, c=c: e.dma_start(out=hdbg[c], in_=H[:, c, :]), reads=[("H", c, tb) for tb in range(NTB)],
                                       writes=[("hdbg", c)], dma=True))
                done = True
                break
            mixer(l)
            if stop == "mixer":
                done = True
                break
            out_proj(l, cur, ckey, xA, "xA")
            if stop in ("outproj", "mixload"):
                done = True
                break
            xattn(l, xA, "xA", xB, "xB")
            cur, ckey = xB, "xB"
        if not done:
            norm_phase(xv(cur), ckey, voff["fin"], S, None, final=True)
        for q in Sched.QUEUES:
            for i in range(Sched.NSLOT):
                g = sc.dgen[q][i]
                if g > 0:
                    out_toks.append((("d", q, i), 16 * g))
        sc.final_wait("sp", out_toks)
        with nc.Block() as block:
            sc.emit(block)
    return nc, sc


_CACHE = {}


def kernel(**inputs):
    maps = pack_inputs(inputs, SPLIT)
    if "nc" not in _CACHE:
        _CACHE["nc"] = build_program(SPLIT)[0]
    nc = _CACHE["nc"]
    res = run_bass_kernel_spmd(nc, maps, core_ids=list(range(N_CORES)))
    out = np.empty((4, S, D), np.float32)
    for b in range(4):
        out[b] = np.asarray(res.results[2 * b]["yT"]).T
    return out
```

```python
import numpy as np
from contextlib import ExitStack
import concourse.bass as bass
import concourse.mybir as mybir
from concourse.bass_utils import run_bass_kernel_spmd

F32 = mybir.dt.float32
BF16 = mybir.dt.bfloat16
AF = mybir.ActivationFunctionType
ALU = mybir.AluOpType

D = 2048
S = 2048
L = 4
NM = 256
NCH = 16
TB = 512
NTB = 4
EPS = 1e-6
NEG = -30000.0
GELU_C = 1.5957691216057308

SPLIT = True
N_CORES = 8


def core_units(hf, split):
    if split:
        return list(range(4 * hf, 4 * hf + 4)), [hf], [2 * hf, 2 * hf + 1]
    return list(range(8)), [0, 1], [0, 1, 2, 3]


def col_slots(hf, split):
    sbh, pairs, groups = core_units(hf, split)
    slots = []
    for h in sbh:
        slots.append([(128 * h, 128), (1024 + 128 * h, 128), (2048 + 128 * h, 128), (3072 + 128 * h, 128)])
    for p in pairs:
        slots.append([(4096 + 128 * p, 128), (4352 + 128 * p, 128), (4608 + 256 * p, 256)])
        slots.append([(5136 + 256 * p, 256), (5120, 16), (None, 240)])
    gord = list(groups) + [g for g in range(4) if g not in groups]
    slots.append([(6160 + 128 * g, 128) for g in gord])
    for i in range(0, len(groups), 2):
        g0, g1 = groups[i], groups[i + 1]
        slots.append([(5648 + 128 * g0, 128), (6672 + 128 * g0, 128), (5648 + 128 * g1, 128), (6672 + 128 * g1, 128)])
    return slots


def local_heads(hf, split):
    sbh, pairs, groups = core_units(hf, split)
    return list(sbh) + [8 + 2 * p + h for p in pairs for h in range(2)] + [12 + g for g in groups]


def cc_groups(split):
    return [[0, 1, 2, 3], [4, 5], [6, 7]] if split else []


def chunk_order(split):
    if not split:
        return list(range(16))
    order = []
    for grp in cc_groups(split):
        for r in range(2):
            lh = local_heads(r, split)
            order += [lh[li] for li in grp]
    return order


def vec_layout(split):
    off = {}
    n = 0
    for nm in ("nmix", "nxa", "nmem"):
        off[nm] = n
        n += L * 16
    off["fin"] = n
    n += 16
    off["onorm"] = n
    n += L * 16
    off["bgate"] = n
    n += L * 2
    return off, n


def pack_inputs(inputs, split, nlw=L, ncores=N_CORES, lite=False):
    f = np.float32
    x = np.asarray(inputs["x"], f)
    mem = np.asarray(inputs["mem"], f)
    w_in = np.asarray(inputs["w_in"], f)[:nlw]
    voff, nv = vec_layout(split)
    per_half = {}
    for hf in (0, 1):
        sbh, pairs, groups = core_units(hf, split)
        slots = col_slots(hf, split)
        ncol = 512 * len(slots)
        wl = np.zeros((nlw, D, ncol), f)
        c = 0
        for sl in slots:
            for (st, w) in sl:
                if st is not None:
                    wl[:, :, c:c + w] = w_in[:, :, st:st + w]
                c += w
        vec = np.zeros((128, nv), f)
        for l in range(L):
            vec[:, voff["nmix"] + 16 * l: voff["nmix"] + 16 * l + 16] = np.asarray(inputs["norm_mix"], f)[l].reshape(16, 128).T
            vec[:, voff["nxa"] + 16 * l: voff["nxa"] + 16 * l + 16] = np.asarray(inputs["norm_xattn"], f)[l].reshape(16, 128).T
            vec[:, voff["nmem"] + 16 * l: voff["nmem"] + 16 * l + 16] = np.asarray(inputs["norm_mem"], f)[l].reshape(16, 128).T
            on = np.asarray(inputs["out_norm"], f)[l].reshape(16, 128)
            for li, m in enumerate(local_heads(hf, split)):
                vec[:, voff["onorm"] + 16 * l + li] = on[m]
            bg = np.asarray(inputs["b_gla_gate"], f)[l].reshape(2, 128)
            for j, p in enumerate(pairs):
                vec[:, voff["bgate"] + 2 * l + j] = bg[p]
        vec[:, voff["fin"]: voff["fin"] + 16] = np.asarray(inputs["final_norm"], f).reshape(16, 128).T
        wup = np.asarray(inputs["w_gla_gate_up"], f).reshape(L, 16, 2, 128)[:nlw, :, pairs, :].reshape(nlw, 16, 128 * len(pairs))
        wsg = np.ascontiguousarray(np.asarray(inputs["w_sgu"], f)[:nlw, groups].transpose(0, 1, 3, 2))
        bsg = np.ascontiguousarray(np.asarray(inputs["b_sgu"], f)[:nlw, groups])
        gord = list(groups) + [g for g in range(4) if g not in groups]
        sgn = np.ascontiguousarray(np.asarray(inputs["sgu_norm"], f)[:nlw].reshape(nlw, 4, 128)[:, gord].reshape(nlw, 512))
        per_half[hf] = dict(w_in=np.ascontiguousarray(wl), vecs=vec, wup=np.ascontiguousarray(wup), wsg=wsg, bsg=bsg, sgn=sgn)
    shared = dict(
        w_out=np.ascontiguousarray(np.asarray(inputs["w_out"], f)[:nlw].reshape(nlw, 16, 128, D)[:, chunk_order(split)].reshape(nlw, D, D)),
        w_xq=np.ascontiguousarray(np.asarray(inputs["w_xq"], f)[:nlw]),
        w_xkv=np.ascontiguousarray(np.asarray(inputs["w_xkv"], f)[:nlw]),
        w_xo=np.ascontiguousarray(np.asarray(inputs["w_xo"], f)[:nlw]),
    )
    if lite:
        for k in ("w_out", "w_xq", "w_xkv", "w_xo"):
            shared[k] = np.zeros((1, 128, 128), f)
    maps = []
    for c in range(ncores):
        b, hf = c // 2, (c % 2 if split else 0)
        m = dict(xT=np.ascontiguousarray(x[b].T), memT=np.ascontiguousarray(mem[b].T))
        m.update(per_half[hf])
        m.update(shared)
        maps.append(m)
    return maps


class Sched:
    COMPUTE = ("pe", "act", "dve", "pool")
    QUEUES = ("sp", "pool", "act")
    NSLOT = 6

    def __init__(self, nc, es):
        self.nc = nc
        self.streams = {e: [] for e in ("pe", "act", "dve", "pool", "sp")}
        self.sems = {}
        for e in self.COMPUTE:
            self.sems[("c", e)] = es.enter_context(nc.semaphore("c_" + e))
        for q in self.QUEUES:
            for i in range(self.NSLOT):
                self.sems[("d", q, i)] = es.enter_context(nc.semaphore("d_%s%d" % (q, i)))
        self.NCC = 4
        for i in range(self.NCC):
            self.sems[("k", i)] = es.enter_context(nc.semaphore("k_%d" % i))
        self.kgen = [0] * self.NCC
        self.knext = 0
        self.ccount = {e: 0 for e in self.COMPUTE}
        self.dgen = {q: [0] * self.NSLOT for q in self.QUEUES}
        self.dnext = {q: 0 for q in self.QUEUES}
        self.waited = {e: {} for e in self.streams}
        self.lastw = {}
        self.readers = {}
        self.nops = 0

    def _need(self, eng, tok, waits):
        sid, val = tok
        if self.waited[eng].get(sid, 0) < val:
            self.waited[eng][sid] = val
            waits.append((sid, val))

    def op(self, eng, fn, reads=(), writes=(), dma=False):
        waits = []
        deps = []
        for k in reads:
            t = self.lastw.get(k)
            if t is not None:
                deps.append(t)
        for k in writes:
            t = self.lastw.get(k)
            if t is not None:
                deps.append(t)
            deps.extend(self.readers.get(k, {}).values())
        for t in deps:
            if (not dma) and eng == "pe" and t[0] == ("c", "pe"):
                continue
            self._need(eng, t, waits)
        if dma == "cc":
            slot = self.knext
            self.knext = (slot + 1) % self.NCC
            sid = ("k", slot)
            if self.kgen[slot] > 0:
                self._need(eng, (sid, self.kgen[slot]), waits)
            self.kgen[slot] += 1
            tok = (sid, self.kgen[slot])
            inc = 1
        elif dma:
            slot = self.dnext[eng]
            self.dnext[eng] = (slot + 1) % self.NSLOT
            prev = self.dgen[eng][slot]
            sid = ("d", eng, slot)
            if prev > 0:
                self._need(eng, (sid, 16 * prev), waits)
            self.dgen[eng][slot] += 1
            tok = (sid, 16 * self.dgen[eng][slot])
            inc = 16
        else:
            self.ccount[eng] += 1
            sid = ("c", eng)
            tok = (sid, self.ccount[eng])
            inc = 1
        self.streams[eng].append((waits, fn, sid, inc))
        for k in reads:
            self.readers.setdefault(k, {})[sid] = tok
        for k in writes:
            self.lastw[k] = tok
            self.readers[k] = {}
        self.nops += 1
        return tok

    def final_wait(self, eng, toks):
        waits = []
        for t in toks:
            self._need(eng, t, waits)
        self.streams[eng].append((waits, None, None, 0))

    def emit(self, block):
        def mk(name):
            def f(eng):
                for waits, fn, sid, inc in self.streams[name]:
                    for (ws, val) in waits:
                        eng.wait_ge(self.sems[ws], val)
                    if fn is not None:
                        ins = fn(eng)
                        ins.then_inc(self.sems[sid], inc)
            return f
        block.tensor(mk("pe"))
        block.scalar(mk("act"))
        block.vector(mk("dve"))
        block.gpsimd(mk("pool"))
        block.sync(mk("sp"))


def build_program(split=SPLIT, nlayers=L, stop=None, debug=False, lite=False, ncores=N_CORES):
    LW = nlayers
    nc = bass.Bass("TRN2", target_bir_lowering=False)
    sbh, pairs, groups = core_units(0, split)
    NSB, NP, NG = len(sbh), len(pairs), len(groups)
    nslots = NSB + 2 * NP + 1 + NG // 2
    voff, nv = vec_layout(split)
    dk = "ExternalOutput" if debug else "Internal"

    xT = nc.dram_tensor("xT", [D, S], F32, kind="ExternalInput").ap()
    memT = nc.dram_tensor("memT", [D, NM], F32, kind="ExternalInput").ap()
    w_in = nc.dram_tensor("w_in", [LW, D, nslots * 512], F32, kind="ExternalInput").ap()
    vecs_d = nc.dram_tensor("vecs", [128, nv], F32, kind="ExternalInput").ap()
    wup_d = nc.dram_tensor("wup", [LW, 16, 128 * NP], F32, kind="ExternalInput").ap()
    wsg_d = nc.dram_tensor("wsg", [LW, NG, 128, 128], F32, kind="ExternalInput").ap()
    bsg_d = nc.dram_tensor("bsg", [LW, NG, 128], F32, kind="ExternalInput").ap()
    if lite:
        w_out = nc.dram_tensor("w_out", [1, 128, 128], F32, kind="ExternalInput").ap()
        w_xq = nc.dram_tensor("w_xq", [1, 128, 128], F32, kind="ExternalInput").ap()
        w_xkv = nc.dram_tensor("w_xkv", [1, 128, 128], F32, kind="ExternalInput").ap()
        w_xo = nc.dram_tensor("w_xo", [1, 128, 128], F32, kind="ExternalInput").ap()
    else:
        w_out = nc.dram_tensor("w_out", [LW, D, D], F32, kind="ExternalInput").ap()
        w_xq = nc.dram_tensor("w_xq", [LW, D, 512], F32, kind="ExternalInput").ap()
        w_xkv = nc.dram_tensor("w_xkv", [LW, D, 1024], F32, kind="ExternalInput").ap()
        w_xo = nc.dram_tensor("w_xo", [LW, 512, D], F32, kind="ExternalInput").ap()
    sgn_d = nc.dram_tensor("sgn", [LW, 512], F32, kind="ExternalInput").ap()
    yT = nc.dram_tensor("yT", [D, S], F32, kind="ExternalOutput").ap()
    xA = nc.dram_tensor("xA", [D, S], F32, kind=dk).ap()
    xB = nc.dram_tensor("xB", [D, S], F32, kind=dk).ap()
    NLH = NSB + 2 * NP + NG
    ccg = cc_groups(split)
    mgd = nc.dram_tensor("mgd", [NLH, 128, S], BF16, kind=("Internal" if split else dk)).ap()
    mga = [nc.dram_tensor("mga%d" % k, [2 * len(g) * 128, S], BF16).ap() for k, g in enumerate(ccg)]
    npairs_cc = ncores // 2
    rgroups = [[2 * i, 2 * i + 1] for i in range(npairs_cc)]
    hdbg = nc.dram_tensor("hdbg", [16, 128, S], BF16, kind=dk).ap() if debug else None

    def xv(ap):
        return ap.rearrange("(c p) s -> c p s", p=128)

    with ExitStack() as es:
        def sb(name, shape, dt):
            return es.enter_context(nc.sbuf_tensor(name, shape, dt))

        def ps(name, shape, dt):
            return es.enter_context(nc.psum_tensor(name, shape, dt))

        H = sb("H", [128, NCH, S], BF16)
        WS = [sb("WS%d" % i, [128, NCH, 512], BF16) for i in range(2)]
        vecs = sb("vecs_sb", [128, nv], F32)
        ones_f = sb("ones_f", [128, 128], F32)
        ones_b = sb("ones_b", [128, 128], BF16)
        ident_b = sb("ident_b", [128, 128], BF16)
        tri_incl = sb("tri_incl", [128, 128], BF16)
        tri_low = sb("tri_low", [128, 128], BF16)
        tri_ui = sb("tri_ui", [128, 128], F32)
        blkmask = sb("blkmask", [128, 128], F32)
        negmask = sb("negmask", [128, 896], BF16)
        rmask = sb("rmask", [128, 512], F32)
        qT = [sb("qT%d" % i, [128, S], BF16) for i in range(2)]
        kT = [sb("kT%d" % i, [128, S], BF16) for i in range(2)]
        vtok = [sb("vtok%d" % i, [128, 16, 128], BF16) for i in range(2)]
        NZ = 5
        zs = [sb("zs%d" % i, [128, 512], F32) for i in range(NZ)]
        spb = [sb("spb%d" % i, [128, 512], BF16) for i in range(NZ)]
        et = [sb("et%d" % i, [128, 512], F32) for i in range(2)]
        Ab = [sb("Ab%d" % i, [128, 512], BF16) for i in range(NZ)]
        NE = 2
        e_o = [sb("e_o%d" % i, [128, 512], F32) for i in range(NE)]
        e_sq = [sb("e_sq%d" % i, [128, 512], F32) for i in range(NE)]
        e_rs = [sb("e_rs%d" % i, [128, 512], F32) for i in range(NE)]
        e_g = [sb("e_g%d" % i, [128, 512], F32) for i in range(NE)]
        e_mg = [sb("e_mg%d" % i, [128, 512], BF16) for i in range(NE)]
        NW = 5
        wk = [sb("wk%d" % i, [128, 512], F32) for i in range(NW)]
        NX = 3
        xt = [sb("xt%d" % i, [128, 2, 512], F32) for i in range(NX)]
        rT = sb("rT_sb", [16, S], BF16)
        wup = sb("wup_sb", [16, 128], BF16)
        g_q = sb("g_q", [128, 512], BF16)
        g_k = sb("g_k", [128, 512], BF16)
        g_kd = sb("g_kd", [128, 512], BF16)
        g_kdt = sb("g_kdt", [128, 4, 128], BF16)
        g_v = sb("g_v", [128, 4, 256], BF16)
        g_at = [sb("g_at%d" % i, [128, 128], BF16) for i in range(2)]
        S32 = sb("S32", [128, 128], F32)
        Sbf = sb("Sbf", [128, 128], BF16)
        gn_bc = sb("gn_bc", [128, 512], F32)
        wsT = [sb("wsT%d" % i, [128, 128], BF16) for i in range(NG)]
        wsTf = sb("wsTf", [128, 128], F32)
        bs_bc = [sb("bs_bc%d" % i, [128, 128], F32) for i in range(NG)]
        s_ss = sb("s_ss", [128, 8], F32)
        Z = [ps("Z%d" % i, [128, 512], F32) for i in range(2)]
        C = [ps("C%d" % i, [128, 512], F32) for i in range(2)]
        O = [ps("O%d" % i, [128, 512], F32) for i in range(2)]
        P = [ps("P%d" % i, [128, 512], F32) for i in range(2)]
        Tb = O[1][:].bitcast(BF16)[:, 0:512]

        g_sp, g_cum, g_eb, g_ebi, g_k32 = wk[0], wk[1], wk[2], wk[3], wk[4]
        K_sp, K_cum, K_eb, K_ebi, K_k32 = ("wk", 0), ("wk", 1), ("wk", 2), ("wk", 3), ("wk", 4)
        kxT = qT[0][:, 0:1024].rearrange("p (h m) -> p h m", h=4)
        vx = qT[0][:, 1024:2048].rearrange("p (b n) -> p b n", b=2)
        K_kxT = [("qT", 0, 0), ("qT", 0, 1)]
        K_vx = [("qT", 0, 2), ("qT", 0, 3)]
        s_vn = [Ab[3], Ab[4]]
        qx = [spb[1], spb[2]]
        K_qx = [("spb", 1), ("spb", 2)]
        pT = [Ab[0], Ab[1], Ab[2], spb[0]]
        K_pT = [("Ab", 0), ("Ab", 1), ("Ab", 2), ("spb", 0)]
        oxT = kT[0][:, :].rearrange("p (h n) -> p h n", h=4)
        print("SBUF bytes remaining:", nc.sbuf_bytes_remaining)

        sc = Sched(nc, es)
        op = sc.op
        ctr = {"p": 0, "e": 0, "w": 0, "x": 0, "z": 0}

        def nxt(k, n):
            v = ctr[k]
            ctr[k] = (v + 1) % n
            return v

        def consts():
            op("pool", lambda e: e.memset(ones_f[:], 1.0), writes=["ones_f"])
            op("pool", lambda e: e.memset(ones_b[:], 1.0), writes=["ones_b"])
            op("pool", lambda e: e.affine_select(out=ident_b[:], in_=ones_b[:], pattern=[[-1, 128]], compare_op=ALU.is_equal,
                                                 fill=0.0, base=0, channel_multiplier=1), reads=["ones_b"], writes=["ident_b"])
            op("pool", lambda e: e.affine_select(out=tri_incl[:], in_=ones_b[:], pattern=[[-1, 128]], compare_op=ALU.is_ge,
                                                 fill=0.0, base=0, channel_multiplier=1), reads=["ones_b"], writes=["tri_incl"])
            op("pool", lambda e: e.affine_select(out=tri_low[:], in_=ones_b[:], pattern=[[1, 128]], compare_op=ALU.is_gt,
                                                 fill=0.0, base=0, channel_multiplier=-1), reads=["ones_b"], writes=["tri_low"])
            op("pool", lambda e: e.affine_select(out=tri_ui[:], in_=ones_f[:], pattern=[[1, 128]], compare_op=ALU.is_ge,
                                                 fill=0.0, base=0, channel_multiplier=-1), reads=["ones_f"], writes=["tri_ui"])
            op("pool", lambda e: e.affine_select(out=blkmask[:], in_=ones_f[:], pattern=[[1, 128]], compare_op=ALU.is_ge,
                                                 fill=0.0, base=0, channel_multiplier=-1), reads=["ones_f"], writes=["blkmask"])
            op("pool", lambda e: e.memset(blkmask[0:64, 64:128], 0.0), reads=["blkmask"], writes=["blkmask"])
            op("pool", lambda e: e.memset(negmask[:], 0.0), writes=["negmask"])
            op("pool", lambda e: e.affine_select(out=negmask[:], in_=negmask[:], pattern=[[1, 896]],
                                                 compare_op=ALU.is_gt, fill=NEG, base=-384, channel_multiplier=-1),
               reads=["negmask"], writes=["negmask"])
            op("pool", lambda e: e.memset(rmask[:], 1.0), writes=["rmask"])
            op("pool", lambda e: e.memset(rmask[:].rearrange("p (c t) -> p c t", t=64)[:, :, 0:1], 0.0), reads=["rmask"], writes=["rmask"])
            op("sp", lambda e: e.dma_start(out=vecs[:], in_=vecs_d[:, :]), writes=["vecs"], dma=True)
            b0 = voff["bgate"]
            op("dve", lambda e: e.tensor_scalar(out=vecs[:, b0:b0 + 2 * L], in0=vecs[:, b0:b0 + 2 * L], scalar1=-1.0, scalar2=None,
                                                op0=ALU.mult), reads=["vecs"], writes=["vecs"])

        def rstd_from_ss(ss_ap, ss_key, out_tile, out_key, inv_n, tmp_tile, tmp_key):
            op("act", lambda e: e.activation(out=tmp_tile, in_=ss_ap, func=AF.Ln, bias=EPS, scale=inv_n),
               reads=[ss_key], writes=[tmp_key])
            op("act", lambda e: e.activation(out=out_tile, in_=tmp_tile, func=AF.Exp, scale=-0.5),
               reads=[tmp_key], writes=[out_key])

        def norm_phase(src, srckey, gcol, ntok, dst_fn, final=False):
            nblk = max(1, ntok // TB)
            w = min(TB, ntok)
            srcv = src.rearrange("(c p) s -> p c s", p=128)
            lq = ["sp", "pool"]
            for tb in range(nblk):
                t0 = tb * w
                ri = nxt("e", NE)
                for c2 in range(NCH // 2):
                    xi = nxt("x", NX)
                    op(lq[c2 % 2], lambda e, c2=c2, xi=xi, t0=t0: e.dma_start(out=xt[xi][:, :, 0:w], in_=srcv[:, 2 * c2:2 * c2 + 2, t0:t0 + w]),
                       reads=[(srckey, 2 * c2, tb), (srckey, 2 * c2 + 1, tb)], writes=[("xt", xi)], dma=True)
                    for j in range(2):
                        c = 2 * c2 + j
                        if c == 0:
                            op("act", lambda e, xi=xi, ri=ri, j=j: e.activation(out=e_o[ri][:, 0:w], in_=xt[xi][:, j, 0:w], func=AF.Square),
                               reads=[("xt", xi)], writes=[("e_o", ri)])
                        else:
                            wi = nxt("w", NW)
                            op("act", lambda e, xi=xi, wi=wi, j=j: e.activation(out=wk[wi][:, 0:w], in_=xt[xi][:, j, 0:w], func=AF.Square),
                               reads=[("xt", xi)], writes=[("wk", wi)])
                            op("dve", lambda e, wi=wi, ri=ri: e.tensor_tensor(out=e_o[ri][:, 0:w], in0=e_o[ri][:, 0:w], in1=wk[wi][:, 0:w], op=ALU.add),
                               reads=[("wk", wi), ("e_o", ri)], writes=[("e_o", ri)])
                si = nxt("p", 2)
                op("pe", lambda e, ri=ri, si=si: e.matmul(P[si][:, 0:w], ones_f[:], e_o[ri][:, 0:w], start=True, stop=True),
                   reads=[("e_o", ri), "ones_f"], writes=[("P", si)])
                rstd_from_ss(P[si][:, 0:w], ("P", si), e_rs[ri][:, 0:w], ("e_rs", ri), 1.0 / D, e_sq[ri][:, 0:w], ("e_sq", ri))
                for c2 in range(NCH // 2):
                    xi = nxt("x", NX)
                    op(lq[c2 % 2], lambda e, c2=c2, xi=xi, t0=t0: e.dma_start(out=xt[xi][:, :, 0:w], in_=srcv[:, 2 * c2:2 * c2 + 2, t0:t0 + w]),
                       reads=[(srckey, 2 * c2, tb), (srckey, 2 * c2 + 1, tb)], writes=[("xt", xi)], dma=True)
                    for j in range(2):
                        c = 2 * c2 + j
                        if not final:
                            dap, dkeys = dst_fn(c, t0, w, tb)
                            op("dve", lambda e, c=c, xi=xi, dap=dap, ri=ri, j=j: e.scalar_tensor_tensor(
                                out=dap, in0=xt[xi][:, j, 0:w], scalar=vecs[:, gcol + c:gcol + c + 1], in1=e_rs[ri][:, 0:w],
                                op0=ALU.mult, op1=ALU.mult), reads=[("xt", xi), ("e_rs", ri), "vecs"], writes=dkeys)
                        else:
                            op("dve", lambda e, c=c, xi=xi, ri=ri, j=j: e.scalar_tensor_tensor(
                                out=xt[xi][:, j, 0:w], in0=xt[xi][:, j, 0:w], scalar=vecs[:, gcol + c:gcol + c + 1], in1=e_rs[ri][:, 0:w],
                                op0=ALU.mult, op1=ALU.mult), reads=[("xt", xi), ("e_rs", ri), "vecs"], writes=[("xt", xi)])
                    if final:
                        yv = yT.rearrange("(c p) s -> p c s", p=128)
                        tok = op("act", lambda e, c2=c2, xi=xi, t0=t0: e.dma_start(out=yv[:, 2 * c2:2 * c2 + 2, t0:t0 + w], in_=xt[xi][:, :, 0:w]),
                                 reads=[("xt", xi)], writes=[("yT", 2 * c2, tb), ("yT", 2 * c2 + 1, tb)], dma=True)
                        out_toks.append(tok)

        def h_dst(c, t0, w, tb):
            return H[:, c, t0:t0 + w], [("H", c, tb)]

        def hkeys(tb):
            return [("H", c, tb) for c in range(NCH)]

        def proj_fm(slot, col0, tb, ncols=128):
            pi = nxt("p", 2)

            def f(e):
                ins = None
                for c in range(NCH):
                    ins = e.matmul(P[pi][0:ncols, :], WS[slot][:, c, col0:col0 + ncols], H[:, c, tb * TB:(tb + 1) * TB],
                                   start=(c == 0), stop=(c == NCH - 1))
                return ins
            op("pe", f, reads=[("WS", slot)] + hkeys(tb), writes=[("P", pi)])
            return pi

        def proj_tm(slot, col0, ncols, tb, blocks):
            pi = nxt("p", 2)

            def f(e):
                ins = None
                for j, blk in enumerate(blocks):
                    for c in range(NCH):
                        ins = e.matmul(P[pi][:, j * ncols:(j + 1) * ncols], H[:, c, blk * 128:(blk + 1) * 128],
                                       WS[slot][:, c, col0:col0 + ncols], start=(c == 0), stop=(c == NCH - 1))
                return ins
            op("pe", f, reads=[("WS", slot)] + hkeys(tb), writes=[("P", pi)])
            return pi

        def load_ws(slot, dram_view):
            op("pool", lambda e: e.dma_start(out=WS[slot][:], in_=dram_view), writes=[("WS", slot)], dma=True)

        def win_view(l, s):
            return w_in[l].rearrange("(c p) n -> p c n", p=128)[:, :, s * 512:(s + 1) * 512]

        def epilogue(l, m, tb, src_ap, src_key, gslot, gcol, banks=None):
            ei = nxt("e", NE)
            op("act", lambda e: e.activation(out=e_o[ei][:], in_=src_ap, func=AF.Copy), reads=[src_key], writes=[("e_o", ei)])
            if banks is None:
                pi = nxt("p", 2)
                G, Gk = P[pi], ("P", pi)
                si = nxt("p", 2)
                SSt, SSk = P[si], ("P", si)
            else:
                G, Gk, SSt, SSk = banks

            def fg(e):
                ins = None
                for c in range(NCH):
                    ins = e.matmul(G[:], WS[gslot][:, c, gcol:gcol + 128], H[:, c, tb * TB:(tb + 1) * TB], start=(c == 0), stop=(c == NCH - 1))
                return ins
            op("pe", fg, reads=[("WS", gslot)] + hkeys(tb), writes=[Gk])
            op("act", lambda e: e.activation(out=e_g[ei][:], in_=G[:], func=AF.Exp, scale=-1.0), reads=[Gk], writes=[("e_g", ei)])
            op("act", lambda e: e.activation(out=e_g[ei][:], in_=e_g[ei][:], func=AF.Ln, bias=1.0), reads=[("e_g", ei)], writes=[("e_g", ei)])
            op("act", lambda e: e.activation(out=e_g[ei][:], in_=e_g[ei][:], func=AF.Exp, scale=-1.0), reads=[("e_g", ei)], writes=[("e_g", ei)])
            op("dve", lambda e: e.tensor_tensor(out=e_g[ei][:], in0=G[:], in1=e_g[ei][:], op=ALU.mult),
               reads=[Gk, ("e_g", ei)], writes=[("e_g", ei)])
            op("act", lambda e: e.activation(out=e_sq[ei][:], in_=e_o[ei][:], func=AF.Square), reads=[("e_o", ei)], writes=[("e_sq", ei)])
            op("pe", lambda e: e.matmul(SSt[:], ones_f[:], e_sq[ei][:], start=True, stop=True), reads=[("e_sq", ei), "ones_f"], writes=[SSk])
            rstd_from_ss(SSt[:], SSk, e_rs[ei][:], ("e_rs", ei), 1.0 / 128, e_sq[ei][:], ("e_sq", ei))
            gc = voff["onorm"] + 16 * l + m
            op("dve", lambda e: e.scalar_tensor_tensor(out=e_o[ei][:], in0=e_o[ei][:], scalar=vecs[:, gc:gc + 1], in1=e_rs[ei][:],
                                                       op0=ALU.mult, op1=ALU.mult), reads=[("e_o", ei), ("e_rs", ei), "vecs"], writes=[("e_o", ei)])
            op("dve", lambda e: e.tensor_tensor(out=e_mg[ei][:], in0=e_o[ei][:], in1=e_g[ei][:], op=ALU.mult),
               reads=[("e_o", ei), ("e_g", ei)], writes=[("e_mg", ei)])
            op("sp", lambda e: e.dma_start(out=mgd[m][:, tb * TB:(tb + 1) * TB], in_=e_mg[ei][:]), reads=[("e_mg", ei)],
               writes=[("MG", m, tb)], dma=True)

        def sb_proj_items(i, slot):
            bi = i % 2
            scale = 128.0 ** -0.5
            items = []

            def group(tb, kind):
                st = {}

                def piece(k):
                    def f():
                        if k == 0:
                            st["pi"] = nxt("p", 2)
                        pi = st["pi"]

                        def mm(e):
                            ins = None
                            if kind == "v":
                                blk = tb * 4 + k
                                for c in range(NCH):
                                    ins = e.matmul(P[pi][:, k * 128:(k + 1) * 128], H[:, c, blk * 128:(blk + 1) * 128],
                                                   WS[slot][:, c, 256:384], start=(c == 0), stop=(c == NCH - 1))
                            else:
                                col0 = 0 if kind == "q" else 128
                                for c in range(4 * k, 4 * k + 4):
                                    ins = e.matmul(P[pi][:], WS[slot][:, c, col0:col0 + 128], H[:, c, tb * TB:(tb + 1) * TB],
                                                   start=(c == 0), stop=(c == NCH - 1))
                            return ins
                        op("pe", mm, reads=[("WS", slot)] + hkeys(tb), writes=[("P", pi)])
                    return f

                def evac():
                    pi = st["pi"]
                    if kind == "q":
                        op("act", lambda e: e.activation(out=qT[bi][:, tb * TB:(tb + 1) * TB], in_=P[pi][:], func=AF.Copy, scale=scale),
                           reads=[("P", pi)], writes=[("qT", bi, tb)])
                    elif kind == "k":
                        op("dve", lambda e: e.tensor_copy(out=kT[bi][:, tb * TB:(tb + 1) * TB], in_=P[pi][:]),
                           reads=[("P", pi)], writes=[("kT", bi, tb)])
                    else:
                        op("dve", lambda e: e.tensor_copy(out=vtok[bi][:, tb * 4:(tb + 1) * 4, :], in_=P[pi][:].rearrange("p (j d) -> p j d", d=128)),
                           reads=[("P", pi)], writes=[("vtok", bi, tb)])
                return [piece(k) for k in range(4)] + [evac]
            for tb in range(NTB):
                for kind in ("k", "v", "q"):
                    items += group(tb, kind)
            return items

        def sb_attn(l, i, slot, bg):
            bi = i % 2

            def tile_ops(ch, kb, qb):
                zi = nxt("z", NZ)
                zb = zi % 2
                ei = zi % 2
                r = kb - 4 * qb
                first = (kb == 4 * qb + 3)
                last = (kb == 0)
                d = {}
                d["Z"] = lambda: op("pe", lambda e: e.matmul(Z[zb][:], kT[bi][:, kb * 128:(kb + 1) * 128], qT[bi][:, qb * TB:(qb + 1) * TB], start=True, stop=True),
                                    reads=[("kT", bi, kb // 4), ("qT", bi, qb)], writes=[("Z", zb)])
                if r >= 0:
                    d["COPY"] = lambda: op("dve", lambda e: e.tensor_tensor(out=zs[zi][:], in0=Z[zb][:], in1=negmask[:, 384 - 128 * r:896 - 128 * r], op=ALU.add),
                                           reads=[("Z", zb), "negmask"], writes=[("zs", zi)])
                else:
                    d["COPY"] = lambda: op("dve", lambda e: e.tensor_copy(out=zs[zi][:], in_=Z[zb][:]), reads=[("Z", zb)], writes=[("zs", zi)])
                d["EXPA"] = lambda: op("act", lambda e: e.activation(out=et[ei][:], in_=zs[zi][:], func=AF.Exp), reads=[("zs", zi)], writes=[("et", ei)])
                d["LN"] = lambda: op("act", lambda e: e.activation(out=spb[zi][:], in_=et[ei][:], func=AF.Ln, bias=1.0), reads=[("et", ei)], writes=[("spb", zi)])
                d["TRI"] = lambda: op("pe", lambda e: e.matmul(C[ch][:], tri_incl[:], spb[zi][:], start=first, stop=last),
                                      reads=[("spb", zi), "tri_incl"], writes=[("C", ch)])
                d["SUB"] = lambda: op("dve", lambda e: e.scalar_tensor_tensor(out=zs[zi][:], in0=C[ch][:], scalar=-1.0, in1=zs[zi][:], op0=ALU.mult, op1=ALU.add),
                                      reads=[("C", ch), ("zs", zi)], writes=[("zs", zi)])
                if not last:
                    d["LOW"] = lambda: op("pe", lambda e: e.matmul(C[ch][:], tri_low[:], spb[zi][:], start=False, stop=False),
                                          reads=[("spb", zi), "tri_low"], writes=[("C", ch)])
                else:
                    d["LOW"] = lambda: None
                d["EXPB"] = lambda: op("act", lambda e: e.activation(out=Ab[zi][:], in_=zs[zi][:], func=AF.Exp), reads=[("zs", zi)], writes=[("Ab", zi)])

                def av():
                    op("pe", lambda e: e.matmul(O[ch][:], vtok[bi][:, kb, :], Ab[zi][:], start=first, stop=last),
                       reads=[("Ab", zi), ("vtok", bi, kb // 4)], writes=[("O", ch)])
                    if last:
                        epilogue(l, i, qb, O[ch][:], ("O", ch), slot, 384, banks=(O[ch], ("O", ch), C[ch], ("C", ch)))
                d["AV"] = av
                return d

            nper = 0
            for (qa, qbb) in ((0, 3), (1, 2)):
                ta = [(0, kb, qa) for kb in range(4 * qa + 3, -1, -1)]
                tbl = [(1, kb, qbb) for kb in range(4 * qbb + 3, -1, -1)]
                T = []
                ia = ib = 0
                while ia < len(ta) or ib < len(tbl):
                    if ib < len(tbl) and (ia >= len(ta) or ib * len(ta) <= ia * len(tbl)):
                        T.append(tbl[ib])
                        ib += 1
                    else:
                        T.append(ta[ia])
                        ia += 1
                n = len(T)
                ops_ = {}
                for p in range(n + 2):
                    if p < n:
                        ops_[p] = tile_ops(*T[p])
                    if p - 2 >= 0:
                        ops_[p - 2]["LOW"]()
                        ops_[p - 2]["EXPB"]()
                    if p < n:
                        ops_[p]["Z"]()
                        ops_[p]["COPY"]()
                    if p - 2 >= 0:
                        ops_[p - 2]["AV"]()
                    if p < n:
                        ops_[p]["EXPA"]()
                        ops_[p]["LN"]()
                    if 0 <= p - 1 < n:
                        ops_[p - 1]["TRI"]()
                        ops_[p - 1]["SUB"]()
                    nper += 1
                    for _ in range(2):
                        if bg:
                            bg.pop(0)()
            while bg:
                bg.pop(0)()

        def gla_pair(l, j, slotA, slotB, first_pair):
            gp = pairs[j]
            if first_pair:
                for tb in range(NTB):
                    pi = proj_fm(slotB, 256, tb, ncols=16)
                    op("act", lambda e, pi=pi, tb=tb: e.activation(out=rT[:, tb * TB:(tb + 1) * TB], in_=P[pi][0:16, :], func=AF.Copy),
                       reads=[("P", pi)], writes=[("rT", tb)])
            op("pool", lambda e: e.dma_start(out=wup[:], in_=wup_d[l][:, j * 128:(j + 1) * 128]), writes=["wup"], dma=True)
            op("dve", lambda e: e.memset(S32[:], 0.0), writes=["S32"])
            op("dve", lambda e: e.memset(Sbf[:], 0.0), writes=["Sbf"])
            bcol = voff["bgate"] + 2 * l + j
            for tb in range(NTB):
                tsl = slice(tb * TB, (tb + 1) * TB)
                pi = nxt("p", 2)
                op("pe", lambda e, pi=pi, tsl=tsl: e.matmul(P[pi][:], wup[:], rT[:, tsl], start=True, stop=True),
                   reads=["wup", ("rT", tb)], writes=[("P", pi)])
                op("act", lambda e, pi=pi: e.activation(out=g_sp[:], in_=P[pi][:], func=AF.Exp, scale=-1.0, bias=vecs[:, bcol:bcol + 1]),
                   reads=[("P", pi), "vecs"], writes=[K_sp])
                op("act", lambda e: e.activation(out=g_sp[:], in_=g_sp[:], func=AF.Ln, bias=1.0), reads=[K_sp], writes=[K_sp])
                op("dve", lambda e: e.tensor_tensor_scan(out=g_cum[:], data0=rmask[:], data1=g_sp[:], initial=0.0, op0=ALU.mult, op1=ALU.add),
                   reads=[K_sp, "rmask"], writes=[K_cum])
                op("act", lambda e: e.activation(out=g_eb[:], in_=g_cum[:], func=AF.Exp, scale=-1.0 / 16), reads=[K_cum], writes=[K_eb])
                op("act", lambda e: e.activation(out=g_ebi[:], in_=g_cum[:], func=AF.Exp, scale=1.0 / 16), reads=[K_cum], writes=[K_ebi])
                pi = proj_fm(slotA, 0, tb)
                op("dve", lambda e, pi=pi: e.scalar_tensor_tensor(out=g_q[:], in0=P[pi][:], scalar=0.125, in1=g_eb[:], op0=ALU.mult, op1=ALU.mult),
                   reads=[("P", pi), K_eb], writes=["g_q"])
                pi = proj_fm(slotA, 128, tb)
                op("dve", lambda e, pi=pi: e.tensor_tensor(out=g_k32[:], in0=P[pi][:], in1=g_ebi[:], op=ALU.mult),
                   reads=[("P", pi), K_ebi], writes=[K_k32])
                op("act", lambda e: e.activation(out=g_k[:], in_=g_k32[:], func=AF.Copy), reads=[K_k32], writes=["g_k"])
                dec_bc = g_eb[:].rearrange("p (c t) -> p c t", t=64)[:, :, 63:64].broadcast_to([128, 8, 64])
                op("dve", lambda e: e.tensor_tensor(out=g_kd[:].rearrange("p (c t) -> p c t", t=64), in0=g_k32[:].rearrange("p (c t) -> p c t", t=64),
                                                    in1=dec_bc, op=ALU.mult), reads=[K_k32, K_eb], writes=["g_kd"])

                def ftr(e):
                    ins = None
                    for j4 in range(4):
                        ins = e.transpose(Tb[:, j4 * 128:(j4 + 1) * 128], g_kd[:, j4 * 128:(j4 + 1) * 128], ident_b[:])
                    return ins
                op("pe", ftr, reads=["g_kd", "ident_b"], writes=[("O", 1)])
                op("dve", lambda e: e.tensor_copy(out=g_kdt[:], in_=Tb.rearrange("p (j d) -> p j d", d=128)), reads=[("O", 1)], writes=["g_kdt"])
                for half in range(2):
                    pi = proj_tm(slotA, 256, 256, tb, [tb * 4 + 2 * half, tb * 4 + 2 * half + 1])
                    op("act", lambda e, pi=pi, half=half: e.activation(out=g_v[:, 2 * half:2 * half + 2, :],
                                                                       in_=P[pi][:].rearrange("p (j d) -> p j d", d=256), func=AF.Copy),
                       reads=[("P", pi)], writes=["g_v"])
                OG = [O[0], C[0]]
                OGk = [("O", 0), ("C", 0)]
                for cp in range(4):
                    csl = slice(cp * 128, (cp + 1) * 128)
                    for h in range(2):
                        hs = slice(h * 64, (h + 1) * 64)
                        ai = (cp * 2 + h) % 2
                        op("pe", lambda e, hs=hs, csl=csl: e.matmul(Z[0][:, 0:128], g_k[hs, csl], g_q[hs, csl], start=True, stop=True),
                           reads=["g_k", "g_q"], writes=[("Z", 0)])
                        op("dve", lambda e, ai=ai: e.tensor_tensor(out=g_at[ai][:], in0=Z[0][:, 0:128], in1=blkmask[:], op=ALU.mult),
                           reads=[("Z", 0), "blkmask"], writes=[("g_at", ai)])
                        op("pe", lambda e, h=h, ai=ai, csl=csl, cp=cp: e.matmul(OG[h][:, csl], g_v[:, cp, h * 128:(h + 1) * 128], g_at[ai][:],
                                                                                 start=True, stop=False),
                           reads=["g_v", ("g_at", ai)], writes=[OGk[h]])
                    for c2 in range(2):
                        ch = cp * 2 + c2
                        tsl64 = slice(cp * 128 + c2 * 64, cp * 128 + c2 * 64 + 64)
                        psl = slice(c2 * 64, c2 * 64 + 64)

                        def finter(e, tsl64=tsl64, c2=c2):
                            ins = None
                            for h in range(2):
                                hs = slice(h * 64, (h + 1) * 64)
                                ins = e.matmul(OG[h][:, tsl64], Sbf[hs, :], g_q[hs, tsl64], start=False, stop=(c2 == 1))
                            return ins
                        op("pe", finter, reads=["Sbf", "g_q"], writes=[("O", 0), ("C", 0)])

                        def fkv(e, psl=psl, cp=cp):
                            ins = None
                            for h in range(2):
                                ins = e.matmul(Z[1][h * 64:(h + 1) * 64, 0:128], g_kdt[psl, cp, h * 64:(h + 1) * 64],
                                               g_v[psl, cp, h * 128:(h + 1) * 128], start=True, stop=True)
                            return ins
                        op("pe", fkv, reads=["g_kdt", "g_v"], writes=[("Z", 1)])
                        dcol = ch * 64 + 63
                        op("dve", lambda e, dcol=dcol: e.scalar_tensor_tensor(out=S32[:], in0=S32[:], scalar=g_eb[:, dcol:dcol + 1], in1=Z[1][:, 0:128],
                                                                              op0=ALU.mult, op1=ALU.add),
                           reads=["S32", K_eb, ("Z", 1)], writes=["S32"])
                        op("dve", lambda e: e.tensor_copy(out=Sbf[:], in_=S32[:]), reads=["S32"], writes=["Sbf"])
                for h in range(2):
                    epilogue(l, NSB + 2 * j + h, tb, OG[h][:], OGk[h], slotB, h * 128)

        def gelu_from_psum(pi, out_tile, out_key):
            a = nxt("w", NW)
            b = nxt("w", NW)
            op("dve", lambda e: e.tensor_copy(out=wk[a][:], in_=P[pi][:]), reads=[("P", pi)], writes=[("wk", a)])
            op("dve", lambda e: e.tensor_tensor(out=wk[b][:], in0=wk[a][:], in1=wk[a][:], op=ALU.mult), reads=[("wk", a)], writes=[("wk", b)])
            op("dve", lambda e: e.tensor_scalar(out=wk[b][:], in0=wk[b][:], scalar1=0.044715, scalar2=1.0, op0=ALU.mult, op1=ALU.add),
               reads=[("wk", b)], writes=[("wk", b)])
            op("dve", lambda e: e.tensor_tensor(out=wk[b][:], in0=wk[b][:], in1=wk[a][:], op=ALU.mult), reads=[("wk", a), ("wk", b)], writes=[("wk", b)])
            op("act", lambda e: e.activation(out=wk[b][:], in_=wk[b][:], func=AF.Exp, scale=-GELU_C), reads=[("wk", b)], writes=[("wk", b)])
            op("act", lambda e: e.activation(out=wk[b][:], in_=wk[b][:], func=AF.Ln, bias=1.0), reads=[("wk", b)], writes=[("wk", b)])
            op("act", lambda e: e.activation(out=wk[b][:], in_=wk[b][:], func=AF.Exp, scale=-1.0), reads=[("wk", b)], writes=[("wk", b)])
            op("dve", lambda e: e.tensor_tensor(out=out_tile, in0=wk[a][:], in1=wk[b][:], op=ALU.mult), reads=[("wk", a), ("wk", b)], writes=[out_key])

        def sgu_unit(l, slotV, uslot, lgs):
            op("sp", lambda e: e.dma_start(out=gn_bc[:], in_=sgn_d[l].partition_broadcast(128)), writes=["gn_bc"], dma=True)
            for lg in lgs:
                op("sp", lambda e, lg=lg: e.dma_start(out=wsTf[:], in_=wsg_d[l][lg]), writes=["wsTf"], dma=True)
                op("dve", lambda e, lg=lg: e.tensor_tensor(out=wsT[lg][:], in0=wsTf[:], in1=tri_ui[:], op=ALU.mult),
                   reads=["wsTf", "tri_ui"], writes=[("wsT", lg)])
                op("sp", lambda e, lg=lg: e.dma_start(out=bs_bc[lg][:], in_=bsg_d[l][lg].partition_broadcast(128)), writes=[("bs_bc", lg)], dma=True)
            MX = [O[0], C[0]]
            MXk = [("O", 0), ("C", 0)]
            pend_fmx = []
            for tb in range(NTB):
                for j4 in range(4):
                    blk = tb * 4 + j4
                    pi = proj_tm(slotV, 0, 512, tb, [blk])
                    a = nxt("w", NW)
                    gelu_from_psum(pi, wk[a][:], ("wk", a))
                    b = nxt("w", NW)
                    sscol = blk % 8
                    op("dve", lambda e, sscol=sscol: e.memset(s_ss[:, sscol:sscol + 1], 0.0), writes=[("s_ss", sscol)])
                    op("act", lambda e, a=a, b=b, sscol=sscol: e.activation(out=wk[b][:], in_=wk[a][:], func=AF.Square, accum_out=s_ss[:, sscol:sscol + 1]),
                       reads=[("wk", a), ("s_ss", sscol)], writes=[("wk", b), ("s_ss", sscol)])
                    op("act", lambda e, sscol=sscol: e.activation(out=s_ss[:, sscol:sscol + 1], in_=s_ss[:, sscol:sscol + 1], func=AF.Ln, bias=EPS, scale=1.0 / 512),
                       reads=[("s_ss", sscol)], writes=[("s_ss", sscol)])
                    op("act", lambda e, sscol=sscol: e.activation(out=s_ss[:, sscol:sscol + 1], in_=s_ss[:, sscol:sscol + 1], func=AF.Exp, scale=-0.5),
                       reads=[("s_ss", sscol)], writes=[("s_ss", sscol)])
                    vi = blk % 2
                    op("dve", lambda e, a=a, vi=vi, sscol=sscol: e.scalar_tensor_tensor(out=s_vn[vi][:], in0=wk[a][:], scalar=s_ss[:, sscol:sscol + 1],
                                                                                       in1=gn_bc[:], op0=ALU.mult, op1=ALU.mult),
                       reads=[("wk", a), ("s_ss", sscol), "gn_bc"], writes=[("Ab", 3 + vi)])

                    def fmx(e, vi=vi, j4=j4):
                        ins = None
                        for k, lg in enumerate(lgs):
                            g = lg
                            ins = e.matmul(MX[k][:, j4 * 128:(j4 + 1) * 128], s_vn[vi][:, g * 128:(g + 1) * 128], wsT[lg][:], start=True, stop=True)
                        return ins
                    def emit_fmx(fmx=fmx, vi=vi):
                        op("pe", fmx, reads=[("Ab", 3 + vi)] + [("wsT", lg) for lg in lgs], writes=MXk[:len(lgs)])
                    if pend_fmx:
                        pend_fmx.pop(0)()
                    pend_fmx.append(emit_fmx)
                    if j4 == 3:
                        if tb + 1 < NTB:
                            pass
                        pend_fmx.pop(0)()
                for k, lg in enumerate(lgs):
                    ucol = k * 256
                    pi = proj_fm(uslot, ucol, tb)
                    a = nxt("w", NW)
                    gelu_from_psum(pi, wk[a][:], ("wk", a))
                    b = nxt("w", NW)
                    op("dve", lambda e, k=k, lg=lg, b=b: e.tensor_tensor(out=wk[b][:].rearrange("p (j t) -> p j t", t=128),
                                                                         in0=MX[k][:].rearrange("p (j t) -> p j t", t=128),
                                                                         in1=bs_bc[lg][:].unsqueeze(1).broadcast_to([128, 4, 128]), op=ALU.add),
                       reads=[MXk[k], ("bs_bc", lg)], writes=[("wk", b)])
                    op("dve", lambda e, a=a, b=b: e.tensor_tensor(out=wk[b][:], in0=wk[b][:], in1=wk[a][:], op=ALU.mult),
                       reads=[("wk", a), ("wk", b)], writes=[("wk", b)])
                    epilogue(l, NSB + 2 * NP + lg, tb, wk[b][:], ("wk", b), uslot, ucol + 128)

        def mixer(l):
            s = 0
            units = []
            for i in range(NSB):
                units.append(("sb", i, [s]))
                s += 1
            for j in range(NP):
                units.append(("gla", j, [s, s + 1]))
                s += 2
            units.append(("sgu", 0, list(range(s, s + 1 + NG // 2))))
            wsn = {"n": 0}

            def alloc(k):
                r = []
                for _ in range(k):
                    r.append(wsn["n"] % 2)
                    wsn["n"] += 1
                return r
            def exchange(k):
                grp = ccg[k]
                n = len(grp)
                src = mgd[grp[0]:grp[0] + n].rearrange("h p s -> (h p) s")
                op("pool", lambda e: e.collective_compute("AllGather", ALU.bypass, replica_groups=rgroups, ins=[src.opt()], outs=[mga[k].opt()]),
                   reads=[("MG", li, tb) for li in grp for tb in range(NTB)], writes=[("MGA", k)], dma="cc")

            for ui, (kind, idx, dsl) in enumerate(units):
                if kind == "sb":
                    if idx == 0:
                        sbws = [alloc(1)[0] for _ in range(NSB)]
                        load_ws(sbws[0], win_view(l, dsl[0]))
                        for f in sb_proj_items(0, sbws[0]):
                            f()
                    bg = []
                    if idx + 1 < NSB:
                        load_ws(sbws[idx + 1], win_view(l, dsl[0] + 1))
                        bg = sb_proj_items(idx + 1, sbws[idx + 1])
                    sb_attn(l, idx, sbws[idx], bg)
                elif kind == "gla":
                    ws = alloc(2)
                    load_ws(ws[0], win_view(l, dsl[0]))
                    load_ws(ws[1], win_view(l, dsl[1]))
                    if split and idx == 0:
                        exchange(0)
                    gla_pair(l, idx, ws[0], ws[1], idx == 0)
                else:
                    for k in range(NG // 2):
                        ws = alloc(2)
                        load_ws(ws[0], win_view(l, dsl[0]))
                        load_ws(ws[1], win_view(l, dsl[1 + k]))
                        if split and k == 0:
                            exchange(1)
                        sgu_unit(l, ws[0], ws[1], [2 * k, 2 * k + 1])
                    if split:
                        exchange(2)

        def out_proj(l, xsrc, skey, xdst, dkey):
            if not split:
                for m in range(16):
                    op("sp", lambda e, m=m: e.dma_start(out=H[:, m, :], in_=mgd[m]), reads=[("MG", m, tb) for tb in range(NTB)],
                       writes=[("H", m, tb) for tb in range(NTB)], dma=True)
            else:
                jj = 0
                for k, grp in enumerate(ccg):
                    for r in range(2):
                        for idx in range(len(grp)):
                            row = (r * len(grp) + idx) * 128
                            op("sp", lambda e, jj=jj, k=k, row=row: e.dma_start(out=H[:, jj, :], in_=mga[k][row:row + 128, :]),
                               reads=[("MGA", k)], writes=[("H", jj, tb) for tb in range(NTB)], dma=True)
                            jj += 1
            if stop == "mixload":
                for c in range(NCH):
                    out_toks.append(op("sp", lambda e, c=c: e.dma_start(out=hdbg[c], in_=H[:, c, :]), reads=[("H", c, tb) for tb in range(NTB)],
                                       writes=[("hdbg", c)], dma=True))
                return
            wv = w_out[l].rearrange("(c p) n -> p c n", p=128)
            for cs in range(4):
                slot = cs % 2
                load_ws(slot, wv[:, :, cs * 512:(cs + 1) * 512])
                for tb in range(NTB):
                    for dcp in range(2):
                        xi = nxt("x", NX)
                        d0 = cs * 4 + 2 * dcp
                        sv = xsrc.rearrange("(c p) s -> p c s", p=128)
                        dv = xdst.rearrange("(c p) s -> p c s", p=128)
                        op("sp", lambda e, xi=xi, d0=d0, tb=tb, sv=sv: e.dma_start(out=xt[xi][:], in_=sv[:, d0:d0 + 2, tb * TB:(tb + 1) * TB]),
                           reads=[(skey, d0, tb), (skey, d0 + 1, tb)], writes=[("xt", xi)], dma=True)
                        for j in range(2):
                            pi = proj_fm(slot, (2 * dcp + j) * 128, tb)
                            op("dve", lambda e, xi=xi, pi=pi, j=j: e.tensor_tensor(out=xt[xi][:, j, :], in0=P[pi][:], in1=xt[xi][:, j, :], op=ALU.add),
                               reads=[("P", pi), ("xt", xi)], writes=[("xt", xi)])
                        op("act", lambda e, xi=xi, d0=d0, tb=tb, dv=dv: e.dma_start(out=dv[:, d0:d0 + 2, tb * TB:(tb + 1) * TB], in_=xt[xi][:]),
                           reads=[("xt", xi)], writes=[(dkey, d0, tb), (dkey, d0 + 1, tb)], dma=True)

        def xattn(l, xsrc, skey, xdst, dkey):

            def mem_dst(c, t0, w, tb):
                return H[:, c, 0:NM], [("H", c, 0)]
            norm_phase(memT, "memT", voff["nmem"] + 16 * l, NM, mem_dst)
            kvv = w_xkv[l].rearrange("(c p) n -> p c n", p=128)
            load_ws(0, kvv[:, :, 0:512])
            load_ws(1, kvv[:, :, 512:1024])
            mkeys = [("H", c, 0) for c in range(NCH)]
            for h in range(4):
                pi = nxt("p", 2)

                def fk(e, pi=pi, h=h):
                    ins = None
                    for c in range(NCH):
                        ins = e.matmul(P[pi][:, 0:NM], WS[0][:, c, h * 128:(h + 1) * 128], H[:, c, 0:NM], start=(c == 0), stop=(c == NCH - 1))
                    return ins
                op("pe", fk, reads=[("WS", 0)] + mkeys, writes=[("P", pi)])
                op("act", lambda e, pi=pi, h=h: e.activation(out=kxT[:, h, :], in_=P[pi][:, 0:NM], func=AF.Copy), reads=[("P", pi)], writes=K_kxT)
            for mb in range(2):
                pi = nxt("p", 2)

                def fv(e, pi=pi, mb=mb):
                    ins = None
                    for c in range(NCH):
                        ins = e.matmul(P[pi][:], H[:, c, mb * 128:(mb + 1) * 128], WS[1][:, c, :], start=(c == 0), stop=(c == NCH - 1))
                    return ins
                op("pe", fv, reads=[("WS", 1)] + mkeys, writes=[("P", pi)])
                op("act", lambda e, pi=pi, mb=mb: e.activation(out=vx[:, mb, :], in_=P[pi][:], func=AF.Copy), reads=[("P", pi)], writes=K_vx)
            norm_phase(xsrc, skey, voff["nxa"] + 16 * l, S, h_dst)
            load_ws(0, w_xq[l].rearrange("(c p) n -> p c n", p=128))
            WO = WS[1][:].rearrange("p c n -> p (c n)").rearrange("p (h n) -> p h n", h=4)
            op("pool", lambda e: e.dma_start(out=WO, in_=w_xo[l].rearrange("(h p) n -> p h n", p=128)), writes=[("WS", 1)], dma=True)
            scale = 128.0 ** -0.5
            sv = xsrc.rearrange("(c p) s -> p c s", p=128)
            dv = xdst.rearrange("(c p) s -> p c s", p=128)
            steps = [(tb, h) for tb in range(NTB) for h in range(4)]
            st = {}

            def stA(s_):
                tb, h = steps[s_]
                pi = proj_fm(0, h * 128, tb)
                qi = s_ % 2
                op("act", lambda e: e.activation(out=qx[qi][:], in_=P[pi][:], func=AF.Copy, scale=scale), reads=[("P", pi)], writes=[K_qx[qi]])

            def stB(s_):
                tb, h = steps[s_]
                qi = s_ % 2
                for mb in range(2):
                    pti = (s_ % 2) * 2 + mb
                    op("pe", lambda e, mb=mb: e.matmul(Z[mb][:], kxT[:, h, mb * 128:(mb + 1) * 128], qx[qi][:], start=True, stop=True),
                       reads=K_kxT + [K_qx[qi]], writes=[("Z", mb)])
                    op("act", lambda e, mb=mb, pti=pti: e.activation(out=pT[pti][:], in_=Z[mb][:], func=AF.Exp), reads=[("Z", mb)], writes=[K_pT[pti]])

            def stC(s_):
                tb, h = steps[s_]
                b_ = s_ % 2
                pk = [K_pT[b_ * 2], K_pT[b_ * 2 + 1]]

                def fden(e):
                    ins = None
                    for mb in range(2):
                        ins = e.matmul(C[b_][:], ones_b[:], pT[b_ * 2 + mb][:], start=(mb == 0), stop=(mb == 1))
                    return ins
                op("pe", fden, reads=["ones_b"] + pk, writes=[("C", b_)])

                def fnum(e):
                    ins = None
                    for mb in range(2):
                        ins = e.matmul(O[b_][:], vx[:, mb, h * 128:(h + 1) * 128], pT[b_ * 2 + mb][:], start=(mb == 0), stop=(mb == 1))
                    return ins
                op("pe", fnum, reads=K_vx + pk, writes=[("O", b_)])
                a_ = nxt("w", NW)
                st[s_] = a_
                op("act", lambda e: e.activation(out=wk[a_][:], in_=C[b_][:], func=AF.Ln), reads=[("C", b_)], writes=[("wk", a_)])
                op("act", lambda e: e.activation(out=wk[a_][:], in_=wk[a_][:], func=AF.Exp, scale=-1.0), reads=[("wk", a_)], writes=[("wk", a_)])

            def stD(s_):
                tb, h = steps[s_]
                b_ = s_ % 2
                a_ = st[s_]
                op("dve", lambda e: e.tensor_tensor(out=oxT[:, h, :], in0=O[b_][:], in1=wk[a_][:], op=ALU.mult),
                   reads=[("O", b_), ("wk", a_)], writes=[("kT", 0, h)])
                if h == 3:
                    stE(tb)

            def stE(tb):
                for d2 in range(NCH // 2):
                    xi = nxt("x", NX)
                    d0 = 2 * d2
                    op("sp", lambda e, xi=xi, d0=d0: e.dma_start(out=xt[xi][:], in_=sv[:, d0:d0 + 2, tb * TB:(tb + 1) * TB]),
                       reads=[(skey, d0, tb), (skey, d0 + 1, tb)], writes=[("xt", xi)], dma=True)
                    for j in range(2):
                        dch = d0 + j
                        pi = nxt("p", 2)

                        def fo(e, pi=pi, dch=dch):
                            ins = None
                            for h in range(4):
                                ins = e.matmul(P[pi][:], WO[:, h, dch * 128:(dch + 1) * 128], oxT[:, h, :], start=(h == 0), stop=(h == 3))
                            return ins
                        op("pe", fo, reads=[("WS", 1)] + [("kT", 0, h) for h in range(4)], writes=[("P", pi)])
                        op("dve", lambda e, xi=xi, pi=pi, j=j: e.tensor_tensor(out=xt[xi][:, j, :], in0=P[pi][:], in1=xt[xi][:, j, :], op=ALU.add),
                           reads=[("P", pi), ("xt", xi)], writes=[("xt", xi)])
                    op("act", lambda e, xi=xi, d0=d0: e.dma_start(out=dv[:, d0:d0 + 2, tb * TB:(tb + 1) * TB], in_=xt[xi][:]),
                       reads=[("xt", xi)], writes=[(dkey, d0, tb), (dkey, d0 + 1, tb)], dma=True)

            ns = len(steps)
            for s_ in range(ns + 3):
                if s_ < ns:
                    stA(s_)
                if 0 <= s_ - 1 < ns:
                    stB(s_ - 1)
                if 0 <= s_ - 2 < ns:
                    stC(s_ - 2)
                if 0 <= s_ - 3 < ns:
                    stD(s_ - 3)

        out_toks = []
        consts()
        cur, ckey = xT, "xT"
        done = False
        for l in range(nlayers):
            norm_phase(cur, ckey, voff["nmix"] + 16 * l, S, h_dst)
            if stop == "norm1":
                for c in range(NCH):
                    out_toks.append(op("sp", lambda e, c=c: e.dma_start(out=hdbg[c], in_=H[:, c, :]), reads=[("H", c, tb) for tb in range(NTB)],
                                       writes=[("hdbg", c)], dma=True))
                done = True
                break
            mixer(l)
            if stop == "mixer":
                done = True
                break
            out_proj(l, cur, ckey, xA, "xA")
            if stop in ("outproj", "mixload"):
                done = True
                break
            xattn(l, xA, "xA", xB, "xB")
            cur, ckey = xB, "xB"
        if not done:
            norm_phase(cur, ckey, voff["fin"], S, None, final=True)
        for q in Sched.QUEUES:
            for i in range(Sched.NSLOT):
                g = sc.dgen[q][i]
                if g > 0:
                    out_toks.append((("d", q, i), 16 * g))
        sc.final_wait("sp", out_toks)
        with nc.Block() as block:
            sc.emit(block)
    return nc, sc


_CACHE = {}


def kernel(**inputs):
    maps = pack_inputs(inputs, SPLIT)
    if "nc" not in _CACHE:
        _CACHE["nc"] = build_program(SPLIT)[0]
    nc = _CACHE["nc"]
    res = run_bass_kernel_spmd(nc, maps, core_ids=list(range(N_CORES)))
    out = np.empty((4, S, D), np.float32)
    for b in range(4):
        out[b] = np.asarray(res.results[2 * b]["yT"]).T
    return out
```

```python
import numpy as np
from contextlib import ExitStack
import concourse.bass as bass
import concourse.mybir as mybir
from concourse.bass_utils import run_bass_kernel_spmd

F32 = mybir.dt.float32
BF16 = mybir.dt.bfloat16
AF = mybir.ActivationFunctionType
ALU = mybir.AluOpType

D = 2048
S = 2048
L = 4
NM = 256
NCH = 16
TB = 512
NTB = 4
EPS = 1e-6
NEG = -30000.0
GELU_C = 1.5957691216057308

SPLIT = True
N_CORES = 8


def core_units(hf, split):
    if split:
        return list(range(4 * hf, 4 * hf + 4)), [hf], [2 * hf, 2 * hf + 1]
    return list(range(8)), [0, 1], [0, 1, 2, 3]


def col_slots(hf, split):
    sbh, pairs, groups = core_units(hf, split)
    slots = []
    for h in sbh:
        slots.append([(128 * h, 128), (1024 + 128 * h, 128), (2048 + 128 * h, 128), (3072 + 128 * h, 128)])
    for p in pairs:
        slots.append([(4096 + 128 * p, 128), (4352 + 128 * p, 128), (4608 + 256 * p, 256)])
        slots.append([(5136 + 256 * p, 256), (5120, 16), (None, 240)])
    gord = list(groups) + [g for g in range(4) if g not in groups]
    slots.append([(6160 + 128 * g, 128) for g in gord])
    for i in range(0, len(groups), 2):
        g0, g1 = groups[i], groups[i + 1]
        slots.append([(5648 + 128 * g0, 128), (6672 + 128 * g0, 128), (5648 + 128 * g1, 128), (6672 + 128 * g1, 128)])
    return slots


def local_heads(hf, split):
    sbh, pairs, groups = core_units(hf, split)
    return list(sbh) + [8 + 2 * p + h for p in pairs for h in range(2)] + [12 + g for g in groups]


def cc_groups(split):
    return [[0, 1, 2, 3], [4, 5], [6, 7]] if split else []


def chunk_order(split):
    if not split:
        return list(range(16))
    order = []
    for grp in cc_groups(split):
        for r in range(2):
            lh = local_heads(r, split)
            order += [lh[li] for li in grp]
    return order


def vec_layout(split):
    off = {}
    n = 0
    for nm in ("nmix", "nxa", "nmem"):
        off[nm] = n
        n += L * 16
    off["fin"] = n
    n += 16
    off["onorm"] = n
    n += L * 16
    off["bgate"] = n
    n += L * 2
    return off, n


def pack_inputs(inputs, split, nlw=L, ncores=N_CORES, lite=False):
    f = np.float32
    x = np.asarray(inputs["x"], f)
    mem = np.asarray(inputs["mem"], f)
    w_in = np.asarray(inputs["w_in"], f)[:nlw]
    voff, nv = vec_layout(split)
    per_half = {}
    for hf in (0, 1):
        sbh, pairs, groups = core_units(hf, split)
        slots = col_slots(hf, split)
        ncol = 512 * len(slots)
        wl = np.zeros((nlw, D, ncol), f)
        c = 0
        for sl in slots:
            for (st, w) in sl:
                if st is not None:
                    wl[:, :, c:c + w] = w_in[:, :, st:st + w]
                c += w
        vec = np.zeros((128, nv), f)
        for l in range(L):
            vec[:, voff["nmix"] + 16 * l: voff["nmix"] + 16 * l + 16] = np.asarray(inputs["norm_mix"], f)[l].reshape(16, 128).T
            vec[:, voff["nxa"] + 16 * l: voff["nxa"] + 16 * l + 16] = np.asarray(inputs["norm_xattn"], f)[l].reshape(16, 128).T
            vec[:, voff["nmem"] + 16 * l: voff["nmem"] + 16 * l + 16] = np.asarray(inputs["norm_mem"], f)[l].reshape(16, 128).T
            on = np.asarray(inputs["out_norm"], f)[l].reshape(16, 128)
            for li, m in enumerate(local_heads(hf, split)):
                vec[:, voff["onorm"] + 16 * l + li] = on[m]
            bg = np.asarray(inputs["b_gla_gate"], f)[l].reshape(2, 128)
            for j, p in enumerate(pairs):
                vec[:, voff["bgate"] + 2 * l + j] = bg[p]
        vec[:, voff["fin"]: voff["fin"] + 16] = np.asarray(inputs["final_norm"], f).reshape(16, 128).T
        wup = np.asarray(inputs["w_gla_gate_up"], f).reshape(L, 16, 2, 128)[:nlw, :, pairs, :].reshape(nlw, 16, 128 * len(pairs))
        wsg = np.ascontiguousarray(np.asarray(inputs["w_sgu"], f)[:nlw, groups].transpose(0, 1, 3, 2))
        bsg = np.ascontiguousarray(np.asarray(inputs["b_sgu"], f)[:nlw, groups])
        gord = list(groups) + [g for g in range(4) if g not in groups]
        sgn = np.ascontiguousarray(np.asarray(inputs["sgu_norm"], f)[:nlw].reshape(nlw, 4, 128)[:, gord].reshape(nlw, 512))
        per_half[hf] = dict(w_in=np.ascontiguousarray(wl), vecs=vec, wup=np.ascontiguousarray(wup), wsg=wsg, bsg=bsg, sgn=sgn)
    shared = dict(
        w_out=np.ascontiguousarray(np.asarray(inputs["w_out"], f)[:nlw].reshape(nlw, 16, 128, D)[:, chunk_order(split)].reshape(nlw, D, D)),
        w_xq=np.ascontiguousarray(np.asarray(inputs["w_xq"], f)[:nlw]),
        w_xkv=np.ascontiguousarray(np.asarray(inputs["w_xkv"], f)[:nlw]),
        w_xo=np.ascontiguousarray(np.asarray(inputs["w_xo"], f)[:nlw]),
    )
    if lite:
        for k in ("w_out", "w_xq", "w_xkv", "w_xo"):
            shared[k] = np.zeros((1, 128, 128), f)
    maps = []
    for c in range(ncores):
        b, hf = c // 2, (c % 2 if split else 0)
        m = dict(xT=np.ascontiguousarray(x[b].T), memT=np.ascontiguousarray(mem[b].T))
        m.update(per_half[hf])
        m.update(shared)
        maps.append(m)
    return maps


class Sched:
    COMPUTE = ("pe", "act", "dve", "pool")
    QUEUES = ("sp", "pool", "act")
    NSLOT = 6

    def __init__(self, nc, es):
        self.nc = nc
        self.streams = {e: [] for e in ("pe", "act", "dve", "pool", "sp")}
        self.sems = {}
        for e in self.COMPUTE:
            self.sems[("c", e)] = es.enter_context(nc.semaphore("c_" + e))
        for q in self.QUEUES:
            for i in range(self.NSLOT):
                self.sems[("d", q, i)] = es.enter_context(nc.semaphore("d_%s%d" % (q, i)))
        self.NCC = 4
        for i in range(self.NCC):
            self.sems[("k", i)] = es.enter_context(nc.semaphore("k_%d" % i))
        self.kgen = [0] * self.NCC
        self.knext = 0
        self.ccount = {e: 0 for e in self.COMPUTE}
        self.dgen = {q: [0] * self.NSLOT for q in self.QUEUES}
        self.dnext = {q: 0 for q in self.QUEUES}
        self.waited = {e: {} for e in self.streams}
        self.lastw = {}
        self.readers = {}
        self.nops = 0

    def _need(self, eng, tok, waits):
        sid, val = tok
        if self.waited[eng].get(sid, 0) < val:
            self.waited[eng][sid] = val
            waits.append((sid, val))

    def op(self, eng, fn, reads=(), writes=(), dma=False):
        waits = []
        deps = []
        for k in reads:
            t = self.lastw.get(k)
            if t is not None:
                deps.append(t)
        for k in writes:
            t = self.lastw.get(k)
            if t is not None:
                deps.append(t)
            deps.extend(self.readers.get(k, {}).values())
        for t in deps:
            if (not dma) and eng == "pe" and t[0] == ("c", "pe"):
                continue
            self._need(eng, t, waits)
        if dma == "cc":
            slot = self.knext
            self.knext = (slot + 1) % self.NCC
            sid = ("k", slot)
            if self.kgen[slot] > 0:
                self._need(eng, (sid, self.kgen[slot]), waits)
            self.kgen[slot] += 1
            tok = (sid, self.kgen[slot])
            inc = 1
        elif dma:
            slot = self.dnext[eng]
            self.dnext[eng] = (slot + 1) % self.NSLOT
            prev = self.dgen[eng][slot]
            sid = ("d", eng, slot)
            if prev > 0:
                self._need(eng, (sid, 16 * prev), waits)
            self.dgen[eng][slot] += 1
            tok = (sid, 16 * self.dgen[eng][slot])
            inc = 16
        else:
            self.ccount[eng] += 1
            sid = ("c", eng)
            tok = (sid, self.ccount[eng])
            inc = 1
        self.streams[eng].append((waits, fn, sid, inc))
        for k in reads:
            self.readers.setdefault(k, {})[sid] = tok
        for k in writes:
            self.lastw[k] = tok
            self.readers[k] = {}
        self.nops += 1
        return tok

    def final_wait(self, eng, toks):
        waits = []
        for t in toks:
            self._need(eng, t, waits)
        self.streams[eng].append((waits, None, None, 0))

    def emit(self, block):
        def mk(name):
            def f(eng):
                for waits, fn, sid, inc in self.streams[name]:
                    for (ws, val) in waits:
                        eng.wait_ge(self.sems[ws], val)
                    if fn is not None:
                        ins = fn(eng)
                        ins.then_inc(self.sems[sid], inc)
            return f
        block.tensor(mk("pe"))
        block.scalar(mk("act"))
        block.vector(mk("dve"))
        block.gpsimd(mk("pool"))
        block.sync(mk("sp"))


def build_program(split=SPLIT, nlayers=L, stop=None, debug=False, lite=False, ncores=N_CORES):
    LW = nlayers
    nc = bass.Bass("TRN2", target_bir_lowering=False)
    sbh, pairs, groups = core_units(0, split)
    NSB, NP, NG = len(sbh), len(pairs), len(groups)
    nslots = NSB + 2 * NP + 1 + NG // 2
    voff, nv = vec_layout(split)
    dk = "ExternalOutput" if debug else "Internal"

    xT = nc.dram_tensor("xT", [D, S], F32, kind="ExternalInput").ap()
    memT = nc.dram_tensor("memT", [D, NM], F32, kind="ExternalInput").ap()
    w_in = nc.dram_tensor("w_in", [LW, D, nslots * 512], F32, kind="ExternalInput").ap()
    vecs_d = nc.dram_tensor("vecs", [128, nv], F32, kind="ExternalInput").ap()
    wup_d = nc.dram_tensor("wup", [LW, 16, 128 * NP], F32, kind="ExternalInput").ap()
    wsg_d = nc.dram_tensor("wsg", [LW, NG, 128, 128], F32, kind="ExternalInput").ap()
    bsg_d = nc.dram_tensor("bsg", [LW, NG, 128], F32, kind="ExternalInput").ap()
    if lite:
        w_out = nc.dram_tensor("w_out", [1, 128, 128], F32, kind="ExternalInput").ap()
        w_xq = nc.dram_tensor("w_xq", [1, 128, 128], F32, kind="ExternalInput").ap()
        w_xkv = nc.dram_tensor("w_xkv", [1, 128, 128], F32, kind="ExternalInput").ap()
        w_xo = nc.dram_tensor("w_xo", [1, 128, 128], F32, kind="ExternalInput").ap()
    else:
        w_out = nc.dram_tensor("w_out", [LW, D, D], F32, kind="ExternalInput").ap()
        w_xq = nc.dram_tensor("w_xq", [LW, D, 512], F32, kind="ExternalInput").ap()
        w_xkv = nc.dram_tensor("w_xkv", [LW, D, 1024], F32, kind="ExternalInput").ap()
        w_xo = nc.dram_tensor("w_xo", [LW, 512, D], F32, kind="ExternalInput").ap()
    sgn_d = nc.dram_tensor("sgn", [LW, 512], F32, kind="ExternalInput").ap()
    yT = nc.dram_tensor("yT", [D, S], F32, kind="ExternalOutput").ap()
    xA = nc.dram_tensor("xA", [D, S], F32, kind=dk).ap()
    xB = nc.dram_tensor("xB", [D, S], F32, kind=dk).ap()
    NLH = NSB + 2 * NP + NG
    ccg = cc_groups(split)
    mgd = nc.dram_tensor("mgd", [NLH, 128, S], BF16, kind=("Internal" if split else dk)).ap()
    mga = [nc.dram_tensor("mga%d" % k, [2 * len(g) * 128, S], BF16).ap() for k, g in enumerate(ccg)]
    npairs_cc = ncores // 2
    rgroups = [[2 * i, 2 * i + 1] for i in range(npairs_cc)]
    hdbg = nc.dram_tensor("hdbg", [16, 128, S], BF16, kind=dk).ap() if debug else None

    def xv(ap):
        return ap.rearrange("(c p) s -> c p s", p=128)

    with ExitStack() as es:
        def sb(name, shape, dt):
            return es.enter_context(nc.sbuf_tensor(name, shape, dt))

        def ps(name, shape, dt):
            return es.enter_context(nc.psum_tensor(name, shape, dt))

        H = sb("H", [128, NCH, S], BF16)
        WS = [sb("WS%d" % i, [128, NCH, 512], BF16) for i in range(2)]
        vecs = sb("vecs_sb", [128, nv], F32)
        ones_f = sb("ones_f", [128, 128], F32)
        ones_b = sb("ones_b", [128, 128], BF16)
        ident_b = sb("ident_b", [128, 128], BF16)
        tri_incl = sb("tri_incl", [128, 128], BF16)
        tri_low = sb("tri_low", [128, 128], BF16)
        tri_ui = sb("tri_ui", [128, 128], F32)
        blkmask = sb("blkmask", [128, 128], F32)
        negmask = sb("negmask", [128, 896], BF16)
        rmask = sb("rmask", [128, 512], F32)
        qT = [sb("qT%d" % i, [128, S], BF16) for i in range(2)]
        kT = [sb("kT%d" % i, [128, S], BF16) for i in range(2)]
        vtok = [sb("vtok%d" % i, [128, 16, 128], BF16) for i in range(2)]
        NZ = 5
        zs = [sb("zs%d" % i, [128, 512], F32) for i in range(NZ)]
        spb = [sb("spb%d" % i, [128, 512], BF16) for i in range(NZ)]
        et = [sb("et%d" % i, [128, 512], F32) for i in range(2)]
        Ab = [sb("Ab%d" % i, [128, 512], BF16) for i in range(NZ)]
        NE = 2
        e_o = [sb("e_o%d" % i, [128, 512], F32) for i in range(NE)]
        e_sq = [sb("e_sq%d" % i, [128, 512], F32) for i in range(NE)]
        e_rs = [sb("e_rs%d" % i, [128, 512], F32) for i in range(NE)]
        e_g = [sb("e_g%d" % i, [128, 512], F32) for i in range(NE)]
        e_mg = [sb("e_mg%d" % i, [128, 512], BF16) for i in range(NE)]
        NW = 5
        wk = [sb("wk%d" % i, [128, 512], F32) for i in range(NW)]
        NX = 3
        xt = [sb("xt%d" % i, [128, 2, 512], F32) for i in range(NX)]
        rT = sb("rT_sb", [16, S], BF16)
        wup = sb("wup_sb", [16, 128], BF16)
        g_q = sb("g_q", [128, 512], BF16)
        g_k = sb("g_k", [128, 512], BF16)
        g_kd = sb("g_kd", [128, 512], BF16)
        g_kdt = sb("g_kdt", [128, 4, 128], BF16)
        g_v = sb("g_v", [128, 4, 256], BF16)
        g_at = [sb("g_at%d" % i, [128, 128], BF16) for i in range(2)]
        S32 = sb("S32", [128, 128], F32)
        Sbf = sb("Sbf", [128, 128], BF16)
        gn_bc = sb("gn_bc", [128, 512], F32)
        wsT = [sb("wsT%d" % i, [128, 128], BF16) for i in range(NG)]
        wsTf = sb("wsTf", [128, 128], F32)
        bs_bc = [sb("bs_bc%d" % i, [128, 128], F32) for i in range(NG)]
        s_ss = sb("s_ss", [128, 8], F32)
        Z = [ps("Z%d" % i, [128, 512], F32) for i in range(2)]
        C = [ps("C%d" % i, [128, 512], F32) for i in range(2)]
        O = [ps("O%d" % i, [128, 512], F32) for i in range(2)]
        P = [ps("P%d" % i, [128, 512], F32) for i in range(2)]
        Tb = O[1][:].bitcast(BF16)[:, 0:512]

        g_sp, g_cum, g_eb, g_ebi, g_k32 = wk[0], wk[1], wk[2], wk[3], wk[4]
        K_sp, K_cum, K_eb, K_ebi, K_k32 = ("wk", 0), ("wk", 1), ("wk", 2), ("wk", 3), ("wk", 4)
        kxT = qT[0][:, 0:1024].rearrange("p (h m) -> p h m", h=4)
        vx = qT[0][:, 1024:2048].rearrange("p (b n) -> p b n", b=2)
        K_kxT = [("qT", 0, 0), ("qT", 0, 1)]
        K_vx = [("qT", 0, 2), ("qT", 0, 3)]
        s_vn = [Ab[3], Ab[4]]
        qx = [spb[1], spb[2]]
        K_qx = [("spb", 1), ("spb", 2)]
        pT = [Ab[0], Ab[1], Ab[2], spb[0]]
        K_pT = [("Ab", 0), ("Ab", 1), ("Ab", 2), ("spb", 0)]
        oxT = kT[0][:, :].rearrange("p (h n) -> p h n", h=4)
        print("SBUF bytes remaining:", nc.sbuf_bytes_remaining)

        sc = Sched(nc, es)
        op = sc.op
        ctr = {"p": 0, "e": 0, "w": 0, "x": 0, "z": 0}

        def nxt(k, n):
            v = ctr[k]
            ctr[k] = (v + 1) % n
            return v

        def consts():
            op("pool", lambda e: e.memset(ones_f[:], 1.0), writes=["ones_f"])
            op("pool", lambda e: e.memset(ones_b[:], 1.0), writes=["ones_b"])
            op("pool", lambda e: e.affine_select(out=ident_b[:], in_=ones_b[:], pattern=[[-1, 128]], compare_op=ALU.is_equal,
                                                 fill=0.0, base=0, channel_multiplier=1), reads=["ones_b"], writes=["ident_b"])
            op("pool", lambda e: e.affine_select(out=tri_incl[:], in_=ones_b[:], pattern=[[-1, 128]], compare_op=ALU.is_ge,
                                                 fill=0.0, base=0, channel_multiplier=1), reads=["ones_b"], writes=["tri_incl"])
            op("pool", lambda e: e.affine_select(out=tri_low[:], in_=ones_b[:], pattern=[[1, 128]], compare_op=ALU.is_gt,
                                                 fill=0.0, base=0, channel_multiplier=-1), reads=["ones_b"], writes=["tri_low"])
            op("pool", lambda e: e.affine_select(out=tri_ui[:], in_=ones_f[:], pattern=[[1, 128]], compare_op=ALU.is_ge,
                                                 fill=0.0, base=0, channel_multiplier=-1), reads=["ones_f"], writes=["tri_ui"])
            op("pool", lambda e: e.affine_select(out=blkmask[:], in_=ones_f[:], pattern=[[1, 128]], compare_op=ALU.is_ge,
                                                 fill=0.0, base=0, channel_multiplier=-1), reads=["ones_f"], writes=["blkmask"])
            op("pool", lambda e: e.memset(blkmask[0:64, 64:128], 0.0), reads=["blkmask"], writes=["blkmask"])
            op("pool", lambda e: e.memset(negmask[:], 0.0), writes=["negmask"])
            op("pool", lambda e: e.affine_select(out=negmask[:], in_=negmask[:], pattern=[[1, 896]],
                                                 compare_op=ALU.is_gt, fill=NEG, base=-384, channel_multiplier=-1),
               reads=["negmask"], writes=["negmask"])
            op("pool", lambda e: e.memset(rmask[:], 1.0), writes=["rmask"])
            op("pool", lambda e: e.memset(rmask[:].rearrange("p (c t) -> p c t", t=64)[:, :, 0:1], 0.0), reads=["rmask"], writes=["rmask"])
            op("sp", lambda e: e.dma_start(out=vecs[:], in_=vecs_d[:, :]), writes=["vecs"], dma=True)
            b0 = voff["bgate"]
            op("dve", lambda e: e.tensor_scalar(out=vecs[:, b0:b0 + 2 * L], in0=vecs[:, b0:b0 + 2 * L], scalar1=-1.0, scalar2=None,
                                                op0=ALU.mult), reads=["vecs"], writes=["vecs"])

        def rstd_from_ss(ss_ap, ss_key, out_tile, out_key, inv_n, tmp_tile, tmp_key):
            op("act", lambda e: e.activation(out=tmp_tile, in_=ss_ap, func=AF.Ln, bias=EPS, scale=inv_n),
               reads=[ss_key], writes=[tmp_key])
            op("act", lambda e: e.activation(out=out_tile, in_=tmp_tile, func=AF.Exp, scale=-0.5),
               reads=[tmp_key], writes=[out_key])

        def norm_phase(src, srckey, gcol, ntok, dst_fn, final=False):
            nblk = max(1, ntok // TB)
            w = min(TB, ntok)
            srcv = src.rearrange("(c p) s -> p c s", p=128)
            lq = ["sp", "pool"]
            for tb in range(nblk):
                t0 = tb * w
                ri = nxt("e", NE)
                for c2 in range(NCH // 2):
                    xi = nxt("x", NX)
                    op(lq[c2 % 2], lambda e, c2=c2, xi=xi, t0=t0: e.dma_start(out=xt[xi][:, :, 0:w], in_=srcv[:, 2 * c2:2 * c2 + 2, t0:t0 + w]),
                       reads=[(srckey, 2 * c2, tb), (srckey, 2 * c2 + 1, tb)], writes=[("xt", xi)], dma=True)
                    for j in range(2):
                        c = 2 * c2 + j
                        if c == 0:
                            op("act", lambda e, xi=xi, ri=ri, j=j: e.activation(out=e_o[ri][:, 0:w], in_=xt[xi][:, j, 0:w], func=AF.Square),
                               reads=[("xt", xi)], writes=[("e_o", ri)])
                        else:
                            wi = nxt("w", NW)
                            op("act", lambda e, xi=xi, wi=wi, j=j: e.activation(out=wk[wi][:, 0:w], in_=xt[xi][:, j, 0:w], func=AF.Square),
                               reads=[("xt", xi)], writes=[("wk", wi)])
                            op("dve", lambda e, wi=wi, ri=ri: e.tensor_tensor(out=e_o[ri][:, 0:w], in0=e_o[ri][:, 0:w], in1=wk[wi][:, 0:w], op=ALU.add),
                               reads=[("wk", wi), ("e_o", ri)], writes=[("e_o", ri)])
                si = nxt("p", 2)
                op("pe", lambda e, ri=ri, si=si: e.matmul(P[si][:, 0:w], ones_f[:], e_o[ri][:, 0:w], start=True, stop=True),
                   reads=[("e_o", ri), "ones_f"], writes=[("P", si)])
                rstd_from_ss(P[si][:, 0:w], ("P", si), e_rs[ri][:, 0:w], ("e_rs", ri), 1.0 / D, e_sq[ri][:, 0:w], ("e_sq", ri))
                for c2 in range(NCH // 2):
                    xi = nxt("x", NX)
                    op(lq[c2 % 2], lambda e, c2=c2, xi=xi, t0=t0: e.dma_start(out=xt[xi][:, :, 0:w], in_=srcv[:, 2 * c2:2 * c2 + 2, t0:t0 + w]),
                       reads=[(srckey, 2 * c2, tb), (srckey, 2 * c2 + 1, tb)], writes=[("xt", xi)], dma=True)
                    for j in range(2):
                        c = 2 * c2 + j
                        if not final:
                            dap, dkeys = dst_fn(c, t0, w, tb)
                            op("dve", lambda e, c=c, xi=xi, dap=dap, ri=ri, j=j: e.scalar_tensor_tensor(
                                out=dap, in0=xt[xi][:, j, 0:w], scalar=vecs[:, gcol + c:gcol + c + 1], in1=e_rs[ri][:, 0:w],
                                op0=ALU.mult, op1=ALU.mult), reads=[("xt", xi), ("e_rs", ri), "vecs"], writes=dkeys)
                        else:
                            op("dve", lambda e, c=c, xi=xi, ri=ri, j=j: e.scalar_tensor_tensor(
                                out=xt[xi][:, j, 0:w], in0=xt[xi][:, j, 0:w], scalar=vecs[:, gcol + c:gcol + c + 1], in1=e_rs[ri][:, 0:w],
                                op0=ALU.mult, op1=ALU.mult), reads=[("xt", xi), ("e_rs", ri), "vecs"], writes=[("xt", xi)])
                    if final:
                        yv = yT.rearrange("(c p) s -> p c s", p=128)
                        tok = op("act", lambda e, c2=c2, xi=xi, t0=t0: e.dma_start(out=yv[:, 2 * c2:2 * c2 + 2, t0:t0 + w], in_=xt[xi][:, :, 0:w]),
                                 reads=[("xt", xi)], writes=[("yT", 2 * c2, tb), ("yT", 2 * c2 + 1, tb)], dma=True)
                        out_toks.append(tok)

        def h_dst(c, t0, w, tb):
            return H[:, c, t0:t0 + w], [("H", c, tb)]

        def hkeys(tb):
            return [("H", c, tb) for c in range(NCH)]

        def proj_fm(slot, col0, tb, ncols=128):
            pi = nxt("p", 2)

            def f(e):
                ins = None
                for c in range(NCH):
                    ins = e.matmul(P[pi][0:ncols, :], WS[slot][:, c, col0:col0 + ncols], H[:, c, tb * TB:(tb + 1) * TB],
                                   start=(c == 0), stop=(c == NCH - 1))
                return ins
            op("pe", f, reads=[("WS", slot)] + hkeys(tb), writes=[("P", pi)])
            return pi

        def proj_tm(slot, col0, ncols, tb, blocks):
            pi = nxt("p", 2)

            def f(e):
                ins = None
                for j, blk in enumerate(blocks):
                    for c in range(NCH):
                        ins = e.matmul(P[pi][:, j * ncols:(j + 1) * ncols], H[:, c, blk * 128:(blk + 1) * 128],
                                       WS[slot][:, c, col0:col0 + ncols], start=(c == 0), stop=(c == NCH - 1))
                return ins
            op("pe", f, reads=[("WS", slot)] + hkeys(tb), writes=[("P", pi)])
            return pi

        def load_ws(slot, dram_view):
            op("pool", lambda e: e.dma_start(out=WS[slot][:], in_=dram_view), writes=[("WS", slot)], dma=True)

        def win_view(l, s):
            return w_in[l].rearrange("(c p) n -> p c n", p=128)[:, :, s * 512:(s + 1) * 512]

        def epilogue_g(l, m, tb, src_ap, src_key, gslot, gcol, banks=None):
            ei = nxt("e", NE)
            if banks is None:
                pi = nxt("p", 2)
                G, Gk = P[pi], ("P", pi)
                si = nxt("p", 2)
                SSt, SSk = P[si], ("P", si)
            else:
                G, Gk, SSt, SSk = banks
            gc = voff["onorm"] + 16 * l + m
            op("act", lambda e: e.activation(out=e_o[ei][:], in_=src_ap, func=AF.Copy), reads=[src_key], writes=[("e_o", ei)])
            yield
            op("act", lambda e: e.activation(out=e_sq[ei][:], in_=e_o[ei][:], func=AF.Square), reads=[("e_o", ei)], writes=[("e_sq", ei)])
            yield
            op("pe", lambda e: e.matmul(SSt[:], ones_f[:], e_sq[ei][:], start=True, stop=True), reads=[("e_sq", ei), "ones_f"], writes=[SSk])
            yield

            def fg(e):
                ins = None
                for c in range(NCH):
                    ins = e.matmul(G[:], WS[gslot][:, c, gcol:gcol + 128], H[:, c, tb * TB:(tb + 1) * TB], start=(c == 0), stop=(c == NCH - 1))
                return ins
            op("pe", fg, reads=[("WS", gslot)] + hkeys(tb), writes=[Gk])
            yield
            op("act", lambda e: e.activation(out=e_sq[ei][:], in_=SSt[:], func=AF.Ln, bias=EPS, scale=1.0 / 128), reads=[SSk], writes=[("e_sq", ei)])
            yield
            op("act", lambda e: e.activation(out=e_rs[ei][:], in_=e_sq[ei][:], func=AF.Exp, scale=-0.5), reads=[("e_sq", ei)], writes=[("e_rs", ei)])
            yield
            op("dve", lambda e: e.scalar_tensor_tensor(out=e_o[ei][:], in0=e_o[ei][:], scalar=vecs[:, gc:gc + 1], in1=e_rs[ei][:],
                                                       op0=ALU.mult, op1=ALU.mult), reads=[("e_o", ei), ("e_rs", ei), "vecs"], writes=[("e_o", ei)])
            yield
            op("act", lambda e: e.activation(out=e_g[ei][:], in_=G[:], func=AF.Exp, scale=-1.0), reads=[Gk], writes=[("e_g", ei)])
            yield
            op("act", lambda e: e.activation(out=e_g[ei][:], in_=e_g[ei][:], func=AF.Ln, bias=1.0), reads=[("e_g", ei)], writes=[("e_g", ei)])
            yield
            op("act", lambda e: e.activation(out=e_g[ei][:], in_=e_g[ei][:], func=AF.Exp, scale=-1.0), reads=[("e_g", ei)], writes=[("e_g", ei)])
            yield
            op("dve", lambda e: e.tensor_tensor(out=e_g[ei][:], in0=G[:], in1=e_g[ei][:], op=ALU.mult),
               reads=[Gk, ("e_g", ei)], writes=[("e_g", ei)])
            yield
            op("dve", lambda e: e.tensor_tensor(out=e_mg[ei][:], in0=e_o[ei][:], in1=e_g[ei][:], op=ALU.mult),
               reads=[("e_o", ei), ("e_g", ei)], writes=[("e_mg", ei)])
            yield
            op("sp", lambda e: e.dma_start(out=mgd[m][:, tb * TB:(tb + 1) * TB], in_=e_mg[ei][:]), reads=[("e_mg", ei)],
               writes=[("MG", m, tb)], dma=True)
            yield

        def drive(*gens):
            gens = list(gens)
            while gens:
                for g in list(gens):
                    try:
                        next(g)
                    except StopIteration:
                        gens.remove(g)

        def epilogue(*a, **k):
            drive(epilogue_g(*a, **k))

        def sb_proj_items(i, slot):
            bi = i % 2
            scale = 128.0 ** -0.5
            items = []

            def group(tb, kind):
                st = {}

                def piece(k):
                    def f():
                        if k == 0:
                            st["pi"] = nxt("p", 2)
                        pi = st["pi"]

                        def mm(e):
                            ins = None
                            if kind == "v":
                                blk = tb * 4 + k
                                for c in range(NCH):
                                    ins = e.matmul(P[pi][:, k * 128:(k + 1) * 128], H[:, c, blk * 128:(blk + 1) * 128],
                                                   WS[slot][:, c, 256:384], start=(c == 0), stop=(c == NCH - 1))
                            else:
                                col0 = 0 if kind == "q" else 128
                                for c in range(4 * k, 4 * k + 4):
                                    ins = e.matmul(P[pi][:], WS[slot][:, c, col0:col0 + 128], H[:, c, tb * TB:(tb + 1) * TB],
                                                   start=(c == 0), stop=(c == NCH - 1))
                            return ins
                        op("pe", mm, reads=[("WS", slot)] + hkeys(tb), writes=[("P", pi)])
                    return f

                def evac():
                    pi = st["pi"]
                    if kind == "q":
                        op("act", lambda e: e.activation(out=qT[bi][:, tb * TB:(tb + 1) * TB], in_=P[pi][:], func=AF.Copy, scale=scale),
                           reads=[("P", pi)], writes=[("qT", bi, tb)])
                    elif kind == "k":
                        op("dve", lambda e: e.tensor_copy(out=kT[bi][:, tb * TB:(tb + 1) * TB], in_=P[pi][:]),
                           reads=[("P", pi)], writes=[("kT", bi, tb)])
                    else:
                        op("dve", lambda e: e.tensor_copy(out=vtok[bi][:, tb * 4:(tb + 1) * 4, :], in_=P[pi][:].rearrange("p (j d) -> p j d", d=128)),
                           reads=[("P", pi)], writes=[("vtok", bi, tb)])
                return [piece(k) for k in range(4)] + [evac]
            for tb in range(NTB):
                for kind in ("k", "v", "q"):
                    items += group(tb, kind)
            return items

        def sb_attn(l, i, slot, bg):
            bi = i % 2

            def tile_ops(ch, kb, qb):
                zi = nxt("z", NZ)
                zb = zi % 2
                ei = zi % 2
                r = kb - 4 * qb
                first = (kb == 4 * qb + 3)
                last = (kb == 0)
                d = {}
                d["Z"] = lambda: op("pe", lambda e: e.matmul(Z[zb][:], kT[bi][:, kb * 128:(kb + 1) * 128], qT[bi][:, qb * TB:(qb + 1) * TB], start=True, stop=True),
                                    reads=[("kT", bi, kb // 4), ("qT", bi, qb)], writes=[("Z", zb)])
                if r >= 0:
                    d["COPY"] = lambda: op("dve", lambda e: e.tensor_tensor(out=zs[zi][:], in0=Z[zb][:], in1=negmask[:, 384 - 128 * r:896 - 128 * r], op=ALU.add),
                                           reads=[("Z", zb), "negmask"], writes=[("zs", zi)])
                else:
                    d["COPY"] = lambda: op("dve", lambda e: e.tensor_copy(out=zs[zi][:], in_=Z[zb][:]), reads=[("Z", zb)], writes=[("zs", zi)])
                d["EXPA"] = lambda: op("act", lambda e: e.activation(out=et[ei][:], in_=zs[zi][:], func=AF.Exp), reads=[("zs", zi)], writes=[("et", ei)])
                d["LN"] = lambda: op("act", lambda e: e.activation(out=spb[zi][:], in_=et[ei][:], func=AF.Ln, bias=1.0), reads=[("et", ei)], writes=[("spb", zi)])
                d["TRI"] = lambda: op("pe", lambda e: e.matmul(C[ch][:], tri_incl[:], spb[zi][:], start=first, stop=True),
                                      reads=[("spb", zi), "tri_incl"], writes=[("C", ch)])
                d["SUB"] = lambda: op("dve", lambda e: e.scalar_tensor_tensor(out=zs[zi][:], in0=C[ch][:], scalar=-1.0, in1=zs[zi][:], op0=ALU.mult, op1=ALU.add),
                                      reads=[("C", ch), ("zs", zi)], writes=[("zs", zi)])
                if not last:
                    d["LOW"] = lambda: op("pe", lambda e: e.matmul(C[ch][:], tri_low[:], spb[zi][:], start=False, stop=True),
                                          reads=[("spb", zi), "tri_low"], writes=[("C", ch)])
                else:
                    d["LOW"] = lambda: None
                d["EXPB"] = lambda: op("act", lambda e: e.activation(out=Ab[zi][:], in_=zs[zi][:], func=AF.Exp), reads=[("zs", zi)], writes=[("Ab", zi)])

                def av():
                    op("pe", lambda e: e.matmul(O[ch][:], vtok[bi][:, kb, :], Ab[zi][:], start=first, stop=last),
                       reads=[("Ab", zi), ("vtok", bi, kb // 4)], writes=[("O", ch)])
                    if last:
                        epilogue(l, i, qb, O[ch][:], ("O", ch), slot, 384, banks=(O[ch], ("O", ch), C[ch], ("C", ch)))
                d["AV"] = av
                return d

            nper = 0
            ta = [(0, kb, qb) for qb in (3, 0) for kb in range(4 * qb + 3, -1, -1)]
            tbl = [(1, kb, qb) for qb in (2, 1) for kb in range(4 * qb + 3, -1, -1)]
            T = []
            for x_, y_ in zip(ta, tbl):
                T += [x_, y_]
            n = len(T)
            ops_ = {}
            for p in range(n + 2):
                if p < n:
                    ops_[p] = tile_ops(*T[p])
                if p - 2 >= 0:
                    ops_[p - 2]["LOW"]()
                    ops_[p - 2]["EXPB"]()
                if p < n:
                    ops_[p]["Z"]()
                    ops_[p]["COPY"]()
                if p - 2 >= 0:
                    ops_[p - 2]["AV"]()
                if p < n:
                    ops_[p]["EXPA"]()
                    ops_[p]["LN"]()
                if 0 <= p - 1 < n:
                    ops_[p - 1]["TRI"]()
                    ops_[p - 1]["SUB"]()
                nper += 1
                for _ in range(2):
                    if bg:
                        bg.pop(0)()
            while bg:
                bg.pop(0)()

        def gla_pair(l, j, slotA, slotB, first_pair):
            gp = pairs[j]
            if first_pair:
                for tb in range(NTB):
                    pi = proj_fm(slotB, 256, tb, ncols=16)
                    op("act", lambda e, pi=pi, tb=tb: e.activation(out=rT[:, tb * TB:(tb + 1) * TB], in_=P[pi][0:16, :], func=AF.Copy),
                       reads=[("P", pi)], writes=[("rT", tb)])
            op("pool", lambda e: e.dma_start(out=wup[:], in_=wup_d[l][:, j * 128:(j + 1) * 128]), writes=["wup"], dma=True)
            op("dve", lambda e: e.memset(S32[:], 0.0), writes=["S32"])
            op("dve", lambda e: e.memset(Sbf[:], 0.0), writes=["Sbf"])
            bcol = voff["bgate"] + 2 * l + j
            for tb in range(NTB):
                tsl = slice(tb * TB, (tb + 1) * TB)
                pi = nxt("p", 2)
                op("pe", lambda e, pi=pi, tsl=tsl: e.matmul(P[pi][:], wup[:], rT[:, tsl], start=True, stop=True),
                   reads=["wup", ("rT", tb)], writes=[("P", pi)])
                op("act", lambda e, pi=pi: e.activation(out=g_sp[:], in_=P[pi][:], func=AF.Exp, scale=-1.0, bias=vecs[:, bcol:bcol + 1]),
                   reads=[("P", pi), "vecs"], writes=[K_sp])
                op("act", lambda e: e.activation(out=g_sp[:], in_=g_sp[:], func=AF.Ln, bias=1.0), reads=[K_sp], writes=[K_sp])
                op("dve", lambda e: e.tensor_tensor_scan(out=g_cum[:], data0=rmask[:], data1=g_sp[:], initial=0.0, op0=ALU.mult, op1=ALU.add),
                   reads=[K_sp, "rmask"], writes=[K_cum])
                op("act", lambda e: e.activation(out=g_eb[:], in_=g_cum[:], func=AF.Exp, scale=-1.0 / 16), reads=[K_cum], writes=[K_eb])
                op("act", lambda e: e.activation(out=g_ebi[:], in_=g_cum[:], func=AF.Exp, scale=1.0 / 16), reads=[K_cum], writes=[K_ebi])
                pi = proj_fm(slotA, 0, tb)
                op("dve", lambda e, pi=pi: e.scalar_tensor_tensor(out=g_q[:], in0=P[pi][:], scalar=0.125, in1=g_eb[:], op0=ALU.mult, op1=ALU.mult),
                   reads=[("P", pi), K_eb], writes=["g_q"])
                pi = proj_fm(slotA, 128, tb)
                op("dve", lambda e, pi=pi: e.tensor_tensor(out=g_k32[:], in0=P[pi][:], in1=g_ebi[:], op=ALU.mult),
                   reads=[("P", pi), K_ebi], writes=[K_k32])
                op("act", lambda e: e.activation(out=g_k[:], in_=g_k32[:], func=AF.Copy), reads=[K_k32], writes=["g_k"])
                dec_bc = g_eb[:].rearrange("p (c t) -> p c t", t=64)[:, :, 63:64].broadcast_to([128, 8, 64])
                op("dve", lambda e: e.tensor_tensor(out=g_kd[:].rearrange("p (c t) -> p c t", t=64), in0=g_k32[:].rearrange("p (c t) -> p c t", t=64),
                                                    in1=dec_bc, op=ALU.mult), reads=[K_k32, K_eb], writes=["g_kd"])

                def ftr(e):
                    ins = None
                    for j4 in range(4):
                        ins = e.transpose(Tb[:, j4 * 128:(j4 + 1) * 128], g_kd[:, j4 * 128:(j4 + 1) * 128], ident_b[:])
                    return ins
                op("pe", ftr, reads=["g_kd", "ident_b"], writes=[("O", 1)])
                op("dve", lambda e: e.tensor_copy(out=g_kdt[:], in_=Tb.rearrange("p (j d) -> p j d", d=128)), reads=[("O", 1)], writes=["g_kdt"])
                for half in range(2):
                    pi = proj_tm(slotA, 256, 256, tb, [tb * 4 + 2 * half, tb * 4 + 2 * half + 1])
                    op("act", lambda e, pi=pi, half=half: e.activation(out=g_v[:, 2 * half:2 * half + 2, :],
                                                                       in_=P[pi][:].rearrange("p (j d) -> p j d", d=256), func=AF.Copy),
                       reads=[("P", pi)], writes=["g_v"])
                OG = [O[0], C[0]]
                OGk = [("O", 0), ("C", 0)]
                for cp in range(4):
                    csl = slice(cp * 128, (cp + 1) * 128)
                    for h in range(2):
                        hs = slice(h * 64, (h + 1) * 64)
                        ai = (cp * 2 + h) % 2
                        op("pe", lambda e, hs=hs, csl=csl: e.matmul(Z[0][:, 0:128], g_k[hs, csl], g_q[hs, csl], start=True, stop=True),
                           reads=["g_k", "g_q"], writes=[("Z", 0)])
                        op("dve", lambda e, ai=ai: e.tensor_tensor(out=g_at[ai][:], in0=Z[0][:, 0:128], in1=blkmask[:], op=ALU.mult),
                           reads=[("Z", 0), "blkmask"], writes=[("g_at", ai)])
                        op("pe", lambda e, h=h, ai=ai, csl=csl, cp=cp: e.matmul(OG[h][:, csl], g_v[:, cp, h * 128:(h + 1) * 128], g_at[ai][:],
                                                                                 start=True, stop=False),
                           reads=["g_v", ("g_at", ai)], writes=[OGk[h]])
                    for c2 in range(2):
                        ch = cp * 2 + c2
                        tsl64 = slice(cp * 128 + c2 * 64, cp * 128 + c2 * 64 + 64)
                        psl = slice(c2 * 64, c2 * 64 + 64)

                        def finter(e, tsl64=tsl64, c2=c2):
                            ins = None
                            for h in range(2):
                                hs = slice(h * 64, (h + 1) * 64)
                                ins = e.matmul(OG[h][:, tsl64], Sbf[hs, :], g_q[hs, tsl64], start=False, stop=(c2 == 1))
                            return ins
                        op("pe", finter, reads=["Sbf", "g_q"], writes=[("O", 0), ("C", 0)])

                        def fkv(e, psl=psl, cp=cp):
                            ins = None
                            for h in range(2):
                                ins = e.matmul(Z[1][h * 64:(h + 1) * 64, 0:128], g_kdt[psl, cp, h * 64:(h + 1) * 64],
                                               g_v[psl, cp, h * 128:(h + 1) * 128], start=True, stop=True)
                            return ins
                        op("pe", fkv, reads=["g_kdt", "g_v"], writes=[("Z", 1)])
                        dcol = ch * 64 + 63
                        op("dve", lambda e, dcol=dcol: e.scalar_tensor_tensor(out=S32[:], in0=S32[:], scalar=g_eb[:, dcol:dcol + 1], in1=Z[1][:, 0:128],
                                                                              op0=ALU.mult, op1=ALU.add),
                           reads=["S32", K_eb, ("Z", 1)], writes=["S32"])
                        op("dve", lambda e: e.tensor_copy(out=Sbf[:], in_=S32[:]), reads=["S32"], writes=["Sbf"])
                drive(*[epilogue_g(l, NSB + 2 * j + h, tb, OG[h][:], OGk[h], slotB, h * 128, banks=(P[h], ("P", h), OG[h], OGk[h])) for h in range(2)])

        def gelu_g(pi, out_tile, out_key):
            a = nxt("w", NW)
            b = nxt("w", NW)
            op("dve", lambda e: e.tensor_copy(out=wk[a][:], in_=P[pi][:]), reads=[("P", pi)], writes=[("wk", a)])
            yield
            op("dve", lambda e: e.tensor_tensor(out=wk[b][:], in0=wk[a][:], in1=wk[a][:], op=ALU.mult), reads=[("wk", a)], writes=[("wk", b)])
            yield
            op("dve", lambda e: e.tensor_scalar(out=wk[b][:], in0=wk[b][:], scalar1=0.044715, scalar2=1.0, op0=ALU.mult, op1=ALU.add),
               reads=[("wk", b)], writes=[("wk", b)])
            yield
            op("dve", lambda e: e.tensor_tensor(out=wk[b][:], in0=wk[b][:], in1=wk[a][:], op=ALU.mult), reads=[("wk", a), ("wk", b)], writes=[("wk", b)])
            yield
            op("act", lambda e: e.activation(out=wk[b][:], in_=wk[b][:], func=AF.Exp, scale=-GELU_C), reads=[("wk", b)], writes=[("wk", b)])
            yield
            op("act", lambda e: e.activation(out=wk[b][:], in_=wk[b][:], func=AF.Ln, bias=1.0), reads=[("wk", b)], writes=[("wk", b)])
            yield
            op("act", lambda e: e.activation(out=wk[b][:], in_=wk[b][:], func=AF.Exp, scale=-1.0), reads=[("wk", b)], writes=[("wk", b)])
            yield
            op("dve", lambda e: e.tensor_tensor(out=out_tile, in0=wk[a][:], in1=wk[b][:], op=ALU.mult), reads=[("wk", a), ("wk", b)], writes=[out_key])
            yield

        def gelu_from_psum(pi, out_tile, out_key):
            drive(gelu_g(pi, out_tile, out_key))

        def sgu_unit(l, slotV, uslot, lgs):
            op("sp", lambda e: e.dma_start(out=gn_bc[:], in_=sgn_d[l].partition_broadcast(128)), writes=["gn_bc"], dma=True)
            for lg in lgs:
                op("sp", lambda e, lg=lg: e.dma_start(out=wsTf[:], in_=wsg_d[l][lg]), writes=["wsTf"], dma=True)
                op("dve", lambda e, lg=lg: e.tensor_tensor(out=wsT[lg][:], in0=wsTf[:], in1=tri_ui[:], op=ALU.mult),
                   reads=["wsTf", "tri_ui"], writes=[("wsT", lg)])
                op("sp", lambda e, lg=lg: e.dma_start(out=bs_bc[lg][:], in_=bsg_d[l][lg].partition_broadcast(128)), writes=[("bs_bc", lg)], dma=True)
            MX = [O[0], C[0]]
            MXk = [("O", 0), ("C", 0)]
            def gelu_ip_g(pi, hold):
                a = nxt("w", NW)
                b = nxt("w", NW)
                hold["a"] = a
                op("dve", lambda e: e.tensor_copy(out=wk[a][:], in_=P[pi][:]), reads=[("P", pi)], writes=[("wk", a)])
                yield
                op("dve", lambda e: e.tensor_tensor(out=wk[b][:], in0=wk[a][:], in1=wk[a][:], op=ALU.mult), reads=[("wk", a)], writes=[("wk", b)])
                yield
                op("dve", lambda e: e.tensor_scalar(out=wk[b][:], in0=wk[b][:], scalar1=0.044715, scalar2=1.0, op0=ALU.mult, op1=ALU.add),
                   reads=[("wk", b)], writes=[("wk", b)])
                yield
                op("dve", lambda e: e.tensor_tensor(out=wk[b][:], in0=wk[b][:], in1=wk[a][:], op=ALU.mult), reads=[("wk", a), ("wk", b)], writes=[("wk", b)])
                yield
                op("act", lambda e: e.activation(out=wk[b][:], in_=wk[b][:], func=AF.Exp, scale=-GELU_C), reads=[("wk", b)], writes=[("wk", b)])
                yield
                op("act", lambda e: e.activation(out=wk[b][:], in_=wk[b][:], func=AF.Ln, bias=1.0), reads=[("wk", b)], writes=[("wk", b)])
                yield
                op("act", lambda e: e.activation(out=wk[b][:], in_=wk[b][:], func=AF.Exp, scale=-1.0), reads=[("wk", b)], writes=[("wk", b)])
                yield
                op("dve", lambda e: e.tensor_tensor(out=wk[a][:], in0=wk[a][:], in1=wk[b][:], op=ALU.mult), reads=[("wk", a), ("wk", b)], writes=[("wk", a)])
                yield

            def vblock_g(tb, j4):
                blk = tb * 4 + j4
                pi = proj_tm(slotV, 0, 512, tb, [blk])
                yield
                hold = {}
                yield from gelu_ip_g(pi, hold)
                a = hold["a"]
                sscol = blk % 8
                vi = blk % 2
                op("dve", lambda e: e.memset(s_ss[:, sscol:sscol + 1], 0.0), writes=[("s_ss", sscol)])
                yield
                op("act", lambda e: e.activation(out=e_sq[0][:], in_=wk[a][:], func=AF.Square, accum_out=s_ss[:, sscol:sscol + 1]),
                   reads=[("wk", a), ("s_ss", sscol)], writes=[("e_sq", 0), ("s_ss", sscol)])
                yield
                op("act", lambda e: e.activation(out=s_ss[:, sscol:sscol + 1], in_=s_ss[:, sscol:sscol + 1], func=AF.Ln, bias=EPS, scale=1.0 / 512),
                   reads=[("s_ss", sscol)], writes=[("s_ss", sscol)])
                yield
                op("act", lambda e: e.activation(out=s_ss[:, sscol:sscol + 1], in_=s_ss[:, sscol:sscol + 1], func=AF.Exp, scale=-0.5),
                   reads=[("s_ss", sscol)], writes=[("s_ss", sscol)])
                yield
                op("dve", lambda e: e.scalar_tensor_tensor(out=s_vn[vi][:], in0=wk[a][:], scalar=s_ss[:, sscol:sscol + 1],
                                                           in1=gn_bc[:], op0=ALU.mult, op1=ALU.mult),
                   reads=[("wk", a), ("s_ss", sscol), "gn_bc"], writes=[("Ab", 3 + vi)])
                yield

                def fmx(e):
                    ins = None
                    for k, lg in enumerate(lgs):
                        ins = e.matmul(MX[k][:, j4 * 128:(j4 + 1) * 128], s_vn[vi][:, lg * 128:(lg + 1) * 128], wsT[lg][:], start=True, stop=True)
                    return ins
                op("pe", fmx, reads=[("Ab", 3 + vi)] + [("wsT", lg) for lg in lgs], writes=MXk[:len(lgs)])
                yield

            for tb in range(NTB):
                drive(vblock_g(tb, 0), vblock_g(tb, 1))
                drive(vblock_g(tb, 2), vblock_g(tb, 3))
                epis = []
                for k, lg in enumerate(lgs):
                    ucol = k * 256
                    pi = proj_fm(uslot, ucol, tb)
                    a = nxt("w", NW)
                    gelu_from_psum(pi, wk[a][:], ("wk", a))
                    b = nxt("w", NW)
                    op("dve", lambda e, k=k, lg=lg, b=b: e.tensor_tensor(out=wk[b][:].rearrange("p (j t) -> p j t", t=128),
                                                                         in0=MX[k][:].rearrange("p (j t) -> p j t", t=128),
                                                                         in1=bs_bc[lg][:].unsqueeze(1).broadcast_to([128, 4, 128]), op=ALU.add),
                       reads=[MXk[k], ("bs_bc", lg)], writes=[("wk", b)])
                    op("dve", lambda e, a=a, b=b: e.tensor_tensor(out=wk[b][:], in0=wk[b][:], in1=wk[a][:], op=ALU.mult),
                       reads=[("wk", a), ("wk", b)], writes=[("wk", b)])
                    epis.append(epilogue_g(l, NSB + 2 * NP + lg, tb, wk[b][:], ("wk", b), uslot, ucol + 128, banks=(P[k], ("P", k), MX[k], MXk[k])))
                drive(*epis)

        def mixer(l):
            s = 0
            units = []
            for i in range(NSB):
                units.append(("sb", i, [s]))
                s += 1
            for j in range(NP):
                units.append(("gla", j, [s, s + 1]))
                s += 2
            units.append(("sgu", 0, list(range(s, s + 1 + NG // 2))))
            wsn = {"n": 0}

            def alloc(k):
                r = []
                for _ in range(k):
                    r.append(wsn["n"] % 2)
                    wsn["n"] += 1
                return r
            def exchange(k):
                grp = ccg[k]
                n = len(grp)
                src = mgd[grp[0]:grp[0] + n].rearrange("h p s -> (h p) s")
                op("pool", lambda e: e.collective_compute("AllGather", ALU.bypass, replica_groups=rgroups, ins=[src.opt()], outs=[mga[k].opt()]),
                   reads=[("MG", li, tb) for li in grp for tb in range(NTB)], writes=[("MGA", k)], dma="cc")

            for ui, (kind, idx, dsl) in enumerate(units):
                if kind == "sb":
                    if idx == 0:
                        sbws = [alloc(1)[0] for _ in range(NSB)]
                        load_ws(sbws[0], win_view(l, dsl[0]))
                        for f in sb_proj_items(0, sbws[0]):
                            f()
                    bg = []
                    if idx + 1 < NSB:
                        load_ws(sbws[idx + 1], win_view(l, dsl[0] + 1))
                        bg = sb_proj_items(idx + 1, sbws[idx + 1])
                    sb_attn(l, idx, sbws[idx], bg)
                elif kind == "gla":
                    ws = alloc(2)
                    load_ws(ws[0], win_view(l, dsl[0]))
                    load_ws(ws[1], win_view(l, dsl[1]))
                    if split and idx == 0:
                        exchange(0)
                    gla_pair(l, idx, ws[0], ws[1], idx == 0)
                else:
                    for k in range(NG // 2):
                        ws = alloc(2)
                        load_ws(ws[0], win_view(l, dsl[0]))
                        load_ws(ws[1], win_view(l, dsl[1 + k]))
                        if split and k == 0:
                            exchange(1)
                        sgu_unit(l, ws[0], ws[1], [2 * k, 2 * k + 1])
                    if split:
                        exchange(2)

        def out_proj(l, xsrc, skey, xdst, dkey):
            if not split:
                for m in range(16):
                    op("sp", lambda e, m=m: e.dma_start(out=H[:, m, :], in_=mgd[m]), reads=[("MG", m, tb) for tb in range(NTB)],
                       writes=[("H", m, tb) for tb in range(NTB)], dma=True)
            else:
                jj = 0
                for k, grp in enumerate(ccg):
                    for r in range(2):
                        for idx in range(len(grp)):
                            row = (r * len(grp) + idx) * 128
                            op("sp", lambda e, jj=jj, k=k, row=row: e.dma_start(out=H[:, jj, :], in_=mga[k][row:row + 128, :]),
                               reads=[("MGA", k)], writes=[("H", jj, tb) for tb in range(NTB)], dma=True)
                            jj += 1
            if stop == "mixload":
                for c in range(NCH):
                    out_toks.append(op("sp", lambda e, c=c: e.dma_start(out=hdbg[c], in_=H[:, c, :]), reads=[("H", c, tb) for tb in range(NTB)],
                                       writes=[("hdbg", c)], dma=True))
                return
            wv = w_out[l].rearrange("(c p) n -> p c n", p=128)
            for cs in range(4):
                slot = cs % 2
                load_ws(slot, wv[:, :, cs * 512:(cs + 1) * 512])
                for tb in range(NTB):
                    for dcp in range(2):
                        xi = nxt("x", NX)
                        d0 = cs * 4 + 2 * dcp
                        sv = xsrc.rearrange("(c p) s -> p c s", p=128)
                        dv = xdst.rearrange("(c p) s -> p c s", p=128)
                        op("sp", lambda e, xi=xi, d0=d0, tb=tb, sv=sv: e.dma_start(out=xt[xi][:], in_=sv[:, d0:d0 + 2, tb * TB:(tb + 1) * TB]),
                           reads=[(skey, d0, tb), (skey, d0 + 1, tb)], writes=[("xt", xi)], dma=True)
                        for j in range(2):
                            pi = proj_fm(slot, (2 * dcp + j) * 128, tb)
                            op("dve", lambda e, xi=xi, pi=pi, j=j: e.tensor_tensor(out=xt[xi][:, j, :], in0=P[pi][:], in1=xt[xi][:, j, :], op=ALU.add),
                               reads=[("P", pi), ("xt", xi)], writes=[("xt", xi)])
                        op("act", lambda e, xi=xi, d0=d0, tb=tb, dv=dv: e.dma_start(out=dv[:, d0:d0 + 2, tb * TB:(tb + 1) * TB], in_=xt[xi][:]),
                           reads=[("xt", xi)], writes=[(dkey, d0, tb), (dkey, d0 + 1, tb)], dma=True)

        def xattn(l, xsrc, skey, xdst, dkey):

            def mem_dst(c, t0, w, tb):
                return H[:, c, 0:NM], [("H", c, 0)]
            norm_phase(memT, "memT", voff["nmem"] + 16 * l, NM, mem_dst)
            kvv = w_xkv[l].rearrange("(c p) n -> p c n", p=128)
            load_ws(0, kvv[:, :, 0:512])
            load_ws(1, kvv[:, :, 512:1024])
            mkeys = [("H", c, 0) for c in range(NCH)]
            for h in range(4):
                pi = nxt("p", 2)

                def fk(e, pi=pi, h=h):
                    ins = None
                    for c in range(NCH):
                        ins = e.matmul(P[pi][:, 0:NM], WS[0][:, c, h * 128:(h + 1) * 128], H[:, c, 0:NM], start=(c == 0), stop=(c == NCH - 1))
                    return ins
                op("pe", fk, reads=[("WS", 0)] + mkeys, writes=[("P", pi)])
                op("act", lambda e, pi=pi, h=h: e.activation(out=kxT[:, h, :], in_=P[pi][:, 0:NM], func=AF.Copy), reads=[("P", pi)], writes=K_kxT)
            for mb in range(2):
                pi = nxt("p", 2)

                def fv(e, pi=pi, mb=mb):
                    ins = None
                    for c in range(NCH):
                        ins = e.matmul(P[pi][:], H[:, c, mb * 128:(mb + 1) * 128], WS[1][:, c, :], start=(c == 0), stop=(c == NCH - 1))
                    return ins
                op("pe", fv, reads=[("WS", 1)] + mkeys, writes=[("P", pi)])
                op("act", lambda e, pi=pi, mb=mb: e.activation(out=vx[:, mb, :], in_=P[pi][:], func=AF.Copy), reads=[("P", pi)], writes=K_vx)
            norm_phase(xsrc, skey, voff["nxa"] + 16 * l, S, h_dst)
            load_ws(0, w_xq[l].rearrange("(c p) n -> p c n", p=128))
            WO = WS[1][:].rearrange("p c n -> p (c n)").rearrange("p (h n) -> p h n", h=4)
            op("pool", lambda e: e.dma_start(out=WO, in_=w_xo[l].rearrange("(h p) n -> p h n", p=128)), writes=[("WS", 1)], dma=True)
            scale = 128.0 ** -0.5
            sv = xsrc.rearrange("(c p) s -> p c s", p=128)
            dv = xdst.rearrange("(c p) s -> p c s", p=128)
            steps = [(tb, h) for tb in range(NTB) for h in range(4)]
            st = {}

            def stA(s_):
                tb, h = steps[s_]
                pi = proj_fm(0, h * 128, tb)
                qi = s_ % 2
                op("act", lambda e: e.activation(out=qx[qi][:], in_=P[pi][:], func=AF.Copy, scale=scale), reads=[("P", pi)], writes=[K_qx[qi]])

            def stB(s_):
                tb, h = steps[s_]
                qi = s_ % 2
                for mb in range(2):
                    pti = (s_ % 2) * 2 + mb
                    op("pe", lambda e, mb=mb: e.matmul(Z[mb][:], kxT[:, h, mb * 128:(mb + 1) * 128], qx[qi][:], start=True, stop=True),
                       reads=K_kxT + [K_qx[qi]], writes=[("Z", mb)])
                    op("act", lambda e, mb=mb, pti=pti: e.activation(out=pT[pti][:], in_=Z[mb][:], func=AF.Exp), reads=[("Z", mb)], writes=[K_pT[pti]])

            def stC(s_):
                tb, h = steps[s_]
                b_ = s_ % 2
                pk = [K_pT[b_ * 2], K_pT[b_ * 2 + 1]]

                def fden(e):
                    ins = None
                    for mb in range(2):
                        ins = e.matmul(C[b_][:], ones_b[:], pT[b_ * 2 + mb][:], start=(mb == 0), stop=(mb == 1))
                    return ins
                op("pe", fden, reads=["ones_b"] + pk, writes=[("C", b_)])

                def fnum(e):
                    ins = None
                    for mb in range(2):
                        ins = e.matmul(O[b_][:], vx[:, mb, h * 128:(h + 1) * 128], pT[b_ * 2 + mb][:], start=(mb == 0), stop=(mb == 1))
                    return ins
                op("pe", fnum, reads=K_vx + pk, writes=[("O", b_)])
                a_ = nxt("w", NW)
                st[s_] = a_
                op("act", lambda e: e.activation(out=wk[a_][:], in_=C[b_][:], func=AF.Ln), reads=[("C", b_)], writes=[("wk", a_)])
                op("act", lambda e: e.activation(out=wk[a_][:], in_=wk[a_][:], func=AF.Exp, scale=-1.0), reads=[("wk", a_)], writes=[("wk", a_)])

            def stD(s_):
                tb, h = steps[s_]
                b_ = s_ % 2
                a_ = st[s_]
                op("dve", lambda e: e.tensor_tensor(out=oxT[:, h, :], in0=O[b_][:], in1=wk[a_][:], op=ALU.mult),
                   reads=[("O", b_), ("wk", a_)], writes=[("kT", 0, h)])
                if h == 3:
                    stE(tb)

            def stE(tb):
                for d2 in range(NCH // 2):
                    xi = nxt("x", NX)
                    d0 = 2 * d2
                    op("sp", lambda e, xi=xi, d0=d0: e.dma_start(out=xt[xi][:], in_=sv[:, d0:d0 + 2, tb * TB:(tb + 1) * TB]),
                       reads=[(skey, d0, tb), (skey, d0 + 1, tb)], writes=[("xt", xi)], dma=True)
                    for j in range(2):
                        dch = d0 + j
                        pi = nxt("p", 2)

                        def fo(e, pi=pi, dch=dch):
                            ins = None
                            for h in range(4):
                                ins = e.matmul(P[pi][:], WO[:, h, dch * 128:(dch + 1) * 128], oxT[:, h, :], start=(h == 0), stop=(h == 3))
                            return ins
                        op("pe", fo, reads=[("WS", 1)] + [("kT", 0, h) for h in range(4)], writes=[("P", pi)])
                        op("dve", lambda e, xi=xi, pi=pi, j=j: e.tensor_tensor(out=xt[xi][:, j, :], in0=P[pi][:], in1=xt[xi][:, j, :], op=ALU.add),
                           reads=[("P", pi), ("xt", xi)], writes=[("xt", xi)])
                    op("act", lambda e, xi=xi, d0=d0: e.dma_start(out=dv[:, d0:d0 + 2, tb * TB:(tb + 1) * TB], in_=xt[xi][:]),
                       reads=[("xt", xi)], writes=[(dkey, d0, tb), (dkey, d0 + 1, tb)], dma=True)

            ns = len(steps)
            for s_ in range(ns + 3):
                if s_ < ns:
                    stA(s_)
                if 0 <= s_ - 1 < ns:
                    stB(s_ - 1)
                if 0 <= s_ - 2 < ns:
                    stC(s_ - 2)
                if 0 <= s_ - 3 < ns:
                    stD(s_ - 3)

        out_toks = []
        consts()
        cur, ckey = xT, "xT"
        done = False
        for l in range(nlayers):
            norm_phase(cur, ckey, voff["nmix"] + 16 * l, S, h_dst)
            if stop == "norm1":
                for c in range(NCH):
                    out_toks.append(op("sp", lambda e, c=c: e.dma_start(out=hdbg[c], in_=H[:, c, :]), reads=[("H", c, tb) for tb in range(NTB)],
                                       writes=[("hdbg", c)], dma=True))
                done = True
                break
            mixer(l)
            if stop == "mixer":
                done = True
                break
            out_proj(l, cur, ckey, xA, "xA")
            if stop in ("outproj", "mixload"):
                done = True
                break
            xattn(l, xA, "xA", xB, "xB")
            cur, ckey = xB, "xB"
        if not done:
            norm_phase(cur, ckey, voff["fin"], S, None, final=True)
        for q in Sched.QUEUES:
            for i in range(Sched.NSLOT):
                g = sc.dgen[q][i]
                if g > 0:
                    out_toks.append((("d", q, i), 16 * g))
        sc.final_wait("sp", out_toks)
        with nc.Block() as block:
            sc.emit(block)
    return nc, sc


_CACHE = {}


def kernel(**inputs):
    maps = pack_inputs(inputs, SPLIT)
    if "nc" not in _CACHE:
        _CACHE["nc"] = build_program(SPLIT)[0]
    nc = _CACHE["nc"]
    res = run_bass_kernel_spmd(nc, maps, core_ids=list(range(N_CORES)))
    out = np.empty((4, S, D), np.float32)
    for b in range(4):
        out[b] = np.asarray(res.results[2 * b]["yT"]).T
    return out
```

```python
import numpy as np
from contextlib import ExitStack
import concourse.bass as bass
import concourse.mybir as mybir
from concourse.bass_utils import run_bass_kernel_spmd

F32 = mybir.dt.float32
BF16 = mybir.dt.bfloat16
AF = mybir.ActivationFunctionType
ALU = mybir.AluOpType

D = 2048
S = 2048
L = 4
NM = 256
NCH = 16
TB = 512
NTB = 4
EPS = 1e-6
NEG = -30000.0
GELU_C = 1.5957691216057308

SPLIT = True
N_CORES = 8


def core_units(hf, split):
    if split:
        return list(range(4 * hf, 4 * hf + 4)), [hf], [2 * hf, 2 * hf + 1]
    return list(range(8)), [0, 1], [0, 1, 2, 3]


def col_slots(hf, split):
    sbh, pairs, groups = core_units(hf, split)
    slots = []
    for h in sbh:
        slots.append([(128 * h, 128), (1024 + 128 * h, 128), (2048 + 128 * h, 128), (3072 + 128 * h, 128)])
    for p in pairs:
        slots.append([(4096 + 128 * p, 128), (4352 + 128 * p, 128), (4608 + 256 * p, 256)])
        slots.append([(5136 + 256 * p, 256), (5120, 16), (None, 240)])
    gord = list(groups) + [g for g in range(4) if g not in groups]
    slots.append([(6160 + 128 * g, 128) for g in gord])
    for i in range(0, len(groups), 2):
        g0, g1 = groups[i], groups[i + 1]
        slots.append([(5648 + 128 * g0, 128), (6672 + 128 * g0, 128), (5648 + 128 * g1, 128), (6672 + 128 * g1, 128)])
    return slots


def local_heads(hf, split):
    sbh, pairs, groups = core_units(hf, split)
    return list(sbh) + [8 + 2 * p + h for p in pairs for h in range(2)] + [12 + g for g in groups]


def cc_groups(split):
    return [[0, 1, 2, 3], [4, 5], [6, 7]] if split else []


def chunk_order(split):
    if not split:
        return list(range(16))
    order = []
    for grp in cc_groups(split):
        for r in range(2):
            lh = local_heads(r, split)
            order += [lh[li] for li in grp]
    return order


def vec_layout(split):
    off = {}
    n = 0
    for nm in ("nmix", "nxa", "nmem"):
        off[nm] = n
        n += L * 16
    off["fin"] = n
    n += 16
    off["onorm"] = n
    n += L * 16
    off["bgate"] = n
    n += L * 2
    return off, n


def pack_inputs(inputs, split, nlw=L, ncores=N_CORES, lite=False):
    f = np.float32
    x = np.asarray(inputs["x"], f)
    mem = np.asarray(inputs["mem"], f)
    w_in = np.asarray(inputs["w_in"], f)[:nlw]
    voff, nv = vec_layout(split)
    per_half = {}
    for hf in (0, 1):
        sbh, pairs, groups = core_units(hf, split)
        slots = col_slots(hf, split)
        ncol = 512 * len(slots)
        wl = np.zeros((nlw, D, ncol), f)
        c = 0
        for sl in slots:
            for (st, w) in sl:
                if st is not None:
                    wl[:, :, c:c + w] = w_in[:, :, st:st + w]
                c += w
        vec = np.zeros((128, nv), f)
        for l in range(L):
            vec[:, voff["nmix"] + 16 * l: voff["nmix"] + 16 * l + 16] = np.asarray(inputs["norm_mix"], f)[l].reshape(16, 128).T
            vec[:, voff["nxa"] + 16 * l: voff["nxa"] + 16 * l + 16] = np.asarray(inputs["norm_xattn"], f)[l].reshape(16, 128).T
            vec[:, voff["nmem"] + 16 * l: voff["nmem"] + 16 * l + 16] = np.asarray(inputs["norm_mem"], f)[l].reshape(16, 128).T
            on = np.asarray(inputs["out_norm"], f)[l].reshape(16, 128)
            for li, m in enumerate(local_heads(hf, split)):
                vec[:, voff["onorm"] + 16 * l + li] = on[m]
            bg = np.asarray(inputs["b_gla_gate"], f)[l].reshape(2, 128)
            for j, p in enumerate(pairs):
                vec[:, voff["bgate"] + 2 * l + j] = bg[p]
        vec[:, voff["fin"]: voff["fin"] + 16] = np.asarray(inputs["final_norm"], f).reshape(16, 128).T
        wup = np.asarray(inputs["w_gla_gate_up"], f).reshape(L, 16, 2, 128)[:nlw, :, pairs, :].reshape(nlw, 16, 128 * len(pairs))
        wsg = np.ascontiguousarray(np.asarray(inputs["w_sgu"], f)[:nlw, groups].transpose(0, 1, 3, 2))
        bsg = np.ascontiguousarray(np.asarray(inputs["b_sgu"], f)[:nlw, groups])
        gord = list(groups) + [g for g in range(4) if g not in groups]
        sgn = np.ascontiguousarray(np.asarray(inputs["sgu_norm"], f)[:nlw].reshape(nlw, 4, 128)[:, gord].reshape(nlw, 512))
        per_half[hf] = dict(w_in=np.ascontiguousarray(wl), vecs=vec, wup=np.ascontiguousarray(wup), wsg=wsg, bsg=bsg, sgn=sgn)
    shared = dict(
        w_out=np.ascontiguousarray(np.asarray(inputs["w_out"], f)[:nlw].reshape(nlw, 16, 128, D)[:, chunk_order(split)].reshape(nlw, D, D)),
        w_xq=np.ascontiguousarray(np.asarray(inputs["w_xq"], f)[:nlw]),
        w_xkv=np.ascontiguousarray(np.asarray(inputs["w_xkv"], f)[:nlw]),
        w_xo=np.ascontiguousarray(np.asarray(inputs["w_xo"], f)[:nlw]),
    )
    if lite:
        for k in ("w_out", "w_xq", "w_xkv", "w_xo"):
            shared[k] = np.zeros((1, 128, 128), f)
    maps = []
    for c in range(ncores):
        b, hf = c // 2, (c % 2 if split else 0)
        m = dict(xT=np.ascontiguousarray(x[b].T), memT=np.ascontiguousarray(mem[b].T))
        m.update(per_half[hf])
        m.update(shared)
        maps.append(m)
    return maps


class Sched:
    COMPUTE = ("pe", "act", "dve", "pool")
    QUEUES = ("sp", "pool", "act")
    NSLOT = 6

    def __init__(self, nc, es):
        self.nc = nc
        self.streams = {e: [] for e in ("pe", "act", "dve", "pool", "sp")}
        self.sems = {}
        for e in self.COMPUTE:
            self.sems[("c", e)] = es.enter_context(nc.semaphore("c_" + e))
        for q in self.QUEUES:
            for i in range(self.NSLOT):
                self.sems[("d", q, i)] = es.enter_context(nc.semaphore("d_%s%d" % (q, i)))
        self.NCC = 4
        for i in range(self.NCC):
            self.sems[("k", i)] = es.enter_context(nc.semaphore("k_%d" % i))
        self.kgen = [0] * self.NCC
        self.knext = 0
        self.ccount = {e: 0 for e in self.COMPUTE}
        self.dgen = {q: [0] * self.NSLOT for q in self.QUEUES}
        self.dnext = {q: 0 for q in self.QUEUES}
        self.waited = {e: {} for e in self.streams}
        self.lastw = {}
        self.readers = {}
        self.nops = 0

    def _need(self, eng, tok, waits):
        sid, val = tok
        if self.waited[eng].get(sid, 0) < val:
            self.waited[eng][sid] = val
            waits.append((sid, val))

    def op(self, eng, fn, reads=(), writes=(), dma=False):
        waits = []
        deps = []
        for k in reads:
            t = self.lastw.get(k)
            if t is not None:
                deps.append(t)
        for k in writes:
            t = self.lastw.get(k)
            if t is not None:
                deps.append(t)
            deps.extend(self.readers.get(k, {}).values())
        for t in deps:
            if (not dma) and eng == "pe" and t[0] == ("c", "pe"):
                continue
            self._need(eng, t, waits)
        if dma == "cc":
            slot = self.knext
            self.knext = (slot + 1) % self.NCC
            sid = ("k", slot)
            if self.kgen[slot] > 0:
                self._need(eng, (sid, self.kgen[slot]), waits)
            self.kgen[slot] += 1
            tok = (sid, self.kgen[slot])
            inc = 1
        elif dma:
            slot = self.dnext[eng]
            self.dnext[eng] = (slot + 1) % self.NSLOT
            prev = self.dgen[eng][slot]
            sid = ("d", eng, slot)
            if prev > 0:
                self._need(eng, (sid, 16 * prev), waits)
            self.dgen[eng][slot] += 1
            tok = (sid, 16 * self.dgen[eng][slot])
            inc = 16
        else:
            self.ccount[eng] += 1
            sid = ("c", eng)
            tok = (sid, self.ccount[eng])
            inc = 1
        self.streams[eng].append((waits, fn, sid, inc))
        for k in reads:
            self.readers.setdefault(k, {})[sid] = tok
        for k in writes:
            self.lastw[k] = tok
            self.readers[k] = {}
        self.nops += 1
        return tok

    def final_wait(self, eng, toks):
        waits = []
        for t in toks:
            self._need(eng, t, waits)
        self.streams[eng].append((waits, None, None, 0))

    def emit(self, block):
        def mk(name):
            def f(eng):
                for waits, fn, sid, inc in self.streams[name]:
                    for (ws, val) in waits:
                        eng.wait_ge(self.sems[ws], val)
                    if fn is not None:
                        ins = fn(eng)
                        ins.then_inc(self.sems[sid], inc)
            return f
        block.tensor(mk("pe"))
        block.scalar(mk("act"))
        block.vector(mk("dve"))
        block.gpsimd(mk("pool"))
        block.sync(mk("sp"))


def build_program(split=SPLIT, nlayers=L, stop=None, debug=False, lite=False, ncores=N_CORES):
    LW = nlayers
    nc = bass.Bass("TRN2", target_bir_lowering=False)
    sbh, pairs, groups = core_units(0, split)
    NSB, NP, NG = len(sbh), len(pairs), len(groups)
    nslots = NSB + 2 * NP + 1 + NG // 2
    voff, nv = vec_layout(split)
    dk = "ExternalOutput" if debug else "Internal"

    xT = nc.dram_tensor("xT", [D, S], F32, kind="ExternalInput").ap()
    memT = nc.dram_tensor("memT", [D, NM], F32, kind="ExternalInput").ap()
    w_in = nc.dram_tensor("w_in", [LW, D, nslots * 512], F32, kind="ExternalInput").ap()
    vecs_d = nc.dram_tensor("vecs", [128, nv], F32, kind="ExternalInput").ap()
    wup_d = nc.dram_tensor("wup", [LW, 16, 128 * NP], F32, kind="ExternalInput").ap()
    wsg_d = nc.dram_tensor("wsg", [LW, NG, 128, 128], F32, kind="ExternalInput").ap()
    bsg_d = nc.dram_tensor("bsg", [LW, NG, 128], F32, kind="ExternalInput").ap()
    if lite:
        w_out = nc.dram_tensor("w_out", [1, 128, 128], F32, kind="ExternalInput").ap()
        w_xq = nc.dram_tensor("w_xq", [1, 128, 128], F32, kind="ExternalInput").ap()
        w_xkv = nc.dram_tensor("w_xkv", [1, 128, 128], F32, kind="ExternalInput").ap()
        w_xo = nc.dram_tensor("w_xo", [1, 128, 128], F32, kind="ExternalInput").ap()
    else:
        w_out = nc.dram_tensor("w_out", [LW, D, D], F32, kind="ExternalInput").ap()
        w_xq = nc.dram_tensor("w_xq", [LW, D, 512], F32, kind="ExternalInput").ap()
        w_xkv = nc.dram_tensor("w_xkv", [LW, D, 1024], F32, kind="ExternalInput").ap()
        w_xo = nc.dram_tensor("w_xo", [LW, 512, D], F32, kind="ExternalInput").ap()
    sgn_d = nc.dram_tensor("sgn", [LW, 512], F32, kind="ExternalInput").ap()
    yT = nc.dram_tensor("yT", [D, S], F32, kind="ExternalOutput").ap()
    xA = nc.dram_tensor("xA", [D, S], F32, kind=dk).ap()
    xB = nc.dram_tensor("xB", [D, S], F32, kind=dk).ap()
    NLH = NSB + 2 * NP + NG
    ccg = cc_groups(split)
    mgd = nc.dram_tensor("mgd", [NLH, 128, S], BF16, kind=("Internal" if split else dk)).ap()
    mga = [nc.dram_tensor("mga%d" % k, [2 * len(g) * 128, S], BF16).ap() for k, g in enumerate(ccg)]
    npairs_cc = ncores // 2
    rgroups = [[2 * i, 2 * i + 1] for i in range(npairs_cc)]
    hdbg = nc.dram_tensor("hdbg", [16, 128, S], BF16, kind=dk).ap() if debug else None

    def xv(ap):
        return ap.rearrange("(c p) s -> c p s", p=128)

    with ExitStack() as es:
        def sb(name, shape, dt):
            return es.enter_context(nc.sbuf_tensor(name, shape, dt))

        def ps(name, shape, dt):
            return es.enter_context(nc.psum_tensor(name, shape, dt))

        H = sb("H", [128, NCH, S], BF16)
        WS = [sb("WS%d" % i, [128, NCH, 512], BF16) for i in range(2)]
        vecs = sb("vecs_sb", [128, nv], F32)
        ones_f = sb("ones_f", [128, 128], F32)
        ones_b = sb("ones_b", [128, 128], BF16)
        ident_b = sb("ident_b", [128, 128], BF16)
        tri_incl = sb("tri_incl", [128, 128], BF16)
        tri_low = sb("tri_low", [128, 128], BF16)
        tri_ui = sb("tri_ui", [128, 128], F32)
        blkmask = sb("blkmask", [128, 128], F32)
        negmask = sb("negmask", [128, 896], BF16)
        rmask = sb("rmask", [128, 512], F32)
        qT = [sb("qT%d" % i, [128, S], BF16) for i in range(2)]
        kT = [sb("kT%d" % i, [128, S], BF16) for i in range(2)]
        vtok = [sb("vtok%d" % i, [128, 16, 128], BF16) for i in range(2)]
        NZ = 5
        zs = [sb("zs%d" % i, [128, 512], F32) for i in range(NZ)]
        spb = [sb("spb%d" % i, [128, 512], BF16) for i in range(NZ)]
        et = [sb("et%d" % i, [128, 512], F32) for i in range(2)]
        Ab = [sb("Ab%d" % i, [128, 512], BF16) for i in range(NZ)]
        NE = 2
        e_o = [sb("e_o%d" % i, [128, 512], F32) for i in range(NE)]
        e_sq = [sb("e_sq%d" % i, [128, 512], F32) for i in range(NE)]
        e_rs = [sb("e_rs%d" % i, [128, 512], F32) for i in range(NE)]
        e_g = [sb("e_g%d" % i, [128, 512], F32) for i in range(NE)]
        e_mg = [sb("e_mg%d" % i, [128, 512], BF16) for i in range(NE)]
        NW = 5
        wk = [sb("wk%d" % i, [128, 512], F32) for i in range(NW)]
        NX = 3
        xt = [sb("xt%d" % i, [128, 2, 512], F32) for i in range(NX)]
        rT = sb("rT_sb", [16, S], BF16)
        wup = sb("wup_sb", [16, 128], BF16)
        g_q = sb("g_q", [128, 512], BF16)
        g_k = sb("g_k", [128, 512], BF16)
        g_kd = sb("g_kd", [128, 512], BF16)
        g_kdt = sb("g_kdt", [128, 4, 128], BF16)
        g_v = sb("g_v", [128, 4, 256], BF16)
        g_at = [sb("g_at%d" % i, [128, 128], BF16) for i in range(2)]
        S32 = sb("S32", [128, 128], F32)
        Sbf = sb("Sbf", [128, 128], BF16)
        gn_bc = sb("gn_bc", [128, 512], F32)
        wsT = [sb("wsT%d" % i, [128, 128], BF16) for i in range(NG)]
        wsTf = sb("wsTf", [128, 128], F32)
        bs_bc = [sb("bs_bc%d" % i, [128, 128], F32) for i in range(NG)]
        s_ss = sb("s_ss", [128, 8], F32)
        Z = [ps("Z%d" % i, [128, 512], F32) for i in range(2)]
        C = [ps("C%d" % i, [128, 512], F32) for i in range(2)]
        O = [ps("O%d" % i, [128, 512], F32) for i in range(2)]
        P = [ps("P%d" % i, [128, 512], F32) for i in range(2)]
        Tb = O[1][:].bitcast(BF16)[:, 0:512]

        g_sp, g_cum, g_eb, g_ebi, g_k32 = wk[0], wk[1], wk[2], wk[3], wk[4]
        K_sp, K_cum, K_eb, K_ebi, K_k32 = ("wk", 0), ("wk", 1), ("wk", 2), ("wk", 3), ("wk", 4)
        kxT = qT[0][:, 0:1024].rearrange("p (h m) -> p h m", h=4)
        vx = qT[0][:, 1024:2048].rearrange("p (b n) -> p b n", b=2)
        K_kxT = [("qT", 0, 0), ("qT", 0, 1)]
        K_vx = [("qT", 0, 2), ("qT", 0, 3)]
        s_vn = [Ab[3], Ab[4]]
        qx = [spb[1], spb[2]]
        K_qx = [("spb", 1), ("spb", 2)]
        pT = [Ab[0], Ab[1], Ab[2], spb[0]]
        K_pT = [("Ab", 0), ("Ab", 1), ("Ab", 2), ("spb", 0)]
        oxT = kT[0][:, :].rearrange("p (h n) -> p h n", h=4)
        print("SBUF bytes remaining:", nc.sbuf_bytes_remaining)

        sc = Sched(nc, es)
        op = sc.op
        ctr = {"p": 0, "e": 0, "w": 0, "x": 0, "z": 0}

        def nxt(k, n):
            v = ctr[k]
            ctr[k] = (v + 1) % n
            return v

        def consts():
            op("pool", lambda e: e.memset(ones_f[:], 1.0), writes=["ones_f"])
            op("pool", lambda e: e.memset(ones_b[:], 1.0), writes=["ones_b"])
            op("pool", lambda e: e.affine_select(out=ident_b[:], in_=ones_b[:], pattern=[[-1, 128]], compare_op=ALU.is_equal,
                                                 fill=0.0, base=0, channel_multiplier=1), reads=["ones_b"], writes=["ident_b"])
            op("pool", lambda e: e.affine_select(out=tri_incl[:], in_=ones_b[:], pattern=[[-1, 128]], compare_op=ALU.is_ge,
                                                 fill=0.0, base=0, channel_multiplier=1), reads=["ones_b"], writes=["tri_incl"])
            op("pool", lambda e: e.affine_select(out=tri_low[:], in_=ones_b[:], pattern=[[1, 128]], compare_op=ALU.is_gt,
                                                 fill=0.0, base=0, channel_multiplier=-1), reads=["ones_b"], writes=["tri_low"])
            op("pool", lambda e: e.affine_select(out=tri_ui[:], in_=ones_f[:], pattern=[[1, 128]], compare_op=ALU.is_ge,
                                                 fill=0.0, base=0, channel_multiplier=-1), reads=["ones_f"], writes=["tri_ui"])
            op("pool", lambda e: e.affine_select(out=blkmask[:], in_=ones_f[:], pattern=[[1, 128]], compare_op=ALU.is_ge,
                                                 fill=0.0, base=0, channel_multiplier=-1), reads=["ones_f"], writes=["blkmask"])
            op("pool", lambda e: e.memset(blkmask[0:64, 64:128], 0.0), reads=["blkmask"], writes=["blkmask"])
            op("pool", lambda e: e.memset(negmask[:], 0.0), writes=["negmask"])
            op("pool", lambda e: e.affine_select(out=negmask[:], in_=negmask[:], pattern=[[1, 896]],
                                                 compare_op=ALU.is_gt, fill=NEG, base=-384, channel_multiplier=-1),
               reads=["negmask"], writes=["negmask"])
            op("pool", lambda e: e.memset(rmask[:], 1.0), writes=["rmask"])
            op("pool", lambda e: e.memset(rmask[:].rearrange("p (c t) -> p c t", t=64)[:, :, 0:1], 0.0), reads=["rmask"], writes=["rmask"])
            op("sp", lambda e: e.dma_start(out=vecs[:], in_=vecs_d[:, :]), writes=["vecs"], dma=True)
            b0 = voff["bgate"]
            op("dve", lambda e: e.tensor_scalar(out=vecs[:, b0:b0 + 2 * L], in0=vecs[:, b0:b0 + 2 * L], scalar1=-1.0, scalar2=None,
                                                op0=ALU.mult), reads=["vecs"], writes=["vecs"])

        def rstd_from_ss(ss_ap, ss_key, out_tile, out_key, inv_n, tmp_tile, tmp_key):
            op("act", lambda e: e.activation(out=tmp_tile, in_=ss_ap, func=AF.Ln, bias=EPS, scale=inv_n),
               reads=[ss_key], writes=[tmp_key])
            op("act", lambda e: e.activation(out=out_tile, in_=tmp_tile, func=AF.Exp, scale=-0.5),
               reads=[tmp_key], writes=[out_key])

        def norm_phase(src, srckey, gcol, ntok, dst_fn, final=False):
            nblk = max(1, ntok // TB)
            w = min(TB, ntok)
            srcv = src.rearrange("(c p) s -> p c s", p=128)
            lq = ["sp", "pool"]
            for tb in range(nblk):
                t0 = tb * w
                ri = nxt("e", NE)
                for c2 in range(NCH // 2):
                    xi = nxt("x", NX)
                    op(lq[c2 % 2], lambda e, c2=c2, xi=xi, t0=t0: e.dma_start(out=xt[xi][:, :, 0:w], in_=srcv[:, 2 * c2:2 * c2 + 2, t0:t0 + w]),
                       reads=[(srckey, 2 * c2, tb), (srckey, 2 * c2 + 1, tb)], writes=[("xt", xi)], dma=True)
                    for j in range(2):
                        c = 2 * c2 + j
                        if c == 0:
                            op("act", lambda e, xi=xi, ri=ri, j=j: e.activation(out=e_o[ri][:, 0:w], in_=xt[xi][:, j, 0:w], func=AF.Square),
                               reads=[("xt", xi)], writes=[("e_o", ri)])
                        else:
                            wi = nxt("w", NW)
                            op("act", lambda e, xi=xi, wi=wi, j=j: e.activation(out=wk[wi][:, 0:w], in_=xt[xi][:, j, 0:w], func=AF.Square),
                               reads=[("xt", xi)], writes=[("wk", wi)])
                            op("dve", lambda e, wi=wi, ri=ri: e.tensor_tensor(out=e_o[ri][:, 0:w], in0=e_o[ri][:, 0:w], in1=wk[wi][:, 0:w], op=ALU.add),
                               reads=[("wk", wi), ("e_o", ri)], writes=[("e_o", ri)])
                si = nxt("p", 2)
                op("pe", lambda e, ri=ri, si=si: e.matmul(P[si][:, 0:w], ones_f[:], e_o[ri][:, 0:w], start=True, stop=True),
                   reads=[("e_o", ri), "ones_f"], writes=[("P", si)])
                rstd_from_ss(P[si][:, 0:w], ("P", si), e_rs[ri][:, 0:w], ("e_rs", ri), 1.0 / D, e_sq[ri][:, 0:w], ("e_sq", ri))
                for c2 in range(NCH // 2):
                    xi = nxt("x", NX)
                    op(lq[c2 % 2], lambda e, c2=c2, xi=xi, t0=t0: e.dma_start(out=xt[xi][:, :, 0:w], in_=srcv[:, 2 * c2:2 * c2 + 2, t0:t0 + w]),
                       reads=[(srckey, 2 * c2, tb), (srckey, 2 * c2 + 1, tb)], writes=[("xt", xi)], dma=True)
                    for j in range(2):
                        c = 2 * c2 + j
                        if not final:
                            dap, dkeys = dst_fn(c, t0, w, tb)
                            op("dve", lambda e, c=c, xi=xi, dap=dap, ri=ri, j=j: e.scalar_tensor_tensor(
                                out=dap, in0=xt[xi][:, j, 0:w], scalar=vecs[:, gcol + c:gcol + c + 1], in1=e_rs[ri][:, 0:w],
                                op0=ALU.mult, op1=ALU.mult), reads=[("xt", xi), ("e_rs", ri), "vecs"], writes=dkeys)
                        else:
                            op("dve", lambda e, c=c, xi=xi, ri=ri, j=j: e.scalar_tensor_tensor(
                                out=xt[xi][:, j, 0:w], in0=xt[xi][:, j, 0:w], scalar=vecs[:, gcol + c:gcol + c + 1], in1=e_rs[ri][:, 0:w],
                                op0=ALU.mult, op1=ALU.mult), reads=[("xt", xi), ("e_rs", ri), "vecs"], writes=[("xt", xi)])
                    if final:
                        yv = yT.rearrange("(c p) s -> p c s", p=128)
                        tok = op("act", lambda e, c2=c2, xi=xi, t0=t0: e.dma_start(out=yv[:, 2 * c2:2 * c2 + 2, t0:t0 + w], in_=xt[xi][:, :, 0:w]),
                                 reads=[("xt", xi)], writes=[("yT", 2 * c2, tb), ("yT", 2 * c2 + 1, tb)], dma=True)
                        out_toks.append(tok)

        def h_dst(c, t0, w, tb):
            return H[:, c, t0:t0 + w], [("H", c, tb)]

        def hkeys(tb):
            return [("H", c, tb) for c in range(NCH)]

        def proj_fm(slot, col0, tb, ncols=128):
            pi = nxt("p", 2)

            def f(e):
                ins = None
                for c in range(NCH):
                    ins = e.matmul(P[pi][0:ncols, :], WS[slot][:, c, col0:col0 + ncols], H[:, c, tb * TB:(tb + 1) * TB],
                                   start=(c == 0), stop=(c == NCH - 1))
                return ins
            op("pe", f, reads=[("WS", slot)] + hkeys(tb), writes=[("P", pi)])
            return pi

        def proj_tm(slot, col0, ncols, tb, blocks):
            pi = nxt("p", 2)

            def f(e):
                ins = None
                for j, blk in enumerate(blocks):
                    for c in range(NCH):
                        ins = e.matmul(P[pi][:, j * ncols:(j + 1) * ncols], H[:, c, blk * 128:(blk + 1) * 128],
                                       WS[slot][:, c, col0:col0 + ncols], start=(c == 0), stop=(c == NCH - 1))
                return ins
            op("pe", f, reads=[("WS", slot)] + hkeys(tb), writes=[("P", pi)])
            return pi

        def load_ws(slot, dram_view):
            op("pool", lambda e: e.dma_start(out=WS[slot][:], in_=dram_view), writes=[("WS", slot)], dma=True)

        def win_view(l, s):
            return w_in[l].rearrange("(c p) n -> p c n", p=128)[:, :, s * 512:(s + 1) * 512]

        def epilogue_g(l, m, tb, src_ap, src_key, gslot, gcol, banks=None, ei=None, src_in_eo=False, gate_done=False):
            if ei is None:
                ei = nxt("e", NE)
            if banks is None:
                pi = nxt("p", 2)
                G, Gk = P[pi], ("P", pi)
                si = nxt("p", 2)
                SSt, SSk = P[si], ("P", si)
            else:
                G, Gk, SSt, SSk = banks
            gc = voff["onorm"] + 16 * l + m
            if not src_in_eo:
                op("act", lambda e: e.activation(out=e_o[ei][:], in_=src_ap, func=AF.Copy), reads=[src_key], writes=[("e_o", ei)])
                yield
            op("act", lambda e: e.activation(out=e_sq[ei][:], in_=e_o[ei][:], func=AF.Square), reads=[("e_o", ei)], writes=[("e_sq", ei)])
            yield
            op("pe", lambda e: e.matmul(SSt[:], ones_f[:], e_sq[ei][:], start=True, stop=True), reads=[("e_sq", ei), "ones_f"], writes=[SSk])
            yield

            def fg(e):
                ins = None
                for c in range(NCH):
                    ins = e.matmul(G[:], WS[gslot][:, c, gcol:gcol + 128], H[:, c, tb * TB:(tb + 1) * TB], start=(c == 0), stop=(c == NCH - 1))
                return ins
            if not gate_done:
                op("pe", fg, reads=[("WS", gslot)] + hkeys(tb), writes=[Gk])
                yield
            op("act", lambda e: e.activation(out=e_sq[ei][:], in_=SSt[:], func=AF.Ln, bias=EPS, scale=1.0 / 128), reads=[SSk], writes=[("e_sq", ei)])
            yield
            op("act", lambda e: e.activation(out=e_rs[ei][:], in_=e_sq[ei][:], func=AF.Exp, scale=-0.5), reads=[("e_sq", ei)], writes=[("e_rs", ei)])
            yield
            op("dve", lambda e: e.scalar_tensor_tensor(out=e_o[ei][:], in0=e_o[ei][:], scalar=vecs[:, gc:gc + 1], in1=e_rs[ei][:],
                                                       op0=ALU.mult, op1=ALU.mult), reads=[("e_o", ei), ("e_rs", ei), "vecs"], writes=[("e_o", ei)])
            yield
            op("act", lambda e: e.activation(out=e_g[ei][:], in_=G[:], func=AF.Exp, scale=-1.0), reads=[Gk], writes=[("e_g", ei)])
            yield
            op("act", lambda e: e.activation(out=e_g[ei][:], in_=e_g[ei][:], func=AF.Ln, bias=1.0), reads=[("e_g", ei)], writes=[("e_g", ei)])
            yield
            op("act", lambda e: e.activation(out=e_g[ei][:], in_=e_g[ei][:], func=AF.Exp, scale=-1.0), reads=[("e_g", ei)], writes=[("e_g", ei)])
            yield
            op("dve", lambda e: e.tensor_tensor(out=e_g[ei][:], in0=G[:], in1=e_g[ei][:], op=ALU.mult),
               reads=[Gk, ("e_g", ei)], writes=[("e_g", ei)])
            yield
            op("dve", lambda e: e.tensor_tensor(out=e_mg[ei][:], in0=e_o[ei][:], in1=e_g[ei][:], op=ALU.mult),
               reads=[("e_o", ei), ("e_g", ei)], writes=[("e_mg", ei)])
            yield
            op("sp", lambda e: e.dma_start(out=mgd[m][:, tb * TB:(tb + 1) * TB], in_=e_mg[ei][:]), reads=[("e_mg", ei)],
               writes=[("MG", m, tb)], dma=True)
            yield

        def drive(*gens):
            gens = list(gens)
            while gens:
                for g in list(gens):
                    try:
                        next(g)
                    except StopIteration:
                        gens.remove(g)

        def epilogue(*a, **k):
            drive(epilogue_g(*a, **k))

        def sb_proj_items(i, slot):
            bi = i % 2
            scale = 128.0 ** -0.5
            items = []

            def group(tb, kind):
                st = {}

                def piece(k):
                    def f():
                        if k == 0:
                            st["pi"] = nxt("p", 2)
                        pi = st["pi"]

                        def mm(e):
                            ins = None
                            if kind == "v":
                                blk = tb * 4 + k
                                for c in range(NCH):
                                    ins = e.matmul(P[pi][:, k * 128:(k + 1) * 128], H[:, c, blk * 128:(blk + 1) * 128],
                                                   WS[slot][:, c, 256:384], start=(c == 0), stop=(c == NCH - 1))
                            else:
                                col0 = 0 if kind == "q" else 128
                                for c in range(4 * k, 4 * k + 4):
                                    ins = e.matmul(P[pi][:], WS[slot][:, c, col0:col0 + 128], H[:, c, tb * TB:(tb + 1) * TB],
                                                   start=(c == 0), stop=(c == NCH - 1))
                            return ins
                        op("pe", mm, reads=[("WS", slot)] + hkeys(tb), writes=[("P", pi)])
                    return f

                def evac():
                    pi = st["pi"]
                    if kind == "q":
                        op("act", lambda e: e.activation(out=qT[bi][:, tb * TB:(tb + 1) * TB], in_=P[pi][:], func=AF.Copy, scale=scale),
                           reads=[("P", pi)], writes=[("qT", bi, tb)])
                    elif kind == "k":
                        op("dve", lambda e: e.tensor_copy(out=kT[bi][:, tb * TB:(tb + 1) * TB], in_=P[pi][:]),
                           reads=[("P", pi)], writes=[("kT", bi, tb)])
                    else:
                        op("dve", lambda e: e.tensor_copy(out=vtok[bi][:, tb * 4:(tb + 1) * 4, :], in_=P[pi][:].rearrange("p (j d) -> p j d", d=128)),
                           reads=[("P", pi)], writes=[("vtok", bi, tb)])
                return [piece(k) for k in range(4)] + [evac]
            for tb in range(NTB):
                for kind in ("k", "v", "q"):
                    items += group(tb, kind)
            return items

        def sb_attn(l, i, slot, bg):
            bi = i % 2

            def tile_ops(ch, kb, qb):
                zi = nxt("z", NZ)
                zb = zi % 2
                ei = zi % 2
                r = kb - 4 * qb
                first = (kb == 4 * qb + 3)
                last = (kb == 0)
                d = {}
                d["Z"] = lambda: op("pe", lambda e: e.matmul(Z[zb][:], kT[bi][:, kb * 128:(kb + 1) * 128], qT[bi][:, qb * TB:(qb + 1) * TB], start=True, stop=True),
                                    reads=[("kT", bi, kb // 4), ("qT", bi, qb)], writes=[("Z", zb)])
                if r >= 0:
                    d["COPY"] = lambda: op("dve", lambda e: e.tensor_tensor(out=zs[zi][:], in0=Z[zb][:], in1=negmask[:, 384 - 128 * r:896 - 128 * r], op=ALU.add),
                                           reads=[("Z", zb), "negmask"], writes=[("zs", zi)])
                else:
                    d["COPY"] = lambda: op("dve", lambda e: e.tensor_copy(out=zs[zi][:], in_=Z[zb][:]), reads=[("Z", zb)], writes=[("zs", zi)])
                d["EXPA"] = lambda: op("act", lambda e: e.activation(out=et[ei][:], in_=zs[zi][:], func=AF.Exp), reads=[("zs", zi)], writes=[("et", ei)])
                d["LN"] = lambda: op("act", lambda e: e.activation(out=spb[zi][:], in_=et[ei][:], func=AF.Ln, bias=1.0), reads=[("et", ei)], writes=[("spb", zi)])
                d["TRI"] = lambda: op("pe", lambda e: e.matmul(C[ch][:], tri_incl[:], spb[zi][:], start=first, stop=last),
                                      reads=[("spb", zi), "tri_incl"], writes=[("C", ch)])
                d["SUB"] = lambda: op("dve", lambda e: e.scalar_tensor_tensor(out=zs[zi][:], in0=C[ch][:], scalar=-1.0, in1=zs[zi][:], op0=ALU.mult, op1=ALU.add),
                                      reads=[("C", ch), ("zs", zi)], writes=[("zs", zi)])
                if not last:
                    d["LOW"] = lambda: op("pe", lambda e: e.matmul(C[ch][:], tri_low[:], spb[zi][:], start=False, stop=False),
                                          reads=[("spb", zi), "tri_low"], writes=[("C", ch)])
                else:
                    d["LOW"] = lambda: None
                d["EXPB"] = lambda: op("act", lambda e: e.activation(out=Ab[zi][:], in_=zs[zi][:], func=AF.Exp), reads=[("zs", zi)], writes=[("Ab", zi)])

                def av():
                    op("pe", lambda e: e.matmul(O[ch][:], vtok[bi][:, kb, :], Ab[zi][:], start=first, stop=last),
                       reads=[("Ab", zi), ("vtok", bi, kb // 4)], writes=[("O", ch)])
                    if last:
                        epilogue(l, i, qb, O[ch][:], ("O", ch), slot, 384, banks=(O[ch], ("O", ch), C[ch], ("C", ch)))
                d["AV"] = av
                return d

            nper = 0
            ta = [(0, kb, qb) for qb in (3, 0) for kb in range(4 * qb + 3, -1, -1)]
            tbl = [(1, kb, qb) for qb in (2, 1) for kb in range(4 * qb + 3, -1, -1)]
            T = []
            for x_, y_ in zip(ta, tbl):
                T += [x_, y_]
            n = len(T)
            ops_ = {}
            for p in range(n + 2):
                if p < n:
                    ops_[p] = tile_ops(*T[p])
                if p - 2 >= 0:
                    ops_[p - 2]["LOW"]()
                    ops_[p - 2]["EXPB"]()
                if p < n:
                    ops_[p]["Z"]()
                    ops_[p]["COPY"]()
                if p - 2 >= 0:
                    ops_[p - 2]["AV"]()
                if p < n:
                    ops_[p]["EXPA"]()
                    ops_[p]["LN"]()
                if 0 <= p - 1 < n:
                    ops_[p - 1]["TRI"]()
                    ops_[p - 1]["SUB"]()
                nper += 1
                for _ in range(2):
                    if bg:
                        bg.pop(0)()
            while bg:
                bg.pop(0)()

        def gla_pair(l, j, slotA, slotB, first_pair):
            gp = pairs[j]
            if first_pair:
                for tb in range(NTB):
                    pi = proj_fm(slotB, 256, tb, ncols=16)
                    op("act", lambda e, pi=pi, tb=tb: e.activation(out=rT[:, tb * TB:(tb + 1) * TB], in_=P[pi][0:16, :], func=AF.Copy),
                       reads=[("P", pi)], writes=[("rT", tb)])
            op("pool", lambda e: e.dma_start(out=wup[:], in_=wup_d[l][:, j * 128:(j + 1) * 128]), writes=["wup"], dma=True)
            op("dve", lambda e: e.memset(S32[:], 0.0), writes=["S32"])
            op("dve", lambda e: e.memset(Sbf[:], 0.0), writes=["Sbf"])
            bcol = voff["bgate"] + 2 * l + j
            for tb in range(NTB):
                tsl = slice(tb * TB, (tb + 1) * TB)
                pi = nxt("p", 2)
                op("pe", lambda e, pi=pi, tsl=tsl: e.matmul(P[pi][:], wup[:], rT[:, tsl], start=True, stop=True),
                   reads=["wup", ("rT", tb)], writes=[("P", pi)])
                op("act", lambda e, pi=pi: e.activation(out=g_sp[:], in_=P[pi][:], func=AF.Exp, scale=-1.0, bias=vecs[:, bcol:bcol + 1]),
                   reads=[("P", pi), "vecs"], writes=[K_sp])
                op("act", lambda e: e.activation(out=g_sp[:], in_=g_sp[:], func=AF.Ln, bias=1.0), reads=[K_sp], writes=[K_sp])
                op("dve", lambda e: e.tensor_tensor_scan(out=g_cum[:], data0=rmask[:], data1=g_sp[:], initial=0.0, op0=ALU.mult, op1=ALU.add),
                   reads=[K_sp, "rmask"], writes=[K_cum])
                op("act", lambda e: e.activation(out=g_eb[:], in_=g_cum[:], func=AF.Exp, scale=-1.0 / 16), reads=[K_cum], writes=[K_eb])
                op("act", lambda e: e.activation(out=g_ebi[:], in_=g_cum[:], func=AF.Exp, scale=1.0 / 16), reads=[K_cum], writes=[K_ebi])
                pi = proj_fm(slotA, 0, tb)
                op("dve", lambda e, pi=pi: e.scalar_tensor_tensor(out=g_q[:], in0=P[pi][:], scalar=0.125, in1=g_eb[:], op0=ALU.mult, op1=ALU.mult),
                   reads=[("P", pi), K_eb], writes=["g_q"])
                pi = proj_fm(slotA, 128, tb)
                op("dve", lambda e, pi=pi: e.tensor_tensor(out=g_k32[:], in0=P[pi][:], in1=g_ebi[:], op=ALU.mult),
                   reads=[("P", pi), K_ebi], writes=[K_k32])
                op("act", lambda e: e.activation(out=g_k[:], in_=g_k32[:], func=AF.Copy), reads=[K_k32], writes=["g_k"])
                dec_bc = g_eb[:].rearrange("p (c t) -> p c t", t=64)[:, :, 63:64].broadcast_to([128, 8, 64])
                op("dve", lambda e: e.tensor_tensor(out=g_kd[:].rearrange("p (c t) -> p c t", t=64), in0=g_k32[:].rearrange("p (c t) -> p c t", t=64),
                                                    in1=dec_bc, op=ALU.mult), reads=[K_k32, K_eb], writes=["g_kd"])

                def ftr(e):
                    ins = None
                    for j4 in range(4):
                        ins = e.transpose(Tb[:, j4 * 128:(j4 + 1) * 128], g_kd[:, j4 * 128:(j4 + 1) * 128], ident_b[:])
                    return ins
                op("pe", ftr, reads=["g_kd", "ident_b"], writes=[("O", 1)])
                op("dve", lambda e: e.tensor_copy(out=g_kdt[:], in_=Tb.rearrange("p (j d) -> p j d", d=128)), reads=[("O", 1)], writes=["g_kdt"])
                for half in range(2):
                    pi = proj_tm(slotA, 256, 256, tb, [tb * 4 + 2 * half, tb * 4 + 2 * half + 1])
                    op("act", lambda e, pi=pi, half=half: e.activation(out=g_v[:, 2 * half:2 * half + 2, :],
                                                                       in_=P[pi][:].rearrange("p (j d) -> p j d", d=256), func=AF.Copy),
                       reads=[("P", pi)], writes=["g_v"])
                OG = [O[0], C[0]]
                OGk = [("O", 0), ("C", 0)]
                for cp in range(4):
                    csl = slice(cp * 128, (cp + 1) * 128)
                    for h in range(2):
                        hs = slice(h * 64, (h + 1) * 64)
                        ai = (cp * 2 + h) % 2
                        op("pe", lambda e, hs=hs, csl=csl: e.matmul(Z[0][:, 0:128], g_k[hs, csl], g_q[hs, csl], start=True, stop=True),
                           reads=["g_k", "g_q"], writes=[("Z", 0)])
                        op("dve", lambda e, ai=ai: e.tensor_tensor(out=g_at[ai][:], in0=Z[0][:, 0:128], in1=blkmask[:], op=ALU.mult),
                           reads=[("Z", 0), "blkmask"], writes=[("g_at", ai)])
                        op("pe", lambda e, h=h, ai=ai, csl=csl, cp=cp: e.matmul(OG[h][:, csl], g_v[:, cp, h * 128:(h + 1) * 128], g_at[ai][:],
                                                                                 start=True, stop=False),
                           reads=["g_v", ("g_at", ai)], writes=[OGk[h]])
                    for c2 in range(2):
                        ch = cp * 2 + c2
                        tsl64 = slice(cp * 128 + c2 * 64, cp * 128 + c2 * 64 + 64)
                        psl = slice(c2 * 64, c2 * 64 + 64)

                        def finter(e, tsl64=tsl64, c2=c2):
                            ins = None
                            for h in range(2):
                                hs = slice(h * 64, (h + 1) * 64)
                                ins = e.matmul(OG[h][:, tsl64], Sbf[hs, :], g_q[hs, tsl64], start=False, stop=(c2 == 1))
                            return ins
                        op("pe", finter, reads=["Sbf", "g_q"], writes=[("O", 0), ("C", 0)])

                        def fkv(e, psl=psl, cp=cp):
                            ins = None
                            for h in range(2):
                                ins = e.matmul(Z[1][h * 64:(h + 1) * 64, 0:128], g_kdt[psl, cp, h * 64:(h + 1) * 64],
                                               g_v[psl, cp, h * 128:(h + 1) * 128], start=True, stop=True)
                            return ins
                        op("pe", fkv, reads=["g_kdt", "g_v"], writes=[("Z", 1)])
                        dcol = ch * 64 + 63
                        op("dve", lambda e, dcol=dcol: e.scalar_tensor_tensor(out=S32[:], in0=S32[:], scalar=g_eb[:, dcol:dcol + 1], in1=Z[1][:, 0:128],
                                                                              op0=ALU.mult, op1=ALU.add),
                           reads=["S32", K_eb, ("Z", 1)], writes=["S32"])
                        op("dve", lambda e: e.tensor_copy(out=Sbf[:], in_=S32[:]), reads=["S32"], writes=["Sbf"])
                drive(*[epilogue_g(l, NSB + 2 * j + h, tb, OG[h][:], OGk[h], slotB, h * 128, banks=(P[h], ("P", h), OG[h], OGk[h])) for h in range(2)])

        def gelu_g(pi, out_tile, out_key):
            a = nxt("w", NW)
            b = nxt("w", NW)
            op("dve", lambda e: e.tensor_copy(out=wk[a][:], in_=P[pi][:]), reads=[("P", pi)], writes=[("wk", a)])
            yield
            op("dve", lambda e: e.tensor_tensor(out=wk[b][:], in0=wk[a][:], in1=wk[a][:], op=ALU.mult), reads=[("wk", a)], writes=[("wk", b)])
            yield
            op("dve", lambda e: e.tensor_scalar(out=wk[b][:], in0=wk[b][:], scalar1=0.044715, scalar2=1.0, op0=ALU.mult, op1=ALU.add),
               reads=[("wk", b)], writes=[("wk", b)])
            yield
            op("dve", lambda e: e.tensor_tensor(out=wk[b][:], in0=wk[b][:], in1=wk[a][:], op=ALU.mult), reads=[("wk", a), ("wk", b)], writes=[("wk", b)])
            yield
            op("act", lambda e: e.activation(out=wk[b][:], in_=wk[b][:], func=AF.Exp, scale=-GELU_C), reads=[("wk", b)], writes=[("wk", b)])
            yield
            op("act", lambda e: e.activation(out=wk[b][:], in_=wk[b][:], func=AF.Ln, bias=1.0), reads=[("wk", b)], writes=[("wk", b)])
            yield
            op("act", lambda e: e.activation(out=wk[b][:], in_=wk[b][:], func=AF.Exp, scale=-1.0), reads=[("wk", b)], writes=[("wk", b)])
            yield
            op("dve", lambda e: e.tensor_tensor(out=out_tile, in0=wk[a][:], in1=wk[b][:], op=ALU.mult), reads=[("wk", a), ("wk", b)], writes=[out_key])
            yield

        def gelu_from_psum(pi, out_tile, out_key):
            drive(gelu_g(pi, out_tile, out_key))

        def sgu_unit(l, slotV, uslot, lgs):
            op("sp", lambda e: e.dma_start(out=gn_bc[:], in_=sgn_d[l].partition_broadcast(128)), writes=["gn_bc"], dma=True)
            for lg in lgs:
                op("sp", lambda e, lg=lg: e.dma_start(out=wsTf[:], in_=wsg_d[l][lg]), writes=["wsTf"], dma=True)
                op("dve", lambda e, lg=lg: e.tensor_tensor(out=wsT[lg][:], in0=wsTf[:], in1=tri_ui[:], op=ALU.mult),
                   reads=["wsTf", "tri_ui"], writes=[("wsT", lg)])
                op("sp", lambda e, lg=lg: e.dma_start(out=bs_bc[lg][:], in_=bsg_d[l][lg].partition_broadcast(128)), writes=[("bs_bc", lg)], dma=True)
            MX = [O[0], C[0]]
            MXk = [("O", 0), ("C", 0)]
            def gelu_ip_g(pi, hold):
                a = nxt("w", NW)
                b = nxt("w", NW)
                hold["a"] = a
                op("dve", lambda e: e.tensor_copy(out=wk[a][:], in_=P[pi][:]), reads=[("P", pi)], writes=[("wk", a)])
                yield
                op("dve", lambda e: e.tensor_tensor(out=wk[b][:], in0=wk[a][:], in1=wk[a][:], op=ALU.mult), reads=[("wk", a)], writes=[("wk", b)])
                yield
                op("dve", lambda e: e.tensor_scalar(out=wk[b][:], in0=wk[b][:], scalar1=0.044715, scalar2=1.0, op0=ALU.mult, op1=ALU.add),
                   reads=[("wk", b)], writes=[("wk", b)])
                yield
                op("dve", lambda e: e.tensor_tensor(out=wk[b][:], in0=wk[b][:], in1=wk[a][:], op=ALU.mult), reads=[("wk", a), ("wk", b)], writes=[("wk", b)])
                yield
                op("act", lambda e: e.activation(out=wk[b][:], in_=wk[b][:], func=AF.Exp, scale=-GELU_C), reads=[("wk", b)], writes=[("wk", b)])
                yield
                op("act", lambda e: e.activation(out=wk[b][:], in_=wk[b][:], func=AF.Ln, bias=1.0), reads=[("wk", b)], writes=[("wk", b)])
                yield
                op("act", lambda e: e.activation(out=wk[b][:], in_=wk[b][:], func=AF.Exp, scale=-1.0), reads=[("wk", b)], writes=[("wk", b)])
                yield
                op("dve", lambda e: e.tensor_tensor(out=wk[a][:], in0=wk[a][:], in1=wk[b][:], op=ALU.mult), reads=[("wk", a), ("wk", b)], writes=[("wk", a)])
                yield

            def vblock_g(tb, j4):
                blk = tb * 4 + j4
                pi = proj_tm(slotV, 0, 512, tb, [blk])
                yield
                hold = {}
                yield from gelu_ip_g(pi, hold)
                a = hold["a"]
                sscol = blk % 8
                vi = blk % 2
                op("dve", lambda e: e.memset(s_ss[:, sscol:sscol + 1], 0.0), writes=[("s_ss", sscol)])
                yield
                op("act", lambda e: e.activation(out=e_sq[0][:], in_=wk[a][:], func=AF.Square, accum_out=s_ss[:, sscol:sscol + 1]),
                   reads=[("wk", a), ("s_ss", sscol)], writes=[("e_sq", 0), ("s_ss", sscol)])
                yield
                op("act", lambda e: e.activation(out=s_ss[:, sscol:sscol + 1], in_=s_ss[:, sscol:sscol + 1], func=AF.Ln, bias=EPS, scale=1.0 / 512),
                   reads=[("s_ss", sscol)], writes=[("s_ss", sscol)])
                yield
                op("act", lambda e: e.activation(out=s_ss[:, sscol:sscol + 1], in_=s_ss[:, sscol:sscol + 1], func=AF.Exp, scale=-0.5),
                   reads=[("s_ss", sscol)], writes=[("s_ss", sscol)])
                yield
                op("dve", lambda e: e.scalar_tensor_tensor(out=s_vn[vi][:], in0=wk[a][:], scalar=s_ss[:, sscol:sscol + 1],
                                                           in1=gn_bc[:], op0=ALU.mult, op1=ALU.mult),
                   reads=[("wk", a), ("s_ss", sscol), "gn_bc"], writes=[("Ab", 3 + vi)])
                yield

                def fmx(e):
                    ins = None
                    for k, lg in enumerate(lgs):
                        ins = e.matmul(MX[k][:, j4 * 128:(j4 + 1) * 128], s_vn[vi][:, lg * 128:(lg + 1) * 128], wsT[lg][:], start=True, stop=True)
                    return ins
                op("pe", fmx, reads=[("Ab", 3 + vi)] + [("wsT", lg) for lg in lgs], writes=MXk[:len(lgs)])
                yield

            UB = [O[1], C[1]]
            UBk = [("O", 1), ("C", 1)]
            GB = [Z[0], Z[1]]
            GBk = [("Z", 0), ("Z", 1)]

            def early(tb):
                for k, lg in enumerate(lgs):
                    for (bank, bkey, col0) in ((UB[k], UBk[k], k * 256), (GB[k], GBk[k], k * 256 + 128)):
                        def f(e, bank=bank, col0=col0):
                            ins = None
                            for c in range(NCH):
                                ins = e.matmul(bank[:], WS[uslot][:, c, col0:col0 + 128], H[:, c, tb * TB:(tb + 1) * TB], start=(c == 0), stop=(c == NCH - 1))
                            return ins
                        op("pe", f, reads=[("WS", uslot)] + hkeys(tb), writes=[bkey])

            def upart_g(tb, k, lg):
                ei = nxt("e", NE)
                X, Xk = e_g[ei], ("e_g", ei)
                Sg, Sk = e_rs[ei], ("e_rs", ei)
                op("dve", lambda e: e.tensor_copy(out=X[:], in_=UB[k][:]), reads=[UBk[k]], writes=[Xk])
                yield
                op("dve", lambda e: e.tensor_tensor(out=Sg[:], in0=X[:], in1=X[:], op=ALU.mult), reads=[Xk], writes=[Sk])
                yield
                op("dve", lambda e: e.tensor_scalar(out=Sg[:], in0=Sg[:], scalar1=0.044715, scalar2=1.0, op0=ALU.mult, op1=ALU.add), reads=[Sk], writes=[Sk])
                yield
                op("dve", lambda e: e.tensor_tensor(out=Sg[:], in0=Sg[:], in1=X[:], op=ALU.mult), reads=[Xk, Sk], writes=[Sk])
                yield
                op("act", lambda e: e.activation(out=Sg[:], in_=Sg[:], func=AF.Exp, scale=-GELU_C), reads=[Sk], writes=[Sk])
                yield
                op("act", lambda e: e.activation(out=Sg[:], in_=Sg[:], func=AF.Ln, bias=1.0), reads=[Sk], writes=[Sk])
                yield
                op("act", lambda e: e.activation(out=Sg[:], in_=Sg[:], func=AF.Exp, scale=-1.0), reads=[Sk], writes=[Sk])
                yield
                op("dve", lambda e: e.tensor_tensor(out=X[:], in0=X[:], in1=Sg[:], op=ALU.mult), reads=[Xk, Sk], writes=[Xk])
                yield
                op("dve", lambda e: e.tensor_tensor(out=e_o[ei][:].rearrange("p (j t) -> p j t", t=128),
                                                    in0=MX[k][:].rearrange("p (j t) -> p j t", t=128),
                                                    in1=bs_bc[lg][:].unsqueeze(1).broadcast_to([128, 4, 128]), op=ALU.add),
                   reads=[MXk[k], ("bs_bc", lg)], writes=[("e_o", ei)])
                yield
                op("dve", lambda e: e.tensor_tensor(out=e_o[ei][:], in0=e_o[ei][:], in1=X[:], op=ALU.mult), reads=[("e_o", ei), Xk], writes=[("e_o", ei)])
                yield
                yield from epilogue_g(l, NSB + 2 * NP + lg, tb, None, None, uslot, k * 256 + 128,
                                      banks=(GB[k], GBk[k], MX[k], MXk[k]), ei=ei, src_in_eo=True, gate_done=True)

            def adv(pair, nsteps):
                for _ in range(nsteps):
                    for g in pair:
                        next(g, None)

            allp = [(vblock_g(tb, 2 * hp), vblock_g(tb, 2 * hp + 1)) for tb in range(NTB) for hp in range(2)]
            early(0)
            adv(allp[0], 2)
            for kk in range(len(allp)):
                if kk + 1 < len(allp):
                    adv(allp[kk + 1], 1)
                drive(*allp[kk])
                if kk + 1 < len(allp):
                    adv(allp[kk + 1], 1)
                if kk % 2 == 1:
                    tb = kk // 2
                    drive(*[upart_g(tb, k, lg) for k, lg in enumerate(lgs)])
                    if tb + 1 < NTB:
                        early(tb + 1)

        def mixer(l):
            s = 0
            units = []
            for i in range(NSB):
                units.append(("sb", i, [s]))
                s += 1
            for j in range(NP):
                units.append(("gla", j, [s, s + 1]))
                s += 2
            units.append(("sgu", 0, list(range(s, s + 1 + NG // 2))))
            wsn = {"n": 0}

            def alloc(k):
                r = []
                for _ in range(k):
                    r.append(wsn["n"] % 2)
                    wsn["n"] += 1
                return r
            def exchange(k):
                grp = ccg[k]
                n = len(grp)
                src = mgd[grp[0]:grp[0] + n].rearrange("h p s -> (h p) s")
                op("pool", lambda e: e.collective_compute("AllGather", ALU.bypass, replica_groups=rgroups, ins=[src.opt()], outs=[mga[k].opt()]),
                   reads=[("MG", li, tb) for li in grp for tb in range(NTB)], writes=[("MGA", k)], dma="cc")

            for ui, (kind, idx, dsl) in enumerate(units):
                if kind == "sb":
                    if idx == 0:
                        sbws = [alloc(1)[0] for _ in range(NSB)]
                        load_ws(sbws[0], win_view(l, dsl[0]))
                        for f in sb_proj_items(0, sbws[0]):
                            f()
                    bg = []
                    if idx + 1 < NSB:
                        load_ws(sbws[idx + 1], win_view(l, dsl[0] + 1))
                        bg = sb_proj_items(idx + 1, sbws[idx + 1])
                    sb_attn(l, idx, sbws[idx], bg)
                elif kind == "gla":
                    ws = alloc(2)
                    load_ws(ws[0], win_view(l, dsl[0]))
                    load_ws(ws[1], win_view(l, dsl[1]))
                    if split and idx == 0:
                        exchange(0)
                    gla_pair(l, idx, ws[0], ws[1], idx == 0)
                else:
                    for k in range(NG // 2):
                        ws = alloc(2)
                        load_ws(ws[0], win_view(l, dsl[0]))
                        load_ws(ws[1], win_view(l, dsl[1 + k]))
                        if split and k == 0:
                            exchange(1)
                        sgu_unit(l, ws[0], ws[1], [2 * k, 2 * k + 1])
                    if split:
                        exchange(2)

        def out_proj(l, xsrc, skey, xdst, dkey):
            if not split:
                for m in range(16):
                    op("sp", lambda e, m=m: e.dma_start(out=H[:, m, :], in_=mgd[m]), reads=[("MG", m, tb) for tb in range(NTB)],
                       writes=[("H", m, tb) for tb in range(NTB)], dma=True)
            else:
                jj = 0
                for k, grp in enumerate(ccg):
                    for r in range(2):
                        for idx in range(len(grp)):
                            row = (r * len(grp) + idx) * 128
                            op("sp", lambda e, jj=jj, k=k, row=row: e.dma_start(out=H[:, jj, :], in_=mga[k][row:row + 128, :]),
                               reads=[("MGA", k)], writes=[("H", jj, tb) for tb in range(NTB)], dma=True)
                            jj += 1
            if stop == "mixload":
                for c in range(NCH):
                    out_toks.append(op("sp", lambda e, c=c: e.dma_start(out=hdbg[c], in_=H[:, c, :]), reads=[("H", c, tb) for tb in range(NTB)],
                                       writes=[("hdbg", c)], dma=True))
                return
            wv = w_out[l].rearrange("(c p) n -> p c n", p=128)
            for cs in range(4):
                slot = cs % 2
                load_ws(slot, wv[:, :, cs * 512:(cs + 1) * 512])
                for tb in range(NTB):
                    for dcp in range(2):
                        xi = nxt("x", NX)
                        d0 = cs * 4 + 2 * dcp
                        sv = xsrc.rearrange("(c p) s -> p c s", p=128)
                        dv = xdst.rearrange("(c p) s -> p c s", p=128)
                        op("sp", lambda e, xi=xi, d0=d0, tb=tb, sv=sv: e.dma_start(out=xt[xi][:], in_=sv[:, d0:d0 + 2, tb * TB:(tb + 1) * TB]),
                           reads=[(skey, d0, tb), (skey, d0 + 1, tb)], writes=[("xt", xi)], dma=True)
                        for j in range(2):
                            pi = proj_fm(slot, (2 * dcp + j) * 128, tb)
                            op("dve", lambda e, xi=xi, pi=pi, j=j: e.tensor_tensor(out=xt[xi][:, j, :], in0=P[pi][:], in1=xt[xi][:, j, :], op=ALU.add),
                               reads=[("P", pi), ("xt", xi)], writes=[("xt", xi)])
                        op("act", lambda e, xi=xi, d0=d0, tb=tb, dv=dv: e.dma_start(out=dv[:, d0:d0 + 2, tb * TB:(tb + 1) * TB], in_=xt[xi][:]),
                           reads=[("xt", xi)], writes=[(dkey, d0, tb), (dkey, d0 + 1, tb)], dma=True)

        def xattn(l, xsrc, skey, xdst, dkey):

            def mem_dst(c, t0, w, tb):
                return H[:, c, 0:NM], [("H", c, 0)]
            norm_phase(memT, "memT", voff["nmem"] + 16 * l, NM, mem_dst)
            kvv = w_xkv[l].rearrange("(c p) n -> p c n", p=128)
            load_ws(0, kvv[:, :, 0:512])
            load_ws(1, kvv[:, :, 512:1024])
            mkeys = [("H", c, 0) for c in range(NCH)]
            for h in range(4):
                pi = nxt("p", 2)

                def fk(e, pi=pi, h=h):
                    ins = None
                    for c in range(NCH):
                        ins = e.matmul(P[pi][:, 0:NM], WS[0][:, c, h * 128:(h + 1) * 128], H[:, c, 0:NM], start=(c == 0), stop=(c == NCH - 1))
                    return ins
                op("pe", fk, reads=[("WS", 0)] + mkeys, writes=[("P", pi)])
                op("act", lambda e, pi=pi, h=h: e.activation(out=kxT[:, h, :], in_=P[pi][:, 0:NM], func=AF.Copy), reads=[("P", pi)], writes=K_kxT)
            for mb in range(2):
                pi = nxt("p", 2)

                def fv(e, pi=pi, mb=mb):
                    ins = None
                    for c in range(NCH):
                        ins = e.matmul(P[pi][:], H[:, c, mb * 128:(mb + 1) * 128], WS[1][:, c, :], start=(c == 0), stop=(c == NCH - 1))
                    return ins
                op("pe", fv, reads=[("WS", 1)] + mkeys, writes=[("P", pi)])
                op("act", lambda e, pi=pi, mb=mb: e.activation(out=vx[:, mb, :], in_=P[pi][:], func=AF.Copy), reads=[("P", pi)], writes=K_vx)
            norm_phase(xsrc, skey, voff["nxa"] + 16 * l, S, h_dst)
            load_ws(0, w_xq[l].rearrange("(c p) n -> p c n", p=128))
            WO = WS[1][:].rearrange("p c n -> p (c n)").rearrange("p (h n) -> p h n", h=4)
            op("pool", lambda e: e.dma_start(out=WO, in_=w_xo[l].rearrange("(h p) n -> p h n", p=128)), writes=[("WS", 1)], dma=True)
            scale = 128.0 ** -0.5
            sv = xsrc.rearrange("(c p) s -> p c s", p=128)
            dv = xdst.rearrange("(c p) s -> p c s", p=128)
            steps = [(tb, h) for tb in range(NTB) for h in range(4)]
            st = {}

            def stA(s_):
                tb, h = steps[s_]
                pi = proj_fm(0, h * 128, tb)
                qi = s_ % 2
                op("act", lambda e: e.activation(out=qx[qi][:], in_=P[pi][:], func=AF.Copy, scale=scale), reads=[("P", pi)], writes=[K_qx[qi]])

            def stB(s_):
                tb, h = steps[s_]
                qi = s_ % 2
                for mb in range(2):
                    pti = (s_ % 2) * 2 + mb
                    op("pe", lambda e, mb=mb: e.matmul(Z[mb][:], kxT[:, h, mb * 128:(mb + 1) * 128], qx[qi][:], start=True, stop=True),
                       reads=K_kxT + [K_qx[qi]], writes=[("Z", mb)])
                    op("act", lambda e, mb=mb, pti=pti: e.activation(out=pT[pti][:], in_=Z[mb][:], func=AF.Exp), reads=[("Z", mb)], writes=[K_pT[pti]])

            def stC(s_):
                tb, h = steps[s_]
                b_ = s_ % 2
                pk = [K_pT[b_ * 2], K_pT[b_ * 2 + 1]]

                def fden(e):
                    ins = None
                    for mb in range(2):
                        ins = e.matmul(C[b_][:], ones_b[:], pT[b_ * 2 + mb][:], start=(mb == 0), stop=(mb == 1))
                    return ins
                op("pe", fden, reads=["ones_b"] + pk, writes=[("C", b_)])

                def fnum(e):
                    ins = None
                    for mb in range(2):
                        ins = e.matmul(O[b_][:], vx[:, mb, h * 128:(h + 1) * 128], pT[b_ * 2 + mb][:], start=(mb == 0), stop=(mb == 1))
                    return ins
                op("pe", fnum, reads=K_vx + pk, writes=[("O", b_)])
                a_ = nxt("w", NW)
                st[s_] = a_
                op("act", lambda e: e.activation(out=wk[a_][:], in_=C[b_][:], func=AF.Ln), reads=[("C", b_)], writes=[("wk", a_)])
                op("act", lambda e: e.activation(out=wk[a_][:], in_=wk[a_][:], func=AF.Exp, scale=-1.0), reads=[("wk", a_)], writes=[("wk", a_)])

            def stD(s_):
                tb, h = steps[s_]
                b_ = s_ % 2
                a_ = st[s_]
                op("dve", lambda e: e.tensor_tensor(out=oxT[:, h, :], in0=O[b_][:], in1=wk[a_][:], op=ALU.mult),
                   reads=[("O", b_), ("wk", a_)], writes=[("kT", 0, h)])
                if h == 3:
                    stE(tb)

            def stE(tb):
                for d2 in range(NCH // 2):
                    xi = nxt("x", NX)
                    d0 = 2 * d2
                    op("sp", lambda e, xi=xi, d0=d0: e.dma_start(out=xt[xi][:], in_=sv[:, d0:d0 + 2, tb * TB:(tb + 1) * TB]),
                       reads=[(skey, d0, tb), (skey, d0 + 1, tb)], writes=[("xt", xi)], dma=True)
                    for j in range(2):
                        dch = d0 + j
                        pi = nxt("p", 2)

                        def fo(e, pi=pi, dch=dch):
                            ins = None
                            for h in range(4):
                                ins = e.matmul(P[pi][:], WO[:, h, dch * 128:(dch + 1) * 128], oxT[:, h, :], start=(h == 0), stop=(h == 3))
                            return ins
                        op("pe", fo, reads=[("WS", 1)] + [("kT", 0, h) for h in range(4)], writes=[("P", pi)])
                        op("dve", lambda e, xi=xi, pi=pi, j=j: e.tensor_tensor(out=xt[xi][:, j, :], in0=P[pi][:], in1=xt[xi][:, j, :], op=ALU.add),
                           reads=[("P", pi), ("xt", xi)], writes=[("xt", xi)])
                    op("act", lambda e, xi=xi, d0=d0: e.dma_start(out=dv[:, d0:d0 + 2, tb * TB:(tb + 1) * TB], in_=xt[xi][:]),
                       reads=[("xt", xi)], writes=[(dkey, d0, tb), (dkey, d0 + 1, tb)], dma=True)

            ns = len(steps)
            for s_ in range(ns + 3):
                if s_ < ns:
                    stA(s_)
                if 0 <= s_ - 1 < ns:
                    stB(s_ - 1)
                if 0 <= s_ - 2 < ns:
                    stC(s_ - 2)
                if 0 <= s_ - 3 < ns:
                    stD(s_ - 3)

        out_toks = []
        consts()
        cur, ckey = xT, "xT"
        done = False
        for l in range(nlayers):
            norm_phase(cur, ckey, voff["nmix"] + 16 * l, S, h_dst)
            if stop == "norm1":
                for c in range(NCH):
                    out_toks.append(op("sp", lambda e, c=c: e.dma_start(out=hdbg[c], in_=H[:, c, :]), reads=[("H", c, tb) for tb in range(NTB)],
                                       writes=[("hdbg", c)], dma=True))
                done = True
                break
            mixer(l)
            if stop == "mixer":
                done = True
                break
            out_proj(l, cur, ckey, xA, "xA")
            if stop in ("outproj", "mixload"):
                done = True
                break
            xattn(l, xA, "xA", xB, "xB")
            cur, ckey = xB, "xB"
        if not done:
            norm_phase(cur, ckey, voff["fin"], S, None, final=True)
        for q in Sched.QUEUES:
            for i in range(Sched.NSLOT):
                g = sc.dgen[q][i]
                if g > 0:
                    out_toks.append((("d", q, i), 16 * g))
        sc.final_wait("sp", out_toks)
        with nc.Block() as block:
            sc.emit(block)
    return nc, sc


_CACHE = {}


def kernel(**inputs):
    maps = pack_inputs(inputs, SPLIT)
    if "nc" not in _CACHE:
        _CACHE["nc"] = build_program(SPLIT)[0]
    nc = _CACHE["nc"]
    res = run_bass_kernel_spmd(nc, maps, core_ids=list(range(N_CORES)))
    out = np.empty((4, S, D), np.float32)
    for b in range(4):
        out[b] = np.asarray(res.results[2 * b]["yT"]).T
    return out
```

```python
import numpy as np
from contextlib import ExitStack
import concourse.bass as bass
import concourse.mybir as mybir
from concourse.bass_utils import run_bass_kernel_spmd

F32 = mybir.dt.float32
BF16 = mybir.dt.bfloat16
AF = mybir.ActivationFunctionType
ALU = mybir.AluOpType

D = 2048
S = 2048
L = 4
NM = 256
NCH = 16
TB = 512
NTB = 4
EPS = 1e-6
NEG = -30000.0
GELU_C = 1.5957691216057308

SPLIT = True
N_CORES = 8


def core_units(hf, split):
    if split:
        return list(range(4 * hf, 4 * hf + 4)), [hf], [2 * hf, 2 * hf + 1]
    return list(range(8)), [0, 1], [0, 1, 2, 3]


def col_slots(hf, split):
    sbh, pairs, groups = core_units(hf, split)
    slots = []
    for h in sbh:
        slots.append([(128 * h, 128), (1024 + 128 * h, 128), (2048 + 128 * h, 128), (3072 + 128 * h, 128)])
    for p in pairs:
        slots.append([(4096 + 128 * p, 128), (4352 + 128 * p, 128), (4608 + 256 * p, 256)])
        slots.append([(5136 + 256 * p, 256), (5120, 16), (None, 240)])
    gord = list(groups) + [g for g in range(4) if g not in groups]
    slots.append([(6160 + 128 * g, 128) for g in gord])
    for i in range(0, len(groups), 2):
        g0, g1 = groups[i], groups[i + 1]
        slots.append([(5648 + 128 * g0, 128), (6672 + 128 * g0, 128), (5648 + 128 * g1, 128), (6672 + 128 * g1, 128)])
    return slots


def local_heads(hf, split):
    sbh, pairs, groups = core_units(hf, split)
    return list(sbh) + [8 + 2 * p + h for p in pairs for h in range(2)] + [12 + g for g in groups]


def cc_groups(split):
    return [[0, 1, 2, 3], [4, 5], [6, 7]] if split else []


def chunk_order(split):
    if not split:
        return list(range(16))
    order = []
    for grp in cc_groups(split):
        for r in range(2):
            lh = local_heads(r, split)
            order += [lh[li] for li in grp]
    return order


def vec_layout(split):
    off = {}
    n = 0
    for nm in ("nmix", "nxa", "nmem"):
        off[nm] = n
        n += L * 16
    off["fin"] = n
    n += 16
    off["onorm"] = n
    n += L * 16
    off["bgate"] = n
    n += L * 2
    return off, n


def pack_inputs(inputs, split, nlw=L, ncores=N_CORES, lite=False):
    f = np.float32
    x = np.asarray(inputs["x"], f)
    mem = np.asarray(inputs["mem"], f)
    w_in = np.asarray(inputs["w_in"], f)[:nlw]
    voff, nv = vec_layout(split)
    per_half = {}
    for hf in (0, 1):
        sbh, pairs, groups = core_units(hf, split)
        slots = col_slots(hf, split)
        ncol = 512 * len(slots)
        wl = np.zeros((nlw, D, ncol), f)
        c = 0
        for sl in slots:
            for (st, w) in sl:
                if st is not None:
                    wl[:, :, c:c + w] = w_in[:, :, st:st + w]
                c += w
        vec = np.zeros((128, nv), f)
        for l in range(L):
            vec[:, voff["nmix"] + 16 * l: voff["nmix"] + 16 * l + 16] = np.asarray(inputs["norm_mix"], f)[l].reshape(16, 128).T
            vec[:, voff["nxa"] + 16 * l: voff["nxa"] + 16 * l + 16] = np.asarray(inputs["norm_xattn"], f)[l].reshape(16, 128).T
            vec[:, voff["nmem"] + 16 * l: voff["nmem"] + 16 * l + 16] = np.asarray(inputs["norm_mem"], f)[l].reshape(16, 128).T
            on = np.asarray(inputs["out_norm"], f)[l].reshape(16, 128)
            for li, m in enumerate(local_heads(hf, split)):
                vec[:, voff["onorm"] + 16 * l + li] = on[m]
            bg = np.asarray(inputs["b_gla_gate"], f)[l].reshape(2, 128)
            for j, p in enumerate(pairs):
                vec[:, voff["bgate"] + 2 * l + j] = bg[p]
        vec[:, voff["fin"]: voff["fin"] + 16] = np.asarray(inputs["final_norm"], f).reshape(16, 128).T
        wup = np.asarray(inputs["w_gla_gate_up"], f).reshape(L, 16, 2, 128)[:nlw, :, pairs, :].reshape(nlw, 16, 128 * len(pairs))
        wsg = np.ascontiguousarray(np.asarray(inputs["w_sgu"], f)[:nlw, groups].transpose(0, 1, 3, 2))
        bsg = np.ascontiguousarray(np.asarray(inputs["b_sgu"], f)[:nlw, groups])
        gord = list(groups) + [g for g in range(4) if g not in groups]
        sgn = np.ascontiguousarray(np.asarray(inputs["sgu_norm"], f)[:nlw].reshape(nlw, 4, 128)[:, gord].reshape(nlw, 512))
        per_half[hf] = dict(w_in=np.ascontiguousarray(wl), vecs=vec, wup=np.ascontiguousarray(wup), wsg=wsg, bsg=bsg, sgn=sgn)
    shared = dict(
        w_out=np.ascontiguousarray(np.asarray(inputs["w_out"], f)[:nlw].reshape(nlw, 16, 128, D)[:, chunk_order(split)].reshape(nlw, D, D)),
        w_xq=np.ascontiguousarray(np.asarray(inputs["w_xq"], f)[:nlw]),
        w_xkv=np.ascontiguousarray(np.asarray(inputs["w_xkv"], f)[:nlw]),
        w_xo=np.ascontiguousarray(np.asarray(inputs["w_xo"], f)[:nlw]),
    )
    if lite:
        for k in ("w_out", "w_xq", "w_xkv", "w_xo"):
            shared[k] = np.zeros((1, 128, 128), f)
    maps = []
    for c in range(ncores):
        b, hf = c // 2, (c % 2 if split else 0)
        m = dict(xT=np.ascontiguousarray(x[b].T), memT=np.ascontiguousarray(mem[b].T))
        m.update(per_half[hf])
        m.update(shared)
        maps.append(m)
    return maps


class Sched:
    COMPUTE = ("pe", "act", "dve", "pool")
    QUEUES = ("sp", "pool", "act")
    NSLOT = 6

    def __init__(self, nc, es):
        self.nc = nc
        self.streams = {e: [] for e in ("pe", "act", "dve", "pool", "sp")}
        self.sems = {}
        for e in self.COMPUTE:
            self.sems[("c", e)] = es.enter_context(nc.semaphore("c_" + e))
        for q in self.QUEUES:
            for i in range(self.NSLOT):
                self.sems[("d", q, i)] = es.enter_context(nc.semaphore("d_%s%d" % (q, i)))
        self.NCC = 4
        for i in range(self.NCC):
            self.sems[("k", i)] = es.enter_context(nc.semaphore("k_%d" % i))
        self.kgen = [0] * self.NCC
        self.knext = 0
        self.ccount = {e: 0 for e in self.COMPUTE}
        self.dgen = {q: [0] * self.NSLOT for q in self.QUEUES}
        self.dnext = {q: 0 for q in self.QUEUES}
        self.waited = {e: {} for e in self.streams}
        self.lastw = {}
        self.readers = {}
        self.nops = 0

    def _need(self, eng, tok, waits):
        sid, val = tok
        if self.waited[eng].get(sid, 0) < val:
            self.waited[eng][sid] = val
            waits.append((sid, val))

    def op(self, eng, fn, reads=(), writes=(), dma=False):
        waits = []
        deps = []
        for k in reads:
            t = self.lastw.get(k)
            if t is not None:
                deps.append(t)
        for k in writes:
            t = self.lastw.get(k)
            if t is not None:
                deps.append(t)
            deps.extend(self.readers.get(k, {}).values())
        for t in deps:
            if (not dma) and eng == "pe" and t[0] == ("c", "pe"):
                continue
            self._need(eng, t, waits)
        if dma == "cc":
            slot = self.knext
            self.knext = (slot + 1) % self.NCC
            sid = ("k", slot)
            if self.kgen[slot] > 0:
                self._need(eng, (sid, self.kgen[slot]), waits)
            self.kgen[slot] += 1
            tok = (sid, self.kgen[slot])
            inc = 1
        elif dma:
            slot = self.dnext[eng]
            self.dnext[eng] = (slot + 1) % self.NSLOT
            prev = self.dgen[eng][slot]
            sid = ("d", eng, slot)
            if prev > 0:
                self._need(eng, (sid, 16 * prev), waits)
            self.dgen[eng][slot] += 1
            tok = (sid, 16 * self.dgen[eng][slot])
            inc = 16
        else:
            self.ccount[eng] += 1
            sid = ("c", eng)
            tok = (sid, self.ccount[eng])
            inc = 1
        self.streams[eng].append((waits, fn, sid, inc))
        for k in reads:
            self.readers.setdefault(k, {})[sid] = tok
        for k in writes:
            self.lastw[k] = tok
            self.readers[k] = {}
        self.nops += 1
        return tok

    def final_wait(self, eng, toks):
        waits = []
        for t in toks:
            self._need(eng, t, waits)
        self.streams[eng].append((waits, None, None, 0))

    def emit(self, block):
        def mk(name):
            def f(eng):
                for waits, fn, sid, inc in self.streams[name]:
                    for (ws, val) in waits:
                        eng.wait_ge(self.sems[ws], val)
                    if fn is not None:
                        ins = fn(eng)
                        ins.then_inc(self.sems[sid], inc)
            return f
        block.tensor(mk("pe"))
        block.scalar(mk("act"))
        block.vector(mk("dve"))
        block.gpsimd(mk("pool"))
        block.sync(mk("sp"))


def build_program(split=SPLIT, nlayers=L, stop=None, debug=False, lite=False, ncores=N_CORES):
    LW = nlayers
    nc = bass.Bass("TRN2", target_bir_lowering=False)
    sbh, pairs, groups = core_units(0, split)
    NSB, NP, NG = len(sbh), len(pairs), len(groups)
    nslots = NSB + 2 * NP + 1 + NG // 2
    voff, nv = vec_layout(split)
    dk = "ExternalOutput" if debug else "Internal"

    xT = nc.dram_tensor("xT", [D, S], F32, kind="ExternalInput").ap()
    memT = nc.dram_tensor("memT", [D, NM], F32, kind="ExternalInput").ap()
    w_in = nc.dram_tensor("w_in", [LW, D, nslots * 512], F32, kind="ExternalInput").ap()
    vecs_d = nc.dram_tensor("vecs", [128, nv], F32, kind="ExternalInput").ap()
    wup_d = nc.dram_tensor("wup", [LW, 16, 128 * NP], F32, kind="ExternalInput").ap()
    wsg_d = nc.dram_tensor("wsg", [LW, NG, 128, 128], F32, kind="ExternalInput").ap()
    bsg_d = nc.dram_tensor("bsg", [LW, NG, 128], F32, kind="ExternalInput").ap()
    if lite:
        w_out = nc.dram_tensor("w_out", [1, 128, 128], F32, kind="ExternalInput").ap()
        w_xq = nc.dram_tensor("w_xq", [1, 128, 128], F32, kind="ExternalInput").ap()
        w_xkv = nc.dram_tensor("w_xkv", [1, 128, 128], F32, kind="ExternalInput").ap()
        w_xo = nc.dram_tensor("w_xo", [1, 128, 128], F32, kind="ExternalInput").ap()
    else:
        w_out = nc.dram_tensor("w_out", [LW, D, D], F32, kind="ExternalInput").ap()
        w_xq = nc.dram_tensor("w_xq", [LW, D, 512], F32, kind="ExternalInput").ap()
        w_xkv = nc.dram_tensor("w_xkv", [LW, D, 1024], F32, kind="ExternalInput").ap()
        w_xo = nc.dram_tensor("w_xo", [LW, 512, D], F32, kind="ExternalInput").ap()
    sgn_d = nc.dram_tensor("sgn", [LW, 512], F32, kind="ExternalInput").ap()
    yT = nc.dram_tensor("yT", [D, S], F32, kind="ExternalOutput").ap()
    xA = nc.dram_tensor("xA", [D, S], F32, kind=dk).ap()
    xB = nc.dram_tensor("xB", [D, S], F32, kind=dk).ap()
    NLH = NSB + 2 * NP + NG
    ccg = cc_groups(split)
    mgd = nc.dram_tensor("mgd", [NLH, 128, S], BF16, kind=("Internal" if split else dk)).ap()
    mga = [nc.dram_tensor("mga%d" % k, [2 * len(g) * 128, S], BF16).ap() for k, g in enumerate(ccg)]
    npairs_cc = ncores // 2
    rgroups = [[2 * i, 2 * i + 1] for i in range(npairs_cc)]
    hdbg = nc.dram_tensor("hdbg", [16, 128, S], BF16, kind=dk).ap() if debug else None

    def xv(ap):
        return ap.rearrange("(c p) s -> c p s", p=128)

    with ExitStack() as es:
        def sb(name, shape, dt):
            return es.enter_context(nc.sbuf_tensor(name, shape, dt))

        def ps(name, shape, dt):
            return es.enter_context(nc.psum_tensor(name, shape, dt))

        H = sb("H", [128, NCH, S], BF16)
        WS = [sb("WS%d" % i, [128, NCH, 512], BF16) for i in range(2)]
        vecs = sb("vecs_sb", [128, nv], F32)
        ones_f = sb("ones_f", [128, 128], F32)
        ones_b = sb("ones_b", [128, 128], BF16)
        ident_b = sb("ident_b", [128, 128], BF16)
        tri_incl = sb("tri_incl", [128, 128], BF16)
        tri_low = sb("tri_low", [128, 128], BF16)
        tri_ui = sb("tri_ui", [128, 128], F32)
        blkmask = sb("blkmask", [128, 128], F32)
        negmask = sb("negmask", [128, 896], BF16)
        rmask = sb("rmask", [128, 512], F32)
        qT = [sb("qT%d" % i, [128, S], BF16) for i in range(2)]
        kT = [sb("kT%d" % i, [128, S], BF16) for i in range(2)]
        vtok = [sb("vtok%d" % i, [128, 16, 128], BF16) for i in range(2)]
        NZ = 5
        zs = [sb("zs%d" % i, [128, 512], F32) for i in range(NZ)]
        spb = [sb("spb%d" % i, [128, 512], BF16) for i in range(NZ)]
        et = [sb("et%d" % i, [128, 512], F32) for i in range(2)]
        Ab = [sb("Ab%d" % i, [128, 512], BF16) for i in range(NZ)]
        NE = 2
        e_o = [sb("e_o%d" % i, [128, 512], F32) for i in range(NE)]
        e_sq = [sb("e_sq%d" % i, [128, 512], F32) for i in range(NE)]
        e_rs = [sb("e_rs%d" % i, [128, 512], F32) for i in range(NE)]
        e_g = [sb("e_g%d" % i, [128, 512], F32) for i in range(NE)]
        e_mg = [sb("e_mg%d" % i, [128, 512], BF16) for i in range(NE)]
        NW = 5
        wk = [sb("wk%d" % i, [128, 512], F32) for i in range(NW)]
        NX = 3
        xt = [sb("xt%d" % i, [128, 2, 512], F32) for i in range(NX)]
        rT = sb("rT_sb", [16, S], BF16)
        wup = sb("wup_sb", [16, 128], BF16)
        g_q = sb("g_q", [128, 512], BF16)
        g_k = sb("g_k", [128, 512], BF16)
        g_kd = sb("g_kd", [128, 512], BF16)
        g_kdt = sb("g_kdt", [128, 4, 128], BF16)
        g_v = sb("g_v", [128, 4, 256], BF16)
        g_at = [sb("g_at%d" % i, [128, 128], BF16) for i in range(2)]
        S32 = sb("S32", [128, 128], F32)
        Sbf = sb("Sbf", [128, 128], BF16)
        gn_bc = sb("gn_bc", [128, 512], F32)
        wsT = [sb("wsT%d" % i, [128, 128], BF16) for i in range(NG)]
        wsTf = sb("wsTf", [128, 128], F32)
        bs_bc = [sb("bs_bc%d" % i, [128, 128], F32) for i in range(NG)]
        s_ss = sb("s_ss", [128, 8], F32)
        Z = [ps("Z%d" % i, [128, 512], F32) for i in range(2)]
        C = [ps("C%d" % i, [128, 512], F32) for i in range(2)]
        O = [ps("O%d" % i, [128, 512], F32) for i in range(2)]
        P = [ps("P%d" % i, [128, 512], F32) for i in range(2)]
        Tb = O[1][:].bitcast(BF16)[:, 0:512]

        g_sp, g_cum, g_eb, g_ebi, g_k32 = wk[0], wk[1], wk[2], wk[3], wk[4]
        K_sp, K_cum, K_eb, K_ebi, K_k32 = ("wk", 0), ("wk", 1), ("wk", 2), ("wk", 3), ("wk", 4)
        kxT = qT[0][:, 0:1024].rearrange("p (h m) -> p h m", h=4)
        vx = qT[0][:, 1024:2048].rearrange("p (b n) -> p b n", b=2)
        K_kxT = [("qT", 0, 0), ("qT", 0, 1)]
        K_vx = [("qT", 0, 2), ("qT", 0, 3)]
        s_vn = [Ab[3], Ab[4]]
        qx = [spb[1], spb[2]]
        K_qx = [("spb", 1), ("spb", 2)]
        pT = [Ab[0], Ab[1], Ab[2], spb[0]]
        K_pT = [("Ab", 0), ("Ab", 1), ("Ab", 2), ("spb", 0)]
        oxT = kT[0][:, :].rearrange("p (h n) -> p h n", h=4)
        print("SBUF bytes remaining:", nc.sbuf_bytes_remaining)

        sc = Sched(nc, es)
        op = sc.op
        ctr = {"p": 0, "e": 0, "w": 0, "x": 0, "z": 0}

        def nxt(k, n):
            v = ctr[k]
            ctr[k] = (v + 1) % n
            return v

        def consts():
            op("pool", lambda e: e.memset(ones_f[:], 1.0), writes=["ones_f"])
            op("pool", lambda e: e.memset(ones_b[:], 1.0), writes=["ones_b"])
            op("pool", lambda e: e.affine_select(out=ident_b[:], in_=ones_b[:], pattern=[[-1, 128]], compare_op=ALU.is_equal,
                                                 fill=0.0, base=0, channel_multiplier=1), reads=["ones_b"], writes=["ident_b"])
            op("pool", lambda e: e.affine_select(out=tri_incl[:], in_=ones_b[:], pattern=[[-1, 128]], compare_op=ALU.is_ge,
                                                 fill=0.0, base=0, channel_multiplier=1), reads=["ones_b"], writes=["tri_incl"])
            op("pool", lambda e: e.affine_select(out=tri_low[:], in_=ones_b[:], pattern=[[1, 128]], compare_op=ALU.is_gt,
                                                 fill=0.0, base=0, channel_multiplier=-1), reads=["ones_b"], writes=["tri_low"])
            op("pool", lambda e: e.affine_select(out=tri_ui[:], in_=ones_f[:], pattern=[[1, 128]], compare_op=ALU.is_ge,
                                                 fill=0.0, base=0, channel_multiplier=-1), reads=["ones_f"], writes=["tri_ui"])
            op("pool", lambda e: e.affine_select(out=blkmask[:], in_=ones_f[:], pattern=[[1, 128]], compare_op=ALU.is_ge,
                                                 fill=0.0, base=0, channel_multiplier=-1), reads=["ones_f"], writes=["blkmask"])
            op("pool", lambda e: e.memset(blkmask[0:64, 64:128], 0.0), reads=["blkmask"], writes=["blkmask"])
            op("pool", lambda e: e.memset(negmask[:], 0.0), writes=["negmask"])
            op("pool", lambda e: e.affine_select(out=negmask[:], in_=negmask[:], pattern=[[1, 896]],
                                                 compare_op=ALU.is_gt, fill=NEG, base=-384, channel_multiplier=-1),
               reads=["negmask"], writes=["negmask"])
            op("pool", lambda e: e.memset(rmask[:], 1.0), writes=["rmask"])
            op("pool", lambda e: e.memset(rmask[:].rearrange("p (c t) -> p c t", t=64)[:, :, 0:1], 0.0), reads=["rmask"], writes=["rmask"])
            op("sp", lambda e: e.dma_start(out=vecs[:], in_=vecs_d[:, :]), writes=["vecs"], dma=True)
            b0 = voff["bgate"]
            op("dve", lambda e: e.tensor_scalar(out=vecs[:, b0:b0 + 2 * L], in0=vecs[:, b0:b0 + 2 * L], scalar1=-1.0, scalar2=None,
                                                op0=ALU.mult), reads=["vecs"], writes=["vecs"])

        def rstd_from_ss(ss_ap, ss_key, out_tile, out_key, inv_n, tmp_tile, tmp_key):
            op("act", lambda e: e.activation(out=tmp_tile, in_=ss_ap, func=AF.Ln, bias=EPS, scale=inv_n),
               reads=[ss_key], writes=[tmp_key])
            op("act", lambda e: e.activation(out=out_tile, in_=tmp_tile, func=AF.Exp, scale=-0.5),
               reads=[tmp_key], writes=[out_key])

        NSLAB = 6

        def slab(i):
            if i < 3:
                return [xt[i][:, 0, :], xt[i][:, 1, :]], [("xt", i)], xt[i]
            if i < 5:
                a_, b_ = 2 * (i - 3), 2 * (i - 3) + 1
                return [zs[a_][:], zs[b_][:]], [("zs", a_), ("zs", b_)], None
            return [et[0][:], et[1][:]], [("et", 0), ("et", 1)], None

        ctr["s"] = 0

        def norm_phase(src, srckey, gcol, ntok, dst_fn, final=False):
            nblk = max(1, ntok // TB)
            w = min(TB, ntok)
            srcv = src.rearrange("(c p) s -> p c s", p=128)
            yv = yT.rearrange("(c p) s -> p c s", p=128)
            lq = ["sp", "pool"]

            def load(c2, tb, t0):
                si_ = nxt("s", NSLAB)
                aps, keys, single = slab(si_)
                rk = [(srckey, 2 * c2, tb), (srckey, 2 * c2 + 1, tb)]
                if single is not None:
                    op(lq[c2 % 2], lambda e: e.dma_start(out=single[:, :, 0:w], in_=srcv[:, 2 * c2:2 * c2 + 2, t0:t0 + w]),
                       reads=rk, writes=keys, dma=True)
                else:
                    for j in range(2):
                        op(lq[(c2 + j) % 2], lambda e, j=j: e.dma_start(out=aps[j][:, 0:w], in_=srcv[:, 2 * c2 + j, t0:t0 + w]),
                           reads=[rk[j]], writes=[keys[j]], dma=True)
                return aps, keys, single

            for tb in range(nblk):
                t0 = tb * w
                ri = nxt("e", NE)
                for c2 in range(NCH // 2):
                    aps, keys, single = load(c2, tb, t0)
                    for j in range(2):
                        c = 2 * c2 + j
                        kj = keys if single is not None else [keys[j]]
                        if c == 0:
                            op("act", lambda e, ri=ri, ap=aps[j]: e.activation(out=e_o[ri][:, 0:w], in_=ap[:, 0:w], func=AF.Square),
                               reads=kj, writes=[("e_o", ri)])
                        else:
                            wi = nxt("w", NW)
                            op("act", lambda e, wi=wi, ap=aps[j]: e.activation(out=wk[wi][:, 0:w], in_=ap[:, 0:w], func=AF.Square),
                               reads=kj, writes=[("wk", wi)])
                            op("dve", lambda e, wi=wi, ri=ri: e.tensor_tensor(out=e_o[ri][:, 0:w], in0=e_o[ri][:, 0:w], in1=wk[wi][:, 0:w], op=ALU.add),
                               reads=[("wk", wi), ("e_o", ri)], writes=[("e_o", ri)])
                si = nxt("p", 2)
                op("pe", lambda e, ri=ri, si=si: e.matmul(P[si][:, 0:w], ones_f[:], e_o[ri][:, 0:w], start=True, stop=True),
                   reads=[("e_o", ri), "ones_f"], writes=[("P", si)])
                rstd_from_ss(P[si][:, 0:w], ("P", si), e_rs[ri][:, 0:w], ("e_rs", ri), 1.0 / D, e_sq[ri][:, 0:w], ("e_sq", ri))
                for c2 in range(NCH // 2):
                    aps, keys, single = load(c2, tb, t0)
                    for j in range(2):
                        c = 2 * c2 + j
                        kj = keys if single is not None else [keys[j]]
                        if not final:
                            dap, dkeys = dst_fn(c, t0, w, tb)
                            op("dve", lambda e, c=c, dap=dap, ri=ri, ap=aps[j]: e.scalar_tensor_tensor(
                                out=dap, in0=ap[:, 0:w], scalar=vecs[:, gcol + c:gcol + c + 1], in1=e_rs[ri][:, 0:w],
                                op0=ALU.mult, op1=ALU.mult), reads=kj + [("e_rs", ri), "vecs"], writes=dkeys)
                        else:
                            op("dve", lambda e, c=c, ri=ri, ap=aps[j]: e.scalar_tensor_tensor(
                                out=ap[:, 0:w], in0=ap[:, 0:w], scalar=vecs[:, gcol + c:gcol + c + 1], in1=e_rs[ri][:, 0:w],
                                op0=ALU.mult, op1=ALU.mult), reads=kj + [("e_rs", ri), "vecs"], writes=kj)
                    if final:
                        if single is not None:
                            tok = op("act", lambda e, c2=c2, t0=t0, single=single: e.dma_start(out=yv[:, 2 * c2:2 * c2 + 2, t0:t0 + w], in_=single[:, :, 0:w]),
                                     reads=keys, writes=[("yT", 2 * c2, tb), ("yT", 2 * c2 + 1, tb)], dma=True)
                            out_toks.append(tok)
                        else:
                            for j in range(2):
                                tok = op("act", lambda e, c2=c2, t0=t0, j=j, ap=aps[j]: e.dma_start(out=yv[:, 2 * c2 + j, t0:t0 + w], in_=ap[:, 0:w]),
                                         reads=[keys[j]], writes=[("yT", 2 * c2 + j, tb)], dma=True)
                                out_toks.append(tok)

        def h_dst(c, t0, w, tb):
            return H[:, c, t0:t0 + w], [("H", c, tb)]

        def hkeys(tb):
            return [("H", c, tb) for c in range(NCH)]

        def proj_fm(slot, col0, tb, ncols=128):
            pi = nxt("p", 2)

            def f(e):
                ins = None
                for c in range(NCH):
                    ins = e.matmul(P[pi][0:ncols, :], WS[slot][:, c, col0:col0 + ncols], H[:, c, tb * TB:(tb + 1) * TB],
                                   start=(c == 0), stop=(c == NCH - 1))
                return ins
            op("pe", f, reads=[("WS", slot)] + hkeys(tb), writes=[("P", pi)])
            return pi

        def proj_tm(slot, col0, ncols, tb, blocks):
            pi = nxt("p", 2)

            def f(e):
                ins = None
                for j, blk in enumerate(blocks):
                    for c in range(NCH):
                        ins = e.matmul(P[pi][:, j * ncols:(j + 1) * ncols], H[:, c, blk * 128:(blk + 1) * 128],
                                       WS[slot][:, c, col0:col0 + ncols], start=(c == 0), stop=(c == NCH - 1))
                return ins
            op("pe", f, reads=[("WS", slot)] + hkeys(tb), writes=[("P", pi)])
            return pi

        def load_ws(slot, dram_view):
            op("pool", lambda e: e.dma_start(out=WS[slot][:], in_=dram_view), writes=[("WS", slot)], dma=True)

        def win_view(l, s):
            return w_in[l].rearrange("(c p) n -> p c n", p=128)[:, :, s * 512:(s + 1) * 512]

        def epilogue_g(l, m, tb, src_ap, src_key, gslot, gcol, banks=None, ei=None, src_in_eo=False, gate_done=False):
            if ei is None:
                ei = nxt("e", NE)
            if banks is None:
                pi = nxt("p", 2)
                G, Gk = P[pi], ("P", pi)
                si = nxt("p", 2)
                SSt, SSk = P[si], ("P", si)
            else:
                G, Gk, SSt, SSk = banks
            gc = voff["onorm"] + 16 * l + m
            if not src_in_eo:
                op("act", lambda e: e.activation(out=e_o[ei][:], in_=src_ap, func=AF.Copy), reads=[src_key], writes=[("e_o", ei)])
                yield
            op("act", lambda e: e.activation(out=e_sq[ei][:], in_=e_o[ei][:], func=AF.Square), reads=[("e_o", ei)], writes=[("e_sq", ei)])
            yield
            op("pe", lambda e: e.matmul(SSt[:], ones_f[:], e_sq[ei][:], start=True, stop=True), reads=[("e_sq", ei), "ones_f"], writes=[SSk])
            yield

            def fg(e):
                ins = None
                for c in range(NCH):
                    ins = e.matmul(G[:], WS[gslot][:, c, gcol:gcol + 128], H[:, c, tb * TB:(tb + 1) * TB], start=(c == 0), stop=(c == NCH - 1))
                return ins
            if not gate_done:
                op("pe", fg, reads=[("WS", gslot)] + hkeys(tb), writes=[Gk])
                yield
            op("act", lambda e: e.activation(out=e_sq[ei][:], in_=SSt[:], func=AF.Ln, bias=EPS, scale=1.0 / 128), reads=[SSk], writes=[("e_sq", ei)])
            yield
            op("act", lambda e: e.activation(out=e_rs[ei][:], in_=e_sq[ei][:], func=AF.Exp, scale=-0.5), reads=[("e_sq", ei)], writes=[("e_rs", ei)])
            yield
            op("dve", lambda e: e.scalar_tensor_tensor(out=e_o[ei][:], in0=e_o[ei][:], scalar=vecs[:, gc:gc + 1], in1=e_rs[ei][:],
                                                       op0=ALU.mult, op1=ALU.mult), reads=[("e_o", ei), ("e_rs", ei), "vecs"], writes=[("e_o", ei)])
            yield
            op("act", lambda e: e.activation(out=e_g[ei][:], in_=G[:], func=AF.Exp, scale=-1.0), reads=[Gk], writes=[("e_g", ei)])
            yield
            op("act", lambda e: e.activation(out=e_g[ei][:], in_=e_g[ei][:], func=AF.Ln, bias=1.0), reads=[("e_g", ei)], writes=[("e_g", ei)])
            yield
            op("act", lambda e: e.activation(out=e_g[ei][:], in_=e_g[ei][:], func=AF.Exp, scale=-1.0), reads=[("e_g", ei)], writes=[("e_g", ei)])
            yield
            op("dve", lambda e: e.tensor_tensor(out=e_g[ei][:], in0=G[:], in1=e_g[ei][:], op=ALU.mult),
               reads=[Gk, ("e_g", ei)], writes=[("e_g", ei)])
            yield
            op("dve", lambda e: e.tensor_tensor(out=e_mg[ei][:], in0=e_o[ei][:], in1=e_g[ei][:], op=ALU.mult),
               reads=[("e_o", ei), ("e_g", ei)], writes=[("e_mg", ei)])
            yield
            op("sp", lambda e: e.dma_start(out=mgd[m][:, tb * TB:(tb + 1) * TB], in_=e_mg[ei][:]), reads=[("e_mg", ei)],
               writes=[("MG", m, tb)], dma=True)
            yield

        def drive(*gens):
            gens = list(gens)
            while gens:
                for g in list(gens):
                    try:
                        next(g)
                    except StopIteration:
                        gens.remove(g)

        def epilogue(*a, **k):
            drive(epilogue_g(*a, **k))

        def sb_proj_items(i, slot):
            bi = i % 2
            scale = 128.0 ** -0.5
            items = []

            def group(tb, kind):
                st = {}

                def piece(k):
                    def f():
                        if k == 0:
                            st["pi"] = nxt("p", 2)
                        pi = st["pi"]

                        def mm(e):
                            ins = None
                            if kind == "v":
                                blk = tb * 4 + k
                                for c in range(NCH):
                                    ins = e.matmul(P[pi][:, k * 128:(k + 1) * 128], H[:, c, blk * 128:(blk + 1) * 128],
                                                   WS[slot][:, c, 256:384], start=(c == 0), stop=(c == NCH - 1))
                            else:
                                col0 = 0 if kind == "q" else 128
                                for c in range(4 * k, 4 * k + 4):
                                    ins = e.matmul(P[pi][:], WS[slot][:, c, col0:col0 + 128], H[:, c, tb * TB:(tb + 1) * TB],
                                                   start=(c == 0), stop=(c == NCH - 1))
                            return ins
                        op("pe", mm, reads=[("WS", slot)] + hkeys(tb), writes=[("P", pi)])
                    return f

                def evac():
                    pi = st["pi"]
                    if kind == "q":
                        op("act", lambda e: e.activation(out=qT[bi][:, tb * TB:(tb + 1) * TB], in_=P[pi][:], func=AF.Copy, scale=scale),
                           reads=[("P", pi)], writes=[("qT", bi, tb)])
                    elif kind == "k":
                        op("dve", lambda e: e.tensor_copy(out=kT[bi][:, tb * TB:(tb + 1) * TB], in_=P[pi][:]),
                           reads=[("P", pi)], writes=[("kT", bi, tb)])
                    else:
                        op("dve", lambda e: e.tensor_copy(out=vtok[bi][:, tb * 4:(tb + 1) * 4, :], in_=P[pi][:].rearrange("p (j d) -> p j d", d=128)),
                           reads=[("P", pi)], writes=[("vtok", bi, tb)])
                return [piece(k) for k in range(4)] + [evac]
            for tb in range(NTB):
                for kind in ("k", "v", "q"):
                    items += group(tb, kind)
            return items

        def sb_attn(l, i, slot, bg):
            bi = i % 2

            def tile_ops(ch, kb, qb):
                zi = nxt("z", NZ)
                zb = zi % 2
                ei = zi % 2
                r = kb - 4 * qb
                first = (kb == 4 * qb + 3)
                last = (kb == 0)
                d = {}
                d["Z"] = lambda: op("pe", lambda e: e.matmul(Z[zb][:], kT[bi][:, kb * 128:(kb + 1) * 128], qT[bi][:, qb * TB:(qb + 1) * TB], start=True, stop=True),
                                    reads=[("kT", bi, kb // 4), ("qT", bi, qb)], writes=[("Z", zb)])
                if r >= 0:
                    d["COPY"] = lambda: op("dve", lambda e: e.tensor_tensor(out=zs[zi][:], in0=Z[zb][:], in1=negmask[:, 384 - 128 * r:896 - 128 * r], op=ALU.add),
                                           reads=[("Z", zb), "negmask"], writes=[("zs", zi)])
                else:
                    d["COPY"] = lambda: op("dve", lambda e: e.tensor_copy(out=zs[zi][:], in_=Z[zb][:]), reads=[("Z", zb)], writes=[("zs", zi)])
                d["EXPA"] = lambda: op("act", lambda e: e.activation(out=et[ei][:], in_=zs[zi][:], func=AF.Exp), reads=[("zs", zi)], writes=[("et", ei)])
                d["LN"] = lambda: op("act", lambda e: e.activation(out=spb[zi][:], in_=et[ei][:], func=AF.Ln, bias=1.0), reads=[("et", ei)], writes=[("spb", zi)])
                d["TRI"] = lambda: op("pe", lambda e: e.matmul(C[ch][:], tri_incl[:], spb[zi][:], start=first, stop=last),
                                      reads=[("spb", zi), "tri_incl"], writes=[("C", ch)])
                d["SUB"] = lambda: op("dve", lambda e: e.scalar_tensor_tensor(out=zs[zi][:], in0=C[ch][:], scalar=-1.0, in1=zs[zi][:], op0=ALU.mult, op1=ALU.add),
                                      reads=[("C", ch), ("zs", zi)], writes=[("zs", zi)])
                if not last:
                    d["LOW"] = lambda: op("pe", lambda e: e.matmul(C[ch][:], tri_low[:], spb[zi][:], start=False, stop=False),
                                          reads=[("spb", zi), "tri_low"], writes=[("C", ch)])
                else:
                    d["LOW"] = lambda: None
                d["EXPB"] = lambda: op("act", lambda e: e.activation(out=Ab[zi][:], in_=zs[zi][:], func=AF.Exp), reads=[("zs", zi)], writes=[("Ab", zi)])

                def av():
                    op("pe", lambda e: e.matmul(O[ch][:], vtok[bi][:, kb, :], Ab[zi][:], start=first, stop=last),
                       reads=[("Ab", zi), ("vtok", bi, kb // 4)], writes=[("O", ch)])
                    if last:
                        epilogue(l, i, qb, O[ch][:], ("O", ch), slot, 384, banks=(O[ch], ("O", ch), C[ch], ("C", ch)))
                d["AV"] = av
                return d

            nper = 0
            ta = [(0, kb, qb) for qb in (3, 0) for kb in range(4 * qb + 3, -1, -1)]
            tbl = [(1, kb, qb) for qb in (2, 1) for kb in range(4 * qb + 3, -1, -1)]
            T = []
            for x_, y_ in zip(ta, tbl):
                T += [x_, y_]
            n = len(T)
            ops_ = {}
            for p in range(n + 2):
                if p < n:
                    ops_[p] = tile_ops(*T[p])
                if p - 2 >= 0:
                    ops_[p - 2]["LOW"]()
                    ops_[p - 2]["EXPB"]()
                if p < n:
                    ops_[p]["Z"]()
                    ops_[p]["COPY"]()
                if p - 2 >= 0:
                    ops_[p - 2]["AV"]()
                if p < n:
                    ops_[p]["EXPA"]()
                    ops_[p]["LN"]()
                if 0 <= p - 1 < n:
                    ops_[p - 1]["TRI"]()
                    ops_[p - 1]["SUB"]()
                nper += 1
                for _ in range(2):
                    if bg:
                        bg.pop(0)()
            while bg:
                bg.pop(0)()

        def gla_pair(l, j, slotA, slotB, first_pair):
            gp = pairs[j]
            if first_pair:
                for tb in range(NTB):
                    pi = proj_fm(slotB, 256, tb, ncols=16)
                    op("act", lambda e, pi=pi, tb=tb: e.activation(out=rT[:, tb * TB:(tb + 1) * TB], in_=P[pi][0:16, :], func=AF.Copy),
                       reads=[("P", pi)], writes=[("rT", tb)])
            op("pool", lambda e: e.dma_start(out=wup[:], in_=wup_d[l][:, j * 128:(j + 1) * 128]), writes=["wup"], dma=True)
            op("dve", lambda e: e.memset(S32[:], 0.0), writes=["S32"])
            op("dve", lambda e: e.memset(Sbf[:], 0.0), writes=["Sbf"])
            bcol = voff["bgate"] + 2 * l + j
            for tb in range(NTB):
                tsl = slice(tb * TB, (tb + 1) * TB)
                pi = nxt("p", 2)
                op("pe", lambda e, pi=pi, tsl=tsl: e.matmul(P[pi][:], wup[:], rT[:, tsl], start=True, stop=True),
                   reads=["wup", ("rT", tb)], writes=[("P", pi)])
                op("act", lambda e, pi=pi: e.activation(out=g_sp[:], in_=P[pi][:], func=AF.Exp, scale=-1.0, bias=vecs[:, bcol:bcol + 1]),
                   reads=[("P", pi), "vecs"], writes=[K_sp])
                op("act", lambda e: e.activation(out=g_sp[:], in_=g_sp[:], func=AF.Ln, bias=1.0), reads=[K_sp], writes=[K_sp])
                op("dve", lambda e: e.tensor_tensor_scan(out=g_cum[:], data0=rmask[:], data1=g_sp[:], initial=0.0, op0=ALU.mult, op1=ALU.add),
                   reads=[K_sp, "rmask"], writes=[K_cum])
                op("act", lambda e: e.activation(out=g_eb[:], in_=g_cum[:], func=AF.Exp, scale=-1.0 / 16), reads=[K_cum], writes=[K_eb])
                op("act", lambda e: e.activation(out=g_ebi[:], in_=g_cum[:], func=AF.Exp, scale=1.0 / 16), reads=[K_cum], writes=[K_ebi])
                pi = proj_fm(slotA, 0, tb)
                op("dve", lambda e, pi=pi: e.scalar_tensor_tensor(out=g_q[:], in0=P[pi][:], scalar=0.125, in1=g_eb[:], op0=ALU.mult, op1=ALU.mult),
                   reads=[("P", pi), K_eb], writes=["g_q"])
                pi = proj_fm(slotA, 128, tb)
                op("dve", lambda e, pi=pi: e.tensor_tensor(out=g_k32[:], in0=P[pi][:], in1=g_ebi[:], op=ALU.mult),
                   reads=[("P", pi), K_ebi], writes=[K_k32])
                op("act", lambda e: e.activation(out=g_k[:], in_=g_k32[:], func=AF.Copy), reads=[K_k32], writes=["g_k"])
                dec_bc = g_eb[:].rearrange("p (c t) -> p c t", t=64)[:, :, 63:64].broadcast_to([128, 8, 64])
                op("dve", lambda e: e.tensor_tensor(out=g_kd[:].rearrange("p (c t) -> p c t", t=64), in0=g_k32[:].rearrange("p (c t) -> p c t", t=64),
                                                    in1=dec_bc, op=ALU.mult), reads=[K_k32, K_eb], writes=["g_kd"])

                def ftr(e):
                    ins = None
                    for j4 in range(4):
                        ins = e.transpose(Tb[:, j4 * 128:(j4 + 1) * 128], g_kd[:, j4 * 128:(j4 + 1) * 128], ident_b[:])
                    return ins
                op("pe", ftr, reads=["g_kd", "ident_b"], writes=[("O", 1)])
                op("dve", lambda e: e.tensor_copy(out=g_kdt[:], in_=Tb.rearrange("p (j d) -> p j d", d=128)), reads=[("O", 1)], writes=["g_kdt"])
                for half in range(2):
                    pi = proj_tm(slotA, 256, 256, tb, [tb * 4 + 2 * half, tb * 4 + 2 * half + 1])
                    op("act", lambda e, pi=pi, half=half: e.activation(out=g_v[:, 2 * half:2 * half + 2, :],
                                                                       in_=P[pi][:].rearrange("p (j d) -> p j d", d=256), func=AF.Copy),
                       reads=[("P", pi)], writes=["g_v"])
                OG = [O[0], C[0]]
                OGk = [("O", 0), ("C", 0)]
                for cp in range(4):
                    csl = slice(cp * 128, (cp + 1) * 128)
                    for h in range(2):
                        hs = slice(h * 64, (h + 1) * 64)
                        ai = (cp * 2 + h) % 2
                        op("pe", lambda e, hs=hs, csl=csl: e.matmul(Z[0][:, 0:128], g_k[hs, csl], g_q[hs, csl], start=True, stop=True),
                           reads=["g_k", "g_q"], writes=[("Z", 0)])
                        op("dve", lambda e, ai=ai: e.tensor_tensor(out=g_at[ai][:], in0=Z[0][:, 0:128], in1=blkmask[:], op=ALU.mult),
                           reads=[("Z", 0), "blkmask"], writes=[("g_at", ai)])
                        op("pe", lambda e, h=h, ai=ai, csl=csl, cp=cp: e.matmul(OG[h][:, csl], g_v[:, cp, h * 128:(h + 1) * 128], g_at[ai][:],
                                                                                 start=True, stop=False),
                           reads=["g_v", ("g_at", ai)], writes=[OGk[h]])
                    for c2 in range(2):
                        ch = cp * 2 + c2
                        tsl64 = slice(cp * 128 + c2 * 64, cp * 128 + c2 * 64 + 64)
                        psl = slice(c2 * 64, c2 * 64 + 64)

                        def finter(e, tsl64=tsl64, c2=c2):
                            ins = None
                            for h in range(2):
                                hs = slice(h * 64, (h + 1) * 64)
                                ins = e.matmul(OG[h][:, tsl64], Sbf[hs, :], g_q[hs, tsl64], start=False, stop=(c2 == 1))
                            return ins
                        op("pe", finter, reads=["Sbf", "g_q"], writes=[("O", 0), ("C", 0)])

                        def fkv(e, psl=psl, cp=cp):
                            ins = None
                            for h in range(2):
                                ins = e.matmul(Z[1][h * 64:(h + 1) * 64, 0:128], g_kdt[psl, cp, h * 64:(h + 1) * 64],
                                               g_v[psl, cp, h * 128:(h + 1) * 128], start=True, stop=True)
                            return ins
                        op("pe", fkv, reads=["g_kdt", "g_v"], writes=[("Z", 1)])
                        dcol = ch * 64 + 63
                        op("dve", lambda e, dcol=dcol: e.scalar_tensor_tensor(out=S32[:], in0=S32[:], scalar=g_eb[:, dcol:dcol + 1], in1=Z[1][:, 0:128],
                                                                              op0=ALU.mult, op1=ALU.add),
                           reads=["S32", K_eb, ("Z", 1)], writes=["S32"])
                        op("dve", lambda e: e.tensor_copy(out=Sbf[:], in_=S32[:]), reads=["S32"], writes=["Sbf"])
                drive(*[epilogue_g(l, NSB + 2 * j + h, tb, OG[h][:], OGk[h], slotB, h * 128, banks=(P[h], ("P", h), OG[h], OGk[h])) for h in range(2)])

        def gelu_g(pi, out_tile, out_key):
            a = nxt("w", NW)
            b = nxt("w", NW)
            op("dve", lambda e: e.tensor_copy(out=wk[a][:], in_=P[pi][:]), reads=[("P", pi)], writes=[("wk", a)])
            yield
            op("dve", lambda e: e.tensor_tensor(out=wk[b][:], in0=wk[a][:], in1=wk[a][:], op=ALU.mult), reads=[("wk", a)], writes=[("wk", b)])
            yield
            op("dve", lambda e: e.tensor_scalar(out=wk[b][:], in0=wk[b][:], scalar1=0.044715, scalar2=1.0, op0=ALU.mult, op1=ALU.add),
               reads=[("wk", b)], writes=[("wk", b)])
            yield
            op("dve", lambda e: e.tensor_tensor(out=wk[b][:], in0=wk[b][:], in1=wk[a][:], op=ALU.mult), reads=[("wk", a), ("wk", b)], writes=[("wk", b)])
            yield
            op("act", lambda e: e.activation(out=wk[b][:], in_=wk[b][:], func=AF.Exp, scale=-GELU_C), reads=[("wk", b)], writes=[("wk", b)])
            yield
            op("act", lambda e: e.activation(out=wk[b][:], in_=wk[b][:], func=AF.Ln, bias=1.0), reads=[("wk", b)], writes=[("wk", b)])
            yield
            op("act", lambda e: e.activation(out=wk[b][:], in_=wk[b][:], func=AF.Exp, scale=-1.0), reads=[("wk", b)], writes=[("wk", b)])
            yield
            op("dve", lambda e: e.tensor_tensor(out=out_tile, in0=wk[a][:], in1=wk[b][:], op=ALU.mult), reads=[("wk", a), ("wk", b)], writes=[out_key])
            yield

        def gelu_from_psum(pi, out_tile, out_key):
            drive(gelu_g(pi, out_tile, out_key))

        def sgu_unit(l, slotV, uslot, lgs):
            op("sp", lambda e: e.dma_start(out=gn_bc[:], in_=sgn_d[l].partition_broadcast(128)), writes=["gn_bc"], dma=True)
            for lg in lgs:
                op("sp", lambda e, lg=lg: e.dma_start(out=wsTf[:], in_=wsg_d[l][lg]), writes=["wsTf"], dma=True)
                op("dve", lambda e, lg=lg: e.tensor_tensor(out=wsT[lg][:], in0=wsTf[:], in1=tri_ui[:], op=ALU.mult),
                   reads=["wsTf", "tri_ui"], writes=[("wsT", lg)])
                op("sp", lambda e, lg=lg: e.dma_start(out=bs_bc[lg][:], in_=bsg_d[l][lg].partition_broadcast(128)), writes=[("bs_bc", lg)], dma=True)
            MX = [O[0], C[0]]
            MXk = [("O", 0), ("C", 0)]
            def gelu_ip_g(pi, hold):
                a = nxt("w", NW)
                b = nxt("w", NW)
                hold["a"] = a
                op("dve", lambda e: e.tensor_copy(out=wk[a][:], in_=P[pi][:]), reads=[("P", pi)], writes=[("wk", a)])
                yield
                op("dve", lambda e: e.tensor_tensor(out=wk[b][:], in0=wk[a][:], in1=wk[a][:], op=ALU.mult), reads=[("wk", a)], writes=[("wk", b)])
                yield
                op("dve", lambda e: e.tensor_scalar(out=wk[b][:], in0=wk[b][:], scalar1=0.044715, scalar2=1.0, op0=ALU.mult, op1=ALU.add),
                   reads=[("wk", b)], writes=[("wk", b)])
                yield
                op("dve", lambda e: e.tensor_tensor(out=wk[b][:], in0=wk[b][:], in1=wk[a][:], op=ALU.mult), reads=[("wk", a), ("wk", b)], writes=[("wk", b)])
                yield
                op("act", lambda e: e.activation(out=wk[b][:], in_=wk[b][:], func=AF.Exp, scale=-GELU_C), reads=[("wk", b)], writes=[("wk", b)])
                yield
                op("act", lambda e: e.activation(out=wk[b][:], in_=wk[b][:], func=AF.Ln, bias=1.0), reads=[("wk", b)], writes=[("wk", b)])
                yield
                op("act", lambda e: e.activation(out=wk[b][:], in_=wk[b][:], func=AF.Exp, scale=-1.0), reads=[("wk", b)], writes=[("wk", b)])
                yield
                op("dve", lambda e: e.tensor_tensor(out=wk[a][:], in0=wk[a][:], in1=wk[b][:], op=ALU.mult), reads=[("wk", a), ("wk", b)], writes=[("wk", a)])
                yield

            def vblock_g(tb, j4):
                blk = tb * 4 + j4
                pi = proj_tm(slotV, 0, 512, tb, [blk])
                yield
                hold = {}
                yield from gelu_ip_g(pi, hold)
                a = hold["a"]
                sscol = blk % 8
                vi = blk % 2
                op("dve", lambda e: e.memset(s_ss[:, sscol:sscol + 1], 0.0), writes=[("s_ss", sscol)])
                yield
                op("act", lambda e: e.activation(out=e_sq[0][:], in_=wk[a][:], func=AF.Square, accum_out=s_ss[:, sscol:sscol + 1]),
                   reads=[("wk", a), ("s_ss", sscol)], writes=[("e_sq", 0), ("s_ss", sscol)])
                yield
                op("act", lambda e: e.activation(out=s_ss[:, sscol:sscol + 1], in_=s_ss[:, sscol:sscol + 1], func=AF.Ln, bias=EPS, scale=1.0 / 512),
                   reads=[("s_ss", sscol)], writes=[("s_ss", sscol)])
                yield
                op("act", lambda e: e.activation(out=s_ss[:, sscol:sscol + 1], in_=s_ss[:, sscol:sscol + 1], func=AF.Exp, scale=-0.5),
                   reads=[("s_ss", sscol)], writes=[("s_ss", sscol)])
                yield
                op("dve", lambda e: e.scalar_tensor_tensor(out=s_vn[vi][:], in0=wk[a][:], scalar=s_ss[:, sscol:sscol + 1],
                                                           in1=gn_bc[:], op0=ALU.mult, op1=ALU.mult),
                   reads=[("wk", a), ("s_ss", sscol), "gn_bc"], writes=[("Ab", 3 + vi)])
                yield

                def fmx(e):
                    ins = None
                    for k, lg in enumerate(lgs):
                        ins = e.matmul(MX[k][:, j4 * 128:(j4 + 1) * 128], s_vn[vi][:, lg * 128:(lg + 1) * 128], wsT[lg][:], start=True, stop=True)
                    return ins
                op("pe", fmx, reads=[("Ab", 3 + vi)] + [("wsT", lg) for lg in lgs], writes=MXk[:len(lgs)])
                yield

            UB = [O[1], C[1]]
            UBk = [("O", 1), ("C", 1)]
            GB = [Z[0], Z[1]]
            GBk = [("Z", 0), ("Z", 1)]

            def early(tb):
                for k, lg in enumerate(lgs):
                    for (bank, bkey, col0) in ((UB[k], UBk[k], k * 256), (GB[k], GBk[k], k * 256 + 128)):
                        def f(e, bank=bank, col0=col0):
                            ins = None
                            for c in range(NCH):
                                ins = e.matmul(bank[:], WS[uslot][:, c, col0:col0 + 128], H[:, c, tb * TB:(tb + 1) * TB], start=(c == 0), stop=(c == NCH - 1))
                            return ins
                        op("pe", f, reads=[("WS", uslot)] + hkeys(tb), writes=[bkey])

            def upart_g(tb, k, lg):
                ei = nxt("e", NE)
                X, Xk = e_g[ei], ("e_g", ei)
                Sg, Sk = e_rs[ei], ("e_rs", ei)
                op("dve", lambda e: e.tensor_copy(out=X[:], in_=UB[k][:]), reads=[UBk[k]], writes=[Xk])
                yield
                op("dve", lambda e: e.tensor_tensor(out=Sg[:], in0=X[:], in1=X[:], op=ALU.mult), reads=[Xk], writes=[Sk])
                yield
                op("dve", lambda e: e.tensor_scalar(out=Sg[:], in0=Sg[:], scalar1=0.044715, scalar2=1.0, op0=ALU.mult, op1=ALU.add), reads=[Sk], writes=[Sk])
                yield
                op("dve", lambda e: e.tensor_tensor(out=Sg[:], in0=Sg[:], in1=X[:], op=ALU.mult), reads=[Xk, Sk], writes=[Sk])
                yield
                op("act", lambda e: e.activation(out=Sg[:], in_=Sg[:], func=AF.Exp, scale=-GELU_C), reads=[Sk], writes=[Sk])
                yield
                op("act", lambda e: e.activation(out=Sg[:], in_=Sg[:], func=AF.Ln, bias=1.0), reads=[Sk], writes=[Sk])
                yield
                op("act", lambda e: e.activation(out=Sg[:], in_=Sg[:], func=AF.Exp, scale=-1.0), reads=[Sk], writes=[Sk])
                yield
                op("dve", lambda e: e.tensor_tensor(out=X[:], in0=X[:], in1=Sg[:], op=ALU.mult), reads=[Xk, Sk], writes=[Xk])
                yield
                op("dve", lambda e: e.tensor_tensor(out=e_o[ei][:].rearrange("p (j t) -> p j t", t=128),
                                                    in0=MX[k][:].rearrange("p (j t) -> p j t", t=128),
                                                    in1=bs_bc[lg][:].unsqueeze(1).broadcast_to([128, 4, 128]), op=ALU.add),
                   reads=[MXk[k], ("bs_bc", lg)], writes=[("e_o", ei)])
                yield
                op("dve", lambda e: e.tensor_tensor(out=e_o[ei][:], in0=e_o[ei][:], in1=X[:], op=ALU.mult), reads=[("e_o", ei), Xk], writes=[("e_o", ei)])
                yield
                yield from epilogue_g(l, NSB + 2 * NP + lg, tb, None, None, uslot, k * 256 + 128,
                                      banks=(GB[k], GBk[k], MX[k], MXk[k]), ei=ei, src_in_eo=True, gate_done=True)

            def adv(pair, nsteps):
                for _ in range(nsteps):
                    for g in pair:
                        next(g, None)

            allp = [(vblock_g(tb, 2 * hp), vblock_g(tb, 2 * hp + 1)) for tb in range(NTB) for hp in range(2)]
            early(0)
            adv(allp[0], 2)
            for kk in range(len(allp)):
                if kk + 1 < len(allp):
                    adv(allp[kk + 1], 1)
                drive(*allp[kk])
                if kk + 1 < len(allp):
                    adv(allp[kk + 1], 1)
                if kk % 2 == 1:
                    tb = kk // 2
                    drive(*[upart_g(tb, k, lg) for k, lg in enumerate(lgs)])
                    if tb + 1 < NTB:
                        early(tb + 1)

        def mixer(l):
            s = 0
            units = []
            for i in range(NSB):
                units.append(("sb", i, [s]))
                s += 1
            for j in range(NP):
                units.append(("gla", j, [s, s + 1]))
                s += 2
            units.append(("sgu", 0, list(range(s, s + 1 + NG // 2))))
            wsn = {"n": 0}

            def alloc(k):
                r = []
                for _ in range(k):
                    r.append(wsn["n"] % 2)
                    wsn["n"] += 1
                return r
            def exchange(k):
                grp = ccg[k]
                n = len(grp)
                src = mgd[grp[0]:grp[0] + n].rearrange("h p s -> (h p) s")
                op("pool", lambda e: e.collective_compute("AllGather", ALU.bypass, replica_groups=rgroups, ins=[src.opt()], outs=[mga[k].opt()]),
                   reads=[("MG", li, tb) for li in grp for tb in range(NTB)], writes=[("MGA", k)], dma="cc")

            for ui, (kind, idx, dsl) in enumerate(units):
                if kind == "sb":
                    if idx == 0:
                        sbws = [alloc(1)[0] for _ in range(NSB)]
                        load_ws(sbws[0], win_view(l, dsl[0]))
                        for f in sb_proj_items(0, sbws[0]):
                            f()
                    bg = []
                    if idx + 1 < NSB:
                        load_ws(sbws[idx + 1], win_view(l, dsl[0] + 1))
                        bg = sb_proj_items(idx + 1, sbws[idx + 1])
                    sb_attn(l, idx, sbws[idx], bg)
                elif kind == "gla":
                    ws = alloc(2)
                    load_ws(ws[0], win_view(l, dsl[0]))
                    load_ws(ws[1], win_view(l, dsl[1]))
                    if split and idx == 0:
                        exchange(0)
                    gla_pair(l, idx, ws[0], ws[1], idx == 0)
                else:
                    for k in range(NG // 2):
                        ws = alloc(2)
                        load_ws(ws[0], win_view(l, dsl[0]))
                        load_ws(ws[1], win_view(l, dsl[1 + k]))
                        if split and k == 0:
                            exchange(1)
                        sgu_unit(l, ws[0], ws[1], [2 * k, 2 * k + 1])
                    if split:
                        exchange(2)

        def out_proj(l, xsrc, skey, xdst, dkey):
            if not split:
                for m in range(16):
                    op("sp", lambda e, m=m: e.dma_start(out=H[:, m, :], in_=mgd[m]), reads=[("MG", m, tb) for tb in range(NTB)],
                       writes=[("H", m, tb) for tb in range(NTB)], dma=True)
            else:
                jj = 0
                for k, grp in enumerate(ccg):
                    for r in range(2):
                        for idx in range(len(grp)):
                            row = (r * len(grp) + idx) * 128
                            op("sp", lambda e, jj=jj, k=k, row=row: e.dma_start(out=H[:, jj, :], in_=mga[k][row:row + 128, :]),
                               reads=[("MGA", k)], writes=[("H", jj, tb) for tb in range(NTB)], dma=True)
                            jj += 1
            if stop == "mixload":
                for c in range(NCH):
                    out_toks.append(op("sp", lambda e, c=c: e.dma_start(out=hdbg[c], in_=H[:, c, :]), reads=[("H", c, tb) for tb in range(NTB)],
                                       writes=[("hdbg", c)], dma=True))
                return
            wv = w_out[l].rearrange("(c p) n -> p c n", p=128)
            for cs in range(4):
                slot = cs % 2
                load_ws(slot, wv[:, :, cs * 512:(cs + 1) * 512])
                for tb in range(NTB):
                    for dcp in range(2):
                        xi = nxt("x", NX)
                        d0 = cs * 4 + 2 * dcp
                        sv = xsrc.rearrange("(c p) s -> p c s", p=128)
                        dv = xdst.rearrange("(c p) s -> p c s", p=128)
                        op("sp", lambda e, xi=xi, d0=d0, tb=tb, sv=sv: e.dma_start(out=xt[xi][:], in_=sv[:, d0:d0 + 2, tb * TB:(tb + 1) * TB]),
                           reads=[(skey, d0, tb), (skey, d0 + 1, tb)], writes=[("xt", xi)], dma=True)
                        for j in range(2):
                            pi = proj_fm(slot, (2 * dcp + j) * 128, tb)
                            op("dve", lambda e, xi=xi, pi=pi, j=j: e.tensor_tensor(out=xt[xi][:, j, :], in0=P[pi][:], in1=xt[xi][:, j, :], op=ALU.add),
                               reads=[("P", pi), ("xt", xi)], writes=[("xt", xi)])
                        op("act", lambda e, xi=xi, d0=d0, tb=tb, dv=dv: e.dma_start(out=dv[:, d0:d0 + 2, tb * TB:(tb + 1) * TB], in_=xt[xi][:]),
                           reads=[("xt", xi)], writes=[(dkey, d0, tb), (dkey, d0 + 1, tb)], dma=True)

        def xattn(l, xsrc, skey, xdst, dkey):

            def mem_dst(c, t0, w, tb):
                return H[:, c, 0:NM], [("H", c, 0)]
            norm_phase(memT, "memT", voff["nmem"] + 16 * l, NM, mem_dst)
            kvv = w_xkv[l].rearrange("(c p) n -> p c n", p=128)
            load_ws(0, kvv[:, :, 0:512])
            load_ws(1, kvv[:, :, 512:1024])
            mkeys = [("H", c, 0) for c in range(NCH)]
            for h in range(4):
                pi = nxt("p", 2)

                def fk(e, pi=pi, h=h):
                    ins = None
                    for c in range(NCH):
                        ins = e.matmul(P[pi][:, 0:NM], WS[0][:, c, h * 128:(h + 1) * 128], H[:, c, 0:NM], start=(c == 0), stop=(c == NCH - 1))
                    return ins
                op("pe", fk, reads=[("WS", 0)] + mkeys, writes=[("P", pi)])
                op("act", lambda e, pi=pi, h=h: e.activation(out=kxT[:, h, :], in_=P[pi][:, 0:NM], func=AF.Copy), reads=[("P", pi)], writes=K_kxT)
            for mb in range(2):
                pi = nxt("p", 2)

                def fv(e, pi=pi, mb=mb):
                    ins = None
                    for c in range(NCH):
                        ins = e.matmul(P[pi][:], H[:, c, mb * 128:(mb + 1) * 128], WS[1][:, c, :], start=(c == 0), stop=(c == NCH - 1))
                    return ins
                op("pe", fv, reads=[("WS", 1)] + mkeys, writes=[("P", pi)])
                op("act", lambda e, pi=pi, mb=mb: e.activation(out=vx[:, mb, :], in_=P[pi][:], func=AF.Copy), reads=[("P", pi)], writes=K_vx)
            norm_phase(xsrc, skey, voff["nxa"] + 16 * l, S, h_dst)
            load_ws(0, w_xq[l].rearrange("(c p) n -> p c n", p=128))
            WO = WS[1][:].rearrange("p c n -> p (c n)").rearrange("p (h n) -> p h n", h=4)
            op("pool", lambda e: e.dma_start(out=WO, in_=w_xo[l].rearrange("(h p) n -> p h n", p=128)), writes=[("WS", 1)], dma=True)
            scale = 128.0 ** -0.5
            sv = xsrc.rearrange("(c p) s -> p c s", p=128)
            dv = xdst.rearrange("(c p) s -> p c s", p=128)
            steps = [(tb, h) for tb in range(NTB) for h in range(4)]
            st = {}

            def stA(s_):
                tb, h = steps[s_]
                pi = proj_fm(0, h * 128, tb)
                qi = s_ % 2
                op("act", lambda e: e.activation(out=qx[qi][:], in_=P[pi][:], func=AF.Copy, scale=scale), reads=[("P", pi)], writes=[K_qx[qi]])

            def stB(s_):
                tb, h = steps[s_]
                qi = s_ % 2
                for mb in range(2):
                    pti = (s_ % 2) * 2 + mb
                    op("pe", lambda e, mb=mb: e.matmul(Z[mb][:], kxT[:, h, mb * 128:(mb + 1) * 128], qx[qi][:], start=True, stop=True),
                       reads=K_kxT + [K_qx[qi]], writes=[("Z", mb)])
                    op("act", lambda e, mb=mb, pti=pti: e.activation(out=pT[pti][:], in_=Z[mb][:], func=AF.Exp), reads=[("Z", mb)], writes=[K_pT[pti]])

            def stC(s_):
                tb, h = steps[s_]
                b_ = s_ % 2
                pk = [K_pT[b_ * 2], K_pT[b_ * 2 + 1]]

                def fden(e):
                    ins = None
                    for mb in range(2):
                        ins = e.matmul(C[b_][:], ones_b[:], pT[b_ * 2 + mb][:], start=(mb == 0), stop=(mb == 1))
                    return ins
                op("pe", fden, reads=["ones_b"] + pk, writes=[("C", b_)])

                def fnum(e):
                    ins = None
                    for mb in range(2):
                        ins = e.matmul(O[b_][:], vx[:, mb, h * 128:(h + 1) * 128], pT[b_ * 2 + mb][:], start=(mb == 0), stop=(mb == 1))
                    return ins
                op("pe", fnum, reads=K_vx + pk, writes=[("O", b_)])
                a_ = nxt("w", NW)
                st[s_] = a_
                op("act", lambda e: e.activation(out=wk[a_][:], in_=C[b_][:], func=AF.Ln), reads=[("C", b_)], writes=[("wk", a_)])
                op("act", lambda e: e.activation(out=wk[a_][:], in_=wk[a_][:], func=AF.Exp, scale=-1.0), reads=[("wk", a_)], writes=[("wk", a_)])

            def stD(s_):
                tb, h = steps[s_]
                b_ = s_ % 2
                a_ = st[s_]
                op("dve", lambda e: e.tensor_tensor(out=oxT[:, h, :], in0=O[b_][:], in1=wk[a_][:], op=ALU.mult),
                   reads=[("O", b_), ("wk", a_)], writes=[("kT", 0, h)])
                if h == 3:
                    stE(tb)

            def stE(tb):
                for d2 in range(NCH // 2):
                    xi = nxt("x", NX)
                    d0 = 2 * d2
                    op("sp", lambda e, xi=xi, d0=d0: e.dma_start(out=xt[xi][:], in_=sv[:, d0:d0 + 2, tb * TB:(tb + 1) * TB]),
                       reads=[(skey, d0, tb), (skey, d0 + 1, tb)], writes=[("xt", xi)], dma=True)
                    for j in range(2):
                        dch = d0 + j
                        pi = nxt("p", 2)

                        def fo(e, pi=pi, dch=dch):
                            ins = None
                            for h in range(4):
                                ins = e.matmul(P[pi][:], WO[:, h, dch * 128:(dch + 1) * 128], oxT[:, h, :], start=(h == 0), stop=(h == 3))
                            return ins
                        op("pe", fo, reads=[("WS", 1)] + [("kT", 0, h) for h in range(4)], writes=[("P", pi)])
                        op("dve", lambda e, xi=xi, pi=pi, j=j: e.tensor_tensor(out=xt[xi][:, j, :], in0=P[pi][:], in1=xt[xi][:, j, :], op=ALU.add),
                           reads=[("P", pi), ("xt", xi)], writes=[("xt", xi)])
                    op("act", lambda e, xi=xi, d0=d0: e.dma_start(out=dv[:, d0:d0 + 2, tb * TB:(tb + 1) * TB], in_=xt[xi][:]),
                       reads=[("xt", xi)], writes=[(dkey, d0, tb), (dkey, d0 + 1, tb)], dma=True)

            ns = len(steps)
            for s_ in range(ns + 3):
                if s_ < ns:
                    stA(s_)
                if 0 <= s_ - 1 < ns:
                    stB(s_ - 1)
                if 0 <= s_ - 2 < ns:
                    stC(s_ - 2)
                if 0 <= s_ - 3 < ns:
                    stD(s_ - 3)

        out_toks = []
        consts()
        cur, ckey = xT, "xT"
        done = False
        for l in range(nlayers):
            norm_phase(cur, ckey, voff["nmix"] + 16 * l, S, h_dst)
            if stop == "norm1":
                for c in range(NCH):
                    out_toks.append(op("sp", lambda e, c=c: e.dma_start(out=hdbg[c], in_=H[:, c, :]), reads=[("H", c, tb) for tb in range(NTB)],
                                       writes=[("hdbg", c)], dma=True))
                done = True
                break
            mixer(l)
            if stop == "mixer":
                done = True
                break
            out_proj(l, cur, ckey, xA, "xA")
            if stop in ("outproj", "mixload"):
                done = True
                break
            xattn(l, xA, "xA", xB, "xB")
            cur, ckey = xB, "xB"
        if not done:
            norm_phase(cur, ckey, voff["fin"], S, None, final=True)
        for q in Sched.QUEUES:
            for i in range(Sched.NSLOT):
                g = sc.dgen[q][i]
                if g > 0:
                    out_toks.append((("d", q, i), 16 * g))
        sc.final_wait("sp", out_toks)
        with nc.Block() as block:
            sc.emit(block)
    return nc, sc


_CACHE = {}


def kernel(**inputs):
    maps = pack_inputs(inputs, SPLIT)
    if "nc" not in _CACHE:
        _CACHE["nc"] = build_program(SPLIT)[0]
    nc = _CACHE["nc"]
    res = run_bass_kernel_spmd(nc, maps, core_ids=list(range(N_CORES)))
    out = np.empty((4, S, D), np.float32)
    for b in range(4):
        out[b] = np.asarray(res.results[2 * b]["yT"]).T
    return out
```

```python
import numpy as np
from contextlib import ExitStack
import concourse.bass as bass
import concourse.mybir as mybir
from concourse.bass_utils import run_bass_kernel_spmd

F32 = mybir.dt.float32
BF16 = mybir.dt.bfloat16
AF = mybir.ActivationFunctionType
ALU = mybir.AluOpType

D = 2048
S = 2048
L = 4
NM = 256
NCH = 16
TB = 512
NTB = 4
EPS = 1e-6
NEG = -30000.0
GELU_C = 1.5957691216057308

SPLIT = True
N_CORES = 8


def core_units(hf, split):
    if split:
        return list(range(4 * hf, 4 * hf + 4)), [hf], [2 * hf, 2 * hf + 1]
    return list(range(8)), [0, 1], [0, 1, 2, 3]


def col_slots(hf, split):
    sbh, pairs, groups = core_units(hf, split)
    slots = []
    for h in sbh:
        slots.append([(128 * h, 128), (1024 + 128 * h, 128), (2048 + 128 * h, 128), (3072 + 128 * h, 128)])
    for p in pairs:
        slots.append([(4096 + 128 * p, 128), (4352 + 128 * p, 128), (4608 + 256 * p, 256)])
        slots.append([(5136 + 256 * p, 256), (5120, 16), (None, 240)])
    gord = list(groups) + [g for g in range(4) if g not in groups]
    slots.append([(6160 + 128 * g, 128) for g in gord])
    for i in range(0, len(groups), 2):
        g0, g1 = groups[i], groups[i + 1]
        slots.append([(5648 + 128 * g0, 128), (6672 + 128 * g0, 128), (5648 + 128 * g1, 128), (6672 + 128 * g1, 128)])
    return slots


def local_heads(hf, split):
    sbh, pairs, groups = core_units(hf, split)
    return list(sbh) + [8 + 2 * p + h for p in pairs for h in range(2)] + [12 + g for g in groups]


def cc_groups(split):
    return [[0, 1, 2, 3], [4, 5], [6, 7]] if split else []


def chunk_order(split):
    if not split:
        return list(range(16))
    order = []
    for grp in cc_groups(split):
        for r in range(2):
            lh = local_heads(r, split)
            order += [lh[li] for li in grp]
    return order


def vec_layout(split):
    off = {}
    n = 0
    for nm in ("nmix", "nxa", "nmem"):
        off[nm] = n
        n += L * 16
    off["fin"] = n
    n += 16
    off["onorm"] = n
    n += L * 16
    off["bgate"] = n
    n += L * 2
    return off, n


def pack_inputs(inputs, split, nlw=L, ncores=N_CORES, lite=False):
    f = np.float32
    x = np.asarray(inputs["x"], f)
    mem = np.asarray(inputs["mem"], f)
    w_in = np.asarray(inputs["w_in"], f)[:nlw]
    voff, nv = vec_layout(split)
    per_half = {}
    for hf in (0, 1):
        sbh, pairs, groups = core_units(hf, split)
        slots = col_slots(hf, split)
        ncol = 512 * len(slots)
        wl = np.zeros((nlw, D, ncol), f)
        c = 0
        for sl in slots:
            for (st, w) in sl:
                if st is not None:
                    wl[:, :, c:c + w] = w_in[:, :, st:st + w]
                c += w
        vec = np.zeros((128, nv), f)
        for l in range(L):
            vec[:, voff["nmix"] + 16 * l: voff["nmix"] + 16 * l + 16] = np.asarray(inputs["norm_mix"], f)[l].reshape(16, 128).T
            vec[:, voff["nxa"] + 16 * l: voff["nxa"] + 16 * l + 16] = np.asarray(inputs["norm_xattn"], f)[l].reshape(16, 128).T
            vec[:, voff["nmem"] + 16 * l: voff["nmem"] + 16 * l + 16] = np.asarray(inputs["norm_mem"], f)[l].reshape(16, 128).T
            on = np.asarray(inputs["out_norm"], f)[l].reshape(16, 128)
            for li, m in enumerate(local_heads(hf, split)):
                vec[:, voff["onorm"] + 16 * l + li] = on[m]
            bg = np.asarray(inputs["b_gla_gate"], f)[l].reshape(2, 128)
            for j, p in enumerate(pairs):
                vec[:, voff["bgate"] + 2 * l + j] = bg[p]
        vec[:, voff["fin"]: voff["fin"] + 16] = np.asarray(inputs["final_norm"], f).reshape(16, 128).T
        wup = np.asarray(inputs["w_gla_gate_up"], f).reshape(L, 16, 2, 128)[:nlw, :, pairs, :].reshape(nlw, 16, 128 * len(pairs))
        wsg = np.ascontiguousarray(np.asarray(inputs["w_sgu"], f)[:nlw, groups].transpose(0, 1, 3, 2))
        bsg = np.ascontiguousarray(np.asarray(inputs["b_sgu"], f)[:nlw, groups])
        gord = list(groups) + [g for g in range(4) if g not in groups]
        sgn = np.ascontiguousarray(np.asarray(inputs["sgu_norm"], f)[:nlw].reshape(nlw, 4, 128)[:, gord].reshape(nlw, 512))
        per_half[hf] = dict(w_in=np.ascontiguousarray(wl), vecs=vec, wup=np.ascontiguousarray(wup), wsg=wsg, bsg=bsg, sgn=sgn)
    shared = dict(
        w_out=np.ascontiguousarray(np.asarray(inputs["w_out"], f)[:nlw].reshape(nlw, 16, 128, D)[:, chunk_order(split)].reshape(nlw, D, D)),
        w_xq=np.ascontiguousarray(np.asarray(inputs["w_xq"], f)[:nlw]),
        w_xkv=np.ascontiguousarray(np.asarray(inputs["w_xkv"], f)[:nlw]),
        w_xo=np.ascontiguousarray(np.asarray(inputs["w_xo"], f)[:nlw]),
    )
    if lite:
        for k in ("w_out", "w_xq", "w_xkv", "w_xo"):
            shared[k] = np.zeros((1, 128, 128), f)
    maps = []
    for c in range(ncores):
        b, hf = c // 2, (c % 2 if split else 0)
        m = dict(xT=np.ascontiguousarray(x[b].T), memT=np.ascontiguousarray(mem[b].T))
        m.update(per_half[hf])
        m.update(shared)
        maps.append(m)
    return maps


class Sched:
    COMPUTE = ("pe", "act", "dve", "pool")
    QUEUES = ("sp", "pool", "act")
    NSLOT = 6

    def __init__(self, nc, es):
        self.nc = nc
        self.streams = {e: [] for e in ("pe", "act", "dve", "pool", "sp")}
        self.sems = {}
        for e in self.COMPUTE:
            self.sems[("c", e)] = es.enter_context(nc.semaphore("c_" + e))
        for q in self.QUEUES:
            for i in range(self.NSLOT):
                self.sems[("d", q, i)] = es.enter_context(nc.semaphore("d_%s%d" % (q, i)))
        self.NCC = 4
        for i in range(self.NCC):
            self.sems[("k", i)] = es.enter_context(nc.semaphore("k_%d" % i))
        self.kgen = [0] * self.NCC
        self.knext = 0
        self.ccount = {e: 0 for e in self.COMPUTE}
        self.dgen = {q: [0] * self.NSLOT for q in self.QUEUES}
        self.dnext = {q: 0 for q in self.QUEUES}
        self.waited = {e: {} for e in self.streams}
        self.lastw = {}
        self.readers = {}
        self.nops = 0

    def _need(self, eng, tok, waits):
        sid, val = tok
        if self.waited[eng].get(sid, 0) < val:
            self.waited[eng][sid] = val
            waits.append((sid, val))

    def op(self, eng, fn, reads=(), writes=(), dma=False):
        waits = []
        deps = []
        for k in reads:
            t = self.lastw.get(k)
            if t is not None:
                deps.append(t)
        for k in writes:
            t = self.lastw.get(k)
            if t is not None:
                deps.append(t)
            deps.extend(self.readers.get(k, {}).values())
        for t in deps:
            if (not dma) and eng == "pe" and t[0] == ("c", "pe"):
                continue
            self._need(eng, t, waits)
        if dma == "cc":
            slot = self.knext
            self.knext = (slot + 1) % self.NCC
            sid = ("k", slot)
            if self.kgen[slot] > 0:
                self._need(eng, (sid, self.kgen[slot]), waits)
            self.kgen[slot] += 1
            tok = (sid, self.kgen[slot])
            inc = 1
        elif dma:
            slot = self.dnext[eng]
            self.dnext[eng] = (slot + 1) % self.NSLOT
            prev = self.dgen[eng][slot]
            sid = ("d", eng, slot)
            if prev > 0:
                self._need(eng, (sid, 16 * prev), waits)
            self.dgen[eng][slot] += 1
            tok = (sid, 16 * self.dgen[eng][slot])
            inc = 16
        else:
            self.ccount[eng] += 1
            sid = ("c", eng)
            tok = (sid, self.ccount[eng])
            inc = 1
        self.streams[eng].append((waits, fn, sid, inc))
        for k in reads:
            self.readers.setdefault(k, {})[sid] = tok
        for k in writes:
            self.lastw[k] = tok
            self.readers[k] = {}
        self.nops += 1
        return tok

    def final_wait(self, eng, toks):
        waits = []
        for t in toks:
            self._need(eng, t, waits)
        self.streams[eng].append((waits, None, None, 0))

    def emit(self, block):
        def mk(name):
            def f(eng):
                for waits, fn, sid, inc in self.streams[name]:
                    for (ws, val) in waits:
                        eng.wait_ge(self.sems[ws], val)
                    if fn is not None:
                        ins = fn(eng)
                        ins.then_inc(self.sems[sid], inc)
            return f
        block.tensor(mk("pe"))
        block.scalar(mk("act"))
        block.vector(mk("dve"))
        block.gpsimd(mk("pool"))
        block.sync(mk("sp"))


def build_program(split=SPLIT, nlayers=L, stop=None, debug=False, lite=False, ncores=N_CORES):
    LW = nlayers
    nc = bass.Bass("TRN2", target_bir_lowering=False)
    sbh, pairs, groups = core_units(0, split)
    NSB, NP, NG = len(sbh), len(pairs), len(groups)
    nslots = NSB + 2 * NP + 1 + NG // 2
    voff, nv = vec_layout(split)
    dk = "ExternalOutput" if debug else "Internal"

    xT = nc.dram_tensor("xT", [D, S], F32, kind="ExternalInput").ap()
    memT = nc.dram_tensor("memT", [D, NM], F32, kind="ExternalInput").ap()
    w_in = nc.dram_tensor("w_in", [LW, D, nslots * 512], F32, kind="ExternalInput").ap()
    vecs_d = nc.dram_tensor("vecs", [128, nv], F32, kind="ExternalInput").ap()
    wup_d = nc.dram_tensor("wup", [LW, 16, 128 * NP], F32, kind="ExternalInput").ap()
    wsg_d = nc.dram_tensor("wsg", [LW, NG, 128, 128], F32, kind="ExternalInput").ap()
    bsg_d = nc.dram_tensor("bsg", [LW, NG, 128], F32, kind="ExternalInput").ap()
    if lite:
        w_out = nc.dram_tensor("w_out", [1, 128, 128], F32, kind="ExternalInput").ap()
        w_xq = nc.dram_tensor("w_xq", [1, 128, 128], F32, kind="ExternalInput").ap()
        w_xkv = nc.dram_tensor("w_xkv", [1, 128, 128], F32, kind="ExternalInput").ap()
        w_xo = nc.dram_tensor("w_xo", [1, 128, 128], F32, kind="ExternalInput").ap()
    else:
        w_out = nc.dram_tensor("w_out", [LW, D, D], F32, kind="ExternalInput").ap()
        w_xq = nc.dram_tensor("w_xq", [LW, D, 512], F32, kind="ExternalInput").ap()
        w_xkv = nc.dram_tensor("w_xkv", [LW, D, 1024], F32, kind="ExternalInput").ap()
        w_xo = nc.dram_tensor("w_xo", [LW, 512, D], F32, kind="ExternalInput").ap()
    sgn_d = nc.dram_tensor("sgn", [LW, 512], F32, kind="ExternalInput").ap()
    yT = nc.dram_tensor("yT", [D, S], F32, kind="ExternalOutput").ap()
    xA = nc.dram_tensor("xA", [D, S], F32, kind=dk).ap()
    xB = nc.dram_tensor("xB", [D, S], F32, kind=dk).ap()
    NLH = NSB + 2 * NP + NG
    ccg = cc_groups(split)
    mgd = nc.dram_tensor("mgd", [NLH, 128, S], BF16, kind=("Internal" if split else dk)).ap()
    mga = [nc.dram_tensor("mga%d" % k, [2 * len(g) * 128, S], BF16).ap() for k, g in enumerate(ccg)]
    npairs_cc = ncores // 2
    rgroups = [[2 * i, 2 * i + 1] for i in range(npairs_cc)]
    hdbg = nc.dram_tensor("hdbg", [16, 128, S], BF16, kind=dk).ap() if debug else None

    def xv(ap):
        return ap.rearrange("(c p) s -> c p s", p=128)

    with ExitStack() as es:
        def sb(name, shape, dt):
            return es.enter_context(nc.sbuf_tensor(name, shape, dt))

        def ps(name, shape, dt):
            return es.enter_context(nc.psum_tensor(name, shape, dt))

        H = sb("H", [128, NCH, S], BF16)
        WS = [sb("WS%d" % i, [128, NCH, 512], BF16) for i in range(2)]
        vecs = sb("vecs_sb", [128, nv], F32)
        ones_f = sb("ones_f", [128, 128], F32)
        ones_b = sb("ones_b", [128, 128], BF16)
        ident_b = sb("ident_b", [128, 128], BF16)
        tri_incl = sb("tri_incl", [128, 128], BF16)
        tri_low = sb("tri_low", [128, 128], BF16)
        tri_ui = sb("tri_ui", [128, 128], F32)
        blkmask = sb("blkmask", [128, 128], F32)
        negmask = sb("negmask", [128, 896], BF16)
        rmask = sb("rmask", [128, 512], F32)
        qT = [sb("qT%d" % i, [128, S], BF16) for i in range(2)]
        kT = [sb("kT%d" % i, [128, S], BF16) for i in range(2)]
        vtok = [sb("vtok%d" % i, [128, 16, 128], BF16) for i in range(2)]
        NZ = 5
        zs = [sb("zs%d" % i, [128, 512], F32) for i in range(NZ)]
        spb = [sb("spb%d" % i, [128, 512], BF16) for i in range(NZ)]
        et = [sb("et%d" % i, [128, 512], F32) for i in range(2)]
        Ab = [sb("Ab%d" % i, [128, 512], BF16) for i in range(NZ)]
        NE = 2
        e_o = [sb("e_o%d" % i, [128, 512], F32) for i in range(NE)]
        e_sq = [sb("e_sq%d" % i, [128, 512], F32) for i in range(NE)]
        e_rs = [sb("e_rs%d" % i, [128, 512], F32) for i in range(NE)]
        e_g = [sb("e_g%d" % i, [128, 512], F32) for i in range(NE)]
        e_mg = [sb("e_mg%d" % i, [128, 512], BF16) for i in range(NE)]
        NW = 5
        wk = [sb("wk%d" % i, [128, 512], F32) for i in range(NW)]
        NX = 3
        xt = [sb("xt%d" % i, [128, 2, 512], F32) for i in range(NX)]
        rT = sb("rT_sb", [16, S], BF16)
        wup = sb("wup_sb", [16, 128], BF16)
        g_q = sb("g_q", [128, 512], BF16)
        g_k = sb("g_k", [128, 512], BF16)
        g_kd = sb("g_kd", [128, 512], BF16)
        g_kdt = sb("g_kdt", [128, 4, 128], BF16)
        g_v = sb("g_v", [128, 4, 256], BF16)
        g_at = [sb("g_at%d" % i, [128, 128], BF16) for i in range(2)]
        S32 = sb("S32", [128, 128], F32)
        Sbf = sb("Sbf", [128, 128], BF16)
        gn_bc = sb("gn_bc", [128, 512], F32)
        wsT = [sb("wsT%d" % i, [128, 128], BF16) for i in range(NG)]
        wsTf = sb("wsTf", [128, 128], F32)
        bs_bc = [sb("bs_bc%d" % i, [128, 128], F32) for i in range(NG)]
        s_ss = sb("s_ss", [128, 8], F32)
        Z = [ps("Z%d" % i, [128, 512], F32) for i in range(2)]
        C = [ps("C%d" % i, [128, 512], F32) for i in range(2)]
        O = [ps("O%d" % i, [128, 512], F32) for i in range(2)]
        P = [ps("P%d" % i, [128, 512], F32) for i in range(2)]
        Tb = O[1][:].bitcast(BF16)[:, 0:512]

        g_sp, g_cum, g_eb, g_ebi, g_k32 = wk[0], wk[1], wk[2], wk[3], wk[4]
        K_sp, K_cum, K_eb, K_ebi, K_k32 = ("wk", 0), ("wk", 1), ("wk", 2), ("wk", 3), ("wk", 4)
        kxT = qT[0][:, 0:1024].rearrange("p (h m) -> p h m", h=4)
        vx = qT[0][:, 1024:2048].rearrange("p (b n) -> p b n", b=2)
        K_kxT = [("qT", 0, 0), ("qT", 0, 1)]
        K_vx = [("qT", 0, 2), ("qT", 0, 3)]
        s_vn = [Ab[3], Ab[4]]
        qx = [spb[1], spb[2]]
        K_qx = [("spb", 1), ("spb", 2)]
        pT = [Ab[0], Ab[1], Ab[2], spb[0]]
        K_pT = [("Ab", 0), ("Ab", 1), ("Ab", 2), ("spb", 0)]
        oxT = kT[0][:, :].rearrange("p (h n) -> p h n", h=4)
        print("SBUF bytes remaining:", nc.sbuf_bytes_remaining)

        sc = Sched(nc, es)
        op = sc.op
        ctr = {"p": 0, "e": 0, "w": 0, "x": 0, "z": 0}

        def nxt(k, n):
            v = ctr[k]
            ctr[k] = (v + 1) % n
            return v

        def consts():
            op("pool", lambda e: e.memset(ones_f[:], 1.0), writes=["ones_f"])
            op("pool", lambda e: e.memset(ones_b[:], 1.0), writes=["ones_b"])
            op("pool", lambda e: e.affine_select(out=ident_b[:], in_=ones_b[:], pattern=[[-1, 128]], compare_op=ALU.is_equal,
                                                 fill=0.0, base=0, channel_multiplier=1), reads=["ones_b"], writes=["ident_b"])
            op("pool", lambda e: e.affine_select(out=tri_incl[:], in_=ones_b[:], pattern=[[-1, 128]], compare_op=ALU.is_ge,
                                                 fill=0.0, base=0, channel_multiplier=1), reads=["ones_b"], writes=["tri_incl"])
            op("pool", lambda e: e.affine_select(out=tri_low[:], in_=ones_b[:], pattern=[[1, 128]], compare_op=ALU.is_gt,
                                                 fill=0.0, base=0, channel_multiplier=-1), reads=["ones_b"], writes=["tri_low"])
            op("pool", lambda e: e.affine_select(out=tri_ui[:], in_=ones_f[:], pattern=[[1, 128]], compare_op=ALU.is_ge,
                                                 fill=0.0, base=0, channel_multiplier=-1), reads=["ones_f"], writes=["tri_ui"])
            op("pool", lambda e: e.affine_select(out=blkmask[:], in_=ones_f[:], pattern=[[1, 128]], compare_op=ALU.is_ge,
                                                 fill=0.0, base=0, channel_multiplier=-1), reads=["ones_f"], writes=["blkmask"])
            op("pool", lambda e: e.memset(blkmask[0:64, 64:128], 0.0), reads=["blkmask"], writes=["blkmask"])
            op("pool", lambda e: e.memset(negmask[:], 0.0), writes=["negmask"])
            op("pool", lambda e: e.affine_select(out=negmask[:], in_=negmask[:], pattern=[[1, 896]],
                                                 compare_op=ALU.is_gt, fill=NEG, base=-384, channel_multiplier=-1),
               reads=["negmask"], writes=["negmask"])
            op("pool", lambda e: e.memset(rmask[:], 1.0), writes=["rmask"])
            op("pool", lambda e: e.memset(rmask[:].rearrange("p (c t) -> p c t", t=64)[:, :, 0:1], 0.0), reads=["rmask"], writes=["rmask"])
            op("sp", lambda e: e.dma_start(out=vecs[:], in_=vecs_d[:, :]), writes=["vecs"], dma=True)
            b0 = voff["bgate"]
            op("dve", lambda e: e.tensor_scalar(out=vecs[:, b0:b0 + 2 * L], in0=vecs[:, b0:b0 + 2 * L], scalar1=-1.0, scalar2=None,
                                                op0=ALU.mult), reads=["vecs"], writes=["vecs"])

        def rstd_from_ss(ss_ap, ss_key, out_tile, out_key, inv_n, tmp_tile, tmp_key):
            op("act", lambda e: e.activation(out=tmp_tile, in_=ss_ap, func=AF.Ln, bias=EPS, scale=inv_n),
               reads=[ss_key], writes=[tmp_key])
            op("act", lambda e: e.activation(out=out_tile, in_=tmp_tile, func=AF.Exp, scale=-0.5),
               reads=[tmp_key], writes=[out_key])

        NSLAB = 9
        _big = [(qT[1][:].bitcast(F32).rearrange("p (j t) -> p j t", j=2), [("qT", 1, t_) for t_ in range(NTB)]),
                (kT[1][:].bitcast(F32).rearrange("p (j t) -> p j t", j=2), [("kT", 1, t_) for t_ in range(NTB)]),
                (vtok[1][:].rearrange("p a b -> p (a b)").bitcast(F32).rearrange("p (j t) -> p j t", j=2), [("vtok", 1, t_) for t_ in range(NTB)])]

        def slab(i):
            order = [0, 3, 1, 4, 2, 5, 6, 7, 8]
            i = order[i]
            if i < 3:
                return [xt[i][:, 0, :], xt[i][:, 1, :]], [("xt", i)], xt[i]
            if i < 6:
                v_, k_ = _big[i - 3]
                return [v_[:, 0, :], v_[:, 1, :]], k_, v_
            if i < 8:
                a_, b_ = 2 * (i - 6), 2 * (i - 6) + 1
                return [zs[a_][:], zs[b_][:]], [("zs", a_), ("zs", b_)], None
            return [et[0][:], et[1][:]], [("et", 0), ("et", 1)], None

        ctr["s"] = 0
        ctr["r"] = 0

        def rslab():
            i = nxt("r", 6)
            if i % 2 == 0:
                return xt[i // 2], [("xt", i // 2)]
            return _big[i // 2]

        def norm_phase(src, srckey, gcol, ntok, dst_fn, final=False):
            nblk = max(1, ntok // TB)
            w = min(TB, ntok)
            srcv = src.rearrange("(c p) s -> p c s", p=128)
            yv = yT.rearrange("(c p) s -> p c s", p=128)
            lq = ["sp", "pool"]

            def load(c2, tb, t0):
                si_ = nxt("s", NSLAB)
                aps, keys, single = slab(si_)
                rk = [(srckey, 2 * c2, tb), (srckey, 2 * c2 + 1, tb)]
                if single is not None:
                    op(lq[c2 % 2], lambda e: e.dma_start(out=single[:, :, 0:w], in_=srcv[:, 2 * c2:2 * c2 + 2, t0:t0 + w]),
                       reads=rk, writes=keys, dma=True)
                else:
                    for j in range(2):
                        op(lq[(c2 + j) % 2], lambda e, j=j: e.dma_start(out=aps[j][:, 0:w], in_=srcv[:, 2 * c2 + j, t0:t0 + w]),
                           reads=[rk[j]], writes=[keys[j]], dma=True)
                return aps, keys, single

            for tb in range(nblk):
                t0 = tb * w
                ri = nxt("e", NE)
                for c2 in range(NCH // 2):
                    aps, keys, single = load(c2, tb, t0)
                    for j in range(2):
                        c = 2 * c2 + j
                        kj = keys if single is not None else [keys[j]]
                        if c == 0:
                            op("act", lambda e, ri=ri, ap=aps[j]: e.activation(out=e_o[ri][:, 0:w], in_=ap[:, 0:w], func=AF.Square),
                               reads=kj, writes=[("e_o", ri)])
                        else:
                            wi = nxt("w", NW)
                            op("act", lambda e, wi=wi, ap=aps[j]: e.activation(out=wk[wi][:, 0:w], in_=ap[:, 0:w], func=AF.Square),
                               reads=kj, writes=[("wk", wi)])
                            op("dve", lambda e, wi=wi, ri=ri: e.tensor_tensor(out=e_o[ri][:, 0:w], in0=e_o[ri][:, 0:w], in1=wk[wi][:, 0:w], op=ALU.add),
                               reads=[("wk", wi), ("e_o", ri)], writes=[("e_o", ri)])
                si = nxt("p", 2)
                op("pe", lambda e, ri=ri, si=si: e.matmul(P[si][:, 0:w], ones_f[:], e_o[ri][:, 0:w], start=True, stop=True),
                   reads=[("e_o", ri), "ones_f"], writes=[("P", si)])
                rstd_from_ss(P[si][:, 0:w], ("P", si), e_rs[ri][:, 0:w], ("e_rs", ri), 1.0 / D, e_sq[ri][:, 0:w], ("e_sq", ri))
                for c2 in range(NCH // 2):
                    aps, keys, single = load(c2, tb, t0)
                    for j in range(2):
                        c = 2 * c2 + j
                        kj = keys if single is not None else [keys[j]]
                        if not final:
                            dap, dkeys = dst_fn(c, t0, w, tb)
                            op("dve", lambda e, c=c, dap=dap, ri=ri, ap=aps[j]: e.scalar_tensor_tensor(
                                out=dap, in0=ap[:, 0:w], scalar=vecs[:, gcol + c:gcol + c + 1], in1=e_rs[ri][:, 0:w],
                                op0=ALU.mult, op1=ALU.mult), reads=kj + [("e_rs", ri), "vecs"], writes=dkeys)
                        else:
                            op("dve", lambda e, c=c, ri=ri, ap=aps[j]: e.scalar_tensor_tensor(
                                out=ap[:, 0:w], in0=ap[:, 0:w], scalar=vecs[:, gcol + c:gcol + c + 1], in1=e_rs[ri][:, 0:w],
                                op0=ALU.mult, op1=ALU.mult), reads=kj + [("e_rs", ri), "vecs"], writes=kj)
                    if final:
                        if single is not None:
                            tok = op("act", lambda e, c2=c2, t0=t0, single=single: e.dma_start(out=yv[:, 2 * c2:2 * c2 + 2, t0:t0 + w], in_=single[:, :, 0:w]),
                                     reads=keys, writes=[("yT", 2 * c2, tb), ("yT", 2 * c2 + 1, tb)], dma=True)
                            out_toks.append(tok)
                        else:
                            for j in range(2):
                                tok = op("act", lambda e, c2=c2, t0=t0, j=j, ap=aps[j]: e.dma_start(out=yv[:, 2 * c2 + j, t0:t0 + w], in_=ap[:, 0:w]),
                                         reads=[keys[j]], writes=[("yT", 2 * c2 + j, tb)], dma=True)
                                out_toks.append(tok)

        def h_dst(c, t0, w, tb):
            return H[:, c, t0:t0 + w], [("H", c, tb)]

        def hkeys(tb):
            return [("H", c, tb) for c in range(NCH)]

        def proj_fm(slot, col0, tb, ncols=128):
            pi = nxt("p", 2)

            def f(e):
                ins = None
                for c in range(NCH):
                    ins = e.matmul(P[pi][0:ncols, :], WS[slot][:, c, col0:col0 + ncols], H[:, c, tb * TB:(tb + 1) * TB],
                                   start=(c == 0), stop=(c == NCH - 1))
                return ins
            op("pe", f, reads=[("WS", slot)] + hkeys(tb), writes=[("P", pi)])
            return pi

        def proj_tm(slot, col0, ncols, tb, blocks):
            pi = nxt("p", 2)

            def f(e):
                ins = None
                for j, blk in enumerate(blocks):
                    for c in range(NCH):
                        ins = e.matmul(P[pi][:, j * ncols:(j + 1) * ncols], H[:, c, blk * 128:(blk + 1) * 128],
                                       WS[slot][:, c, col0:col0 + ncols], start=(c == 0), stop=(c == NCH - 1))
                return ins
            op("pe", f, reads=[("WS", slot)] + hkeys(tb), writes=[("P", pi)])
            return pi

        def load_ws(slot, dram_view):
            op("pool", lambda e: e.dma_start(out=WS[slot][:], in_=dram_view), writes=[("WS", slot)], dma=True)

        def win_view(l, s):
            return w_in[l].rearrange("(c p) n -> p c n", p=128)[:, :, s * 512:(s + 1) * 512]

        def epilogue_g(l, m, tb, src_ap, src_key, gslot, gcol, banks=None, ei=None, src_in_eo=False, gate_done=False):
            if ei is None:
                ei = nxt("e", NE)
            if banks is None:
                pi = nxt("p", 2)
                G, Gk = P[pi], ("P", pi)
                si = nxt("p", 2)
                SSt, SSk = P[si], ("P", si)
            else:
                G, Gk, SSt, SSk = banks
            gc = voff["onorm"] + 16 * l + m
            if not src_in_eo:
                op("act", lambda e: e.activation(out=e_o[ei][:], in_=src_ap, func=AF.Copy), reads=[src_key], writes=[("e_o", ei)])
                yield
            op("act", lambda e: e.activation(out=e_sq[ei][:], in_=e_o[ei][:], func=AF.Square), reads=[("e_o", ei)], writes=[("e_sq", ei)])
            yield
            op("pe", lambda e: e.matmul(SSt[:], ones_f[:], e_sq[ei][:], start=True, stop=True), reads=[("e_sq", ei), "ones_f"], writes=[SSk])
            yield

            def fg(e):
                ins = None
                for c in range(NCH):
                    ins = e.matmul(G[:], WS[gslot][:, c, gcol:gcol + 128], H[:, c, tb * TB:(tb + 1) * TB], start=(c == 0), stop=(c == NCH - 1))
                return ins
            if not gate_done:
                op("pe", fg, reads=[("WS", gslot)] + hkeys(tb), writes=[Gk])
                yield
            op("act", lambda e: e.activation(out=e_sq[ei][:], in_=SSt[:], func=AF.Ln, bias=EPS, scale=1.0 / 128), reads=[SSk], writes=[("e_sq", ei)])
            yield
            op("act", lambda e: e.activation(out=e_rs[ei][:], in_=e_sq[ei][:], func=AF.Exp, scale=-0.5), reads=[("e_sq", ei)], writes=[("e_rs", ei)])
            yield
            op("dve", lambda e: e.scalar_tensor_tensor(out=e_o[ei][:], in0=e_o[ei][:], scalar=vecs[:, gc:gc + 1], in1=e_rs[ei][:],
                                                       op0=ALU.mult, op1=ALU.mult), reads=[("e_o", ei), ("e_rs", ei), "vecs"], writes=[("e_o", ei)])
            yield
            op("act", lambda e: e.activation(out=e_g[ei][:], in_=G[:], func=AF.Exp, scale=-1.0), reads=[Gk], writes=[("e_g", ei)])
            yield
            op("act", lambda e: e.activation(out=e_g[ei][:], in_=e_g[ei][:], func=AF.Ln, bias=1.0), reads=[("e_g", ei)], writes=[("e_g", ei)])
            yield
            op("act", lambda e: e.activation(out=e_g[ei][:], in_=e_g[ei][:], func=AF.Exp, scale=-1.0), reads=[("e_g", ei)], writes=[("e_g", ei)])
            yield
            op("dve", lambda e: e.tensor_tensor(out=e_g[ei][:], in0=G[:], in1=e_g[ei][:], op=ALU.mult),
               reads=[Gk, ("e_g", ei)], writes=[("e_g", ei)])
            yield
            op("dve", lambda e: e.tensor_tensor(out=e_mg[ei][:], in0=e_o[ei][:], in1=e_g[ei][:], op=ALU.mult),
               reads=[("e_o", ei), ("e_g", ei)], writes=[("e_mg", ei)])
            yield
            op("sp", lambda e: e.dma_start(out=mgd[m][:, tb * TB:(tb + 1) * TB], in_=e_mg[ei][:]), reads=[("e_mg", ei)],
               writes=[("MG", m, tb)], dma=True)
            yield

        def drive(*gens):
            gens = list(gens)
            while gens:
                for g in list(gens):
                    try:
                        next(g)
                    except StopIteration:
                        gens.remove(g)

        def epilogue(*a, **k):
            drive(epilogue_g(*a, **k))

        def sb_proj_items(i, slot):
            bi = i % 2
            scale = 128.0 ** -0.5
            items = []

            def group(tb, kind):
                st = {}

                def piece(k):
                    def f():
                        if k == 0:
                            st["pi"] = nxt("p", 2)
                        pi = st["pi"]

                        def mm(e):
                            ins = None
                            if kind == "v":
                                blk = tb * 4 + k
                                for c in range(NCH):
                                    ins = e.matmul(P[pi][:, k * 128:(k + 1) * 128], H[:, c, blk * 128:(blk + 1) * 128],
                                                   WS[slot][:, c, 256:384], start=(c == 0), stop=(c == NCH - 1))
                            else:
                                col0 = 0 if kind == "q" else 128
                                for c in range(4 * k, 4 * k + 4):
                                    ins = e.matmul(P[pi][:], WS[slot][:, c, col0:col0 + 128], H[:, c, tb * TB:(tb + 1) * TB],
                                                   start=(c == 0), stop=(c == NCH - 1))
                            return ins
                        op("pe", mm, reads=[("WS", slot)] + hkeys(tb), writes=[("P", pi)])
                    return f

                def evac():
                    pi = st["pi"]
                    if kind == "q":
                        op("act", lambda e: e.activation(out=qT[bi][:, tb * TB:(tb + 1) * TB], in_=P[pi][:], func=AF.Copy, scale=scale),
                           reads=[("P", pi)], writes=[("qT", bi, tb)])
                    elif kind == "k":
                        op("dve", lambda e: e.tensor_copy(out=kT[bi][:, tb * TB:(tb + 1) * TB], in_=P[pi][:]),
                           reads=[("P", pi)], writes=[("kT", bi, tb)])
                    else:
                        op("dve", lambda e: e.tensor_copy(out=vtok[bi][:, tb * 4:(tb + 1) * 4, :], in_=P[pi][:].rearrange("p (j d) -> p j d", d=128)),
                           reads=[("P", pi)], writes=[("vtok", bi, tb)])
                return [piece(k) for k in range(4)] + [evac]
            for tb in range(NTB):
                for kind in ("k", "v", "q"):
                    items += group(tb, kind)
            return items

        def sb_attn(l, i, slot, bg):
            bi = i % 2

            def tile_ops(ch, kb, qb):
                zi = nxt("z", NZ)
                zb = zi % 2
                ei = zi % 2
                r = kb - 4 * qb
                first = (kb == 4 * qb + 3)
                last = (kb == 0)
                d = {}
                d["Z"] = lambda: op("pe", lambda e: e.matmul(Z[zb][:], kT[bi][:, kb * 128:(kb + 1) * 128], qT[bi][:, qb * TB:(qb + 1) * TB], start=True, stop=True),
                                    reads=[("kT", bi, kb // 4), ("qT", bi, qb)], writes=[("Z", zb)])
                if r >= 0:
                    d["COPY"] = lambda: op("dve", lambda e: e.tensor_tensor(out=zs[zi][:], in0=Z[zb][:], in1=negmask[:, 384 - 128 * r:896 - 128 * r], op=ALU.add),
                                           reads=[("Z", zb), "negmask"], writes=[("zs", zi)])
                else:
                    d["COPY"] = lambda: op("dve", lambda e: e.tensor_copy(out=zs[zi][:], in_=Z[zb][:]), reads=[("Z", zb)], writes=[("zs", zi)])
                d["EXPA"] = lambda: op("act", lambda e: e.activation(out=et[ei][:], in_=zs[zi][:], func=AF.Exp), reads=[("zs", zi)], writes=[("et", ei)])
                d["LN"] = lambda: op("act", lambda e: e.activation(out=spb[zi][:], in_=et[ei][:], func=AF.Ln, bias=1.0), reads=[("et", ei)], writes=[("spb", zi)])
                d["TRI"] = lambda: op("pe", lambda e: e.matmul(C[ch][:], tri_incl[:], spb[zi][:], start=first, stop=last),
                                      reads=[("spb", zi), "tri_incl"], writes=[("C", ch)])
                d["SUB"] = lambda: op("dve", lambda e: e.scalar_tensor_tensor(out=zs[zi][:], in0=C[ch][:], scalar=-1.0, in1=zs[zi][:], op0=ALU.mult, op1=ALU.add),
                                      reads=[("C", ch), ("zs", zi)], writes=[("zs", zi)])
                if not last:
                    d["LOW"] = lambda: op("pe", lambda e: e.matmul(C[ch][:], tri_low[:], spb[zi][:], start=False, stop=False),
                                          reads=[("spb", zi), "tri_low"], writes=[("C", ch)])
                else:
                    d["LOW"] = lambda: None
                d["EXPB"] = lambda: op("act", lambda e: e.activation(out=Ab[zi][:], in_=zs[zi][:], func=AF.Exp), reads=[("zs", zi)], writes=[("Ab", zi)])

                def av():
                    op("pe", lambda e: e.matmul(O[ch][:], vtok[bi][:, kb, :], Ab[zi][:], start=first, stop=last),
                       reads=[("Ab", zi), ("vtok", bi, kb // 4)], writes=[("O", ch)])
                    if last:
                        epilogue(l, i, qb, O[ch][:], ("O", ch), slot, 384, banks=(O[ch], ("O", ch), C[ch], ("C", ch)))
                d["AV"] = av
                return d

            nper = 0
            ta = [(0, kb, qb) for qb in (3, 0) for kb in range(4 * qb + 3, -1, -1)]
            tbl = [(1, kb, qb) for qb in (2, 1) for kb in range(4 * qb + 3, -1, -1)]
            T = []
            for x_, y_ in zip(ta, tbl):
                T += [x_, y_]
            n = len(T)
            ops_ = {}
            for p in range(n + 2):
                if p < n:
                    ops_[p] = tile_ops(*T[p])
                if p - 2 >= 0:
                    ops_[p - 2]["LOW"]()
                    ops_[p - 2]["EXPB"]()
                if p < n:
                    ops_[p]["Z"]()
                    ops_[p]["COPY"]()
                if p - 2 >= 0:
                    ops_[p - 2]["AV"]()
                if p < n:
                    ops_[p]["EXPA"]()
                    ops_[p]["LN"]()
                if 0 <= p - 1 < n:
                    ops_[p - 1]["TRI"]()
                    ops_[p - 1]["SUB"]()
                nper += 1
                for _ in range(2):
                    if bg:
                        bg.pop(0)()
            while bg:
                bg.pop(0)()

        def gla_pair(l, j, slotA, slotB, first_pair):
            gp = pairs[j]
            if first_pair:
                for tb in range(NTB):
                    pi = proj_fm(slotB, 256, tb, ncols=16)
                    op("act", lambda e, pi=pi, tb=tb: e.activation(out=rT[:, tb * TB:(tb + 1) * TB], in_=P[pi][0:16, :], func=AF.Copy),
                       reads=[("P", pi)], writes=[("rT", tb)])
            op("pool", lambda e: e.dma_start(out=wup[:], in_=wup_d[l][:, j * 128:(j + 1) * 128]), writes=["wup"], dma=True)
            op("dve", lambda e: e.memset(S32[:], 0.0), writes=["S32"])
            op("dve", lambda e: e.memset(Sbf[:], 0.0), writes=["Sbf"])
            bcol = voff["bgate"] + 2 * l + j
            for tb in range(NTB):
                tsl = slice(tb * TB, (tb + 1) * TB)
                pi = nxt("p", 2)
                op("pe", lambda e, pi=pi, tsl=tsl: e.matmul(P[pi][:], wup[:], rT[:, tsl], start=True, stop=True),
                   reads=["wup", ("rT", tb)], writes=[("P", pi)])
                op("act", lambda e, pi=pi: e.activation(out=g_sp[:], in_=P[pi][:], func=AF.Exp, scale=-1.0, bias=vecs[:, bcol:bcol + 1]),
                   reads=[("P", pi), "vecs"], writes=[K_sp])
                op("act", lambda e: e.activation(out=g_sp[:], in_=g_sp[:], func=AF.Ln, bias=1.0), reads=[K_sp], writes=[K_sp])
                op("dve", lambda e: e.tensor_tensor_scan(out=g_cum[:], data0=rmask[:], data1=g_sp[:], initial=0.0, op0=ALU.mult, op1=ALU.add),
                   reads=[K_sp, "rmask"], writes=[K_cum])
                op("act", lambda e: e.activation(out=g_eb[:], in_=g_cum[:], func=AF.Exp, scale=-1.0 / 16), reads=[K_cum], writes=[K_eb])
                op("act", lambda e: e.activation(out=g_ebi[:], in_=g_cum[:], func=AF.Exp, scale=1.0 / 16), reads=[K_cum], writes=[K_ebi])
                pi = proj_fm(slotA, 0, tb)
                op("dve", lambda e, pi=pi: e.scalar_tensor_tensor(out=g_q[:], in0=P[pi][:], scalar=0.125, in1=g_eb[:], op0=ALU.mult, op1=ALU.mult),
                   reads=[("P", pi), K_eb], writes=["g_q"])
                pi = proj_fm(slotA, 128, tb)
                op("dve", lambda e, pi=pi: e.tensor_tensor(out=g_k32[:], in0=P[pi][:], in1=g_ebi[:], op=ALU.mult),
                   reads=[("P", pi), K_ebi], writes=[K_k32])
                op("act", lambda e: e.activation(out=g_k[:], in_=g_k32[:], func=AF.Copy), reads=[K_k32], writes=["g_k"])
                dec_bc = g_eb[:].rearrange("p (c t) -> p c t", t=64)[:, :, 63:64].broadcast_to([128, 8, 64])
                op("dve", lambda e: e.tensor_tensor(out=g_kd[:].rearrange("p (c t) -> p c t", t=64), in0=g_k32[:].rearrange("p (c t) -> p c t", t=64),
                                                    in1=dec_bc, op=ALU.mult), reads=[K_k32, K_eb], writes=["g_kd"])

                def ftr(e):
                    ins = None
                    for j4 in range(4):
                        ins = e.transpose(Tb[:, j4 * 128:(j4 + 1) * 128], g_kd[:, j4 * 128:(j4 + 1) * 128], ident_b[:])
                    return ins
                op("pe", ftr, reads=["g_kd", "ident_b"], writes=[("O", 1)])
                op("dve", lambda e: e.tensor_copy(out=g_kdt[:], in_=Tb.rearrange("p (j d) -> p j d", d=128)), reads=[("O", 1)], writes=["g_kdt"])
                for half in range(2):
                    pi = proj_tm(slotA, 256, 256, tb, [tb * 4 + 2 * half, tb * 4 + 2 * half + 1])
                    op("act", lambda e, pi=pi, half=half: e.activation(out=g_v[:, 2 * half:2 * half + 2, :],
                                                                       in_=P[pi][:].rearrange("p (j d) -> p j d", d=256), func=AF.Copy),
                       reads=[("P", pi)], writes=["g_v"])
                OG = [O[0], C[0]]
                OGk = [("O", 0), ("C", 0)]
                for cp in range(4):
                    csl = slice(cp * 128, (cp + 1) * 128)
                    for h in range(2):
                        hs = slice(h * 64, (h + 1) * 64)
                        ai = (cp * 2 + h) % 2
                        op("pe", lambda e, hs=hs, csl=csl: e.matmul(Z[0][:, 0:128], g_k[hs, csl], g_q[hs, csl], start=True, stop=True),
                           reads=["g_k", "g_q"], writes=[("Z", 0)])
                        op("dve", lambda e, ai=ai: e.tensor_tensor(out=g_at[ai][:], in0=Z[0][:, 0:128], in1=blkmask[:], op=ALU.mult),
                           reads=[("Z", 0), "blkmask"], writes=[("g_at", ai)])
                        op("pe", lambda e, h=h, ai=ai, csl=csl, cp=cp: e.matmul(OG[h][:, csl], g_v[:, cp, h * 128:(h + 1) * 128], g_at[ai][:],
                                                                                 start=True, stop=False),
                           reads=["g_v", ("g_at", ai)], writes=[OGk[h]])
                    for c2 in range(2):
                        ch = cp * 2 + c2
                        tsl64 = slice(cp * 128 + c2 * 64, cp * 128 + c2 * 64 + 64)
                        psl = slice(c2 * 64, c2 * 64 + 64)

                        def finter(e, tsl64=tsl64, c2=c2):
                            ins = None
                            for h in range(2):
                                hs = slice(h * 64, (h + 1) * 64)
                                ins = e.matmul(OG[h][:, tsl64], Sbf[hs, :], g_q[hs, tsl64], start=False, stop=(c2 == 1))
                            return ins
                        op("pe", finter, reads=["Sbf", "g_q"], writes=[("O", 0), ("C", 0)])

                        def fkv(e, psl=psl, cp=cp):
                            ins = None
                            for h in range(2):
                                ins = e.matmul(Z[1][h * 64:(h + 1) * 64, 0:128], g_kdt[psl, cp, h * 64:(h + 1) * 64],
                                               g_v[psl, cp, h * 128:(h + 1) * 128], start=True, stop=True)
                            return ins
                        op("pe", fkv, reads=["g_kdt", "g_v"], writes=[("Z", 1)])
                        dcol = ch * 64 + 63
                        op("dve", lambda e, dcol=dcol: e.scalar_tensor_tensor(out=S32[:], in0=S32[:], scalar=g_eb[:, dcol:dcol + 1], in1=Z[1][:, 0:128],
                                                                              op0=ALU.mult, op1=ALU.add),
                           reads=["S32", K_eb, ("Z", 1)], writes=["S32"])
                        op("dve", lambda e: e.tensor_copy(out=Sbf[:], in_=S32[:]), reads=["S32"], writes=["Sbf"])
                drive(*[epilogue_g(l, NSB + 2 * j + h, tb, OG[h][:], OGk[h], slotB, h * 128, banks=(P[h], ("P", h), OG[h], OGk[h])) for h in range(2)])

        def gelu_g(pi, out_tile, out_key):
            a = nxt("w", NW)
            b = nxt("w", NW)
            op("dve", lambda e: e.tensor_copy(out=wk[a][:], in_=P[pi][:]), reads=[("P", pi)], writes=[("wk", a)])
            yield
            op("dve", lambda e: e.tensor_tensor(out=wk[b][:], in0=wk[a][:], in1=wk[a][:], op=ALU.mult), reads=[("wk", a)], writes=[("wk", b)])
            yield
            op("dve", lambda e: e.tensor_scalar(out=wk[b][:], in0=wk[b][:], scalar1=0.044715, scalar2=1.0, op0=ALU.mult, op1=ALU.add),
               reads=[("wk", b)], writes=[("wk", b)])
            yield
            op("dve", lambda e: e.tensor_tensor(out=wk[b][:], in0=wk[b][:], in1=wk[a][:], op=ALU.mult), reads=[("wk", a), ("wk", b)], writes=[("wk", b)])
            yield
            op("act", lambda e: e.activation(out=wk[b][:], in_=wk[b][:], func=AF.Exp, scale=-GELU_C), reads=[("wk", b)], writes=[("wk", b)])
            yield
            op("act", lambda e: e.activation(out=wk[b][:], in_=wk[b][:], func=AF.Ln, bias=1.0), reads=[("wk", b)], writes=[("wk", b)])
            yield
            op("act", lambda e: e.activation(out=wk[b][:], in_=wk[b][:], func=AF.Exp, scale=-1.0), reads=[("wk", b)], writes=[("wk", b)])
            yield
            op("dve", lambda e: e.tensor_tensor(out=out_tile, in0=wk[a][:], in1=wk[b][:], op=ALU.mult), reads=[("wk", a), ("wk", b)], writes=[out_key])
            yield

        def gelu_from_psum(pi, out_tile, out_key):
            drive(gelu_g(pi, out_tile, out_key))

        def sgu_unit(l, slotV, uslot, lgs):
            op("sp", lambda e: e.dma_start(out=gn_bc[:], in_=sgn_d[l].partition_broadcast(128)), writes=["gn_bc"], dma=True)
            for lg in lgs:
                op("sp", lambda e, lg=lg: e.dma_start(out=wsTf[:], in_=wsg_d[l][lg]), writes=["wsTf"], dma=True)
                op("dve", lambda e, lg=lg: e.tensor_tensor(out=wsT[lg][:], in0=wsTf[:], in1=tri_ui[:], op=ALU.mult),
                   reads=["wsTf", "tri_ui"], writes=[("wsT", lg)])
                op("sp", lambda e, lg=lg: e.dma_start(out=bs_bc[lg][:], in_=bsg_d[l][lg].partition_broadcast(128)), writes=[("bs_bc", lg)], dma=True)
            MX = [O[0], C[0]]
            MXk = [("O", 0), ("C", 0)]
            def gelu_ip_g(pi, hold):
                a = nxt("w", NW)
                b = nxt("w", NW)
                hold["a"] = a
                op("dve", lambda e: e.tensor_copy(out=wk[a][:], in_=P[pi][:]), reads=[("P", pi)], writes=[("wk", a)])
                yield
                op("dve", lambda e: e.tensor_tensor(out=wk[b][:], in0=wk[a][:], in1=wk[a][:], op=ALU.mult), reads=[("wk", a)], writes=[("wk", b)])
                yield
                op("dve", lambda e: e.tensor_scalar(out=wk[b][:], in0=wk[b][:], scalar1=0.044715, scalar2=1.0, op0=ALU.mult, op1=ALU.add),
                   reads=[("wk", b)], writes=[("wk", b)])
                yield
                op("dve", lambda e: e.tensor_tensor(out=wk[b][:], in0=wk[b][:], in1=wk[a][:], op=ALU.mult), reads=[("wk", a), ("wk", b)], writes=[("wk", b)])
                yield
                op("act", lambda e: e.activation(out=wk[b][:], in_=wk[b][:], func=AF.Exp, scale=-GELU_C), reads=[("wk", b)], writes=[("wk", b)])
                yield
                op("act", lambda e: e.activation(out=wk[b][:], in_=wk[b][:], func=AF.Ln, bias=1.0), reads=[("wk", b)], writes=[("wk", b)])
                yield
                op("act", lambda e: e.activation(out=wk[b][:], in_=wk[b][:], func=AF.Exp, scale=-1.0), reads=[("wk", b)], writes=[("wk", b)])
                yield
                op("dve", lambda e: e.tensor_tensor(out=wk[a][:], in0=wk[a][:], in1=wk[b][:], op=ALU.mult), reads=[("wk", a), ("wk", b)], writes=[("wk", a)])
                yield

            def vblock_g(tb, j4):
                blk = tb * 4 + j4
                pi = proj_tm(slotV, 0, 512, tb, [blk])
                yield
                hold = {}
                yield from gelu_ip_g(pi, hold)
                a = hold["a"]
                sscol = blk % 8
                vi = blk % 2
                op("dve", lambda e: e.memset(s_ss[:, sscol:sscol + 1], 0.0), writes=[("s_ss", sscol)])
                yield
                op("act", lambda e: e.activation(out=e_sq[0][:], in_=wk[a][:], func=AF.Square, accum_out=s_ss[:, sscol:sscol + 1]),
                   reads=[("wk", a), ("s_ss", sscol)], writes=[("e_sq", 0), ("s_ss", sscol)])
                yield
                op("act", lambda e: e.activation(out=s_ss[:, sscol:sscol + 1], in_=s_ss[:, sscol:sscol + 1], func=AF.Ln, bias=EPS, scale=1.0 / 512),
                   reads=[("s_ss", sscol)], writes=[("s_ss", sscol)])
                yield
                op("act", lambda e: e.activation(out=s_ss[:, sscol:sscol + 1], in_=s_ss[:, sscol:sscol + 1], func=AF.Exp, scale=-0.5),
                   reads=[("s_ss", sscol)], writes=[("s_ss", sscol)])
                yield
                op("dve", lambda e: e.scalar_tensor_tensor(out=s_vn[vi][:], in0=wk[a][:], scalar=s_ss[:, sscol:sscol + 1],
                                                           in1=gn_bc[:], op0=ALU.mult, op1=ALU.mult),
                   reads=[("wk", a), ("s_ss", sscol), "gn_bc"], writes=[("Ab", 3 + vi)])
                yield

                def fmx(e):
                    ins = None
                    for k, lg in enumerate(lgs):
                        ins = e.matmul(MX[k][:, j4 * 128:(j4 + 1) * 128], s_vn[vi][:, lg * 128:(lg + 1) * 128], wsT[lg][:], start=True, stop=True)
                    return ins
                op("pe", fmx, reads=[("Ab", 3 + vi)] + [("wsT", lg) for lg in lgs], writes=MXk[:len(lgs)])
                yield

            UB = [O[1], C[1]]
            UBk = [("O", 1), ("C", 1)]
            GB = [Z[0], Z[1]]
            GBk = [("Z", 0), ("Z", 1)]

            def early(tb):
                for k, lg in enumerate(lgs):
                    for (bank, bkey, col0) in ((UB[k], UBk[k], k * 256), (GB[k], GBk[k], k * 256 + 128)):
                        def f(e, bank=bank, col0=col0):
                            ins = None
                            for c in range(NCH):
                                ins = e.matmul(bank[:], WS[uslot][:, c, col0:col0 + 128], H[:, c, tb * TB:(tb + 1) * TB], start=(c == 0), stop=(c == NCH - 1))
                            return ins
                        op("pe", f, reads=[("WS", uslot)] + hkeys(tb), writes=[bkey])

            def upart_g(tb, k, lg):
                ei = nxt("e", NE)
                X, Xk = e_g[ei], ("e_g", ei)
                Sg, Sk = e_rs[ei], ("e_rs", ei)
                op("dve", lambda e: e.tensor_copy(out=X[:], in_=UB[k][:]), reads=[UBk[k]], writes=[Xk])
                yield
                op("dve", lambda e: e.tensor_tensor(out=Sg[:], in0=X[:], in1=X[:], op=ALU.mult), reads=[Xk], writes=[Sk])
                yield
                op("dve", lambda e: e.tensor_scalar(out=Sg[:], in0=Sg[:], scalar1=0.044715, scalar2=1.0, op0=ALU.mult, op1=ALU.add), reads=[Sk], writes=[Sk])
                yield
                op("dve", lambda e: e.tensor_tensor(out=Sg[:], in0=Sg[:], in1=X[:], op=ALU.mult), reads=[Xk, Sk], writes=[Sk])
                yield
                op("act", lambda e: e.activation(out=Sg[:], in_=Sg[:], func=AF.Exp, scale=-GELU_C), reads=[Sk], writes=[Sk])
                yield
                op("act", lambda e: e.activation(out=Sg[:], in_=Sg[:], func=AF.Ln, bias=1.0), reads=[Sk], writes=[Sk])
                yield
                op("act", lambda e: e.activation(out=Sg[:], in_=Sg[:], func=AF.Exp, scale=-1.0), reads=[Sk], writes=[Sk])
                yield
                op("dve", lambda e: e.tensor_tensor(out=X[:], in0=X[:], in1=Sg[:], op=ALU.mult), reads=[Xk, Sk], writes=[Xk])
                yield
                op("dve", lambda e: e.tensor_tensor(out=e_o[ei][:].rearrange("p (j t) -> p j t", t=128),
                                                    in0=MX[k][:].rearrange("p (j t) -> p j t", t=128),
                                                    in1=bs_bc[lg][:].unsqueeze(1).broadcast_to([128, 4, 128]), op=ALU.add),
                   reads=[MXk[k], ("bs_bc", lg)], writes=[("e_o", ei)])
                yield
                op("dve", lambda e: e.tensor_tensor(out=e_o[ei][:], in0=e_o[ei][:], in1=X[:], op=ALU.mult), reads=[("e_o", ei), Xk], writes=[("e_o", ei)])
                yield
                yield from epilogue_g(l, NSB + 2 * NP + lg, tb, None, None, uslot, k * 256 + 128,
                                      banks=(GB[k], GBk[k], MX[k], MXk[k]), ei=ei, src_in_eo=True, gate_done=True)

            def adv(pair, nsteps):
                for _ in range(nsteps):
                    for g in pair:
                        next(g, None)

            allp = [(vblock_g(tb, 2 * hp), vblock_g(tb, 2 * hp + 1)) for tb in range(NTB) for hp in range(2)]
            early(0)
            adv(allp[0], 2)
            for kk in range(len(allp)):
                if kk + 1 < len(allp):
                    adv(allp[kk + 1], 1)
                drive(*allp[kk])
                if kk + 1 < len(allp):
                    adv(allp[kk + 1], 1)
                if kk % 2 == 1:
                    tb = kk // 2
                    drive(*[upart_g(tb, k, lg) for k, lg in enumerate(lgs)])
                    if tb + 1 < NTB:
                        early(tb + 1)

        def mixer(l):
            s = 0
            units = []
            for i in range(NSB):
                units.append(("sb", i, [s]))
                s += 1
            for j in range(NP):
                units.append(("gla", j, [s, s + 1]))
                s += 2
            units.append(("sgu", 0, list(range(s, s + 1 + NG // 2))))
            wsn = {"n": 0}

            def alloc(k):
                r = []
                for _ in range(k):
                    r.append(wsn["n"] % 2)
                    wsn["n"] += 1
                return r
            def exchange(k):
                grp = ccg[k]
                n = len(grp)
                src = mgd[grp[0]:grp[0] + n].rearrange("h p s -> (h p) s")
                op("pool", lambda e: e.collective_compute("AllGather", ALU.bypass, replica_groups=rgroups, ins=[src.opt()], outs=[mga[k].opt()]),
                   reads=[("MG", li, tb) for li in grp for tb in range(NTB)], writes=[("MGA", k)], dma="cc")

            for ui, (kind, idx, dsl) in enumerate(units):
                if kind == "sb":
                    if idx == 0:
                        sbws = [alloc(1)[0] for _ in range(NSB)]
                        load_ws(sbws[0], win_view(l, dsl[0]))
                        for f in sb_proj_items(0, sbws[0]):
                            f()
                    bg = []
                    if idx + 1 < NSB:
                        load_ws(sbws[idx + 1], win_view(l, dsl[0] + 1))
                        bg = sb_proj_items(idx + 1, sbws[idx + 1])
                    sb_attn(l, idx, sbws[idx], bg)
                elif kind == "gla":
                    ws = alloc(2)
                    load_ws(ws[0], win_view(l, dsl[0]))
                    load_ws(ws[1], win_view(l, dsl[1]))
                    if split and idx == 0:
                        exchange(0)
                    gla_pair(l, idx, ws[0], ws[1], idx == 0)
                else:
                    for k in range(NG // 2):
                        ws = alloc(2)
                        load_ws(ws[0], win_view(l, dsl[0]))
                        load_ws(ws[1], win_view(l, dsl[1 + k]))
                        if split and k == 0:
                            exchange(1)
                        sgu_unit(l, ws[0], ws[1], [2 * k, 2 * k + 1])
                    if split:
                        exchange(2)

        def out_proj(l, xsrc, skey, xdst, dkey):
            if not split:
                for m in range(16):
                    op("sp", lambda e, m=m: e.dma_start(out=H[:, m, :], in_=mgd[m]), reads=[("MG", m, tb) for tb in range(NTB)],
                       writes=[("H", m, tb) for tb in range(NTB)], dma=True)
            else:
                jj = 0
                for k, grp in enumerate(ccg):
                    for r in range(2):
                        for idx in range(len(grp)):
                            row = (r * len(grp) + idx) * 128
                            op("sp", lambda e, jj=jj, k=k, row=row: e.dma_start(out=H[:, jj, :], in_=mga[k][row:row + 128, :]),
                               reads=[("MGA", k)], writes=[("H", jj, tb) for tb in range(NTB)], dma=True)
                            jj += 1
            if stop == "mixload":
                for c in range(NCH):
                    out_toks.append(op("sp", lambda e, c=c: e.dma_start(out=hdbg[c], in_=H[:, c, :]), reads=[("H", c, tb) for tb in range(NTB)],
                                       writes=[("hdbg", c)], dma=True))
                return
            wv = w_out[l].rearrange("(c p) n -> p c n", p=128)
            for cs in range(4):
                slot = cs % 2
                load_ws(slot, wv[:, :, cs * 512:(cs + 1) * 512])
                for tb in range(NTB):
                    for dcp in range(2):
                        T_, Tk = rslab()
                        d0 = cs * 4 + 2 * dcp
                        sv = xsrc.rearrange("(c p) s -> p c s", p=128)
                        dv = xdst.rearrange("(c p) s -> p c s", p=128)
                        op("sp", lambda e, T_=T_, d0=d0, tb=tb, sv=sv: e.dma_start(out=T_[:, :, :], in_=sv[:, d0:d0 + 2, tb * TB:(tb + 1) * TB]),
                           reads=[(skey, d0, tb), (skey, d0 + 1, tb)], writes=Tk, dma=True)
                        for j in range(2):
                            pi = proj_fm(slot, (2 * dcp + j) * 128, tb)
                            op("dve", lambda e, T_=T_, pi=pi, j=j: e.tensor_tensor(out=T_[:, j, :], in0=P[pi][:], in1=T_[:, j, :], op=ALU.add),
                               reads=[("P", pi)] + Tk, writes=Tk)
                        op("act", lambda e, T_=T_, d0=d0, tb=tb, dv=dv: e.dma_start(out=dv[:, d0:d0 + 2, tb * TB:(tb + 1) * TB], in_=T_[:, :, :]),
                           reads=Tk, writes=[(dkey, d0, tb), (dkey, d0 + 1, tb)], dma=True)

        def xattn(l, xsrc, skey, xdst, dkey):

            def mem_dst(c, t0, w, tb):
                return H[:, c, 0:NM], [("H", c, 0)]
            norm_phase(memT, "memT", voff["nmem"] + 16 * l, NM, mem_dst)
            kvv = w_xkv[l].rearrange("(c p) n -> p c n", p=128)
            load_ws(0, kvv[:, :, 0:512])
            load_ws(1, kvv[:, :, 512:1024])
            mkeys = [("H", c, 0) for c in range(NCH)]
            for h in range(4):
                pi = nxt("p", 2)

                def fk(e, pi=pi, h=h):
                    ins = None
                    for c in range(NCH):
                        ins = e.matmul(P[pi][:, 0:NM], WS[0][:, c, h * 128:(h + 1) * 128], H[:, c, 0:NM], start=(c == 0), stop=(c == NCH - 1))
                    return ins
                op("pe", fk, reads=[("WS", 0)] + mkeys, writes=[("P", pi)])
                op("act", lambda e, pi=pi, h=h: e.activation(out=kxT[:, h, :], in_=P[pi][:, 0:NM], func=AF.Copy), reads=[("P", pi)], writes=K_kxT)
            for mb in range(2):
                pi = nxt("p", 2)

                def fv(e, pi=pi, mb=mb):
                    ins = None
                    for c in range(NCH):
                        ins = e.matmul(P[pi][:], H[:, c, mb * 128:(mb + 1) * 128], WS[1][:, c, :], start=(c == 0), stop=(c == NCH - 1))
                    return ins
                op("pe", fv, reads=[("WS", 1)] + mkeys, writes=[("P", pi)])
                op("act", lambda e, pi=pi, mb=mb: e.activation(out=vx[:, mb, :], in_=P[pi][:], func=AF.Copy), reads=[("P", pi)], writes=K_vx)
            norm_phase(xsrc, skey, voff["nxa"] + 16 * l, S, h_dst)
            load_ws(0, w_xq[l].rearrange("(c p) n -> p c n", p=128))
            WO = WS[1][:].rearrange("p c n -> p (c n)").rearrange("p (h n) -> p h n", h=4)
            op("pool", lambda e: e.dma_start(out=WO, in_=w_xo[l].rearrange("(h p) n -> p h n", p=128)), writes=[("WS", 1)], dma=True)
            scale = 128.0 ** -0.5
            sv = xsrc.rearrange("(c p) s -> p c s", p=128)
            dv = xdst.rearrange("(c p) s -> p c s", p=128)
            steps = [(tb, h) for tb in range(NTB) for h in range(4)]
            st = {}

            def stA(s_):
                tb, h = steps[s_]
                pi = proj_fm(0, h * 128, tb)
                qi = s_ % 2
                op("act", lambda e: e.activation(out=qx[qi][:], in_=P[pi][:], func=AF.Copy, scale=scale), reads=[("P", pi)], writes=[K_qx[qi]])

            def stB(s_):
                tb, h = steps[s_]
                qi = s_ % 2
                for mb in range(2):
                    pti = (s_ % 2) * 2 + mb
                    op("pe", lambda e, mb=mb: e.matmul(Z[mb][:], kxT[:, h, mb * 128:(mb + 1) * 128], qx[qi][:], start=True, stop=True),
                       reads=K_kxT + [K_qx[qi]], writes=[("Z", mb)])
                    op("act", lambda e, mb=mb, pti=pti: e.activation(out=pT[pti][:], in_=Z[mb][:], func=AF.Exp), reads=[("Z", mb)], writes=[K_pT[pti]])

            def stC(s_):
                tb, h = steps[s_]
                b_ = s_ % 2
                pk = [K_pT[b_ * 2], K_pT[b_ * 2 + 1]]

                def fden(e):
                    ins = None
                    for mb in range(2):
                        ins = e.matmul(C[b_][:], ones_b[:], pT[b_ * 2 + mb][:], start=(mb == 0), stop=(mb == 1))
                    return ins
                op("pe", fden, reads=["ones_b"] + pk, writes=[("C", b_)])

                def fnum(e):
                    ins = None
                    for mb in range(2):
                        ins = e.matmul(O[b_][:], vx[:, mb, h * 128:(h + 1) * 128], pT[b_ * 2 + mb][:], start=(mb == 0), stop=(mb == 1))
                    return ins
                op("pe", fnum, reads=K_vx + pk, writes=[("O", b_)])
                a_ = nxt("w", NW)
                st[s_] = a_
                op("act", lambda e: e.activation(out=wk[a_][:], in_=C[b_][:], func=AF.Ln), reads=[("C", b_)], writes=[("wk", a_)])
                op("act", lambda e: e.activation(out=wk[a_][:], in_=wk[a_][:], func=AF.Exp, scale=-1.0), reads=[("wk", a_)], writes=[("wk", a_)])

            def stD(s_):
                tb, h = steps[s_]
                b_ = s_ % 2
                a_ = st[s_]
                op("dve", lambda e: e.tensor_tensor(out=oxT[:, h, :], in0=O[b_][:], in1=wk[a_][:], op=ALU.mult),
                   reads=[("O", b_), ("wk", a_)], writes=[("kT", 0, h)])
                if h == 3:
                    stE(tb)

            def stE(tb):
                for d2 in range(NCH // 2):
                    T_, Tk = rslab()
                    d0 = 2 * d2
                    op("sp", lambda e, T_=T_, d0=d0: e.dma_start(out=T_[:, :, :], in_=sv[:, d0:d0 + 2, tb * TB:(tb + 1) * TB]),
                       reads=[(skey, d0, tb), (skey, d0 + 1, tb)], writes=Tk, dma=True)
                    for j in range(2):
                        dch = d0 + j
                        pi = nxt("p", 2)

                        def fo(e, pi=pi, dch=dch):
                            ins = None
                            for h in range(4):
                                ins = e.matmul(P[pi][:], WO[:, h, dch * 128:(dch + 1) * 128], oxT[:, h, :], start=(h == 0), stop=(h == 3))
                            return ins
                        op("pe", fo, reads=[("WS", 1)] + [("kT", 0, h) for h in range(4)], writes=[("P", pi)])
                        op("dve", lambda e, T_=T_, pi=pi, j=j: e.tensor_tensor(out=T_[:, j, :], in0=P[pi][:], in1=T_[:, j, :], op=ALU.add),
                           reads=[("P", pi)] + Tk, writes=Tk)
                    op("act", lambda e, T_=T_, d0=d0: e.dma_start(out=dv[:, d0:d0 + 2, tb * TB:(tb + 1) * TB], in_=T_[:, :, :]),
                       reads=Tk, writes=[(dkey, d0, tb), (dkey, d0 + 1, tb)], dma=True)

            ns = len(steps)
            for s_ in range(ns + 3):
                if s_ < ns:
                    stA(s_)
                if 0 <= s_ - 1 < ns:
                    stB(s_ - 1)
                if 0 <= s_ - 2 < ns:
                    stC(s_ - 2)
                if 0 <= s_ - 3 < ns:
                    stD(s_ - 3)

        out_toks = []
        consts()
        cur, ckey = xT, "xT"
        done = False
        for l in range(nlayers):
            norm_phase(cur, ckey, voff["nmix"] + 16 * l, S, h_dst)
            if stop == "norm1":
                for c in range(NCH):
                    out_toks.append(op("sp", lambda e, c=c: e.dma_start(out=hdbg[c], in_=H[:, c, :]), reads=[("H", c, tb) for tb in range(NTB)],
                                       writes=[("hdbg", c)], dma=True))
                done = True
                break
            mixer(l)
            if stop == "mixer":
                done = True
                break
            out_proj(l, cur, ckey, xA, "xA")
            if stop in ("outproj", "mixload"):
                done = True
                break
            xattn(l, xA, "xA", xB, "xB")
            cur, ckey = xB, "xB"
        if not done:
            norm_phase(cur, ckey, voff["fin"], S, None, final=True)
        for q in Sched.QUEUES:
            for i in range(Sched.NSLOT):
                g = sc.dgen[q][i]
                if g > 0:
                    out_toks.append((("d", q, i), 16 * g))
        sc.final_wait("sp", out_toks)
        with nc.Block() as block:
            sc.emit(block)
    return nc, sc


_CACHE = {}


def kernel(**inputs):
    maps = pack_inputs(inputs, SPLIT)
    if "nc" not in _CACHE:
        _CACHE["nc"] = build_program(SPLIT)[0]
    nc = _CACHE["nc"]
    res = run_bass_kernel_spmd(nc, maps, core_ids=list(range(N_CORES)))
    out = np.empty((4, S, D), np.float32)
    for b in range(4):
        out[b] = np.asarray(res.results[2 * b]["yT"]).T
    return out
```
